# Optimizing a Trainium2 kernel written in Bass

```python
import jax, jax.numpy as jnp
from jax import lax
import numpy as np

D_MODEL = 1024
BATCH = 8
SEQ = 8192
DEPTH = 1
DEC_BATCH = 32
DEC_SEQ = 64
PAST_LEN = 4096

CHUNK = 64
N_HEADS = 16
HEAD_DIM = 64
D_ATTN = N_HEADS * HEAD_DIM
D_CONV = D_MODEL
CONV_K = 31
D_FF = 2816
FFN_K = 3
PLE_DIM = 256
Q_BLOCK = 128
LN_EPS = 1e-5
ALPHA = (2.0 * DEPTH) ** 0.25
BETA = (8.0 * DEPTH) ** -0.25

IN_SPLITS = (D_CONV, 2 * D_CONV, 2 * D_CONV + D_ATTN, 2 * D_CONV + 2 * D_ATTN,
             2 * D_CONV + 3 * D_ATTN, 2 * D_CONV + 3 * D_ATTN + N_HEADS,
             2 * D_CONV + 3 * D_ATTN + N_HEADS + D_MODEL)
N_IN = 2 * D_CONV + 3 * D_ATTN + N_HEADS + 2 * D_MODEL

kernel_name = 'streaming_conformer_fox_hybrid_step'


def layer_norm(x, g, b):
    xf = x.astype(jnp.float32)
    mu = jnp.mean(xf, axis=-1, keepdims=True)
    var = jnp.mean(jnp.square(xf - mu), axis=-1, keepdims=True)
    y = (xf - mu) * lax.rsqrt(var + LN_EPS)
    return (y * g.astype(jnp.float32) + b.astype(jnp.float32)).astype(x.dtype)


def causal_dwconv(hist, u, w, b):
    xp = jnp.concatenate([hist.astype(u.dtype), u], axis=1)
    c = u.shape[-1]
    y = lax.conv_general_dilated(xp, w.astype(u.dtype)[:, None, :], window_strides=(1,),
                                 padding='VALID', dimension_numbers=('NWC', 'WIO', 'NWC'),
                                 feature_group_count=c)
    return y + b.astype(u.dtype), xp[:, xp.shape[1] - hist.shape[1]:]


def fox_block(q, cq, q_pos, k, v, ck, k_pos):
    s = jnp.einsum('bqhd,bkhd->bhqk', q, k, preferred_element_type=jnp.float32) * (HEAD_DIM ** -0.5)
    decay = jnp.transpose(cq, (0, 2, 1))[..., :, None] - jnp.transpose(ck, (0, 2, 1))[..., None, :]
    mask = k_pos[None, :] <= q_pos[:, None]
    p = jax.nn.softmax(jnp.where(mask, s + decay, -jnp.inf), axis=-1)
    return jnp.einsum('bhqk,bkhd->bqhd', p.astype(v.dtype), v)


def fox_prompt(q, k, v, logf):
    b, s = q.shape[0], q.shape[1]
    c = jnp.cumsum(logf.astype(jnp.float32), axis=1)
    nb = s // Q_BLOCK
    qb = q.reshape(b, nb, Q_BLOCK, N_HEADS, HEAD_DIM).transpose(1, 0, 2, 3, 4)
    cb = c.reshape(b, nb, Q_BLOCK, N_HEADS).transpose(1, 0, 2, 3)
    starts = jnp.arange(nb, dtype=jnp.int32) * Q_BLOCK
    k_pos = jnp.arange(s, dtype=jnp.int32)

    def step(args):
        qi, ci, st = args
        return fox_block(qi, ci, st + jnp.arange(Q_BLOCK, dtype=jnp.int32), k, v, c, k_pos)

    o = lax.map(step, (qb, cb, starts))
    return o.transpose(1, 0, 2, 3, 4).reshape(b, s, N_HEADS, HEAD_DIM)


def fox_sample(q, k_new, v_new, logf_new, k_hist, v_hist, logf_hist):
    p_len, t = k_hist.shape[1], q.shape[1]
    k = jnp.concatenate([k_hist.astype(k_new.dtype), k_new], axis=1)
    v = jnp.concatenate([v_hist.astype(v_new.dtype), v_new], axis=1)
    c = jnp.cumsum(jnp.concatenate([logf_hist.astype(jnp.float32), logf_new.astype(jnp.float32)], axis=1), axis=1)
    q_pos = p_len + jnp.arange(t, dtype=jnp.int32)
    k_pos = jnp.arange(p_len + t, dtype=jnp.int32)
    return fox_block(q, c[:, p_len:], q_pos, k, v, c, k_pos)


def token_mixer(h, conv_hist, attend, w_in, b_f, conv_w, conv_b, conv_g, conv_beta,
                w_conv_out, w_attn_out, w_o):
    bsz, t = h.shape[0], h.shape[1]
    z = h @ w_in
    ga_, gg_, q, k, v, fl, g_conv, g_attn = jnp.split(z, IN_SPLITS, axis=-1)
    u = ga_ * jax.nn.sigmoid(gg_)
    uc, conv_new = causal_dwconv(conv_hist, u, conv_w, conv_b)
    conv_out = jax.nn.silu(layer_norm(uc, conv_g, conv_beta)) @ w_conv_out
    q = q.reshape(bsz, t, N_HEADS, HEAD_DIM)
    k = k.reshape(bsz, t, N_HEADS, HEAD_DIM)
    v = v.reshape(bsz, t, N_HEADS, HEAD_DIM)
    logf = jax.nn.log_sigmoid(fl.astype(jnp.float32) + b_f.astype(jnp.float32))
    o = attend(q, k, v, logf)
    attn_out = o.reshape(bsz, t, D_ATTN) @ w_attn_out
    mix = jax.nn.sigmoid(g_conv) * conv_out + jax.nn.sigmoid(g_attn) * attn_out
    return mix @ w_o, conv_new, k, v, logf


def conv_ffn(x, hist, w_up, dw_w, dw_b, w_down):
    a, b = jnp.split(x @ w_up, 2, axis=-1)
    ac, new_hist = causal_dwconv(hist, a, dw_w, dw_b)
    return (jax.nn.silu(ac) * b) @ w_down, new_hist


def trunk_layer(x, p, conv_hist, ffn_hist, attend, i, w_in, b_f, conv_dw_w, conv_dw_b, conv_ln_g,
                conv_ln_b, w_conv_out, w_attn_out, w_o, ln1_g, ln1_b, w_ffn_up, ffn_dw_w, ffn_dw_b,
                w_ffn_down, ln2_g, ln2_b, w_ple, w_ple_gate):
    mix, conv_new, k, v, logf = token_mixer(x, conv_hist, attend, w_in[i], b_f[i], conv_dw_w[i],
                                            conv_dw_b[i], conv_ln_g[i], conv_ln_b[i],
                                            w_conv_out[i], w_attn_out[i], w_o[i])
    x = layer_norm(ALPHA * x + mix, ln1_g[i], ln1_b[i])
    f, ffn_new = conv_ffn(x, ffn_hist, w_ffn_up[i], ffn_dw_w[i], ffn_dw_b[i], w_ffn_down[i])
    ple = jax.nn.sigmoid(x @ w_ple_gate[i]) * (p.astype(x.dtype) @ w_ple[i])
    x = layer_norm(ALPHA * x + f + ple, ln2_g[i], ln2_b[i])
    return x, k, v, logf, conv_new, ffn_new


def setup_inputs(seed: int = 0) -> dict:
    key = jax.random.key(seed)
    ks = jax.random.split(key, 32)

    def nrm(k, shape, s):
        return jax.random.normal(k, shape, jnp.float32) * s

    b_f = jnp.linspace(1.0, 6.0, N_HEADS, dtype=jnp.float32)[None, :] + nrm(ks[0], (DEPTH, N_HEADS), 0.1)
    return {
        'x_prompt': nrm(ks[1], (BATCH, SEQ, D_MODEL), 1.0),
        'x_sample': nrm(ks[2], (DEC_BATCH, DEC_SEQ, D_MODEL), 1.0),
        'cache_k': nrm(ks[3], (DEPTH, DEC_BATCH, PAST_LEN, N_HEADS, HEAD_DIM), 1.0),
        'cache_v': nrm(ks[4], (DEPTH, DEC_BATCH, PAST_LEN, N_HEADS, HEAD_DIM), 1.0),
        'cache_logf': jax.nn.log_sigmoid(b_f[:, None, None, :] + nrm(ks[5], (DEPTH, DEC_BATCH, PAST_LEN, N_HEADS), 1.0)),
        'state_conv': nrm(ks[6], (DEPTH, DEC_BATCH, CONV_K - 1, D_CONV), 0.5),
        'state_ffn_conv': nrm(ks[7], (DEPTH, DEC_BATCH, FFN_K - 1, D_FF), 1.0),
        'p_prompt': nrm(ks[8], (DEPTH, BATCH, SEQ, PLE_DIM), 1.0),
        'p_sample': nrm(ks[9], (DEPTH, DEC_BATCH, DEC_SEQ, PLE_DIM), 1.0),
        'ln0_g': 1.0 + nrm(ks[10], (D_MODEL,), 0.02),
        'ln0_b': nrm(ks[11], (D_MODEL,), 0.02),
        'w_in': nrm(ks[12], (DEPTH, D_MODEL, N_IN), D_MODEL ** -0.5),
        'b_f': b_f,
        'conv_dw_w': nrm(ks[13], (DEPTH, CONV_K, D_CONV), CONV_K ** -0.5),
        'conv_dw_b': nrm(ks[14], (DEPTH, D_CONV), 0.02),
        'conv_ln_g': 1.0 + nrm(ks[15], (DEPTH, D_CONV), 0.02),
        'conv_ln_b': nrm(ks[16], (DEPTH, D_CONV), 0.02),
        'w_conv_out': nrm(ks[17], (DEPTH, D_CONV, D_MODEL), BETA * D_CONV ** -0.5),
        'w_attn_out': nrm(ks[18], (DEPTH, D_ATTN, D_MODEL), BETA * D_ATTN ** -0.5),
        'w_o': nrm(ks[19], (DEPTH, D_MODEL, D_MODEL), BETA * D_MODEL ** -0.5),
        'ln1_g': 1.0 + nrm(ks[20], (DEPTH, D_MODEL), 0.02),
        'ln1_b': nrm(ks[21], (DEPTH, D_MODEL), 0.02),
        'w_ffn_up': nrm(ks[22], (DEPTH, D_MODEL, 2 * D_FF), D_MODEL ** -0.5),
        'ffn_dw_w': nrm(ks[23], (DEPTH, FFN_K, D_FF), FFN_K ** -0.5),
        'ffn_dw_b': nrm(ks[24], (DEPTH, D_FF), 0.02),
        'w_ffn_down': nrm(ks[25], (DEPTH, D_FF, D_MODEL), BETA * D_FF ** -0.5),
        'ln2_g': 1.0 + nrm(ks[26], (DEPTH, D_MODEL), 0.02),
        'ln2_b': nrm(ks[27], (DEPTH, D_MODEL), 0.02),
        'w_ple': nrm(ks[28], (DEPTH, PLE_DIM, D_MODEL), PLE_DIM ** -0.5),
        'w_ple_gate': nrm(ks[29], (DEPTH, D_MODEL, D_MODEL), D_MODEL ** -0.5),
    }


def reference(x_prompt, x_sample, cache_k, cache_v, cache_logf, state_conv, state_ffn_conv,
              p_prompt, p_sample, ln0_g, ln0_b, w_in, b_f, conv_dw_w, conv_dw_b, conv_ln_g,
              conv_ln_b, w_conv_out, w_attn_out, w_o, ln1_g, ln1_b, w_ffn_up, ffn_dw_w, ffn_dw_b,
              w_ffn_down, ln2_g, ln2_b, w_ple, w_ple_gate):
    weights = (w_in, b_f, conv_dw_w, conv_dw_b, conv_ln_g, conv_ln_b, w_conv_out, w_attn_out, w_o,
               ln1_g, ln1_b, w_ffn_up, ffn_dw_w, ffn_dw_b, w_ffn_down, ln2_g, ln2_b, w_ple, w_ple_gate)

    xp = layer_norm(x_prompt, ln0_g, ln0_b)
    bp = x_prompt.shape[0]
    kp, vp, lp, cp, fp = [], [], [], [], []
    for i in range(DEPTH):
        conv_h0 = jnp.zeros((bp, CONV_K - 1, D_CONV), xp.dtype)
        ffn_h0 = jnp.zeros((bp, FFN_K - 1, D_FF), xp.dtype)
        xp, k_i, v_i, l_i, c_i, f_i = trunk_layer(xp, p_prompt[i], conv_h0, ffn_h0, fox_prompt, i, *weights)
        kp.append(k_i); vp.append(v_i); lp.append(l_i); cp.append(c_i); fp.append(f_i)

    xs = layer_norm(x_sample, ln0_g, ln0_b)
    ks_, vs_, ls_, cs_, fs_ = [], [], [], [], []
    for i in range(DEPTH):
        def attend(q, k, v, logf, i=i):
            return fox_sample(q, k, v, logf, cache_k[i], cache_v[i], cache_logf[i])
        xs, k_i, v_i, l_i, c_i, f_i = trunk_layer(xs, p_sample[i], state_conv[i], state_ffn_conv[i], attend, i, *weights)
        ks_.append(k_i); vs_.append(v_i); ls_.append(l_i); cs_.append(c_i); fs_.append(f_i)

    return (xp, xs,
            jnp.stack(kp), jnp.stack(vp), jnp.stack(lp), jnp.stack(cp), jnp.stack(fp),
            jnp.stack(ks_), jnp.stack(vs_), jnp.stack(ls_), jnp.stack(cs_), jnp.stack(fs_))
```

```python
import numpy as np
import concourse.bass as bass
import concourse.mybir as mybir
from concourse.bass_utils import run_bass_kernel_spmd

F32 = mybir.dt.float32
BF16 = mybir.dt.bfloat16
AF = mybir.ActivationFunctionType
ALU = mybir.AluOpType

ENGS = ("pe", "act", "dve", "pool", "sp")

NCORES = 8
D = 1024
NH = 16
DFF = 2816
NFC = 22
PLE = 256
SEQ = 8192
PAST = 4096
DSEQ = 64
NSTR = 4
NTILES = 16
N_IN = 7184
ALPHA = float(2.0 ** 0.25)
EPS = 1e-5
NEG = -30000.0
NWB = 2
NKV = 3


class DmaSem:
    def __init__(self, handle):
        self.h = handle
        self.count = 0


class Op:
    __slots__ = ("eng", "fn", "deps", "need_inc", "seq", "idx", "dsem", "dval", "is_dma")


class Prog:
    def __init__(self, nc):
        self.nc = nc
        self.streams = {e: [] for e in ENGS}
        self.last_w = {}
        self.readers = {}
        self.scoped = set()
        self.inherit = []
        self.touched = set()

    @staticmethod
    def _root(k):
        while isinstance(k, tuple):
            k = k[0]
        return k

    def end_scope(self):
        last = {}
        dmas = {}
        keys = [k for k in list(self.last_w.keys()) + list(self.readers.keys()) if self._root(k) in self.scoped]
        ops = []
        for k in set(keys):
            w = self.last_w.pop(k, None)
            if w is not None:
                ops.append(w)
            ops.extend(self.readers.pop(k, ()))
        ops.extend(self.inherit)
        for o in ops:
            if o.is_dma:
                key = id(o.dsem)
                if key not in dmas or dmas[key].dval < o.dval:
                    dmas[key] = o
            else:
                if o.eng not in last or last[o.eng].idx < o.idx:
                    last[o.eng] = o
        self.inherit = list(last.values()) + list(dmas.values())
        self.touched = set()

    def op(self, eng, fn, reads=(), writes=(), dsem=None):
        o = Op()
        o.eng = eng
        o.fn = fn
        o.need_inc = False
        o.seq = None
        o.is_dma = dsem is not None
        o.dsem = dsem
        if dsem is not None:
            dsem.count += 16
            o.dval = dsem.count
        else:
            o.dval = None
        deps = {}
        for s in reads:
            w = self.last_w.get(s)
            if w is not None:
                deps[id(w)] = w
        for s in writes:
            w = self.last_w.get(s)
            if w is not None and (w.is_dma or w.eng != eng or o.is_dma):
                deps[id(w)] = w
            for r in self.readers.get(s, ()):
                if r.is_dma or r.eng != eng or o.is_dma:
                    deps[id(r)] = r
            if self._root(s) in self.scoped and s not in self.touched:
                self.touched.add(s)
                for r in self.inherit:
                    if r.is_dma or r.eng != eng or o.is_dma:
                        deps[id(r)] = r
        o.deps = []
        for d in deps.values():
            if d.is_dma:
                v = d.dsem.count - (16 if d.dsem is dsem else 0)
                o.deps.append((d, v))
            elif not (d.eng == "pe" and eng == "pe" and not o.is_dma):
                d.need_inc = True
                o.deps.append((d, None))
        for s in reads:
            self.readers.setdefault(s, []).append(o)
        for s in writes:
            self.last_w[s] = o
            self.readers[s] = []
        o.idx = len(self.streams[eng])
        self.streams[eng].append(o)
        return o

    def emit(self, final_sems=()):
        nc = self.nc
        sems = {e: nc.alloc_semaphore(name=f"s_{e}") for e in ENGS}
        for e in ENGS:
            c = 0
            for o in self.streams[e]:
                if o.need_inc and not o.is_dma:
                    c += 1
                    o.seq = c
        with nc.Block() as block:
            def body(e):
                def run(engh):
                    waited = {}
                    for o in self.streams[e]:
                        for d, dv in o.deps:
                            if d.is_dma:
                                key = ("d", id(d.dsem))
                                val = dv
                                semh = d.dsem.h
                            else:
                                key = ("e", d.eng)
                                val = d.seq
                                semh = sems[d.eng]
                            if waited.get(key, 0) >= val:
                                continue
                            waited[key] = val
                            engh.wait_ge(semh, val)
                        ins = o.fn(engh)
                        if o.is_dma:
                            ins.then_inc(o.dsem.h, 16)
                        elif o.need_inc:
                            ins.then_inc(sems[e], 1)
                    if e == "sp":
                        for ds in final_sems:
                            if ds.count > 0:
                                engh.wait_ge(ds.h, ds.count)
                return run
            block.tensor(body("pe"))
            block.scalar(body("act"))
            block.vector(body("dve"))
            block.gpsimd(body("pool"))
            block.sync(body("sp"))


def build_nc(ntiles=NTILES, do_sample=True, debug=False):
    nc = bass.Bass("TRN2", target_bir_lowering=False)
    P = Prog(nc)
    allsems = []

    def tap(name, ap, reads):
        if not debug:
            return
        shape = list(ap.shape)
        d = nc.dram_tensor(name, shape, ap.dtype, kind="ExternalOutput").ap()
        P.op("sp", lambda e: e.dma_start(out=d, in_=ap), reads=reads, dsem=newsem("t_" + name))

    def newsem(name):
        s = DmaSem(nc.alloc_semaphore(name=name))
        allsems.append(s)
        return s

    def din(name, shape):
        return nc.dram_tensor(name, shape, F32, kind="ExternalInput").ap()

    def dout(name, shape):
        return nc.dram_tensor(name, shape, F32, kind="ExternalOutput").ap()

    x_p = din("x_p", [SEQ, D]); p_p = din("p_p", [SEQ, PLE])
    x_s = din("x_s", [NSTR * DSEQ, D]); p_s = din("p_s", [NSTR * DSEQ, PLE])
    cache_k = din("cache_k", [NSTR, PAST, D]); cache_v = din("cache_v", [NSTR, PAST, D])
    cache_lf = din("cache_lf", [NSTR, PAST, NH])
    st_conv = din("st_conv", [NSTR, 30, D]); st_ffn = din("st_ffn", [NSTR, 2, DFF])
    ln0_g = din("ln0_g", [D]); ln0_b = din("ln0_b", [D])
    w_in = din("w_in", [D, N_IN]); b_f = din("b_f", [NH])
    conv_dw_w = din("conv_dw_w", [31, D]); conv_dw_b = din("conv_dw_b", [D])
    conv_ln_g = din("conv_ln_g", [D]); conv_ln_b = din("conv_ln_b", [D])
    w_conv_out = din("w_conv_out", [D, D]); w_attn_out = din("w_attn_out", [D, D]); w_o = din("w_o", [D, D])
    ln1_g = din("ln1_g", [D]); ln1_b = din("ln1_b", [D])
    w_ffn_up = din("w_ffn_up", [D, 2 * DFF]); ffn_dw_w = din("ffn_dw_w", [3, DFF]); ffn_dw_b = din("ffn_dw_b", [DFF])
    w_ffn_down = din("w_ffn_down", [DFF, D])
    ln2_g = din("ln2_g", [D]); ln2_b = din("ln2_b", [D])
    w_ple = din("w_ple", [PLE, D]); w_ple_gate = din("w_ple_gate", [D, D])

    y_p = dout("y_p", [SEQ, D]); y_s = dout("y_s", [NSTR * DSEQ, D])
    k_p = dout("k_p", [SEQ, D]); v_p = dout("v_p", [SEQ, D]); lf_p = dout("lf_p", [SEQ, NH])
    cv_p = dout("cv_p", [30, D]); ff_p = dout("ff_p", [2, DFF])
    k_s = dout("k_s", [NSTR * DSEQ, D]); v_s = dout("v_s", [NSTR * DSEQ, D]); lf_s = dout("lf_s", [NSTR * DSEQ, NH])
    cv_s = dout("cv_s", [NSTR, 30, D]); ff_s = dout("ff_s", [NSTR, 2, DFF])

    kt_scr = nc.dram_tensor("kt_scr", [8, NTILES, 80, 1024], BF16).ap()
    v_scr = nc.dram_tensor("v_scr", [8, NTILES, 128, 768], BF16).ap()

    def sbt(name, shape, dt):
        return nc.alloc_sbuf_tensor(name, shape, dt)

    ident_f = sbt("ident_f", [128, 128], F32)
    ident_b = sbt("ident_b", [128, 128], BF16)
    maskb = sbt("maskb", [128, 128], BF16)
    onesm = sbt("onesm", [128, 128], BF16)
    onesb = sbt("onesb", [128, 64], BF16)
    zerob = sbt("zerob", [128, 128], BF16)
    ones16 = sbt("ones16", [16, 512], F32)
    ones3 = sbt("ones3", [3, 1024], BF16)
    cvec = sbt("cvec", [128, 8, 34], F32)
    fvec = sbt("fvec", [128, NFC, 4], F32)
    wfl = sbt("wfl", [128, 8, 16], BF16)
    nbf = sbt("nbf", [16, 1], F32)
    carry = sbt("carry", [16, 1], F32)
    hend = sbt("hend", [16, 4], F32)
    gbuf = sbt("gbuf", [128, 2, 1024], F32)
    wbuf = [sbt(f"wbuf{i}", [128, 8192], BF16) for i in range(NWB)]
    xres = sbt("xres", [128, 4, 1024], F32)
    xT = sbt("xT", [128, 8, 512], BF16)
    oT = sbt("oT", [128, 8, 512], BF16)
    gated_c = sbt("gated_c", [128, 8, 512], BF16)
    QT = sbt("QT", [128, 16, 512], BF16)
    KTs = sbt("KTs", [128, 8192], BF16)
    uhist = sbt("uhist", [128, 8, 4, 30], BF16)
    ahist = sbt("ahist", [128, NFC, 4, 2], F32)
    ARENA_W = 18944
    arena = sbt("arena", [128, ARENA_W], F32)
    psb = [nc.alloc_psum_tensor(f"psb{i}", [128, 512], F32) for i in range(8)]

    P.scoped.add("A")
    ar = {"off": 0, "n": 0}

    def ar_reset():
        ar["off"] = 0

    def alloc(shape, dt, parts=128):
        n = 1
        for s in shape[1:]:
            n *= s
        words = n if dt == F32 else (n + 1) // 2
        words = (words + 7) // 8 * 8
        off = ar["off"]
        assert off + words <= ARENA_W, ("arena overflow", off, words)
        ar["off"] = off + words
        v = arena[0:shape[0], off:off + words]
        if dt == BF16:
            v = v.bitcast(BF16)[:, 0:n]
        else:
            v = v[:, 0:n]
        if len(shape) > 2:
            names = " ".join(f"d{i}" for i in range(1, len(shape)))
            kw = {f"d{i}": shape[i] for i in range(1, len(shape))}
            v = v.rearrange(f"p ({names}) -> p {names}", **kw)
        ar["n"] += 1
        return v, ("A", ar["n"])

    bank_rr = [0]

    def nb():
        b = bank_rr[0] % 8
        bank_rr[0] += 1
        return b

    def PS(b):
        return ("ps", b)

    P.op("pool", lambda e: e.memset(ident_f[:], 1.0), writes=["ident_f"])
    P.op("pool", lambda e: e.affine_select(out=ident_f[:], in_=ident_f[:], pattern=[[1, 128]], compare_op=ALU.is_equal,
                                           fill=0.0, base=0, channel_multiplier=-1), reads=["ident_f"], writes=["ident_f"])
    P.op("pool", lambda e: e.tensor_copy(out=ident_b[:], in_=ident_f[:]), reads=["ident_f"], writes=["ident_b"])
    mask32, mask32_s = alloc([128, 128], F32)
    P.op("pool", lambda e: e.memset(mask32[:], 0.0), writes=[mask32_s])
    P.op("pool", lambda e: e.affine_select(out=mask32[:], in_=mask32[:], pattern=[[1, 128]], compare_op=ALU.is_ge,
                                           fill=NEG, base=0, channel_multiplier=-1), reads=[mask32_s], writes=[mask32_s])
    P.op("pool", lambda e: e.tensor_copy(out=maskb[:], in_=mask32[:]), reads=[mask32_s], writes=["maskb"])
    P.op("pool", lambda e: e.memset(onesm[:], 1.0 / 1024.0), writes=["onesm"])
    P.op("pool", lambda e: e.memset(onesb[:], 1.0), writes=["onesb"])
    P.op("pool", lambda e: e.memset(zerob[:], 0.0), writes=["zerob"])
    if debug:
        P.op("pool", lambda e: e.memset(oT[:, :, :], 7.0), writes=[("oT", g_, j_) for g_ in range(8) for j_ in range(2)])
    P.op("pool", lambda e: e.memset(ones16[:], 1.0), writes=["ones16"])
    P.op("pool", lambda e: e.memset(ones3[:], 1.0), writes=["ones3"])
    P.op("pool", lambda e: e.memset(carry[:], 0.0), writes=["carry"])
    P.op("pool", lambda e: e.memset(uhist[:], 0.0), writes=["uhist"])
    P.op("pool", lambda e: e.memset(ahist[:], 0.0), writes=["ahist"])
    P.op("pool", lambda e: e.memset(QT[64:96, :, :], 1.0), writes=["QTc"])
    P.op("pool", lambda e: e.memset(KTs[64:96, :], 0.0), writes=["KTc0"])
    s_c1 = newsem("c1")
    for g in range(8):
        P.op("sp", lambda e, g=g: e.dma_start(out=KTs[67:70, 1024 * g:1024 * (g + 1)], in_=ones3[:, :]),
             reads=["ones3", "KTc0"], writes=[("KTc1", g)], dsem=s_c1)
    KTC = ["KTc0"] + [("KTc1", g) for g in range(8)]

    vrows, vrows_s = alloc([34, 1024], F32, 34)
    frows, frows_s = alloc([4, DFF], F32, 4)
    bft, bft_s = alloc([16, 1], F32, 16)
    s_c2 = newsem("c2")
    vr = [(vrows_s, "r", i) for i in range(5)]
    P.op("sp", lambda e: e.dma_start(out=vrows[0:1, :], in_=conv_dw_b.rearrange("(o n) -> o n", o=1)), writes=[vr[0], vrows_s], dsem=s_c2)
    P.op("sp", lambda e: e.dma_start(out=vrows[1:2, :], in_=conv_ln_g.rearrange("(o n) -> o n", o=1)), writes=[vr[1]], dsem=s_c2)
    P.op("sp", lambda e: e.dma_start(out=vrows[2:3, :], in_=conv_ln_b.rearrange("(o n) -> o n", o=1)), writes=[vr[2]], dsem=s_c2)
    P.op("sp", lambda e: e.dma_start(out=vrows[3:34, :], in_=conv_dw_w), writes=[vr[3]], dsem=s_c2)
    P.op("sp", lambda e: e.dma_start(out=frows[0:3, :], in_=ffn_dw_w), writes=[vr[4], frows_s], dsem=s_c2)
    P.op("sp", lambda e: e.dma_start(out=frows[3:4, :], in_=ffn_dw_b.rearrange("(o n) -> o n", o=1)), writes=[(frows_s, "r5")], dsem=s_c2)
    P.op("sp", lambda e: e.dma_start(out=bft[:, :], in_=b_f.rearrange("(h o) -> h o", o=1)), writes=[(bft_s, "r6"), bft_s], dsem=s_c2)
    VR = vr + [(frows_s, "r5"), (bft_s, "r6"), vrows_s, frows_s, bft_s]
    P.op("dve", lambda e: e.tensor_scalar(out=nbf[:], in0=bft[:, :], scalar1=-1.0, scalar2=None, op0=ALU.mult),
         reads=VR, writes=["nbf"])
    for c in range(8):
        b = nb()
        P.op("pe", lambda e, c=c, b=b: e.transpose(out=psb[b][:, 0:34], in_=vrows[0:34, 128 * c:128 * (c + 1)], identity=ident_f[0:34, 0:34]),
             reads=VR + ["ident_f"], writes=[PS(b)])
        P.op("dve", lambda e, c=c, b=b: e.tensor_copy(out=cvec[:, c, :], in_=psb[b][:, 0:34]), writes=[PS(b), ("cvec", c)])
    for c in range(NFC):
        b = nb()
        P.op("pe", lambda e, c=c, b=b: e.transpose(out=psb[b][:, 0:4], in_=frows[0:4, 128 * c:128 * (c + 1)], identity=ident_f[0:4, 0:4]),
             reads=VR + ["ident_f"], writes=[PS(b)])
        P.op("dve", lambda e, c=c, b=b: e.tensor_copy(out=fvec[:, c, :], in_=psb[b][:, 0:4]), writes=[PS(b), ("fvec", c)])
    CVEC = [("cvec", c) for c in range(8)]
    FVEC = [("fvec", c) for c in range(NFC)]

    wgroups = {}

    def wsrc(W, r0, nk, c0, w):
        return W[r0:r0 + nk * 128, :].rearrange("(kc p) n -> p kc n", p=128)[:, :, c0:c0 + w]

    def wgroup(name, nk, ncols, srcs):
        scr = nc.dram_tensor("ws_" + name, [128, nk, ncols], BF16).ap()
        sem = newsem("wp_" + name)
        slots = []
        for j, (src, off, w) in enumerate(srcs):
            sl = ("wscr", name, j)
            slots.append(sl)
            P.op("pool", lambda e, src=src, off=off, w=w: e.dma_start(out=scr[:, :, off:off + w], in_=src), writes=[sl], dsem=sem)
        wgroups[name] = (scr, nk, ncols, slots)

    wgroup("FL", 8, 16, [(wsrc(w_in, 0, 8, 5120, 16), 0, 16)])
    wgroup("PLE", 2, 1024, [(wsrc(w_ple, 0, 2, 0, 1024), 0, 1024)])
    wgroup("A1", 8, 1024, [(wsrc(w_in, 0, 8, 0, 512), 0, 512), (wsrc(w_in, 0, 8, 1024, 512), 512, 512)])
    wgroup("A2", 8, 1024, [(wsrc(w_in, 0, 8, 512, 512), 0, 512), (wsrc(w_in, 0, 8, 1536, 512), 512, 512)])
    wgroup("Q", 8, 1024, [(wsrc(w_in, 0, 8, 2048, 1024), 0, 1024)])
    wgroup("K", 8, 1024, [(wsrc(w_in, 0, 8, 3072, 1024), 0, 1024)])
    wgroup("V", 8, 1024, [(wsrc(w_in, 0, 8, 4096, 1024), 0, 1024)])
    wgroup("GC", 8, 1024, [(wsrc(w_in, 0, 8, 5136, 1024), 0, 1024)])
    wgroup("CO", 8, 1024, [(wsrc(w_conv_out, 0, 8, 0, 1024), 0, 1024)])
    wgroup("GA", 8, 1024, [(wsrc(w_in, 0, 8, 6160, 1024), 0, 1024)])
    wgroup("AO", 8, 1024, [(wsrc(w_attn_out, 0, 8, 0, 1024), 0, 1024)])
    wgroup("O", 8, 1024, [(wsrc(w_o, 0, 8, 0, 1024), 0, 1024)])
    wgroup("PG", 8, 1024, [(wsrc(w_ple_gate, 0, 8, 0, 1024), 0, 1024)])
    for g in range(6):
        hw = 512 if g < 5 else 256
        wgroup(f"UP{g}", 8, 2 * hw, [(wsrc(w_ffn_up, 0, 8, 512 * g, hw), 0, hw), (wsrc(w_ffn_up, 0, 8, DFF + 512 * g, hw), hw, hw)])
    for n in range(2):
        for kh in range(2):
            wgroup(f"DN{kh}{n}", 11, 512, [(wsrc(w_ffn_down, kh * 11 * 128, 11, 512 * n, 512), 0, 512)])

    s_wres = newsem("wres")
    P.op("sp", lambda e: e.dma_start(out=wfl[:], in_=wgroups["FL"][0]), reads=wgroups["FL"][3], writes=["wfl"], dsem=s_wres)

    tile_seq = ["A1", "A2", "Q", "K", "V", "GC", "CO", "GA", "AO", "O", "PG", "PLE"] + [f"UP{g}" for g in range(6)] + ["DN00", "DN10", "DN01", "DN11"]
    ntot = ntiles + (1 if do_sample else 0)
    wseq = tile_seq * ntot
    wsem = [newsem(f"wl{i}") for i in range(NWB)]
    wst = {"pos": 0, "loaded": 0}

    def w_ensure(n):
        while wst["loaded"] <= n and wst["loaded"] < len(wseq):
            m = wst["loaded"]
            scr, nk, ncols, slots = wgroups[wseq[m]]
            b = m % NWB
            dst = wbuf[b][:, 0:nk * ncols].rearrange("p (k n) -> p k n", k=nk)
            P.op("sp", lambda e, dst=dst, scr=scr: e.dma_start(out=dst, in_=scr), reads=slots, writes=[("wb", b)], dsem=wsem[b])
            wst["loaded"] += 1

    def w_get(name, ahead=NWB - 1):
        n = wst["pos"]
        assert wseq[n] == name, (wseq[n], name)
        w_ensure(n + ahead)
        scr, nk, ncols, slots = wgroups[name]
        b = n % NWB
        wst["pos"] += 1
        return wbuf[b][:, 0:nk * ncols].rearrange("p (k n) -> p k n", k=nk), ("wb", b)

    s_x = [newsem(f"x{i}") for i in range(2)]
    s_g = newsem("g"); s_b = newsem("b")
    s_y = [newsem(f"y{i}") for i in range(4)]
    s_kvo = [newsem(f"kvo{i}") for i in range(2)]
    s_lf = newsem("lf")
    s_ktw = [newsem(f"ktw{i}") for i in range(8)]
    s_vw = [newsem(f"vw{i}") for i in range(8)]
    s_rq = newsem("rq"); s_rk = newsem("rk")
    s_kvK = [newsem(f"kvK{i}") for i in range(NKV)]
    s_kvV = [newsem(f"kvV{i}") for i in range(NKV)]
    s_p = newsem("p")
    s_st = [newsem(f"st{i}") for i in range(2)]
    s_vst = newsem("vst")

    def run_tile(kind, ti):
        prompt = kind == "p"
        Tt = 512 if prompt else 256
        nsub = Tt // 128
        nseg, L = (1, 512) if prompt else (4, 64)
        t0 = ti * 512
        last = prompt and ti == NTILES - 1
        need_state = last or not prompt
        x_src = x_p[t0:t0 + Tt, :] if prompt else x_s
        p_src = p_p[t0:t0 + Tt, :] if prompt else p_s
        y_dst = y_p[t0:t0 + Tt, :] if prompt else y_s
        k_dst = k_p[t0:t0 + Tt, :] if prompt else k_s
        v_dst = v_p[t0:t0 + Tt, :] if prompt else v_s
        lf_dst = lf_p[t0:t0 + Tt, :] if prompt else lf_s
        state_subs = [3] if prompt else [0, 1]

        def load_gb(g_ap, b_ap):
            P.op("sp", lambda e: e.dma_start(out=gbuf[:, 0, :], in_=g_ap.partition_broadcast(128)), writes=["gb0"], dsem=s_g)
            P.op("sp", lambda e: e.dma_start(out=gbuf[:, 1, :], in_=b_ap.partition_broadcast(128)), writes=["gb1"], dsem=s_b)

        def ln_rows(src, src_slots, sub, tmp):
            st, st_s, mv, mv_s, lnv, lnv_s, rstd, rstd_s = tmp
            XR = ("xres", sub)
            P.op("dve", lambda e: e.bn_stats(out=st[:, 0, :], in_=src[:, 0:512]), reads=src_slots, writes=[st_s])
            P.op("dve", lambda e: e.bn_stats(out=st[:, 1, :], in_=src[:, 512:1024]), reads=src_slots, writes=[(st_s, 1)])
            P.op("dve", lambda e: e.bn_aggr(out=mv[:, :], in_=st[:, :, :].rearrange("p a b -> p (a b)")), reads=[st_s, (st_s, 1)], writes=[mv_s])
            P.op("act", lambda e: e.activation(out=lnv[:, :], in_=mv[:, 1:2], func=AF.Ln, bias=EPS, scale=1.0), reads=[mv_s], writes=[lnv_s])
            P.op("act", lambda e: e.activation(out=rstd[:, :], in_=lnv[:, :], func=AF.Exp, scale=-0.5), reads=[lnv_s], writes=[rstd_s])
            P.op("dve", lambda e: e.tensor_scalar(out=xres[:, sub, :], in0=src, scalar1=mv[:, 0:1], scalar2=rstd[:, 0:1],
                                                  op0=ALU.subtract, op1=ALU.mult), reads=src_slots + [mv_s, rstd_s], writes=[XR])
            P.op("pool", lambda e: e.tensor_tensor(out=xres[:, sub, :], in0=xres[:, sub, :], in1=gbuf[:, 0, :], op=ALU.mult),
                 reads=[XR, "gb0"], writes=[XR])
            P.op("pool", lambda e: e.tensor_tensor(out=xres[:, sub, :], in0=xres[:, sub, :], in1=gbuf[:, 1, :], op=ALU.add),
                 reads=[XR, "gb1"], writes=[XR])

        def to_featmajor(sub, xb, xb_s):
            XR = ("xres", sub)
            P.op("act", lambda e: e.activation(out=xb[:, :], in_=xres[:, sub, :], func=AF.Identity), reads=[XR], writes=[xb_s])
            b = nb()
            psv = psb[b][:].bitcast(BF16)
            for c in range(8):
                P.op("pe", lambda e, c=c: e.transpose(out=psv[:, 128 * c:128 * (c + 1)], in_=xb[:, 128 * c:128 * (c + 1)], identity=ident_b[:, :]),
                     reads=[xb_s, "ident_b"], writes=[PS(b)])
            P.op("dve", lambda e: e.tensor_copy(out=xT[:, :, 128 * sub:128 * (sub + 1)], in_=psv[:, :].rearrange("p (c t) -> p c t", c=8)),
                 writes=[PS(b), ("xT", sub)])

        XT = [("xT", s) for s in range(nsub)]

        ar_reset()
        xin = [alloc([128, 1024], F32) for _ in range(2)]
        xb2 = [alloc([128, 1024], BF16) for _ in range(2)]
        lnt = []
        for _ in range(2):
            st, st_s = alloc([128, 2, 6], F32); mv, mv_s = alloc([128, 2], F32)
            lnv, lnv_s = alloc([128, 1], F32); rstd, rstd_s = alloc([128, 1], F32)
            lnt.append((st, st_s, mv, mv_s, lnv, lnv_s, rstd, rstd_s))
        kvst = [alloc([128, 1024], F32) for _ in range(2)]

        load_gb(ln0_g, ln0_b)
        for sub in range(nsub):
            xi, xi_s = xin[sub % 2]
            P.op("sp", lambda e, sub=sub, xi=xi: e.dma_start(out=xi[:, :], in_=x_src[128 * sub:128 * (sub + 1), :]), writes=[xi_s], dsem=s_x[sub % 2])
            ln_rows(xi, [xi_s], sub, lnt[sub % 2])
            to_featmajor(sub, *xb2[sub % 2])

        if ti == 0:
            tap(('d_' if prompt else 's_') + 'xT', xT[:, :, :], XT)
        lA, lA_s = alloc([16, 512], F32, 16)
        lB, lB_s = alloc([16, 512], F32, 16)
        cT, cT_s = alloc([16, 512], F32, 16)
        r1, r1_s = alloc([16, 512], F32, 16)
        SQ, SQ_s = alloc([16, 3, 512], BF16, 16)
        SK, SK_s = alloc([16, 3, 512], BF16, 16)
        lft, lft_s = alloc([128, 4, 16], F32)
        b = nb()
        for kc in range(8):
            P.op("pe", lambda e, kc=kc, b=b: e.matmul(psb[b][0:16, 0:Tt], lhsT=wfl[:, kc, :], rhs=xT[:, kc, 0:Tt], start=(kc == 0), stop=(kc == 7)),
                 reads=XT + ["wfl"], writes=[PS(b)])
        P.op("act", lambda e, b=b: e.activation(out=lA[:, 0:Tt], in_=psb[b][0:16, 0:Tt], func=AF.Exp, bias=nbf[:, 0:1], scale=-1.0),
             reads=["nbf"], writes=[PS(b), lA_s])
        P.op("act", lambda e: e.activation(out=lB[:, 0:Tt], in_=lA[:, 0:Tt], func=AF.Ln, bias=1.0, scale=1.0), reads=[lA_s], writes=[lB_s])
        P.op("dve", lambda e: e.tensor_scalar(out=lA[:, 0:Tt], in0=lB[:, 0:Tt], scalar1=-1.0, scalar2=None, op0=ALU.mult),
             reads=[lB_s], writes=[lA_s])
        if prompt:
            P.op("dve", lambda e: e.tensor_tensor_scan(out=cT[:, 0:512], data0=ones16[:, 0:512], data1=lA[:, 0:512], initial=carry[:, 0:1],
                                                       op0=ALU.mult, op1=ALU.add), reads=[lA_s, "ones16", "carry"], writes=[cT_s])
            P.op("dve", lambda e: e.tensor_copy(out=carry[:, 0:1], in_=cT[:, 511:512]), reads=[cT_s], writes=["carry"])
        else:
            for s in range(4):
                P.op("dve", lambda e, s=s: e.tensor_tensor_scan(out=cT[:, 64 * s:64 * (s + 1)], data0=ones16[:, 0:64], data1=lA[:, 64 * s:64 * (s + 1)],
                                                                initial=hend[:, s:s + 1], op0=ALU.mult, op1=ALU.add),
                     reads=[lA_s, "ones16", "hend"], writes=[cT_s])
        P.op("dve", lambda e: e.tensor_scalar(out=SQ[:, 0, 0:Tt], in0=cT[:, 0:Tt], scalar1=8.0, scalar2=None, op0=ALU.mult), reads=[cT_s], writes=[SQ_s])
        P.op("dve", lambda e: e.scalar_tensor_tensor(out=r1[:, 0:Tt], in0=cT[:, 0:Tt], scalar=8.0, in1=SQ[:, 0, 0:Tt], op0=ALU.mult, op1=ALU.subtract),
             reads=[cT_s, SQ_s], writes=[r1_s])
        P.op("dve", lambda e: e.tensor_copy(out=SQ[:, 1, 0:Tt], in_=r1[:, 0:Tt]), reads=[r1_s], writes=[(SQ_s, 1)])
        P.op("dve", lambda e: e.tensor_tensor(out=lB[:, 0:Tt], in0=r1[:, 0:Tt], in1=SQ[:, 1, 0:Tt], op=ALU.subtract), reads=[r1_s, (SQ_s, 1)], writes=[lB_s])
        P.op("dve", lambda e: e.tensor_copy(out=SQ[:, 2, 0:Tt], in_=lB[:, 0:Tt]), reads=[lB_s], writes=[(SQ_s, 2)])
        SQA = [SQ_s, (SQ_s, 1), (SQ_s, 2)]
        P.op("dve", lambda e: e.tensor_scalar(out=SK[:, :, 0:Tt], in0=SQ[:, :, 0:Tt], scalar1=-1.0, scalar2=None, op0=ALU.mult), reads=SQA, writes=[SK_s])
        b = nb()
        for sub in range(nsub):
            P.op("pe", lambda e, sub=sub, b=b: e.transpose(out=psb[b][:, 16 * sub:16 * (sub + 1)], in_=lA[:, 128 * sub:128 * (sub + 1)], identity=ident_f[0:16, 0:16]),
                 reads=[lA_s, "ident_f"], writes=[PS(b)])
        P.op("dve", lambda e, b=b: e.tensor_copy(out=lft[:, 0:nsub, :], in_=psb[b][:, 0:16 * nsub].rearrange("p (s h) -> p s h", h=16)), writes=[PS(b), lft_s])
        P.op("pool", lambda e: e.dma_start(out=lf_dst.rearrange("(s p) h -> p s h", p=128), in_=lft[:, 0:nsub, :]), reads=[lft_s], dsem=s_lf)
        RQ = [("rq", h) for h in range(NH)]
        RK = [("rk", h) for h in range(NH)]
        if prompt:
            KTv = KTs[:, :].rearrange("p (g b h i) -> p g b h i", g=8, b=4, h=2)
        else:
            KTv = KTs[:, 0:4096].rearrange("p (h t) -> p h t", h=16)
        for h in range(NH):
            P.op("pool", lambda e, h=h: e.dma_start(out=QT[67:70, h, 0:Tt], in_=SQ[h:h + 1, :, 0:Tt]), reads=SQA + ["QTc"], writes=[RQ[h]], dsem=s_rq)
            if prompt:
                P.op("pool", lambda e, h=h: e.dma_start(out=KTv[64:67, h // 2, :, h % 2, :], in_=SK[h:h + 1, :, 0:512]), reads=[SK_s] + KTC, writes=[RK[h]], dsem=s_rk)
            else:
                P.op("pool", lambda e, h=h: e.dma_start(out=KTv[64:67, h, :], in_=SK[h:h + 1, :, 0:256]), reads=[SK_s] + KTC, writes=[RK[h]], dsem=s_rk)

        ucT, ucT_s = alloc([128, 8, 512], BF16)
        utile = [alloc([128, nseg, 30 + L], BF16) for _ in range(2)]
        sgt = [alloc([128, 512], BF16) for _ in range(2)]
        acc = [alloc([128, 512], F32) for _ in range(4)]
        sqt = [alloc([128, 512], BF16) for _ in range(2)]
        ust, ust_s = kvst[0]
        bmean = 6
        bmsq = 7
        def nb6():
            while True:
                b = nb()
                if b < 6:
                    return b

        def seg3(ap):
            return ap.rearrange("p (s l) -> p s l", s=nseg)

        if not prompt:
            sc, sc_s = alloc([120, 1024], F32, 120)
            P.op("sp", lambda e: e.dma_start(out=sc[:, :], in_=st_conv.rearrange("s r d -> (s r) d")), writes=[sc_s], dsem=s_st[0])
            for c in range(8):
                b = nb6()
                P.op("pe", lambda e, c=c, b=b: e.transpose(out=psb[b][:, 0:120], in_=sc[0:120, 128 * c:128 * (c + 1)], identity=ident_f[0:120, 0:120]),
                     reads=[sc_s, "ident_f"], writes=[PS(b)])
                P.op("dve", lambda e, c=c, b=b: e.tensor_copy(out=uhist[:, c, :, :], in_=psb[b][:, 0:120].rearrange("p (s r) -> p s r", s=4)),
                     writes=[PS(b), ("uhist", c)])

        for half in range(2):
            W, W_s = w_get("A1" if half == 0 else "A2")
            for cc in range(4):
                c = 4 * half + cc
                ba = nb6(); bg = nb6()
                for kc in range(8):
                    P.op("pe", lambda e, kc=kc, cc=cc, ba=ba, W=W: e.matmul(psb[ba][:, 0:Tt], lhsT=W[:, kc, 128 * cc:128 * (cc + 1)], rhs=xT[:, kc, 0:Tt], start=(kc == 0), stop=(kc == 7)),
                         reads=XT + [W_s], writes=[PS(ba)])
                for kc in range(8):
                    P.op("pe", lambda e, kc=kc, cc=cc, bg=bg, W=W: e.matmul(psb[bg][:, 0:Tt], lhsT=W[:, kc, 512 + 128 * cc:512 + 128 * (cc + 1)], rhs=xT[:, kc, 0:Tt], start=(kc == 0), stop=(kc == 7)),
                         reads=XT + [W_s], writes=[PS(bg)])
                sg, sg_s = sgt[c % 2]
                u, u_s = utile[c % 2]
                P.op("act", lambda e, bg=bg, sg=sg: e.activation(out=sg[:, 0:Tt], in_=psb[bg][:, 0:Tt], func=AF.Sigmoid), writes=[PS(bg), sg_s])
                P.op("pool", lambda e, c=c, u=u: e.tensor_copy(out=u[:, :, 0:30], in_=uhist[:, c, 0:nseg, :]), reads=[("uhist", c)], writes=[(u_s, "h")])
                P.op("dve", lambda e, ba=ba, sg=sg, u=u: e.tensor_tensor(out=u[:, :, 30:30 + L], in0=seg3(psb[ba][:, 0:Tt]), in1=seg3(sg[:, 0:Tt]), op=ALU.mult),
                     reads=[sg_s], writes=[PS(ba), u_s])
                if prompt:
                    P.op("pool", lambda e, c=c, u=u: e.tensor_copy(out=uhist[:, c, 0, :], in_=u[:, 0, L:L + 30]), reads=[u_s, (u_s, "h")], writes=[("uhist", c)])
                US = [u_s, (u_s, "h")]
                a0, a0_s = acc[(2 * c) % 4]
                a1, a1_s = acc[(2 * c + 1) % 4]
                P.op("act", lambda e, c=c, u=u, a0=a0: e.activation(out=seg3(a0[:, 0:Tt]), in_=u[:, :, 0:L], func=AF.Identity, scale=cvec[:, c, 3:4], bias=cvec[:, c, 0:1]),
                     reads=US + CVEC, writes=[a0_s])
                P.op("dve", lambda e, c=c, u=u, a1=a1: e.tensor_scalar(out=seg3(a1[:, 0:Tt]), in0=u[:, :, 1:1 + L], scalar1=cvec[:, c, 4:5], scalar2=None, op0=ALU.mult),
                     reads=US + CVEC, writes=[a1_s])
                for k in range(2, 31):
                    a, a_s = (a0, a0_s) if k % 2 == 0 else (a1, a1_s)
                    P.op("dve", lambda e, c=c, k=k, u=u, a=a: e.scalar_tensor_tensor(out=seg3(a[:, 0:Tt]), in0=u[:, :, k:k + L], scalar=cvec[:, c, 3 + k:4 + k],
                                                                                  in1=seg3(a[:, 0:Tt]), op0=ALU.mult, op1=ALU.add),
                         reads=US + CVEC + [a_s], writes=[a_s])
                P.op("dve", lambda e, a0=a0, a1=a1: e.tensor_tensor(out=a0[:, 0:Tt], in0=a0[:, 0:Tt], in1=a1[:, 0:Tt], op=ALU.add), reads=[a0_s, a1_s], writes=[a0_s])
                sq, sq_s = sqt[c % 2]
                P.op("act", lambda e, c=c, a0=a0: e.activation(out=ucT[:, c, 0:Tt], in_=a0[:, 0:Tt], func=AF.Identity), reads=[a0_s], writes=[(ucT_s, c)])
                P.op("pool", lambda e, a0=a0, sq=sq: e.tensor_tensor(out=sq[:, 0:Tt], in0=a0[:, 0:Tt], in1=a0[:, 0:Tt], op=ALU.mult), reads=[a0_s], writes=[sq_s])
                P.op("pe", lambda e, c=c: e.matmul(psb[bmean][:, 0:Tt], lhsT=onesm[:, :], rhs=ucT[:, c, 0:Tt], start=(c == 0), stop=(c == 7)),
                     reads=[(ucT_s, c), "onesm"], writes=[PS(bmean)])
                P.op("pe", lambda e, c=c, sq=sq: e.matmul(psb[bmsq][:, 0:Tt], lhsT=onesm[:, :], rhs=sq[:, 0:Tt], start=(c == 0), stop=(c == 7)),
                     reads=[sq_s, "onesm"], writes=[PS(bmsq)])
            if need_state:
                for sub in state_subs:
                    ba = nb6(); bg = nb6()
                    for kc in range(8):
                        P.op("pe", lambda e, kc=kc, sub=sub, ba=ba, W=W: e.matmul(psb[ba][:, :], lhsT=xT[:, kc, 128 * sub:128 * (sub + 1)], rhs=W[:, kc, 0:512], start=(kc == 0), stop=(kc == 7)),
                             reads=XT + [W_s], writes=[PS(ba)])
                    for kc in range(8):
                        P.op("pe", lambda e, kc=kc, sub=sub, bg=bg, W=W: e.matmul(psb[bg][:, :], lhsT=xT[:, kc, 128 * sub:128 * (sub + 1)], rhs=W[:, kc, 512:1024], start=(kc == 0), stop=(kc == 7)),
                             reads=XT + [W_s], writes=[PS(bg)])
                    ut, ut_s = kvst[sub % 2]
                    sgf, sgf_s = acc[3]
                    P.op("act", lambda e, bg=bg, sgf=sgf: e.activation(out=sgf[:, :], in_=psb[bg][:, :], func=AF.Sigmoid), writes=[PS(bg), sgf_s])
                    P.op("dve", lambda e, ba=ba, sgf=sgf, ut=ut, half=half: e.tensor_tensor(out=ut[:, 512 * half:512 * (half + 1)], in0=psb[ba][:, :], in1=sgf[:, :], op=ALU.mult),
                         reads=[sgf_s], writes=[PS(ba), (ut_s, half)])
                    if prompt:
                        P.op("pool", lambda e, ut=ut, half=half: e.dma_start(out=cv_p[:, 512 * half:512 * (half + 1)], in_=ut[98:128, 512 * half:512 * (half + 1)]),
                             reads=[(ut_s, half)], dsem=s_kvo[sub % 2])
                    else:
                        for j in range(2):
                            P.op("pool", lambda e, ut=ut, half=half, j=j, sub=sub: e.dma_start(out=cv_s[2 * sub + j, :, 512 * half:512 * (half + 1)],
                                                                                           in_=ut[64 * j + 34:64 * j + 64, 512 * half:512 * (half + 1)]),
                                 reads=[(ut_s, half)], dsem=s_kvo[sub % 2])

        mean_sb, mean_s = alloc([128, 512], F32)
        rstd_sb, rstd_sbs = alloc([128, 512], F32)
        m2, m2_s = acc[0]
        P.op("act", lambda e: e.activation(out=mean_sb[:, 0:Tt], in_=psb[bmean][:, 0:Tt], func=AF.Identity), writes=[PS(bmean), mean_s])
        P.op("pool", lambda e: e.tensor_tensor(out=m2[:, 0:Tt], in0=mean_sb[:, 0:Tt], in1=mean_sb[:, 0:Tt], op=ALU.mult), reads=[mean_s], writes=[m2_s])
        P.op("dve", lambda e: e.tensor_tensor(out=m2[:, 0:Tt], in0=psb[bmsq][:, 0:Tt], in1=m2[:, 0:Tt], op=ALU.subtract), reads=[m2_s], writes=[PS(bmsq), m2_s])
        P.op("act", lambda e: e.activation(out=m2[:, 0:Tt], in_=m2[:, 0:Tt], func=AF.Ln, bias=EPS, scale=1.0), reads=[m2_s], writes=[m2_s])
        P.op("act", lambda e: e.activation(out=rstd_sb[:, 0:Tt], in_=m2[:, 0:Tt], func=AF.Exp, scale=-0.5), reads=[m2_s], writes=[rstd_sbs])
        for c in range(8):
            t, t_s = acc[1 + c % 3]
            P.op("dve", lambda e, c=c, t=t: e.tensor_tensor(out=t[:, 0:Tt], in0=ucT[:, c, 0:Tt], in1=mean_sb[:, 0:Tt], op=ALU.subtract),
                 reads=[(ucT_s, c), mean_s], writes=[t_s])
            P.op("pool", lambda e, t=t: e.tensor_tensor(out=t[:, 0:Tt], in0=t[:, 0:Tt], in1=rstd_sb[:, 0:Tt], op=ALU.mult), reads=[t_s, rstd_sbs], writes=[t_s])
            P.op("act", lambda e, c=c, t=t: e.activation(out=ucT[:, c, 0:Tt], in_=t[:, 0:Tt], func=AF.Silu, scale=cvec[:, c, 1:2], bias=cvec[:, c, 2:3]),
                 reads=[t_s] + CVEC, writes=[(ucT_s, c)])

        if ti == 0:
            tap(('d_' if prompt else 's_') + 'ucT', ucT[:, :, :], [(ucT_s, c) for c in range(8)])
        W, W_s = w_get("Q")
        for c in range(8):
            b = nb()
            for kc in range(8):
                P.op("pe", lambda e, kc=kc, c=c, b=b, W=W: e.matmul(psb[b][:, 0:Tt], lhsT=W[:, kc, 128 * c:128 * (c + 1)], rhs=xT[:, kc, 0:Tt], start=(kc == 0), stop=(kc == 7)),
                     reads=XT + [W_s], writes=[PS(b)])
            P.op("act", lambda e, c=c, b=b: e.activation(out=QT[0:64, 2 * c, 0:Tt], in_=psb[b][0:64, 0:Tt], func=AF.Identity), writes=[PS(b), ("QT", 2 * c)])
            P.op("dve", lambda e, c=c, b=b: e.tensor_copy(out=QT[0:64, 2 * c + 1, 0:Tt], in_=psb[b][64:128, 0:Tt]), writes=[PS(b), ("QT", 2 * c + 1)])
        W, W_s = w_get("K")
        for c in range(8):
            b = nb()
            for kc in range(8):
                P.op("pe", lambda e, kc=kc, c=c, b=b, W=W: e.matmul(psb[b][:, 0:Tt], lhsT=W[:, kc, 128 * c:128 * (c + 1)], rhs=xT[:, kc, 0:Tt], start=(kc == 0), stop=(kc == 7)),
                     reads=XT + [W_s], writes=[PS(b)])
            if prompt:
                P.op("act", lambda e, c=c, b=b: e.activation(out=KTv[0:64, c, :, 0, :], in_=psb[b][0:64, :].rearrange("p (b i) -> p b i", b=4), func=AF.Identity),
                     writes=[PS(b), ("KT", 2 * c)])
                P.op("dve", lambda e, c=c, b=b: e.tensor_copy(out=KTv[0:64, c, :, 1, :], in_=psb[b][64:128, :].rearrange("p (b i) -> p b i", b=4)),
                     writes=[PS(b), ("KT", 2 * c + 1)])
            else:
                P.op("act", lambda e, c=c, b=b: e.activation(out=KTv[0:64, 2 * c, :], in_=psb[b][0:64, 0:Tt], func=AF.Identity), writes=[PS(b), ("KT", 2 * c)])
                P.op("dve", lambda e, c=c, b=b: e.tensor_copy(out=KTv[0:64, 2 * c + 1, :], in_=psb[b][64:128, 0:Tt]), writes=[PS(b), ("KT", 2 * c + 1)])

        def tokmajor_out(W, W_s, dst, which):
            for sub in range(nsub):
                stg, stg_s = kvst[sub % 2]
                for half in range(2):
                    b = nb()
                    for kc in range(8):
                        P.op("pe", lambda e, kc=kc, sub=sub, half=half, b=b: e.matmul(psb[b][:, :], lhsT=xT[:, kc, 128 * sub:128 * (sub + 1)], rhs=W[:, kc, 512 * half:512 * (half + 1)],
                                                                                     start=(kc == 0), stop=(kc == 7)), reads=XT + [W_s], writes=[PS(b)])
                    if half == 0:
                        P.op("act", lambda e, b=b, stg=stg: e.activation(out=stg[:, 0:512], in_=psb[b][:, :], func=AF.Identity), writes=[PS(b), (stg_s, 0)])
                    else:
                        P.op("dve", lambda e, b=b, stg=stg: e.tensor_copy(out=stg[:, 512:1024], in_=psb[b][:, :]), writes=[PS(b), (stg_s, 1)])
                P.op("sp", lambda e, sub=sub, stg=stg: e.dma_start(out=dst[128 * sub:128 * (sub + 1), :], in_=stg[:, :]), reads=[(stg_s, 0), (stg_s, 1)], dsem=s_kvo[sub % 2])
                if which == "v":
                    if prompt:
                        for par in range(2):
                            P.op("pool", lambda e, sub=sub, stg=stg, par=par: e.tensor_copy(
                                out=Vs[:, :, sub, 128 * par:128 * par + 64],
                                in_=stg[:, :].rearrange("p (g r) -> p g r", g=8)[:, :, 64 * par:64 * par + 64]),
                                reads=[(stg_s, 0), (stg_s, 1)], writes=[(Vs_s, sub, par)])
                    else:
                        P.op("pool", lambda e, sub=sub, stg=stg: e.tensor_copy(out=Vn[:, sub, :], in_=stg[:, :]), reads=[(stg_s, 0), (stg_s, 1)], writes=[(Vn_s, sub)])

        tokmajor_out(W, W_s, k_dst, "k")
        if prompt:
            Vs, Vs_s = alloc([128, 8, 4, 192], BF16)
            P.op("pool", lambda e: e.memset(Vs[:, :, :, 64:128], 1.0), writes=[(Vs_s, "ones")])
        else:
            Vn = KTs[:, 4096:6144].rearrange("p (s d) -> p s d", s=2)
            Vn_s = "VnP"
        W, W_s = w_get("V")
        tokmajor_out(W, W_s, v_dst, "v")
        if prompt:
            for g in range(8):
                P.op("sp", lambda e, g=g: e.dma_start(out=kt_scr[g, ti, :, :], in_=KTs[0:80, 1024 * g:1024 * (g + 1)]),
                     reads=[("KT", 2 * g), ("KT", 2 * g + 1)] + RK + KTC, writes=[("ktscr", g, ti)], dsem=s_ktw[g])
                P.op("sp", lambda e, g=g: e.dma_start(out=v_scr[g, ti, :, :], in_=Vs[:, g, :, :].rearrange("p b r -> p (b r)")),
                     reads=[(Vs_s, s_, p_) for s_ in range(4) for p_ in range(2)] + [(Vs_s, "ones")], writes=[("vscr", g, ti)], dsem=s_vw[g])

        W, W_s = w_get("GC")
        for c in range(8):
            b = nb()
            for kc in range(8):
                P.op("pe", lambda e, kc=kc, c=c, b=b, W=W: e.matmul(psb[b][:, 0:Tt], lhsT=W[:, kc, 128 * c:128 * (c + 1)], rhs=xT[:, kc, 0:Tt], start=(kc == 0), stop=(kc == 7)),
                     reads=XT + [W_s], writes=[PS(b)])
            P.op("act", lambda e, c=c, b=b: e.activation(out=gated_c[:, c, 0:Tt], in_=psb[b][:, 0:Tt], func=AF.Sigmoid), writes=[PS(b), ("gc", c)])
        W, W_s = w_get("CO")
        UCT = [(ucT_s, c) for c in range(8)]
        for c in range(8):
            b = nb()
            for kc in range(8):
                P.op("pe", lambda e, kc=kc, c=c, b=b, W=W: e.matmul(psb[b][:, 0:Tt], lhsT=W[:, kc, 128 * c:128 * (c + 1)], rhs=ucT[:, kc, 0:Tt], start=(kc == 0), stop=(kc == 7)),
                     reads=UCT + [W_s], writes=[PS(b)])
            P.op("dve", lambda e, c=c, b=b: e.tensor_tensor(out=gated_c[:, c, 0:Tt], in0=psb[b][:, 0:Tt], in1=gated_c[:, c, 0:Tt], op=ALU.mult),
                 reads=[("gc", c)], writes=[PS(b), ("gc", c)])
        if ti == 0:
            tap(('d_' if prompt else 's_') + 'gc', gated_c[:, :, :], [('gc', c) for c in range(8)])
            tap(('d_' if prompt else 's_') + 'QT', QT[0:80, :, :], [('QT', h) for h in range(NH)] + RQ + ['QTc'])
            tap(('d_' if prompt else 's_') + 'KT', KTs[0:80, :], [('KT', h) for h in range(NH)] + RK + KTC)
        P.end_scope()

        ar_reset()
        QTA = [("QT", h) for h in range(NH)] + RQ + ["QTc"]
        rec = [alloc([128, 512], F32) for _ in range(2)]
        PT = [alloc([128, 512], BF16) for _ in range(4)]
        sb_rr = [0]
        pt_rr = [0]
        if prompt:
            kvb = []
            for i in range(NKV):
                ktb, ktb_s = alloc([128, 4, 2, 128], BF16)
                vb, vb_s = alloc([128, 4, 192], BF16)
                kvb.append((ktb, ktb_s, vb, vb_s))
            jobs = [(g, sb) for g in range(8) for sb in range(ti + 1)]
            kvst_ = {"loaded": 0}

            def kv_ensure(n):
                while kvst_["loaded"] <= n and kvst_["loaded"] < len(jobs):
                    m = kvst_["loaded"]
                    g, sb = jobs[m]
                    ktb, ktb_s, vb, vb_s = kvb[m % NKV]
                    P.op("sp", lambda e, g=g, sb=sb, ktb=ktb: e.dma_start(out=ktb[0:80, :, :, :].rearrange("p b h i -> p (b h i)"), in_=kt_scr[g, sb, :, :]),
                         reads=[("ktscr", g, sb)], writes=[ktb_s], dsem=s_kvK[m % NKV])
                    P.op("sp", lambda e, g=g, sb=sb, vb=vb: e.dma_start(out=vb[:, :, :].rearrange("p b r -> p (b r)"), in_=v_scr[g, sb, :, :]),
                         reads=[("vscr", g, sb)], writes=[vb_s], dsem=s_kvV[m % NKV])
                    kvst_["loaded"] += 1

            for n, (g, sb) in enumerate(jobs):
                kv_ensure(n + NKV - 1)
                ktb, ktb_s, vb, vb_s = kvb[n % NKV]
                ob = [4 + 2 * (g % 2), 5 + 2 * (g % 2)]
                diag = sb == ti
                for blk in range(4):
                    q0 = 128 * blk if diag else 0
                    for hh in range(2):
                        h = 2 * g + hh
                        sbk = sb_rr[0] % 4
                        sb_rr[0] += 1
                        pt, pt_s = PT[pt_rr[0] % 4]
                        pt_rr[0] += 1
                        P.op("pe", lambda e, blk=blk, hh=hh, h=h, sbk=sbk, q0=q0, ktb=ktb, diag=diag: e.matmul(
                            psb[sbk][:, q0:512], lhsT=ktb[0:80, blk, hh, :], rhs=QT[0:80, h, q0:512], start=True, stop=(not diag)),
                            reads=[ktb_s] + QTA, writes=[PS(sbk)])
                        if diag:
                            P.op("pe", lambda e, sbk=sbk, q0=q0: e.matmul(psb[sbk][:, q0:q0 + 128], lhsT=ident_b[:, :], rhs=maskb[:, :], start=False, stop=True),
                                 reads=["ident_b", "maskb"], writes=[PS(sbk)])
                        P.op("act", lambda e, sbk=sbk, q0=q0, pt=pt: e.activation(out=pt[:, q0:512], in_=psb[sbk][:, q0:512], func=AF.Exp, scale=0.125),
                             writes=[PS(sbk), pt_s])
                        first = sb == 0 and blk == 0
                        lastk = diag and blk == 3
                        P.op("pe", lambda e, blk=blk, hh=hh, q0=q0, pt=pt, vb=vb, first=first, lastk=lastk, ob=ob: e.matmul(
                            psb[ob[hh]][:, q0:512], lhsT=vb[:, blk, 64 * hh:64 * hh + 128], rhs=pt[:, q0:512], start=first, stop=lastk),
                            reads=[vb_s, pt_s], writes=[PS(ob[hh])])
                if diag:
                    rc, rc_s = rec[g % 2]
                    P.op("dve", lambda e, rc=rc, ob=ob: e.reciprocal(out=rc[0:64, :], in_=psb[ob[0]][64:128, :]), writes=[PS(ob[0]), (rc_s, 0)])
                    P.op("dve", lambda e, rc=rc, ob=ob, g=g: e.tensor_tensor(out=oT[0:64, g, :], in0=psb[ob[0]][0:64, :], in1=rc[0:64, :], op=ALU.mult),
                         reads=[(rc_s, 0)], writes=[PS(ob[0]), ("oT", g, 0)])
                    P.op("dve", lambda e, rc=rc, ob=ob: e.reciprocal(out=rc[64:128, :], in_=psb[ob[1]][0:64, :]), writes=[PS(ob[1]), (rc_s, 1)])
                    P.op("dve", lambda e, rc=rc, ob=ob, g=g: e.tensor_tensor(out=oT[64:128, g, :], in0=psb[ob[1]][64:128, :], in1=rc[64:128, :], op=ALU.mult),
                         reads=[(rc_s, 1)], writes=[PS(ob[1]), ("oT", g, 1)])
        else:
            sample_attention(QTA, RK, KTv, Vn, Vn_s, rec, PT)
        P.end_scope()

        ar_reset()
        OT = [("oT", g, j) for g in range(8) for j in range(2)]
        gT, gT_s = alloc([128, 8, 512], BF16)
        hT, hT_s = alloc([128, NFC, 512], BF16)
        lnt = []
        for _ in range(2):
            st, st_s = alloc([128, 2, 6], F32); mv, mv_s = alloc([128, 2], F32)
            lnv, lnv_s = alloc([128, 1], F32); rstd, rstd_s = alloc([128, 1], F32)
            lnt.append((st, st_s, mv, mv_s, lnv, lnv_s, rstd, rstd_s))
        xb2 = [alloc([128, 1024], BF16) for _ in range(2)]
        tmpf = [alloc([128, 512], F32) for _ in range(4)]
        aext = [alloc([128, nseg, 2 + L], F32) for _ in range(2)]
        pb, pb_s = alloc([128, 4, 256], BF16)
        pT, pT_s = alloc([128, 2, 512], BF16)

        if ti == 0:
            tap(('d_' if prompt else 's_') + 'oT', oT[:, :, :], OT)
        W, W_s = w_get("GA")
        for c in range(8):
            b = nb()
            for kc in range(8):
                P.op("pe", lambda e, kc=kc, c=c, b=b, W=W: e.matmul(psb[b][:, 0:Tt], lhsT=W[:, kc, 128 * c:128 * (c + 1)], rhs=xT[:, kc, 0:Tt], start=(kc == 0), stop=(kc == 7)),
                     reads=XT + [W_s], writes=[PS(b)])
            P.op("act", lambda e, c=c, b=b: e.activation(out=gT[:, c, 0:Tt], in_=psb[b][:, 0:Tt], func=AF.Sigmoid), writes=[PS(b), (gT_s, c)])
        W, W_s = w_get("AO")
        for c in range(8):
            b = nb()
            for kc in range(8):
                P.op("pe", lambda e, kc=kc, c=c, b=b, W=W: e.matmul(psb[b][:, 0:Tt], lhsT=W[:, kc, 128 * c:128 * (c + 1)], rhs=oT[:, kc, 0:Tt], start=(kc == 0), stop=(kc == 7)),
                     reads=OT + [W_s], writes=[PS(b)])
            P.op("dve", lambda e, c=c, b=b: e.tensor_tensor(out=gT[:, c, 0:Tt], in0=psb[b][:, 0:Tt], in1=gT[:, c, 0:Tt], op=ALU.mult),
                 reads=[(gT_s, c)], writes=[PS(b), (gT_s, c)])
            P.op("pool", lambda e, c=c: e.tensor_tensor(out=gT[:, c, 0:Tt], in0=gT[:, c, 0:Tt], in1=gated_c[:, c, 0:Tt], op=ALU.add),
                 reads=[(gT_s, c), ("gc", c)], writes=[(gT_s, c)])
        GT = [(gT_s, c) for c in range(8)]
        if ti == 0:
            tap(('d_' if prompt else 's_') + 'gT', gT[:, :, :], GT)
        W, W_s = w_get("O")
        load_gb(ln1_g, ln1_b)
        for sub in range(nsub):
            for half in range(2):
                b = nb()
                for kc in range(8):
                    P.op("pe", lambda e, kc=kc, sub=sub, half=half, b=b, W=W: e.matmul(psb[b][:, :], lhsT=gT[:, kc, 128 * sub:128 * (sub + 1)], rhs=W[:, kc, 512 * half:512 * (half + 1)],
                                                                                      start=(kc == 0), stop=(kc == 7)), reads=GT + [W_s], writes=[PS(b)])
                P.op("dve", lambda e, sub=sub, half=half, b=b: e.scalar_tensor_tensor(out=xres[:, sub, 512 * half:512 * (half + 1)], in0=xres[:, sub, 512 * half:512 * (half + 1)],
                                                                                   scalar=ALPHA, in1=psb[b][:, :], op0=ALU.mult, op1=ALU.add),
                     reads=[("xres", sub)], writes=[PS(b), ("xres", sub)])
            ln_rows(xres[:, sub, :], [("xres", sub)], sub, lnt[sub % 2])
            to_featmajor(sub, *xb2[sub % 2])

        if ti == 0:
            tap(('d_' if prompt else 's_') + 'x1', xres[:, :, :], [('xres', q_) for q_ in range(4)])
        P.op("pool", lambda e: e.dma_start(out=pb[:, 0:nsub, :], in_=p_src.rearrange("(s p) d -> p s d", p=128)), writes=[pb_s], dsem=s_p)
        for sub in range(nsub):
            b = nb()
            psv = psb[b][:].bitcast(BF16)
            for k2 in range(2):
                P.op("pe", lambda e, sub=sub, k2=k2, psv=psv: e.transpose(out=psv[:, 128 * k2:128 * (k2 + 1)], in_=pb[:, sub, 128 * k2:128 * (k2 + 1)], identity=ident_b[:, :]),
                     reads=[pb_s, "ident_b"], writes=[PS(b)])
            P.op("act", lambda e, sub=sub, psv=psv, b=b: e.activation(out=pT[:, :, 128 * sub:128 * (sub + 1)], in_=psv[:, 0:256].rearrange("p (c t) -> p c t", c=2), func=AF.Identity),
                 writes=[PS(b), (pT_s, sub)])
        PTS = [(pT_s, s) for s in range(nsub)]
        W, W_s = w_get("PG")
        wple, wple_s = w_get("PLE", ahead=0)
        for sub in range(nsub):
            for half in range(2):
                bg_ = nb(); bp = nb()
                for kc in range(8):
                    P.op("pe", lambda e, kc=kc, sub=sub, half=half, bg_=bg_, W=W: e.matmul(psb[bg_][:, :], lhsT=xT[:, kc, 128 * sub:128 * (sub + 1)], rhs=W[:, kc, 512 * half:512 * (half + 1)],
                                                                                          start=(kc == 0), stop=(kc == 7)), reads=XT + [W_s], writes=[PS(bg_)])
                for k2 in range(2):
                    P.op("pe", lambda e, k2=k2, sub=sub, half=half, bp=bp, wple=wple: e.matmul(psb[bp][:, :], lhsT=pT[:, k2, 128 * sub:128 * (sub + 1)], rhs=wple[:, k2, 512 * half:512 * (half + 1)],
                                                                                   start=(k2 == 0), stop=(k2 == 1)), reads=PTS + [wple_s], writes=[PS(bp)])
                tg, tg_s = tmpf[(2 * sub + half) % 2]
                P.op("act", lambda e, bg_=bg_, tg=tg: e.activation(out=tg[:, :], in_=psb[bg_][:, :], func=AF.Sigmoid), writes=[PS(bg_), tg_s])
                P.op("dve", lambda e, bp=bp, tg=tg: e.tensor_tensor(out=tg[:, :], in0=psb[bp][:, :], in1=tg[:, :], op=ALU.mult), reads=[tg_s], writes=[PS(bp), tg_s])
                P.op("dve", lambda e, sub=sub, half=half, tg=tg: e.scalar_tensor_tensor(out=xres[:, sub, 512 * half:512 * (half + 1)], in0=xres[:, sub, 512 * half:512 * (half + 1)],
                                                                                     scalar=ALPHA, in1=tg[:, :], op0=ALU.mult, op1=ALU.add),
                     reads=[("xres", sub), tg_s], writes=[("xres", sub)])

        if not prompt:
            sf, sf_s = alloc([8, DFF], F32, 8)
            P.op("sp", lambda e: e.dma_start(out=sf[:, :], in_=st_ffn.rearrange("s r d -> (s r) d")), writes=[sf_s], dsem=s_st[1])
            for c in range(NFC):
                b = nb()
                P.op("pe", lambda e, c=c, b=b: e.transpose(out=psb[b][:, 0:8], in_=sf[0:8, 128 * c:128 * (c + 1)], identity=ident_f[0:8, 0:8]),
                     reads=[sf_s, "ident_f"], writes=[PS(b)])
                P.op("dve", lambda e, c=c, b=b: e.tensor_copy(out=ahist[:, c, :, :], in_=psb[b][:, 0:8].rearrange("p (s r) -> p s r", s=4)), writes=[PS(b), ("ahist", c)])
        for g in range(6):
            W, W_s = w_get(f"UP{g}")
            hw = 512 if g < 5 else 256
            for cc in range(hw // 128):
                c = 4 * g + cc
                ba = nb(); bb = nb()
                for kc in range(8):
                    P.op("pe", lambda e, kc=kc, cc=cc, ba=ba, W=W: e.matmul(psb[ba][:, 0:Tt], lhsT=W[:, kc, 128 * cc:128 * (cc + 1)], rhs=xT[:, kc, 0:Tt], start=(kc == 0), stop=(kc == 7)),
                         reads=XT + [W_s], writes=[PS(ba)])
                for kc in range(8):
                    P.op("pe", lambda e, kc=kc, cc=cc, bb=bb, W=W, hw=hw: e.matmul(psb[bb][:, 0:Tt], lhsT=W[:, kc, hw + 128 * cc:hw + 128 * (cc + 1)], rhs=xT[:, kc, 0:Tt], start=(kc == 0), stop=(kc == 7)),
                         reads=XT + [W_s], writes=[PS(bb)])
                ae, ae_s = aext[c % 2]
                ac, ac_s = tmpf[2 + c % 2]
                P.op("act", lambda e, ba=ba, ae=ae: e.activation(out=ae[:, :, 2:2 + L], in_=seg3(psb[ba][:, 0:Tt]), func=AF.Identity), writes=[PS(ba), ae_s])
                P.op("pool", lambda e, c=c, ae=ae: e.tensor_copy(out=ae[:, :, 0:2], in_=ahist[:, c, 0:nseg, :]), reads=[("ahist", c)], writes=[(ae_s, "h")])
                if prompt:
                    P.op("pool", lambda e, c=c, ae=ae: e.tensor_copy(out=ahist[:, c, 0, :], in_=ae[:, 0, L:L + 2]), reads=[ae_s, (ae_s, "h")], writes=[("ahist", c)])
                AE = [ae_s, (ae_s, "h")]
                P.op("act", lambda e, c=c, ae=ae, ac=ac: e.activation(out=seg3(ac[:, 0:Tt]), in_=ae[:, :, 0:L], func=AF.Identity, scale=fvec[:, c, 0:1], bias=fvec[:, c, 3:4]),
                     reads=AE + FVEC, writes=[ac_s])
                for k in (1, 2):
                    P.op("dve", lambda e, c=c, k=k, ae=ae, ac=ac: e.scalar_tensor_tensor(out=seg3(ac[:, 0:Tt]), in0=ae[:, :, k:k + L], scalar=fvec[:, c, k:k + 1],
                                                                                      in1=seg3(ac[:, 0:Tt]), op0=ALU.mult, op1=ALU.add),
                         reads=AE + FVEC + [ac_s], writes=[ac_s])
                P.op("act", lambda e, ac=ac: e.activation(out=ac[:, 0:Tt], in_=ac[:, 0:Tt], func=AF.Silu), reads=[ac_s], writes=[ac_s])
                P.op("dve", lambda e, c=c, bb=bb, ac=ac: e.tensor_tensor(out=hT[:, c, 0:Tt], in0=psb[bb][:, 0:Tt], in1=ac[:, 0:Tt], op=ALU.mult),
                     reads=[ac_s], writes=[PS(bb), (hT_s, c)])
            if need_state:
                for sub in state_subs:
                    b = nb()
                    for kc in range(8):
                        P.op("pe", lambda e, kc=kc, sub=sub, b=b, W=W, hw=hw: e.matmul(psb[b][:, 0:hw], lhsT=xT[:, kc, 128 * sub:128 * (sub + 1)], rhs=W[:, kc, 0:hw], start=(kc == 0), stop=(kc == 7)),
                             reads=XT + [W_s], writes=[PS(b)])
                    sg_, sg_s = tmpf[sub % 2]
                    P.op("act", lambda e, b=b, sg_=sg_, hw=hw: e.activation(out=sg_[:, 0:hw], in_=psb[b][:, 0:hw], func=AF.Identity), writes=[PS(b), sg_s])
                    if prompt:
                        P.op("pool", lambda e, sg_=sg_, g=g, hw=hw: e.dma_start(out=ff_p[:, 512 * g:512 * g + hw], in_=sg_[126:128, 0:hw]), reads=[sg_s], dsem=s_st[sub % 2])
                    else:
                        for j in range(2):
                            P.op("pool", lambda e, sg_=sg_, g=g, hw=hw, j=j, sub=sub: e.dma_start(out=ff_s[2 * sub + j, :, 512 * g:512 * g + hw], in_=sg_[64 * j + 62:64 * j + 64, 0:hw]),
                                 reads=[sg_s], dsem=s_st[sub % 2])
        HT = [(hT_s, c) for c in range(NFC)]
        if ti == 0:
            tap(('d_' if prompt else 's_') + 'hT', hT[:, :, :], HT)
        load_gb(ln2_g, ln2_b)
        for n in range(2):
            banks = [nb() for _ in range(nsub)]
            for kh in range(2):
                W, W_s = w_get(f"DN{kh}{n}")
                for sub in range(nsub):
                    for j in range(11):
                        P.op("pe", lambda e, j=j, kh=kh, sub=sub, W=W, b=banks[sub]: e.matmul(psb[b][:, :], lhsT=hT[:, 11 * kh + j, 128 * sub:128 * (sub + 1)], rhs=W[:, j, :],
                                                                                             start=(kh == 0 and j == 0), stop=(kh == 1 and j == 10)),
                             reads=HT + [W_s], writes=[PS(banks[sub])])
            for sub in range(nsub):
                P.op("dve", lambda e, sub=sub, n=n, b=banks[sub]: e.tensor_tensor(out=xres[:, sub, 512 * n:512 * (n + 1)], in0=psb[b][:, :], in1=xres[:, sub, 512 * n:512 * (n + 1)], op=ALU.add),
                     reads=[("xres", sub)], writes=[PS(banks[sub]), ("xres", sub)])
        if ti == 0:
            tap(('d_' if prompt else 's_') + 'r2', xres[:, :, :], [('xres', q_) for q_ in range(4)])
        for sub in range(nsub):
            ln_rows(xres[:, sub, :], [("xres", sub)], sub, lnt[sub % 2])
            P.op("sp", lambda e, sub=sub: e.dma_start(out=y_dst[128 * sub:128 * (sub + 1), :], in_=xres[:, sub, :]), reads=[("xres", sub)], dsem=s_y[sub])
        P.end_scope()

    s_ck = [newsem(f"ck{i}") for i in range(2)]
    s_cv = [newsem(f"cv{i}") for i in range(2)]
    s_clf = newsem("clf")
    s_rkh = [newsem(f"rkh{i}") for i in range(2)]
    s_ktc = newsem("ktc")
    s_cks = newsem("cks")
    ck_scr = nc.dram_tensor("ck_scr", [NSTR, NH, 3, PAST], BF16).ap()
    hcar = sbt("hcar", [16, 1], F32)

    def hist_bufs():
        lfh, lfh_s = alloc([128, 16, 16], F32)
        lfT, lfT_s = alloc([16, 2048], F32, 16)
        cH, cH_s = alloc([16, 2048], F32, 16)
        SKh, SKh_s = alloc([16, 3, 2048], BF16, 16)
        return lfh, lfh_s, lfT, lfT_s, cH, cH_s, SKh, SKh_s

    def hist_half(s, hf, bufs, want_split):
        lfh, lfh_s, lfT, lfT_s, cH, cH_s, SKh, SKh_s = bufs
        r0 = 2048 * hf
        if hf == 0:
            P.op("dve", lambda e: e.memset(hcar[:, :], 0.0), writes=["hcar"])
        P.op("sp", lambda e: e.dma_start(out=lfh[:, :, :], in_=cache_lf[s, r0:r0 + 2048, :].rearrange("(b p) h -> p b h", p=128)), writes=[lfh_s], dsem=s_clf)
        for q in range(4):
            b = nb6s()
            for j in range(4):
                blk = 4 * q + j
                P.op("pe", lambda e, blk=blk, j=j, b=b: e.transpose(out=psb[b][0:16, 128 * j:128 * (j + 1)], in_=lfh[:, blk, :], identity=ident_f[:, :]),
                     reads=[lfh_s, "ident_f"], writes=[PS(b)])
            P.op("act", lambda e, q=q, b=b: e.activation(out=lfT[:, 512 * q:512 * (q + 1)], in_=psb[b][0:16, :], func=AF.Identity), writes=[PS(b), (lfT_s, q)])
        for q in range(4):
            ini = hcar[:, 0:1] if q == 0 else cH[:, 512 * q - 1:512 * q]
            rd = ["hcar"] if q == 0 else [(cH_s, q - 1)]
            P.op("dve", lambda e, q=q, ini=ini: e.tensor_tensor_scan(out=cH[:, 512 * q:512 * (q + 1)], data0=ones16[:, 0:512], data1=lfT[:, 512 * q:512 * (q + 1)],
                                                                   initial=ini, op0=ALU.mult, op1=ALU.add), reads=[(lfT_s, q), "ones16"] + rd, writes=[(cH_s, q)])
        CH = [(cH_s, q) for q in range(4)]
        LT = [(lfT_s, q) for q in range(4)]
        P.op("dve", lambda e: e.tensor_copy(out=hcar[:, 0:1], in_=cH[:, 2047:2048]), reads=CH, writes=["hcar"])
        if want_split:
            P.op("dve", lambda e: e.tensor_scalar(out=SKh[:, 0, :], in0=cH[:, :], scalar1=-8.0, scalar2=None, op0=ALU.mult), reads=CH, writes=[(SKh_s, 0)])
            P.op("dve", lambda e: e.scalar_tensor_tensor(out=lfT[:, :], in0=cH[:, :], scalar=-8.0, in1=SKh[:, 0, :], op0=ALU.mult, op1=ALU.subtract),
                 reads=CH + [(SKh_s, 0)], writes=LT)
            P.op("dve", lambda e: e.tensor_copy(out=SKh[:, 1, :], in_=lfT[:, :]), reads=LT, writes=[(SKh_s, 1)])
            P.op("dve", lambda e: e.tensor_tensor(out=cH[:, :], in0=lfT[:, :], in1=SKh[:, 1, :], op=ALU.subtract), reads=LT + [(SKh_s, 1)], writes=CH)
            P.op("dve", lambda e: e.tensor_copy(out=SKh[:, 2, :], in_=cH[:, :]), reads=CH, writes=[(SKh_s, 2)])
            P.op("sp", lambda e: e.dma_start(out=ck_scr[s, :, :, r0:r0 + 2048], in_=SKh[:, :, :]), reads=[(SKh_s, j) for j in range(3)],
                 writes=[("ckscr", s, hf)], dsem=s_cks)

    def sample_prepass():
        ar_reset()
        bufs = hist_bufs()
        for s in range(NSTR):
            for hf in range(2):
                hist_half(s, hf, bufs, False)
            P.op("dve", lambda e, s=s: e.tensor_copy(out=hend[:, s:s + 1], in_=hcar[:, 0:1]), reads=["hcar"], writes=["hend"])
        P.end_scope()

    def sample_attention(QTA, RK, KTv, Vn, Vn_s, rec, PT):
        bufs = hist_bufs()
        kc_ = [alloc([128, 2, 1024], BF16) for _ in range(2)]
        vc_ = [alloc([128, 2, 1024], BF16) for _ in range(2)]
        ktc = [alloc([128, 2, 16, 128], BF16) for _ in range(2)]
        for i in range(2):
            kt, kt_s = ktc[i]
            P.op("pool", lambda e, kt=kt: e.memset(kt[64:96, :, :, :], 0.0), writes=[(kt_s, "c0")])
            for g in range(4):
                P.op("sp", lambda e, kt=kt, g=g: e.dma_start(out=kt[67:70, :, :, :].rearrange("p b h i -> p (b h i)")[:, 1024 * g:1024 * (g + 1)], in_=ones3[:, :]),
                     reads=["ones3", (kt_s, "c0")], writes=[(kt_s, "c1", g)], dsem=s_ktc)
        KTNEW = [("KT", h) for h in range(NH)] + RK + KTC
        OB = [4, 5]
        for s in range(NSTR):
            for hf in range(2):
                hist_half(s, hf, bufs, True)
            qs = slice(64 * s, 64 * (s + 1))
            pp = 64 * (s % 2)
            for grp in range(17):
                new = grp == 16
                nblk = 1 if new else 2
                if not new:
                    kcb, kcb_s = kc_[grp % 2]
                    vcb, vcb_s = vc_[grp % 2]
                    kt, kt_s = ktc[grp % 2]
                    r0 = 256 * grp
                    P.op("pool", lambda e, kcb=kcb, r0=r0, s=s: e.dma_start(out=kcb[:, :, :], in_=cache_k[s, r0:r0 + 256, :].rearrange("(b p) d -> p b d", p=128)),
                         writes=[kcb_s], dsem=s_ck[grp % 2])
                    P.op("pool", lambda e, vcb=vcb, r0=r0, s=s: e.dma_start(out=vcb[:, :, :], in_=cache_v[s, r0:r0 + 256, :].rearrange("(b p) d -> p b d", p=128)),
                         writes=[vcb_s], dsem=s_cv[grp % 2])
                    RKH = [(kt_s, "rk", h) for h in range(NH)]
                    for h in range(NH):
                        P.op("sp", lambda e, h=h, kt=kt, r0=r0, s=s: e.dma_start(out=kt[64:67, :, h, :], in_=ck_scr[s, h, :, r0:r0 + 256]),
                             reads=[("ckscr", s, r0 // 2048), (kt_s, "c0")], writes=[RKH[h]], dsem=s_rkh[grp % 2])
                    for c in range(8):
                        b = nb6s()
                        psv = psb[b][:].bitcast(BF16)
                        for j in range(2):
                            P.op("pe", lambda e, c=c, j=j, psv=psv, kcb=kcb: e.transpose(out=psv[:, 128 * j:128 * (j + 1)], in_=kcb[:, j, 128 * c:128 * (c + 1)], identity=ident_b[:, :]),
                                 reads=[kcb_s, "ident_b"], writes=[PS(b)])
                        P.op("act", lambda e, c=c, psv=psv, kt=kt: e.activation(out=kt[0:64, :, 2 * c, :], in_=psv[0:64, 0:256].rearrange("p (b i) -> p b i", b=2), func=AF.Identity),
                             writes=[PS(b), (kt_s, 2 * c)])
                        P.op("dve", lambda e, c=c, psv=psv, kt=kt: e.tensor_copy(out=kt[0:64, :, 2 * c + 1, :], in_=psv[64:128, 0:256].rearrange("p (b i) -> p b i", b=2)),
                             writes=[PS(b), (kt_s, 2 * c + 1)])
                    KTG = [(kt_s, h) for h in range(NH)] + RKH + [(kt_s, "c0")] + [(kt_s, "c1", g) for g in range(4)]
                    if s == 0 and grp == 0:
                        tap('s_kt0', kt[0:80, :, :, :], KTG)
                        tap('s_vc0', vcb[:, :, :], [vcb_s])
                for blk in range(nblk):
                    pts = []
                    for hb in range(2):
                        b = nb6s()
                        pt, pt_s = PT[(2 * blk + hb) % 4]
                        pts.append((pt, pt_s))
                        for hh in range(8):
                            h = 8 * hb + hh
                            if new:
                                P.op("pe", lambda e, h=h, hh=hh, b=b, pp=pp, qs=qs: e.matmul(psb[b][pp:pp + 64, 64 * hh:64 * (hh + 1)], lhsT=KTv[0:80, h, qs], rhs=QT[0:80, h, qs],
                                                                                            start=True, stop=False, skip_group_check=True), reads=KTNEW + QTA, writes=[PS(b)])
                                P.op("pe", lambda e, hh=hh, b=b, pp=pp: e.matmul(psb[b][pp:pp + 64, 64 * hh:64 * (hh + 1)], lhsT=ident_b[:, 0:64], rhs=maskb[:, 0:64],
                                                                                start=False, stop=True, skip_group_check=True), reads=["ident_b", "maskb"], writes=[PS(b)])
                            else:
                                P.op("pe", lambda e, h=h, hh=hh, b=b, blk=blk, kt=kt, qs=qs: e.matmul(psb[b][:, 64 * hh:64 * (hh + 1)], lhsT=kt[0:80, blk, h, :], rhs=QT[0:80, h, qs],
                                                                                                     start=True, stop=True, skip_group_check=True), reads=KTG + QTA, writes=[PS(b)])
                        if new:
                            P.op("pool", lambda e, pt=pt: e.memset(pt[:, :], 0.0), writes=[pt_s])
                            P.op("act", lambda e, b=b, pt=pt, pp=pp: e.activation(out=pt[pp:pp + 64, :], in_=psb[b][pp:pp + 64, :], func=AF.Exp, scale=0.125), writes=[PS(b), pt_s])
                        else:
                            P.op("act", lambda e, b=b, pt=pt: e.activation(out=pt[:, :], in_=psb[b][:, :], func=AF.Exp, scale=0.125), writes=[PS(b), pt_s])
                    if s == 0 and grp == 0 and blk == 0:
                        tap('s_pt0', pts[0][0][:, :], [pts[0][1]])
                    for hb in range(2):
                        pt, pt_s = pts[hb]
                        first = grp == 0 and blk == 0
                        if first:
                            P.op("pe", lambda e, hb=hb, pt=pt: e.matmul(psb[OB[hb]][:, :], lhsT=zerob[:, :], rhs=pt[:, :], start=True, stop=False, skip_group_check=True),
                                 reads=["zerob", pt_s], writes=[PS(OB[hb])])
                        for hh in range(8):
                            h = 8 * hb + hh
                            if new:
                                P.op("pe", lambda e, h=h, hh=hh, hb=hb, pt=pt, pp=pp, s=s: e.matmul(psb[OB[hb]][0:64, 64 * hh:64 * (hh + 1)], lhsT=Vn[:, s // 2, 64 * h:64 * (h + 1)],
                                                                                                   rhs=pt[:, 64 * hh:64 * (hh + 1)], start=False, stop=True, skip_group_check=True),
                                     reads=[(Vn_s, s // 2), pt_s], writes=[PS(OB[hb])])
                            else:
                                P.op("pe", lambda e, h=h, hh=hh, hb=hb, pt=pt, blk=blk, vcb=vcb, first=first: e.matmul(
                                    psb[OB[hb]][0:64, 64 * hh:64 * (hh + 1)], lhsT=vcb[:, blk, 64 * h:64 * (h + 1)], rhs=pt[:, 64 * hh:64 * (hh + 1)],
                                    start=False, stop=False, skip_group_check=True), reads=[vcb_s, pt_s], writes=[PS(OB[hb])])
                        if new:
                            P.op("pe", lambda e, hb=hb, pt=pt, pp=pp: e.matmul(psb[OB[hb]][64:128, :], lhsT=onesb[:, :], rhs=pt[:, :], start=False, stop=True, skip_group_check=True),
                                 reads=["onesb", pt_s], writes=[PS(OB[hb])])
                        else:
                            P.op("pe", lambda e, hb=hb, pt=pt, first=first: e.matmul(psb[OB[hb]][64:128, :], lhsT=onesb[:, :], rhs=pt[:, :], start=False, stop=False, skip_group_check=True),
                                 reads=["onesb", pt_s], writes=[PS(OB[hb])])
            if debug and s == 0:
                dbgn, dbgn_s = alloc([128, 512], F32)
                P.op("act", lambda e, dbgn=dbgn: e.activation(out=dbgn[:, :], in_=psb[OB[0]][:, :], func=AF.Identity), writes=[PS(OB[0]), dbgn_s])
                tap('s_num0', dbgn[:, :], [dbgn_s])
            for hb in range(2):
                rc, rc_s = rec[hb]
                P.op("dve", lambda e, rc=rc, hb=hb: e.reciprocal(out=rc[0:64, :], in_=psb[OB[hb]][64:128, :]), writes=[PS(OB[hb]), rc_s])
                if s == 0 and hb == 0:
                    tap('s_rc0', rc[0:64, :], [rc_s])
                for par in range(2):
                    P.op("dve", lambda e, rc=rc, hb=hb, par=par, qs=qs: e.tensor_tensor(
                        out=oT[64 * par:64 * par + 64, 4 * hb:4 * hb + 4, qs],
                        in0=psb[OB[hb]][0:64, :].rearrange("p (c r q) -> p c r q", c=4, r=2)[:, :, par, :],
                        in1=rc[0:64, :].rearrange("p (c r q) -> p c r q", c=4, r=2)[:, :, par, :], op=ALU.mult),
                        reads=[rc_s], writes=[PS(OB[hb])] + [("oT", 4 * hb + c, par) for c in range(4)])

    rr6 = [0]

    def nb6s():
        b = rr6[0] % 4
        rr6[0] += 1
        return b

    P.end_scope()
    for ti in range(ntiles):
        run_tile("p", ti)
    if do_sample:
        sample_prepass()
        run_tile("s", 0)
    assert wst["pos"] == len(wseq)
    P.emit(final_sems=allsems)
    return nc, P


_CACHE = {}


def _f32(a):
    return np.ascontiguousarray(np.asarray(a, dtype=np.float32))


def kernel(x_prompt, x_sample, cache_k, cache_v, cache_logf, state_conv, state_ffn_conv,
           p_prompt, p_sample, ln0_g, ln0_b, w_in, b_f, conv_dw_w, conv_dw_b, conv_ln_g,
           conv_ln_b, w_conv_out, w_attn_out, w_o, ln1_g, ln1_b, w_ffn_up, ffn_dw_w, ffn_dw_b,
           w_ffn_down, ln2_g, ln2_b, w_ple, w_ple_gate):
    if "nc" not in _CACHE:
        _CACHE["nc"] = build_nc()[0]
    nc = _CACHE["nc"]
    shared = {
        "ln0_g": _f32(ln0_g), "ln0_b": _f32(ln0_b), "w_in": _f32(w_in)[0], "b_f": _f32(b_f)[0],
        "conv_dw_w": _f32(conv_dw_w)[0], "conv_dw_b": _f32(conv_dw_b)[0], "conv_ln_g": _f32(conv_ln_g)[0],
        "conv_ln_b": _f32(conv_ln_b)[0], "w_conv_out": _f32(w_conv_out)[0], "w_attn_out": _f32(w_attn_out)[0],
        "w_o": _f32(w_o)[0], "ln1_g": _f32(ln1_g)[0], "ln1_b": _f32(ln1_b)[0], "w_ffn_up": _f32(w_ffn_up)[0],
        "ffn_dw_w": _f32(ffn_dw_w)[0], "ffn_dw_b": _f32(ffn_dw_b)[0], "w_ffn_down": _f32(w_ffn_down)[0],
        "ln2_g": _f32(ln2_g)[0], "ln2_b": _f32(ln2_b)[0], "w_ple": _f32(w_ple)[0], "w_ple_gate": _f32(w_ple_gate)[0],
    }
    x_prompt = _f32(x_prompt); p_prompt = _f32(p_prompt)[0]
    x_sample = _f32(x_sample); p_sample = _f32(p_sample)[0]
    ck = _f32(cache_k)[0].reshape(32, PAST, D); cv = _f32(cache_v)[0].reshape(32, PAST, D)
    clf = _f32(cache_logf)[0]; sc = _f32(state_conv)[0]; sf = _f32(state_ffn_conv)[0]
    in_maps = []
    for c in range(NCORES):
        m = dict(shared)
        s0 = NSTR * c
        m.update({
            "x_p": x_prompt[c], "p_p": p_prompt[c],
            "x_s": x_sample[s0:s0 + NSTR].reshape(NSTR * DSEQ, D), "p_s": p_sample[s0:s0 + NSTR].reshape(NSTR * DSEQ, PLE),
            "cache_k": ck[s0:s0 + NSTR], "cache_v": cv[s0:s0 + NSTR], "cache_lf": clf[s0:s0 + NSTR],
            "st_conv": sc[s0:s0 + NSTR], "st_ffn": sf[s0:s0 + NSTR],
        })
        in_maps.append(m)
    res = run_bass_kernel_spmd(nc, in_maps, core_ids=list(range(NCORES)))
    R = res.results

    def cat(name, shape):
        return np.stack([np.asarray(r[name], dtype=np.float32) for r in R], 0).reshape(shape)

    y_p = cat("y_p", (8, SEQ, D))
    y_s = cat("y_s", (32, DSEQ, D))
    k_p = cat("k_p", (1, 8, SEQ, NH, 64)); v_p = cat("v_p", (1, 8, SEQ, NH, 64))
    lf_p = cat("lf_p", (1, 8, SEQ, NH))
    cv_p = cat("cv_p", (1, 8, 30, D)); ff_p = cat("ff_p", (1, 8, 2, DFF))
    k_s = cat("k_s", (1, 32, DSEQ, NH, 64)); v_s = cat("v_s", (1, 32, DSEQ, NH, 64))
    lf_s = cat("lf_s", (1, 32, DSEQ, NH))
    cv_s = cat("cv_s", (1, 32, 30, D)); ff_s = cat("ff_s", (1, 32, 2, DFF))
    return (y_p, y_s, k_p, v_p, lf_p, cv_p, ff_p, k_s, v_s, lf_s, cv_s, ff_s)
```

```python
import numpy as np
import concourse.bass as bass
import concourse.mybir as mybir
from concourse.bass_utils import run_bass_kernel_spmd

F32 = mybir.dt.float32
BF16 = mybir.dt.bfloat16
AF = mybir.ActivationFunctionType
ALU = mybir.AluOpType

ENGS = ("pe", "act", "dve", "pool", "sp")

NCORES = 8
D = 1024
NH = 16
DFF = 2816
NFC = 22
PLE = 256
SEQ = 8192
PAST = 4096
DSEQ = 64
NSTR = 4
NTILES = 16
N_IN = 7184
ALPHA = float(2.0 ** 0.25)
EPS = 1e-5
NEG = -30000.0
NWB = 2
NKV = 3


class DmaSem:
    def __init__(self, handle):
        self.h = handle
        self.count = 0


class Op:
    __slots__ = ("eng", "fn", "deps", "need_inc", "seq", "idx", "dsem", "dval", "is_dma")


class Prog:
    def __init__(self, nc):
        self.nc = nc
        self.streams = {e: [] for e in ENGS}
        self.last_w = {}
        self.readers = {}
        self.scoped = set()
        self.inherit = []
        self.touched = set()

    @staticmethod
    def _root(k):
        while isinstance(k, tuple):
            k = k[0]
        return k

    def end_scope(self):
        last = {}
        dmas = {}
        keys = [k for k in list(self.last_w.keys()) + list(self.readers.keys()) if self._root(k) in self.scoped]
        ops = []
        for k in set(keys):
            w = self.last_w.pop(k, None)
            if w is not None:
                ops.append(w)
            ops.extend(self.readers.pop(k, ()))
        ops.extend(self.inherit)
        for o in ops:
            if o.is_dma:
                key = id(o.dsem)
                if key not in dmas or dmas[key].dval < o.dval:
                    dmas[key] = o
            else:
                if o.eng not in last or last[o.eng].idx < o.idx:
                    last[o.eng] = o
        self.inherit = list(last.values()) + list(dmas.values())
        self.touched = set()

    def op(self, eng, fn, reads=(), writes=(), dsem=None):
        o = Op()
        o.eng = eng
        o.fn = fn
        o.need_inc = False
        o.seq = None
        o.is_dma = dsem is not None
        o.dsem = dsem
        if dsem is not None:
            dsem.count += 16
            o.dval = dsem.count
        else:
            o.dval = None
        deps = {}
        for s in reads:
            w = self.last_w.get(s)
            if w is not None:
                deps[id(w)] = w
        for s in writes:
            w = self.last_w.get(s)
            if w is not None and (w.is_dma or w.eng != eng or o.is_dma):
                deps[id(w)] = w
            for r in self.readers.get(s, ()):
                if r.is_dma or r.eng != eng or o.is_dma:
                    deps[id(r)] = r
            if self._root(s) in self.scoped and s not in self.touched:
                self.touched.add(s)
                for r in self.inherit:
                    if r.is_dma or r.eng != eng or o.is_dma:
                        deps[id(r)] = r
        o.deps = []
        for d in deps.values():
            if d.is_dma:
                v = d.dsem.count - (16 if d.dsem is dsem else 0)
                o.deps.append((d, v))
            elif not (d.eng == "pe" and eng == "pe" and not o.is_dma):
                d.need_inc = True
                o.deps.append((d, None))
        for s in reads:
            self.readers.setdefault(s, []).append(o)
        for s in writes:
            self.last_w[s] = o
            self.readers[s] = []
        o.idx = len(self.streams[eng])
        self.streams[eng].append(o)
        return o

    def emit(self, final_sems=()):
        nc = self.nc
        sems = {e: nc.alloc_semaphore(name=f"s_{e}") for e in ENGS}
        for e in ENGS:
            c = 0
            for o in self.streams[e]:
                if o.need_inc and not o.is_dma:
                    c += 1
                    o.seq = c
        with nc.Block() as block:
            def body(e):
                def run(engh):
                    waited = {}
                    for o in self.streams[e]:
                        for d, dv in o.deps:
                            if d.is_dma:
                                key = ("d", id(d.dsem))
                                val = dv
                                semh = d.dsem.h
                            else:
                                key = ("e", d.eng)
                                val = d.seq
                                semh = sems[d.eng]
                            if waited.get(key, 0) >= val:
                                continue
                            waited[key] = val
                            engh.wait_ge(semh, val)
                        ins = o.fn(engh)
                        if o.is_dma:
                            ins.then_inc(o.dsem.h, 16)
                        elif o.need_inc:
                            ins.then_inc(sems[e], 1)
                    if e == "sp":
                        for ds in final_sems:
                            if ds.count > 0:
                                engh.wait_ge(ds.h, ds.count)
                return run
            block.tensor(body("pe"))
            block.scalar(body("act"))
            block.vector(body("dve"))
            block.gpsimd(body("pool"))
            block.sync(body("sp"))


def build_nc(ntiles=NTILES, do_sample=True, debug=False):
    nc = bass.Bass("TRN2", target_bir_lowering=False)
    P = Prog(nc)
    allsems = []

    def tap(name, ap, reads):
        if not debug:
            return
        shape = list(ap.shape)
        d = nc.dram_tensor(name, shape, ap.dtype, kind="ExternalOutput").ap()
        P.op("sp", lambda e: e.dma_start(out=d, in_=ap), reads=reads, dsem=newsem("t_" + name))

    def newsem(name):
        s = DmaSem(nc.alloc_semaphore(name=name))
        allsems.append(s)
        return s

    def din(name, shape):
        return nc.dram_tensor(name, shape, F32, kind="ExternalInput").ap()

    def dout(name, shape):
        return nc.dram_tensor(name, shape, F32, kind="ExternalOutput").ap()

    x_p = din("x_p", [SEQ, D]); p_p = din("p_p", [SEQ, PLE])
    x_s = din("x_s", [NSTR * DSEQ, D]); p_s = din("p_s", [NSTR * DSEQ, PLE])
    cache_k = din("cache_k", [NSTR, PAST, D]); cache_v = din("cache_v", [NSTR, PAST, D])
    cache_lf = din("cache_lf", [NSTR, PAST, NH])
    st_conv = din("st_conv", [NSTR, 30, D]); st_ffn = din("st_ffn", [NSTR, 2, DFF])
    ln0_g = din("ln0_g", [D]); ln0_b = din("ln0_b", [D])
    w_in = din("w_in", [D, N_IN]); b_f = din("b_f", [NH])
    conv_dw_w = din("conv_dw_w", [31, D]); conv_dw_b = din("conv_dw_b", [D])
    conv_ln_g = din("conv_ln_g", [D]); conv_ln_b = din("conv_ln_b", [D])
    w_conv_out = din("w_conv_out", [D, D]); w_attn_out = din("w_attn_out", [D, D]); w_o = din("w_o", [D, D])
    ln1_g = din("ln1_g", [D]); ln1_b = din("ln1_b", [D])
    w_ffn_up = din("w_ffn_up", [D, 2 * DFF]); ffn_dw_w = din("ffn_dw_w", [3, DFF]); ffn_dw_b = din("ffn_dw_b", [DFF])
    w_ffn_down = din("w_ffn_down", [DFF, D])
    ln2_g = din("ln2_g", [D]); ln2_b = din("ln2_b", [D])
    w_ple = din("w_ple", [PLE, D]); w_ple_gate = din("w_ple_gate", [D, D])

    y_p = dout("y_p", [SEQ, D]); y_s = dout("y_s", [NSTR * DSEQ, D])
    k_p = dout("k_p", [SEQ, D]); v_p = dout("v_p", [SEQ, D]); lf_p = dout("lf_p", [SEQ, NH])
    cv_p = dout("cv_p", [30, D]); ff_p = dout("ff_p", [2, DFF])
    k_s = dout("k_s", [NSTR * DSEQ, D]); v_s = dout("v_s", [NSTR * DSEQ, D]); lf_s = dout("lf_s", [NSTR * DSEQ, NH])
    cv_s = dout("cv_s", [NSTR, 30, D]); ff_s = dout("ff_s", [NSTR, 2, DFF])

    kt_scr = nc.dram_tensor("kt_scr", [8, NTILES, 80, 1024], BF16).ap()
    v_scr = nc.dram_tensor("v_scr", [8, NTILES, 128, 768], BF16).ap()

    def sbt(name, shape, dt):
        return nc.alloc_sbuf_tensor(name, shape, dt)

    ident_f = sbt("ident_f", [128, 128], F32)
    ident_b = sbt("ident_b", [128, 128], BF16)
    maskb = sbt("maskb", [128, 128], BF16)
    onesm = sbt("onesm", [128, 128], BF16)
    onesb = sbt("onesb", [128, 64], BF16)
    zerob = sbt("zerob", [128, 128], BF16)
    ones16 = sbt("ones16", [16, 512], F32)
    ones3 = sbt("ones3", [3, 1024], BF16)
    cvec = sbt("cvec", [128, 8, 34], F32)
    fvec = sbt("fvec", [128, NFC, 4], F32)
    wfl = sbt("wfl", [128, 8, 16], BF16)
    nbf = sbt("nbf", [16, 1], F32)
    carry = sbt("carry", [16, 1], F32)
    hend = sbt("hend", [16, 4], F32)
    gbuf = sbt("gbuf", [128, 2, 1024], F32)
    wbuf = [sbt(f"wbuf{i}", [128, 8192], BF16) for i in range(NWB)]
    xres = sbt("xres", [128, 4, 1024], F32)
    xT = sbt("xT", [128, 8, 512], BF16)
    oT = sbt("oT", [128, 8, 512], BF16)
    gated_c = sbt("gated_c", [128, 8, 512], BF16)
    QT = sbt("QT", [128, 16, 512], BF16)
    KTs = sbt("KTs", [128, 8192], BF16)
    uhist = sbt("uhist", [128, 8, 4, 30], BF16)
    ahist = sbt("ahist", [128, NFC, 4, 2], F32)
    ARENA_W = 18944
    arena = sbt("arena", [128, ARENA_W], F32)
    psb = [nc.alloc_psum_tensor(f"psb{i}", [128, 512], F32) for i in range(8)]

    P.scoped.add("A")
    ar = {"off": 0, "n": 0}

    def ar_reset():
        ar["off"] = 0

    def alloc(shape, dt, parts=128):
        n = 1
        for s in shape[1:]:
            n *= s
        words = n if dt == F32 else (n + 1) // 2
        words = (words + 7) // 8 * 8
        off = ar["off"]
        assert off + words <= ARENA_W, ("arena overflow", off, words)
        ar["off"] = off + words
        v = arena[0:shape[0], off:off + words]
        if dt == BF16:
            v = v.bitcast(BF16)[:, 0:n]
        else:
            v = v[:, 0:n]
        if len(shape) > 2:
            names = " ".join(f"d{i}" for i in range(1, len(shape)))
            kw = {f"d{i}": shape[i] for i in range(1, len(shape))}
            v = v.rearrange(f"p ({names}) -> p {names}", **kw)
        ar["n"] += 1
        return v, ("A", ar["n"])

    bank_rr = [0]

    def nb():
        b = bank_rr[0] % 8
        bank_rr[0] += 1
        return b

    def PS(b):
        return ("ps", b)

    P.op("pool", lambda e: e.memset(ident_f[:], 1.0), writes=["ident_f"])
    P.op("pool", lambda e: e.affine_select(out=ident_f[:], in_=ident_f[:], pattern=[[1, 128]], compare_op=ALU.is_equal,
                                           fill=0.0, base=0, channel_multiplier=-1), reads=["ident_f"], writes=["ident_f"])
    P.op("pool", lambda e: e.tensor_copy(out=ident_b[:], in_=ident_f[:]), reads=["ident_f"], writes=["ident_b"])
    mask32, mask32_s = alloc([128, 128], F32)
    P.op("pool", lambda e: e.memset(mask32[:], 0.0), writes=[mask32_s])
    P.op("pool", lambda e: e.affine_select(out=mask32[:], in_=mask32[:], pattern=[[1, 128]], compare_op=ALU.is_ge,
                                           fill=NEG, base=0, channel_multiplier=-1), reads=[mask32_s], writes=[mask32_s])
    P.op("pool", lambda e: e.tensor_copy(out=maskb[:], in_=mask32[:]), reads=[mask32_s], writes=["maskb"])
    P.op("pool", lambda e: e.memset(onesm[:], 1.0 / 1024.0), writes=["onesm"])
    P.op("pool", lambda e: e.memset(onesb[:], 1.0), writes=["onesb"])
    P.op("pool", lambda e: e.memset(zerob[:], 0.0), writes=["zerob"])
    if debug:
        P.op("pool", lambda e: e.memset(oT[:, :, :], 7.0), writes=[("oT", g_, j_) for g_ in range(8) for j_ in range(2)])
    P.op("pool", lambda e: e.memset(ones16[:], 1.0), writes=["ones16"])
    P.op("pool", lambda e: e.memset(ones3[:], 1.0), writes=["ones3"])
    P.op("pool", lambda e: e.memset(carry[:], 0.0), writes=["carry"])
    P.op("pool", lambda e: e.memset(uhist[:], 0.0), writes=["uhist"])
    P.op("pool", lambda e: e.memset(ahist[:], 0.0), writes=["ahist"])
    P.op("pool", lambda e: e.memset(QT[64:96, :, :], 1.0), writes=["QTc"])
    P.op("pool", lambda e: e.memset(KTs[64:96, :], 0.0), writes=["KTc0"])
    s_c1 = newsem("c1")
    for g in range(8):
        P.op("sp", lambda e, g=g: e.dma_start(out=KTs[67:70, 1024 * g:1024 * (g + 1)], in_=ones3[:, :]),
             reads=["ones3", "KTc0"], writes=[("KTc1", g)], dsem=s_c1)
    KTC = ["KTc0"] + [("KTc1", g) for g in range(8)]

    vrows, vrows_s = alloc([34, 1024], F32, 34)
    frows, frows_s = alloc([4, DFF], F32, 4)
    bft, bft_s = alloc([16, 1], F32, 16)
    s_c2 = newsem("c2")
    vr = [(vrows_s, "r", i) for i in range(5)]
    P.op("sp", lambda e: e.dma_start(out=vrows[0:1, :], in_=conv_dw_b.rearrange("(o n) -> o n", o=1)), writes=[vr[0], vrows_s], dsem=s_c2)
    P.op("sp", lambda e: e.dma_start(out=vrows[1:2, :], in_=conv_ln_g.rearrange("(o n) -> o n", o=1)), writes=[vr[1]], dsem=s_c2)
    P.op("sp", lambda e: e.dma_start(out=vrows[2:3, :], in_=conv_ln_b.rearrange("(o n) -> o n", o=1)), writes=[vr[2]], dsem=s_c2)
    P.op("sp", lambda e: e.dma_start(out=vrows[3:34, :], in_=conv_dw_w), writes=[vr[3]], dsem=s_c2)
    P.op("sp", lambda e: e.dma_start(out=frows[0:3, :], in_=ffn_dw_w), writes=[vr[4], frows_s], dsem=s_c2)
    P.op("sp", lambda e: e.dma_start(out=frows[3:4, :], in_=ffn_dw_b.rearrange("(o n) -> o n", o=1)), writes=[(frows_s, "r5")], dsem=s_c2)
    P.op("sp", lambda e: e.dma_start(out=bft[:, :], in_=b_f.rearrange("(h o) -> h o", o=1)), writes=[(bft_s, "r6"), bft_s], dsem=s_c2)
    VR = vr + [(frows_s, "r5"), (bft_s, "r6"), vrows_s, frows_s, bft_s]
    P.op("dve", lambda e: e.tensor_scalar(out=nbf[:], in0=bft[:, :], scalar1=-1.0, scalar2=None, op0=ALU.mult),
         reads=VR, writes=["nbf"])
    for c in range(8):
        b = nb()
        P.op("pe", lambda e, c=c, b=b: e.transpose(out=psb[b][:, 0:34], in_=vrows[0:34, 128 * c:128 * (c + 1)], identity=ident_f[0:34, 0:34]),
             reads=VR + ["ident_f"], writes=[PS(b)])
        P.op("dve", lambda e, c=c, b=b: e.tensor_copy(out=cvec[:, c, :], in_=psb[b][:, 0:34]), writes=[PS(b), ("cvec", c)])
    for c in range(NFC):
        b = nb()
        P.op("pe", lambda e, c=c, b=b: e.transpose(out=psb[b][:, 0:4], in_=frows[0:4, 128 * c:128 * (c + 1)], identity=ident_f[0:4, 0:4]),
             reads=VR + ["ident_f"], writes=[PS(b)])
        P.op("dve", lambda e, c=c, b=b: e.tensor_copy(out=fvec[:, c, :], in_=psb[b][:, 0:4]), writes=[PS(b), ("fvec", c)])
    CVEC = [("cvec", c) for c in range(8)]
    FVEC = [("fvec", c) for c in range(NFC)]

    wgroups = {}

    def wsrc(W, r0, nk, c0, w):
        return W[r0:r0 + nk * 128, :].rearrange("(kc p) n -> p kc n", p=128)[:, :, c0:c0 + w]

    def wgroup(name, nk, ncols, srcs):
        scr = nc.dram_tensor("ws_" + name, [128, nk, ncols], BF16).ap()
        sem = newsem("wp_" + name)
        slots = []
        for j, (src, off, w) in enumerate(srcs):
            sl = ("wscr", name, j)
            slots.append(sl)
            P.op("pool", lambda e, src=src, off=off, w=w: e.dma_start(out=scr[:, :, off:off + w], in_=src), writes=[sl], dsem=sem)
        wgroups[name] = (scr, nk, ncols, slots)

    wgroup("FL", 8, 16, [(wsrc(w_in, 0, 8, 5120, 16), 0, 16)])
    wgroup("PLE", 2, 1024, [(wsrc(w_ple, 0, 2, 0, 1024), 0, 1024)])
    wgroup("A1", 8, 1024, [(wsrc(w_in, 0, 8, 0, 512), 0, 512), (wsrc(w_in, 0, 8, 1024, 512), 512, 512)])
    wgroup("A2", 8, 1024, [(wsrc(w_in, 0, 8, 512, 512), 0, 512), (wsrc(w_in, 0, 8, 1536, 512), 512, 512)])
    wgroup("Q", 8, 1024, [(wsrc(w_in, 0, 8, 2048, 1024), 0, 1024)])
    wgroup("K", 8, 1024, [(wsrc(w_in, 0, 8, 3072, 1024), 0, 1024)])
    wgroup("V", 8, 1024, [(wsrc(w_in, 0, 8, 4096, 1024), 0, 1024)])
    wgroup("GC", 8, 1024, [(wsrc(w_in, 0, 8, 5136, 1024), 0, 1024)])
    wgroup("CO", 8, 1024, [(wsrc(w_conv_out, 0, 8, 0, 1024), 0, 1024)])
    wgroup("GA", 8, 1024, [(wsrc(w_in, 0, 8, 6160, 1024), 0, 1024)])
    wgroup("AO", 8, 1024, [(wsrc(w_attn_out, 0, 8, 0, 1024), 0, 1024)])
    wgroup("O", 8, 1024, [(wsrc(w_o, 0, 8, 0, 1024), 0, 1024)])
    wgroup("PG", 8, 1024, [(wsrc(w_ple_gate, 0, 8, 0, 1024), 0, 1024)])
    for g in range(6):
        hw = 512 if g < 5 else 256
        wgroup(f"UP{g}", 8, 2 * hw, [(wsrc(w_ffn_up, 0, 8, 512 * g, hw), 0, hw), (wsrc(w_ffn_up, 0, 8, DFF + 512 * g, hw), hw, hw)])
    for n in range(2):
        for kh in range(2):
            wgroup(f"DN{kh}{n}", 11, 512, [(wsrc(w_ffn_down, kh * 11 * 128, 11, 512 * n, 512), 0, 512)])

    s_wres = newsem("wres")
    P.op("sp", lambda e: e.dma_start(out=wfl[:], in_=wgroups["FL"][0]), reads=wgroups["FL"][3], writes=["wfl"], dsem=s_wres)

    tile_seq = ["A1", "A2", "Q", "K", "V", "GC", "CO", "GA", "AO", "O", "PG", "PLE"] + [f"UP{g}" for g in range(6)] + ["DN00", "DN10", "DN01", "DN11"]
    ntot = ntiles + (1 if do_sample else 0)
    wseq = tile_seq * ntot
    wsem = [newsem(f"wl{i}") for i in range(NWB)]
    wst = {"pos": 0, "loaded": 0}

    def w_ensure(n):
        while wst["loaded"] <= n and wst["loaded"] < len(wseq):
            m = wst["loaded"]
            scr, nk, ncols, slots = wgroups[wseq[m]]
            b = m % NWB
            dst = wbuf[b][:, 0:nk * ncols].rearrange("p (k n) -> p k n", k=nk)
            P.op("sp", lambda e, dst=dst, scr=scr: e.dma_start(out=dst, in_=scr), reads=slots, writes=[("wb", b)], dsem=wsem[b])
            wst["loaded"] += 1

    def w_get(name, ahead=NWB - 1):
        n = wst["pos"]
        assert wseq[n] == name, (wseq[n], name)
        w_ensure(n + ahead)
        scr, nk, ncols, slots = wgroups[name]
        b = n % NWB
        wst["pos"] += 1
        return wbuf[b][:, 0:nk * ncols].rearrange("p (k n) -> p k n", k=nk), ("wb", b)

    s_x = [newsem(f"x{i}") for i in range(2)]
    s_g = newsem("g"); s_b = newsem("b")
    s_y = [newsem(f"y{i}") for i in range(4)]
    s_kvo = [newsem(f"kvo{i}") for i in range(2)]
    s_lf = newsem("lf")
    s_ktw = [newsem(f"ktw{i}") for i in range(8)]
    s_vw = [newsem(f"vw{i}") for i in range(8)]
    s_rq = newsem("rq"); s_rk = newsem("rk")
    s_kvK = [newsem(f"kvK{i}") for i in range(NKV)]
    s_kvV = [newsem(f"kvV{i}") for i in range(NKV)]
    s_p = newsem("p")
    s_st = [newsem(f"st{i}") for i in range(2)]
    s_vst = newsem("vst")

    def run_tile(kind, ti):
        prompt = kind == "p"
        Tt = 512 if prompt else 256
        nsub = Tt // 128
        nseg, L = (1, 512) if prompt else (4, 64)
        t0 = ti * 512
        last = prompt and ti == NTILES - 1
        need_state = last or not prompt
        x_src = x_p[t0:t0 + Tt, :] if prompt else x_s
        p_src = p_p[t0:t0 + Tt, :] if prompt else p_s
        y_dst = y_p[t0:t0 + Tt, :] if prompt else y_s
        k_dst = k_p[t0:t0 + Tt, :] if prompt else k_s
        v_dst = v_p[t0:t0 + Tt, :] if prompt else v_s
        lf_dst = lf_p[t0:t0 + Tt, :] if prompt else lf_s
        state_subs = [3] if prompt else [0, 1]

        def load_gb(g_ap, b_ap):
            P.op("sp", lambda e: e.dma_start(out=gbuf[:, 0, :], in_=g_ap.partition_broadcast(128)), writes=["gb0"], dsem=s_g)
            P.op("sp", lambda e: e.dma_start(out=gbuf[:, 1, :], in_=b_ap.partition_broadcast(128)), writes=["gb1"], dsem=s_b)

        def ln_rows(src, src_slots, sub, tmp):
            st, st_s, mv, mv_s, lnv, lnv_s, rstd, rstd_s = tmp
            XR = ("xres", sub)
            P.op("dve", lambda e: e.bn_stats(out=st[:, 0, :], in_=src[:, 0:512]), reads=src_slots, writes=[st_s])
            P.op("dve", lambda e: e.bn_stats(out=st[:, 1, :], in_=src[:, 512:1024]), reads=src_slots, writes=[(st_s, 1)])
            P.op("dve", lambda e: e.bn_aggr(out=mv[:, :], in_=st[:, :, :].rearrange("p a b -> p (a b)")), reads=[st_s, (st_s, 1)], writes=[mv_s])
            P.op("act", lambda e: e.activation(out=lnv[:, :], in_=mv[:, 1:2], func=AF.Ln, bias=EPS, scale=1.0), reads=[mv_s], writes=[lnv_s])
            P.op("act", lambda e: e.activation(out=rstd[:, :], in_=lnv[:, :], func=AF.Exp, scale=-0.5), reads=[lnv_s], writes=[rstd_s])
            P.op("dve", lambda e: e.tensor_scalar(out=xres[:, sub, :], in0=src, scalar1=mv[:, 0:1], scalar2=rstd[:, 0:1],
                                                  op0=ALU.subtract, op1=ALU.mult), reads=src_slots + [mv_s, rstd_s], writes=[XR])
            P.op("pool", lambda e: e.tensor_tensor(out=xres[:, sub, :], in0=xres[:, sub, :], in1=gbuf[:, 0, :], op=ALU.mult),
                 reads=[XR, "gb0"], writes=[XR])
            P.op("pool", lambda e: e.tensor_tensor(out=xres[:, sub, :], in0=xres[:, sub, :], in1=gbuf[:, 1, :], op=ALU.add),
                 reads=[XR, "gb1"], writes=[XR])

        def to_featmajor(sub, xb, xb_s):
            XR = ("xres", sub)
            P.op("act", lambda e: e.activation(out=xb[:, :], in_=xres[:, sub, :], func=AF.Identity), reads=[XR], writes=[xb_s])
            b = nb()
            psv = psb[b][:].bitcast(BF16)
            for c in range(8):
                P.op("pe", lambda e, c=c: e.transpose(out=psv[:, 128 * c:128 * (c + 1)], in_=xb[:, 128 * c:128 * (c + 1)], identity=ident_b[:, :]),
                     reads=[xb_s, "ident_b"], writes=[PS(b)])
            P.op("dve", lambda e: e.tensor_copy(out=xT[:, :, 128 * sub:128 * (sub + 1)], in_=psv[:, :].rearrange("p (c t) -> p c t", c=8)),
                 writes=[PS(b), ("xT", sub)])

        XT = [("xT", s) for s in range(nsub)]

        ar_reset()
        xin = [alloc([128, 1024], F32) for _ in range(2)]
        xb2 = [alloc([128, 1024], BF16)] * 2
        lnt = []
        for _ in range(2):
            st, st_s = alloc([128, 2, 6], F32); mv, mv_s = alloc([128, 2], F32)
            lnv, lnv_s = alloc([128, 1], F32); rstd, rstd_s = alloc([128, 1], F32)
            lnt.append((st, st_s, mv, mv_s, lnv, lnv_s, rstd, rstd_s))
        kvst = [alloc([128, 1024], F32) for _ in range(2)]

        load_gb(ln0_g, ln0_b)
        for sub in range(nsub):
            xi, xi_s = xin[sub % 2]
            P.op("sp", lambda e, sub=sub, xi=xi: e.dma_start(out=xi[:, :], in_=x_src[128 * sub:128 * (sub + 1), :]), writes=[xi_s], dsem=s_x[sub % 2])
            ln_rows(xi, [xi_s], sub, lnt[sub % 2])
            to_featmajor(sub, *xb2[sub % 2])

        if ti == 0:
            tap(('d_' if prompt else 's_') + 'xT', xT[:, :, :], XT)
        lA, lA_s = alloc([16, 512], F32, 16)
        lB, lB_s = alloc([16, 512], F32, 16)
        cT, cT_s = alloc([16, 512], F32, 16)
        r1, r1_s = alloc([16, 512], F32, 16)
        SQ, SQ_s = alloc([16, 3, 512], BF16, 16)
        SK, SK_s = alloc([16, 3, 512], BF16, 16)
        lft, lft_s = alloc([128, 4, 16], F32)
        b = nb()
        for kc in range(8):
            P.op("pe", lambda e, kc=kc, b=b: e.matmul(psb[b][0:16, 0:Tt], lhsT=wfl[:, kc, :], rhs=xT[:, kc, 0:Tt], start=(kc == 0), stop=(kc == 7)),
                 reads=XT + ["wfl"], writes=[PS(b)])
        P.op("act", lambda e, b=b: e.activation(out=lA[:, 0:Tt], in_=psb[b][0:16, 0:Tt], func=AF.Exp, bias=nbf[:, 0:1], scale=-1.0),
             reads=["nbf"], writes=[PS(b), lA_s])
        P.op("act", lambda e: e.activation(out=lB[:, 0:Tt], in_=lA[:, 0:Tt], func=AF.Ln, bias=1.0, scale=1.0), reads=[lA_s], writes=[lB_s])
        P.op("dve", lambda e: e.tensor_scalar(out=lA[:, 0:Tt], in0=lB[:, 0:Tt], scalar1=-1.0, scalar2=None, op0=ALU.mult),
             reads=[lB_s], writes=[lA_s])
        if prompt:
            P.op("dve", lambda e: e.tensor_tensor_scan(out=cT[:, 0:512], data0=ones16[:, 0:512], data1=lA[:, 0:512], initial=carry[:, 0:1],
                                                       op0=ALU.mult, op1=ALU.add), reads=[lA_s, "ones16", "carry"], writes=[cT_s])
            P.op("dve", lambda e: e.tensor_copy(out=carry[:, 0:1], in_=cT[:, 511:512]), reads=[cT_s], writes=["carry"])
        else:
            for s in range(4):
                P.op("dve", lambda e, s=s: e.tensor_tensor_scan(out=cT[:, 64 * s:64 * (s + 1)], data0=ones16[:, 0:64], data1=lA[:, 64 * s:64 * (s + 1)],
                                                                initial=hend[:, s:s + 1], op0=ALU.mult, op1=ALU.add),
                     reads=[lA_s, "ones16", "hend"], writes=[cT_s])
        P.op("dve", lambda e: e.tensor_scalar(out=SQ[:, 0, 0:Tt], in0=cT[:, 0:Tt], scalar1=8.0, scalar2=None, op0=ALU.mult), reads=[cT_s], writes=[SQ_s])
        P.op("dve", lambda e: e.scalar_tensor_tensor(out=r1[:, 0:Tt], in0=cT[:, 0:Tt], scalar=8.0, in1=SQ[:, 0, 0:Tt], op0=ALU.mult, op1=ALU.subtract),
             reads=[cT_s, SQ_s], writes=[r1_s])
        P.op("dve", lambda e: e.tensor_copy(out=SQ[:, 1, 0:Tt], in_=r1[:, 0:Tt]), reads=[r1_s], writes=[(SQ_s, 1)])
        P.op("dve", lambda e: e.tensor_tensor(out=lB[:, 0:Tt], in0=r1[:, 0:Tt], in1=SQ[:, 1, 0:Tt], op=ALU.subtract), reads=[r1_s, (SQ_s, 1)], writes=[lB_s])
        P.op("dve", lambda e: e.tensor_copy(out=SQ[:, 2, 0:Tt], in_=lB[:, 0:Tt]), reads=[lB_s], writes=[(SQ_s, 2)])
        SQA = [SQ_s, (SQ_s, 1), (SQ_s, 2)]
        P.op("dve", lambda e: e.tensor_scalar(out=SK[:, :, 0:Tt], in0=SQ[:, :, 0:Tt], scalar1=-1.0, scalar2=None, op0=ALU.mult), reads=SQA, writes=[SK_s])
        b = nb()
        for sub in range(nsub):
            P.op("pe", lambda e, sub=sub, b=b: e.transpose(out=psb[b][:, 16 * sub:16 * (sub + 1)], in_=lA[:, 128 * sub:128 * (sub + 1)], identity=ident_f[0:16, 0:16]),
                 reads=[lA_s, "ident_f"], writes=[PS(b)])
        P.op("dve", lambda e, b=b: e.tensor_copy(out=lft[:, 0:nsub, :], in_=psb[b][:, 0:16 * nsub].rearrange("p (s h) -> p s h", h=16)), writes=[PS(b), lft_s])
        P.op("pool", lambda e: e.dma_start(out=lf_dst.rearrange("(s p) h -> p s h", p=128), in_=lft[:, 0:nsub, :]), reads=[lft_s], dsem=s_lf)
        RQ = [("rq", h) for h in range(NH)]
        RK = [("rk", h) for h in range(NH)]
        if prompt:
            KTv = KTs[:, :].rearrange("p (g b h i) -> p g b h i", g=8, b=4, h=2)
        else:
            KTv = KTs[:, 0:4096].rearrange("p (h t) -> p h t", h=16)
        for h in range(NH):
            P.op("pool", lambda e, h=h: e.dma_start(out=QT[67:70, h, 0:Tt], in_=SQ[h:h + 1, :, 0:Tt]), reads=SQA + ["QTc"], writes=[RQ[h]], dsem=s_rq)
            if prompt:
                P.op("pool", lambda e, h=h: e.dma_start(out=KTv[64:67, h // 2, :, h % 2, :], in_=SK[h:h + 1, :, 0:512]), reads=[SK_s] + KTC, writes=[RK[h]], dsem=s_rk)
            else:
                P.op("pool", lambda e, h=h: e.dma_start(out=KTv[64:67, h, :], in_=SK[h:h + 1, :, 0:256]), reads=[SK_s] + KTC, writes=[RK[h]], dsem=s_rk)

        ucT, ucT_s = alloc([128, 8, 512], BF16)
        sgt = [alloc([128, 512], BF16) for _ in range(2)]
        acc = [alloc([128, 512], F32) for _ in range(4)]
        sqt = [alloc([128, 512], BF16) for _ in range(2)]
        ust, ust_s = kvst[0]
        bmean = 6
        bmsq = 7
        def nb6():
            while True:
                b = nb()
                if b < 6:
                    return b

        def seg3(ap):
            return ap.rearrange("p (s l) -> p s l", s=nseg)

        if not prompt:
            sc, sc_s = alloc([120, 1024], F32, 120)
            P.op("sp", lambda e: e.dma_start(out=sc[:, :], in_=st_conv.rearrange("s r d -> (s r) d")), writes=[sc_s], dsem=s_st[0])
            for c in range(8):
                b = nb6()
                P.op("pe", lambda e, c=c, b=b: e.transpose(out=psb[b][:, 0:120], in_=sc[0:120, 128 * c:128 * (c + 1)], identity=ident_f[0:120, 0:120]),
                     reads=[sc_s, "ident_f"], writes=[PS(b)])
                P.op("dve", lambda e, c=c, b=b: e.tensor_copy(out=uhist[:, c, :, :], in_=psb[b][:, 0:120].rearrange("p (s r) -> p s r", s=4)),
                     writes=[PS(b), ("uhist", c)])

        utl = [alloc([128, nseg, 30 + L], BF16) for _ in range(8)]
        for half in range(2):
            W, W_s = w_get("A1" if half == 0 else "A2")
            for cc in range(4):
                c = 4 * half + cc
                ba = nb6(); bg = nb6()
                for kc in range(8):
                    P.op("pe", lambda e, kc=kc, cc=cc, ba=ba, W=W: e.matmul(psb[ba][:, 0:Tt], lhsT=W[:, kc, 128 * cc:128 * (cc + 1)], rhs=xT[:, kc, 0:Tt], start=(kc == 0), stop=(kc == 7)),
                         reads=XT + [W_s], writes=[PS(ba)])
                for kc in range(8):
                    P.op("pe", lambda e, kc=kc, cc=cc, bg=bg, W=W: e.matmul(psb[bg][:, 0:Tt], lhsT=W[:, kc, 512 + 128 * cc:512 + 128 * (cc + 1)], rhs=xT[:, kc, 0:Tt], start=(kc == 0), stop=(kc == 7)),
                         reads=XT + [W_s], writes=[PS(bg)])
                sg, sg_s = sgt[c % 2]
                u, u_s = utl[c]
                P.op("act", lambda e, bg=bg, sg=sg: e.activation(out=sg[:, 0:Tt], in_=psb[bg][:, 0:Tt], func=AF.Sigmoid), writes=[PS(bg), sg_s])
                P.op("pool", lambda e, c=c, u=u: e.tensor_copy(out=u[:, :, 0:30], in_=uhist[:, c, 0:nseg, :]), reads=[("uhist", c)], writes=[(u_s, "h")])
                P.op("dve", lambda e, ba=ba, sg=sg, u=u: e.tensor_tensor(out=u[:, :, 30:30 + L], in0=seg3(psb[ba][:, 0:Tt]), in1=seg3(sg[:, 0:Tt]), op=ALU.mult),
                     reads=[sg_s], writes=[PS(ba), u_s])
                if prompt:
                    P.op("pool", lambda e, c=c, u=u: e.tensor_copy(out=uhist[:, c, 0, :], in_=u[:, 0, L:L + 30]), reads=[u_s, (u_s, "h")], writes=[("uhist", c)])
            if need_state:
                for sub in state_subs:
                    ba = nb6(); bg = nb6()
                    for kc in range(8):
                        P.op("pe", lambda e, kc=kc, sub=sub, ba=ba, W=W: e.matmul(psb[ba][:, :], lhsT=xT[:, kc, 128 * sub:128 * (sub + 1)], rhs=W[:, kc, 0:512], start=(kc == 0), stop=(kc == 7)),
                             reads=XT + [W_s], writes=[PS(ba)])
                    for kc in range(8):
                        P.op("pe", lambda e, kc=kc, sub=sub, bg=bg, W=W: e.matmul(psb[bg][:, :], lhsT=xT[:, kc, 128 * sub:128 * (sub + 1)], rhs=W[:, kc, 512:1024], start=(kc == 0), stop=(kc == 7)),
                             reads=XT + [W_s], writes=[PS(bg)])
                    ut, ut_s = kvst[sub % 2]
                    sgf, sgf_s = acc[3]
                    P.op("act", lambda e, bg=bg, sgf=sgf: e.activation(out=sgf[:, :], in_=psb[bg][:, :], func=AF.Sigmoid), writes=[PS(bg), sgf_s])
                    P.op("dve", lambda e, ba=ba, sgf=sgf, ut=ut, half=half: e.tensor_tensor(out=ut[:, 512 * half:512 * (half + 1)], in0=psb[ba][:, :], in1=sgf[:, :], op=ALU.mult),
                         reads=[sgf_s], writes=[PS(ba), (ut_s, half)])
                    if prompt:
                        P.op("pool", lambda e, ut=ut, half=half: e.dma_start(out=cv_p[:, 512 * half:512 * (half + 1)], in_=ut[98:128, 512 * half:512 * (half + 1)]),
                             reads=[(ut_s, half)], dsem=s_kvo[sub % 2])
                    else:
                        for j in range(2):
                            P.op("pool", lambda e, ut=ut, half=half, j=j, sub=sub: e.dma_start(out=cv_s[2 * sub + j, :, 512 * half:512 * (half + 1)],
                                                                                           in_=ut[64 * j + 34:64 * j + 64, 512 * half:512 * (half + 1)]),
                                 reads=[(ut_s, half)], dsem=s_kvo[sub % 2])
        for c in range(8):
            u, u_s = utl[c]
            US = [u_s, (u_s, "h")]
            a0, a0_s = acc[(2 * c) % 4]
            a1, a1_s = acc[(2 * c + 1) % 4]
            P.op("dve", lambda e, c=c, u=u, a0=a0: e.tensor_scalar(out=seg3(a0[:, 0:Tt]), in0=u[:, :, 0:L], scalar1=cvec[:, c, 3:4], scalar2=cvec[:, c, 0:1], op0=ALU.mult, op1=ALU.add),
                 reads=US + CVEC, writes=[a0_s])
            P.op("dve", lambda e, c=c, u=u, a1=a1: e.tensor_scalar(out=seg3(a1[:, 0:Tt]), in0=u[:, :, 1:1 + L], scalar1=cvec[:, c, 4:5], scalar2=None, op0=ALU.mult),
                 reads=US + CVEC, writes=[a1_s])
            for k in range(2, 31):
                a, a_s = (a0, a0_s) if k % 2 == 0 else (a1, a1_s)
                P.op("dve", lambda e, c=c, k=k, u=u, a=a: e.scalar_tensor_tensor(out=seg3(a[:, 0:Tt]), in0=u[:, :, k:k + L], scalar=cvec[:, c, 3 + k:4 + k],
                                                                              in1=seg3(a[:, 0:Tt]), op0=ALU.mult, op1=ALU.add),
                     reads=US + CVEC + [a_s], writes=[a_s])
            P.op("dve", lambda e, c=c, a0=a0, a1=a1: e.tensor_tensor(out=ucT[:, c, 0:Tt], in0=a0[:, 0:Tt], in1=a1[:, 0:Tt], op=ALU.add), reads=[a0_s, a1_s], writes=[(ucT_s, c)])

        W, W_s = w_get("Q")
        for c in range(8):
            b = nb()
            for kc in range(8):
                P.op("pe", lambda e, kc=kc, c=c, b=b, W=W: e.matmul(psb[b][:, 0:Tt], lhsT=W[:, kc, 128 * c:128 * (c + 1)], rhs=xT[:, kc, 0:Tt], start=(kc == 0), stop=(kc == 7)),
                     reads=XT + [W_s], writes=[PS(b)])
            P.op("act", lambda e, c=c, b=b: e.activation(out=QT[0:64, 2 * c, 0:Tt], in_=psb[b][0:64, 0:Tt], func=AF.Identity), writes=[PS(b), ("QT", 2 * c)])
            P.op("act", lambda e, c=c, b=b: e.activation(out=QT[0:64, 2 * c + 1, 0:Tt], in_=psb[b][64:128, 0:Tt], func=AF.Identity), writes=[PS(b), ("QT", 2 * c + 1)])
        W, W_s = w_get("K")
        for c in range(8):
            b = nb()
            for kc in range(8):
                P.op("pe", lambda e, kc=kc, c=c, b=b, W=W: e.matmul(psb[b][:, 0:Tt], lhsT=W[:, kc, 128 * c:128 * (c + 1)], rhs=xT[:, kc, 0:Tt], start=(kc == 0), stop=(kc == 7)),
                     reads=XT + [W_s], writes=[PS(b)])
            if prompt:
                P.op("act", lambda e, c=c, b=b: e.activation(out=KTv[0:64, c, :, 0, :], in_=psb[b][0:64, :].rearrange("p (b i) -> p b i", b=4), func=AF.Identity),
                     writes=[PS(b), ("KT", 2 * c)])
                P.op("act", lambda e, c=c, b=b: e.activation(out=KTv[0:64, c, :, 1, :], in_=psb[b][64:128, :].rearrange("p (b i) -> p b i", b=4), func=AF.Identity),
                     writes=[PS(b), ("KT", 2 * c + 1)])
            else:
                P.op("act", lambda e, c=c, b=b: e.activation(out=KTv[0:64, 2 * c, :], in_=psb[b][0:64, 0:Tt], func=AF.Identity), writes=[PS(b), ("KT", 2 * c)])
                P.op("act", lambda e, c=c, b=b: e.activation(out=KTv[0:64, 2 * c + 1, :], in_=psb[b][64:128, 0:Tt], func=AF.Identity), writes=[PS(b), ("KT", 2 * c + 1)])

        def tokmajor_out(W, W_s, dst, which):
            for sub in range(nsub):
                stg, stg_s = kvst[sub % 2]
                for half in range(2):
                    b = nb()
                    for kc in range(8):
                        P.op("pe", lambda e, kc=kc, sub=sub, half=half, b=b: e.matmul(psb[b][:, :], lhsT=xT[:, kc, 128 * sub:128 * (sub + 1)], rhs=W[:, kc, 512 * half:512 * (half + 1)],
                                                                                     start=(kc == 0), stop=(kc == 7)), reads=XT + [W_s], writes=[PS(b)])
                    if half == 0:
                        P.op("act", lambda e, b=b, stg=stg: e.activation(out=stg[:, 0:512], in_=psb[b][:, :], func=AF.Identity), writes=[PS(b), (stg_s, 0)])
                    else:
                        P.op("act", lambda e, b=b, stg=stg: e.activation(out=stg[:, 512:1024], in_=psb[b][:, :], func=AF.Identity), writes=[PS(b), (stg_s, 1)])
                P.op("sp", lambda e, sub=sub, stg=stg: e.dma_start(out=dst[128 * sub:128 * (sub + 1), :], in_=stg[:, :]), reads=[(stg_s, 0), (stg_s, 1)], dsem=s_kvo[sub % 2])
                if which == "v":
                    if prompt:
                        for par in range(2):
                            P.op("pool", lambda e, sub=sub, stg=stg, par=par: e.tensor_copy(
                                out=Vs[:, :, sub, 128 * par:128 * par + 64],
                                in_=stg[:, :].rearrange("p (g r) -> p g r", g=8)[:, :, 64 * par:64 * par + 64]),
                                reads=[(stg_s, 0), (stg_s, 1)], writes=[(Vs_s, sub, par)])
                    else:
                        P.op("pool", lambda e, sub=sub, stg=stg: e.tensor_copy(out=Vn[:, sub, :], in_=stg[:, :]), reads=[(stg_s, 0), (stg_s, 1)], writes=[(Vn_s, sub)])

        tokmajor_out(W, W_s, k_dst, "k")
        if prompt:
            Vs, Vs_s = alloc([128, 8, 4, 192], BF16)
            P.op("pool", lambda e: e.memset(Vs[:, :, :, 64:128], 1.0), writes=[(Vs_s, "ones")])
        else:
            Vn = KTs[:, 4096:6144].rearrange("p (s d) -> p s d", s=2)
            Vn_s = "VnP"
        W, W_s = w_get("V")
        tokmajor_out(W, W_s, v_dst, "v")
        if prompt:
            for g in range(8):
                P.op("sp", lambda e, g=g: e.dma_start(out=kt_scr[g, ti, :, :], in_=KTs[0:80, 1024 * g:1024 * (g + 1)]),
                     reads=[("KT", 2 * g), ("KT", 2 * g + 1)] + RK + KTC, writes=[("ktscr", g, ti)], dsem=s_ktw[g])
                P.op("sp", lambda e, g=g: e.dma_start(out=v_scr[g, ti, :, :], in_=Vs[:, g, :, :].rearrange("p b r -> p (b r)")),
                     reads=[(Vs_s, s_, p_) for s_ in range(4) for p_ in range(2)] + [(Vs_s, "ones")], writes=[("vscr", g, ti)], dsem=s_vw[g])

        W, W_s = w_get("GC")
        for c in range(8):
            b = nb()
            for kc in range(8):
                P.op("pe", lambda e, kc=kc, c=c, b=b, W=W: e.matmul(psb[b][:, 0:Tt], lhsT=W[:, kc, 128 * c:128 * (c + 1)], rhs=xT[:, kc, 0:Tt], start=(kc == 0), stop=(kc == 7)),
                     reads=XT + [W_s], writes=[PS(b)])
            P.op("act", lambda e, c=c, b=b: e.activation(out=gated_c[:, c, 0:Tt], in_=psb[b][:, 0:Tt], func=AF.Sigmoid), writes=[PS(b), ("gc", c)])
        for c in range(8):
            sq, sq_s = sqt[c % 2]
            P.op("act", lambda e, c=c, sq=sq: e.activation(out=sq[:, 0:Tt], in_=ucT[:, c, 0:Tt], func=AF.Square), reads=[(ucT_s, c)], writes=[sq_s])
            P.op("pe", lambda e, c=c: e.matmul(psb[bmean][:, 0:Tt], lhsT=onesm[:, :], rhs=ucT[:, c, 0:Tt], start=(c == 0), stop=(c == 7)),
                 reads=[(ucT_s, c), "onesm"], writes=[PS(bmean)])
            P.op("pe", lambda e, c=c, sq=sq: e.matmul(psb[bmsq][:, 0:Tt], lhsT=onesm[:, :], rhs=sq[:, 0:Tt], start=(c == 0), stop=(c == 7)),
                 reads=[sq_s, "onesm"], writes=[PS(bmsq)])
        mean_sb, mean_s = xin[0][0][:, 0:512], xin[0][1]
        rstd_sb, rstd_sbs = xin[1][0][:, 0:512], xin[1][1]
        m2, m2_s = acc[0]
        P.op("act", lambda e: e.activation(out=mean_sb[:, 0:Tt], in_=psb[bmean][:, 0:Tt], func=AF.Identity), writes=[PS(bmean), mean_s])
        P.op("pool", lambda e: e.tensor_tensor(out=m2[:, 0:Tt], in0=mean_sb[:, 0:Tt], in1=mean_sb[:, 0:Tt], op=ALU.mult), reads=[mean_s], writes=[m2_s])
        P.op("dve", lambda e: e.tensor_tensor(out=m2[:, 0:Tt], in0=psb[bmsq][:, 0:Tt], in1=m2[:, 0:Tt], op=ALU.subtract), reads=[m2_s], writes=[PS(bmsq), m2_s])
        P.op("act", lambda e: e.activation(out=m2[:, 0:Tt], in_=m2[:, 0:Tt], func=AF.Ln, bias=EPS, scale=1.0), reads=[m2_s], writes=[m2_s])
        P.op("act", lambda e: e.activation(out=rstd_sb[:, 0:Tt], in_=m2[:, 0:Tt], func=AF.Exp, scale=-0.5), reads=[m2_s], writes=[rstd_sbs])
        for c in range(8):
            t, t_s = acc[1 + c % 3]
            P.op("dve", lambda e, c=c, t=t: e.tensor_tensor(out=t[:, 0:Tt], in0=ucT[:, c, 0:Tt], in1=mean_sb[:, 0:Tt], op=ALU.subtract),
                 reads=[(ucT_s, c), mean_s], writes=[t_s])
            P.op("pool", lambda e, t=t: e.tensor_tensor(out=t[:, 0:Tt], in0=t[:, 0:Tt], in1=rstd_sb[:, 0:Tt], op=ALU.mult), reads=[t_s, rstd_sbs], writes=[t_s])
            P.op("act", lambda e, c=c, t=t: e.activation(out=ucT[:, c, 0:Tt], in_=t[:, 0:Tt], func=AF.Silu, scale=cvec[:, c, 1:2], bias=cvec[:, c, 2:3]),
                 reads=[t_s] + CVEC, writes=[(ucT_s, c)])

        if ti == 0:
            tap(('d_' if prompt else 's_') + 'ucT', ucT[:, :, :], [(ucT_s, c) for c in range(8)])
        W, W_s = w_get("CO")
        UCT = [(ucT_s, c) for c in range(8)]
        for c in range(8):
            b = nb()
            for kc in range(8):
                P.op("pe", lambda e, kc=kc, c=c, b=b, W=W: e.matmul(psb[b][:, 0:Tt], lhsT=W[:, kc, 128 * c:128 * (c + 1)], rhs=ucT[:, kc, 0:Tt], start=(kc == 0), stop=(kc == 7)),
                     reads=UCT + [W_s], writes=[PS(b)])
            P.op("dve", lambda e, c=c, b=b: e.tensor_tensor(out=gated_c[:, c, 0:Tt], in0=psb[b][:, 0:Tt], in1=gated_c[:, c, 0:Tt], op=ALU.mult),
                 reads=[("gc", c)], writes=[PS(b), ("gc", c)])
        if ti == 0:
            tap(('d_' if prompt else 's_') + 'gc', gated_c[:, :, :], [('gc', c) for c in range(8)])
            tap(('d_' if prompt else 's_') + 'QT', QT[0:80, :, :], [('QT', h) for h in range(NH)] + RQ + ['QTc'])
            tap(('d_' if prompt else 's_') + 'KT', KTs[0:80, :], [('KT', h) for h in range(NH)] + RK + KTC)
        P.end_scope()

        ar_reset()
        QTA = [("QT", h) for h in range(NH)] + RQ + ["QTc"]
        rec = [alloc([128, 512], F32) for _ in range(2)]
        PT = [alloc([128, 512], BF16) for _ in range(4)]
        sb_rr = [0]
        pt_rr = [0]
        if prompt:
            kvb = []
            for i in range(NKV):
                ktb, ktb_s = alloc([128, 4, 2, 128], BF16)
                vb, vb_s = alloc([128, 4, 192], BF16)
                kvb.append((ktb, ktb_s, vb, vb_s))
            jobs = [(g, sb) for g in range(8) for sb in range(ti + 1)]
            kvst_ = {"loaded": 0}

            def kv_ensure(n):
                while kvst_["loaded"] <= n and kvst_["loaded"] < len(jobs):
                    m = kvst_["loaded"]
                    g, sb = jobs[m]
                    ktb, ktb_s, vb, vb_s = kvb[m % NKV]
                    P.op("sp", lambda e, g=g, sb=sb, ktb=ktb: e.dma_start(out=ktb[0:80, :, :, :].rearrange("p b h i -> p (b h i)"), in_=kt_scr[g, sb, :, :]),
                         reads=[("ktscr", g, sb)], writes=[ktb_s], dsem=s_kvK[m % NKV])
                    P.op("sp", lambda e, g=g, sb=sb, vb=vb: e.dma_start(out=vb[:, :, :].rearrange("p b r -> p (b r)"), in_=v_scr[g, sb, :, :]),
                         reads=[("vscr", g, sb)], writes=[vb_s], dsem=s_kvV[m % NKV])
                    kvst_["loaded"] += 1

            LA = 2
            steps = []
            for n, (g, sb) in enumerate(jobs):
                ktb, ktb_s, vb, vb_s = kvb[n % NKV]
                ob = [4 + 2 * (g % 2), 5 + 2 * (g % 2)]
                diag = sb == ti
                k_in_job = 0
                for blk in range(4):
                    q0 = 128 * blk if diag else 0
                    for hh in range(2):
                        h = 2 * g + hh
                        sbk = sb_rr[0] % 4
                        sb_rr[0] += 1
                        pt, pt_s = PT[pt_rr[0] % 4]
                        pt_rr[0] += 1
                        first = sb == 0 and blk == 0
                        lastk = diag and blk == 3

                        def front(n=n, k_in_job=k_in_job, blk=blk, hh=hh, h=h, sbk=sbk, q0=q0, ktb=ktb, ktb_s=ktb_s, diag=diag, pt=pt, pt_s=pt_s):
                            if k_in_job == LA:
                                kv_ensure(n + NKV - 1)
                            P.op("pe", lambda e: e.matmul(psb[sbk][:, q0:512], lhsT=ktb[0:80, blk, hh, :], rhs=QT[0:80, h, q0:512], start=True, stop=(not diag)),
                                 reads=[ktb_s] + QTA, writes=[PS(sbk)])
                            if diag:
                                P.op("pe", lambda e: e.matmul(psb[sbk][:, q0:q0 + 128], lhsT=ident_b[:, :], rhs=maskb[:, :], start=False, stop=True),
                                     reads=["ident_b", "maskb"], writes=[PS(sbk)])
                            P.op("act", lambda e: e.activation(out=pt[:, q0:512], in_=psb[sbk][:, q0:512], func=AF.Exp, scale=0.125),
                                 writes=[PS(sbk), pt_s])

                        def back(g=g, blk=blk, hh=hh, q0=q0, pt=pt, pt_s=pt_s, vb=vb, vb_s=vb_s, first=first, lastk=lastk, ob=ob, diag=diag):
                            P.op("pe", lambda e: e.matmul(psb[ob[hh]][:, q0:512], lhsT=vb[:, blk, 64 * hh:64 * hh + 128], rhs=pt[:, q0:512], start=first, stop=lastk),
                                 reads=[vb_s, pt_s], writes=[PS(ob[hh])])
                            if diag and blk == 3 and hh == 1:
                                rc, rc_s = rec[g % 2]
                                P.op("dve", lambda e: e.reciprocal(out=rc[0:64, :], in_=psb[ob[0]][64:128, :]), writes=[PS(ob[0]), (rc_s, 0)])
                                P.op("dve", lambda e: e.tensor_tensor(out=oT[0:64, g, :], in0=psb[ob[0]][0:64, :], in1=rc[0:64, :], op=ALU.mult),
                                     reads=[(rc_s, 0)], writes=[PS(ob[0]), ("oT", g, 0)])
                                P.op("dve", lambda e: e.reciprocal(out=rc[64:128, :], in_=psb[ob[1]][0:64, :]), writes=[PS(ob[1]), (rc_s, 1)])
                                P.op("dve", lambda e: e.tensor_tensor(out=oT[64:128, g, :], in0=psb[ob[1]][64:128, :], in1=rc[64:128, :], op=ALU.mult),
                                     reads=[(rc_s, 1)], writes=[PS(ob[1]), ("oT", g, 1)])
                        steps.append((front, back))
                        k_in_job += 1
            kv_ensure(NKV - 2)
            for i in range(len(steps) + LA):
                if i < len(steps):
                    steps[i][0]()
                if i >= LA:
                    steps[i - LA][1]()
        else:
            sample_attention(QTA, RK, KTv, Vn, Vn_s, rec, PT)
        P.end_scope()

        ar_reset()
        OT = [("oT", g, j) for g in range(8) for j in range(2)]
        gT, gT_s = alloc([128, 8, 512], BF16)
        hT, hT_s = alloc([128, NFC, 512], BF16)
        lnt = []
        for _ in range(2):
            st, st_s = alloc([128, 2, 6], F32); mv, mv_s = alloc([128, 2], F32)
            lnv, lnv_s = alloc([128, 1], F32); rstd, rstd_s = alloc([128, 1], F32)
            lnt.append((st, st_s, mv, mv_s, lnv, lnv_s, rstd, rstd_s))
        xb2 = [alloc([128, 1024], BF16) for _ in range(2)]
        tmpf = [alloc([128, 512], F32) for _ in range(4)]
        aext = [alloc([128, nseg, 2 + L], F32) for _ in range(2)]
        pb, pb_s = alloc([128, 4, 256], BF16)
        pT, pT_s = alloc([128, 2, 512], BF16)

        if ti == 0:
            tap(('d_' if prompt else 's_') + 'oT', oT[:, :, :], OT)
        W, W_s = w_get("GA")
        for c in range(8):
            b = nb()
            for kc in range(8):
                P.op("pe", lambda e, kc=kc, c=c, b=b, W=W: e.matmul(psb[b][:, 0:Tt], lhsT=W[:, kc, 128 * c:128 * (c + 1)], rhs=xT[:, kc, 0:Tt], start=(kc == 0), stop=(kc == 7)),
                     reads=XT + [W_s], writes=[PS(b)])
            P.op("act", lambda e, c=c, b=b: e.activation(out=gT[:, c, 0:Tt], in_=psb[b][:, 0:Tt], func=AF.Sigmoid), writes=[PS(b), (gT_s, c)])
        W, W_s = w_get("AO")
        for c in range(8):
            b = nb()
            for kc in range(8):
                P.op("pe", lambda e, kc=kc, c=c, b=b, W=W: e.matmul(psb[b][:, 0:Tt], lhsT=W[:, kc, 128 * c:128 * (c + 1)], rhs=oT[:, kc, 0:Tt], start=(kc == 0), stop=(kc == 7)),
                     reads=OT + [W_s], writes=[PS(b)])
            P.op("dve", lambda e, c=c, b=b: e.tensor_tensor(out=gT[:, c, 0:Tt], in0=psb[b][:, 0:Tt], in1=gT[:, c, 0:Tt], op=ALU.mult),
                 reads=[(gT_s, c)], writes=[PS(b), (gT_s, c)])
            P.op("pool", lambda e, c=c: e.tensor_tensor(out=gT[:, c, 0:Tt], in0=gT[:, c, 0:Tt], in1=gated_c[:, c, 0:Tt], op=ALU.add),
                 reads=[(gT_s, c), ("gc", c)], writes=[(gT_s, c)])
        GT = [(gT_s, c) for c in range(8)]
        if ti == 0:
            tap(('d_' if prompt else 's_') + 'gT', gT[:, :, :], GT)
        W, W_s = w_get("O")
        load_gb(ln1_g, ln1_b)
        for sub in range(nsub):
            for half in range(2):
                b = nb()
                for kc in range(8):
                    P.op("pe", lambda e, kc=kc, sub=sub, half=half, b=b, W=W: e.matmul(psb[b][:, :], lhsT=gT[:, kc, 128 * sub:128 * (sub + 1)], rhs=W[:, kc, 512 * half:512 * (half + 1)],
                                                                                      start=(kc == 0), stop=(kc == 7)), reads=GT + [W_s], writes=[PS(b)])
                P.op("dve", lambda e, sub=sub, half=half, b=b: e.scalar_tensor_tensor(out=xres[:, sub, 512 * half:512 * (half + 1)], in0=xres[:, sub, 512 * half:512 * (half + 1)],
                                                                                   scalar=ALPHA, in1=psb[b][:, :], op0=ALU.mult, op1=ALU.add),
                     reads=[("xres", sub)], writes=[PS(b), ("xres", sub)])
            ln_rows(xres[:, sub, :], [("xres", sub)], sub, lnt[sub % 2])
            to_featmajor(sub, *xb2[sub % 2])

        if ti == 0:
            tap(('d_' if prompt else 's_') + 'x1', xres[:, :, :], [('xres', q_) for q_ in range(4)])
        P.op("pool", lambda e: e.dma_start(out=pb[:, 0:nsub, :], in_=p_src.rearrange("(s p) d -> p s d", p=128)), writes=[pb_s], dsem=s_p)
        for sub in range(nsub):
            b = nb()
            psv = psb[b][:].bitcast(BF16)
            for k2 in range(2):
                P.op("pe", lambda e, sub=sub, k2=k2, psv=psv: e.transpose(out=psv[:, 128 * k2:128 * (k2 + 1)], in_=pb[:, sub, 128 * k2:128 * (k2 + 1)], identity=ident_b[:, :]),
                     reads=[pb_s, "ident_b"], writes=[PS(b)])
            P.op("act", lambda e, sub=sub, psv=psv, b=b: e.activation(out=pT[:, :, 128 * sub:128 * (sub + 1)], in_=psv[:, 0:256].rearrange("p (c t) -> p c t", c=2), func=AF.Identity),
                 writes=[PS(b), (pT_s, sub)])
        PTS = [(pT_s, s) for s in range(nsub)]
        W, W_s = w_get("PG")
        wple, wple_s = w_get("PLE", ahead=0)
        for sub in range(nsub):
            for half in range(2):
                bg_ = nb(); bp = nb()
                for kc in range(8):
                    P.op("pe", lambda e, kc=kc, sub=sub, half=half, bg_=bg_, W=W: e.matmul(psb[bg_][:, :], lhsT=xT[:, kc, 128 * sub:128 * (sub + 1)], rhs=W[:, kc, 512 * half:512 * (half + 1)],
                                                                                          start=(kc == 0), stop=(kc == 7)), reads=XT + [W_s], writes=[PS(bg_)])
                for k2 in range(2):
                    P.op("pe", lambda e, k2=k2, sub=sub, half=half, bp=bp, wple=wple: e.matmul(psb[bp][:, :], lhsT=pT[:, k2, 128 * sub:128 * (sub + 1)], rhs=wple[:, k2, 512 * half:512 * (half + 1)],
                                                                                   start=(k2 == 0), stop=(k2 == 1)), reads=PTS + [wple_s], writes=[PS(bp)])
                tg, tg_s = tmpf[(2 * sub + half) % 2]
                P.op("act", lambda e, bg_=bg_, tg=tg: e.activation(out=tg[:, :], in_=psb[bg_][:, :], func=AF.Sigmoid), writes=[PS(bg_), tg_s])
                P.op("dve", lambda e, bp=bp, tg=tg: e.tensor_tensor(out=tg[:, :], in0=psb[bp][:, :], in1=tg[:, :], op=ALU.mult), reads=[tg_s], writes=[PS(bp), tg_s])
                P.op("dve", lambda e, sub=sub, half=half, tg=tg: e.scalar_tensor_tensor(out=xres[:, sub, 512 * half:512 * (half + 1)], in0=xres[:, sub, 512 * half:512 * (half + 1)],
                                                                                     scalar=ALPHA, in1=tg[:, :], op0=ALU.mult, op1=ALU.add),
                     reads=[("xres", sub), tg_s], writes=[("xres", sub)])

        if not prompt:
            sf, sf_s = alloc([8, DFF], F32, 8)
            P.op("sp", lambda e: e.dma_start(out=sf[:, :], in_=st_ffn.rearrange("s r d -> (s r) d")), writes=[sf_s], dsem=s_st[1])
            for c in range(NFC):
                b = nb()
                P.op("pe", lambda e, c=c, b=b: e.transpose(out=psb[b][:, 0:8], in_=sf[0:8, 128 * c:128 * (c + 1)], identity=ident_f[0:8, 0:8]),
                     reads=[sf_s, "ident_f"], writes=[PS(b)])
                P.op("dve", lambda e, c=c, b=b: e.tensor_copy(out=ahist[:, c, :, :], in_=psb[b][:, 0:8].rearrange("p (s r) -> p s r", s=4)), writes=[PS(b), ("ahist", c)])
        for g in range(6):
            W, W_s = w_get(f"UP{g}")
            hw = 512 if g < 5 else 256
            for cc in range(hw // 128):
                c = 4 * g + cc
                ba = nb(); bb = nb()
                for kc in range(8):
                    P.op("pe", lambda e, kc=kc, cc=cc, ba=ba, W=W: e.matmul(psb[ba][:, 0:Tt], lhsT=W[:, kc, 128 * cc:128 * (cc + 1)], rhs=xT[:, kc, 0:Tt], start=(kc == 0), stop=(kc == 7)),
                         reads=XT + [W_s], writes=[PS(ba)])
                for kc in range(8):
                    P.op("pe", lambda e, kc=kc, cc=cc, bb=bb, W=W, hw=hw: e.matmul(psb[bb][:, 0:Tt], lhsT=W[:, kc, hw + 128 * cc:hw + 128 * (cc + 1)], rhs=xT[:, kc, 0:Tt], start=(kc == 0), stop=(kc == 7)),
                         reads=XT + [W_s], writes=[PS(bb)])
                ae, ae_s = aext[c % 2]
                ac, ac_s = tmpf[2 + c % 2]
                P.op("act", lambda e, ba=ba, ae=ae: e.activation(out=ae[:, :, 2:2 + L], in_=seg3(psb[ba][:, 0:Tt]), func=AF.Identity), writes=[PS(ba), ae_s])
                P.op("pool", lambda e, c=c, ae=ae: e.tensor_copy(out=ae[:, :, 0:2], in_=ahist[:, c, 0:nseg, :]), reads=[("ahist", c)], writes=[(ae_s, "h")])
                if prompt:
                    P.op("pool", lambda e, c=c, ae=ae: e.tensor_copy(out=ahist[:, c, 0, :], in_=ae[:, 0, L:L + 2]), reads=[ae_s, (ae_s, "h")], writes=[("ahist", c)])
                AE = [ae_s, (ae_s, "h")]
                P.op("act", lambda e, c=c, ae=ae, ac=ac: e.activation(out=seg3(ac[:, 0:Tt]), in_=ae[:, :, 0:L], func=AF.Identity, scale=fvec[:, c, 0:1], bias=fvec[:, c, 3:4]),
                     reads=AE + FVEC, writes=[ac_s])
                for k in (1, 2):
                    P.op("dve", lambda e, c=c, k=k, ae=ae, ac=ac: e.scalar_tensor_tensor(out=seg3(ac[:, 0:Tt]), in0=ae[:, :, k:k + L], scalar=fvec[:, c, k:k + 1],
                                                                                      in1=seg3(ac[:, 0:Tt]), op0=ALU.mult, op1=ALU.add),
                         reads=AE + FVEC + [ac_s], writes=[ac_s])
                P.op("act", lambda e, ac=ac: e.activation(out=ac[:, 0:Tt], in_=ac[:, 0:Tt], func=AF.Silu), reads=[ac_s], writes=[ac_s])
                P.op("dve", lambda e, c=c, bb=bb, ac=ac: e.tensor_tensor(out=hT[:, c, 0:Tt], in0=psb[bb][:, 0:Tt], in1=ac[:, 0:Tt], op=ALU.mult),
                     reads=[ac_s], writes=[PS(bb), (hT_s, c)])
            if need_state:
                for sub in state_subs:
                    b = nb()
                    for kc in range(8):
                        P.op("pe", lambda e, kc=kc, sub=sub, b=b, W=W, hw=hw: e.matmul(psb[b][:, 0:hw], lhsT=xT[:, kc, 128 * sub:128 * (sub + 1)], rhs=W[:, kc, 0:hw], start=(kc == 0), stop=(kc == 7)),
                             reads=XT + [W_s], writes=[PS(b)])
                    sg_, sg_s = tmpf[sub % 2]
                    P.op("act", lambda e, b=b, sg_=sg_, hw=hw: e.activation(out=sg_[:, 0:hw], in_=psb[b][:, 0:hw], func=AF.Identity), writes=[PS(b), sg_s])
                    if prompt:
                        P.op("pool", lambda e, sg_=sg_, g=g, hw=hw: e.dma_start(out=ff_p[:, 512 * g:512 * g + hw], in_=sg_[126:128, 0:hw]), reads=[sg_s], dsem=s_st[sub % 2])
                    else:
                        for j in range(2):
                            P.op("pool", lambda e, sg_=sg_, g=g, hw=hw, j=j, sub=sub: e.dma_start(out=ff_s[2 * sub + j, :, 512 * g:512 * g + hw], in_=sg_[64 * j + 62:64 * j + 64, 0:hw]),
                                 reads=[sg_s], dsem=s_st[sub % 2])
        HT = [(hT_s, c) for c in range(NFC)]
        if ti == 0:
            tap(('d_' if prompt else 's_') + 'hT', hT[:, :, :], HT)
        load_gb(ln2_g, ln2_b)
        for n in range(2):
            banks = [nb() for _ in range(nsub)]
            for kh in range(2):
                W, W_s = w_get(f"DN{kh}{n}")
                for sub in range(nsub):
                    for j in range(11):
                        P.op("pe", lambda e, j=j, kh=kh, sub=sub, W=W, b=banks[sub]: e.matmul(psb[b][:, :], lhsT=hT[:, 11 * kh + j, 128 * sub:128 * (sub + 1)], rhs=W[:, j, :],
                                                                                             start=(kh == 0 and j == 0), stop=(kh == 1 and j == 10)),
                             reads=HT + [W_s], writes=[PS(banks[sub])])
            for sub in range(nsub):
                P.op("dve", lambda e, sub=sub, n=n, b=banks[sub]: e.tensor_tensor(out=xres[:, sub, 512 * n:512 * (n + 1)], in0=psb[b][:, :], in1=xres[:, sub, 512 * n:512 * (n + 1)], op=ALU.add),
                     reads=[("xres", sub)], writes=[PS(banks[sub]), ("xres", sub)])
        if ti == 0:
            tap(('d_' if prompt else 's_') + 'r2', xres[:, :, :], [('xres', q_) for q_ in range(4)])
        for sub in range(nsub):
            ln_rows(xres[:, sub, :], [("xres", sub)], sub, lnt[sub % 2])
            P.op("sp", lambda e, sub=sub: e.dma_start(out=y_dst[128 * sub:128 * (sub + 1), :], in_=xres[:, sub, :]), reads=[("xres", sub)], dsem=s_y[sub])
        P.end_scope()

    s_ck = [newsem(f"ck{i}") for i in range(2)]
    s_cv = [newsem(f"cv{i}") for i in range(2)]
    s_clf = newsem("clf")
    s_rkh = [newsem(f"rkh{i}") for i in range(2)]
    s_ktc = newsem("ktc")
    s_cks = newsem("cks")
    ck_scr = nc.dram_tensor("ck_scr", [NSTR, NH, 3, PAST], BF16).ap()
    hcar = sbt("hcar", [16, 1], F32)

    def hist_bufs():
        lfh, lfh_s = alloc([128, 16, 16], F32)
        lfT, lfT_s = alloc([16, 2048], F32, 16)
        cH, cH_s = alloc([16, 2048], F32, 16)
        SKh, SKh_s = alloc([16, 3, 2048], BF16, 16)
        return lfh, lfh_s, lfT, lfT_s, cH, cH_s, SKh, SKh_s

    def hist_half(s, hf, bufs, want_split):
        lfh, lfh_s, lfT, lfT_s, cH, cH_s, SKh, SKh_s = bufs
        r0 = 2048 * hf
        if hf == 0:
            P.op("dve", lambda e: e.memset(hcar[:, :], 0.0), writes=["hcar"])
        P.op("sp", lambda e: e.dma_start(out=lfh[:, :, :], in_=cache_lf[s, r0:r0 + 2048, :].rearrange("(b p) h -> p b h", p=128)), writes=[lfh_s], dsem=s_clf)
        for q in range(4):
            b = nb6s()
            for j in range(4):
                blk = 4 * q + j
                P.op("pe", lambda e, blk=blk, j=j, b=b: e.transpose(out=psb[b][0:16, 128 * j:128 * (j + 1)], in_=lfh[:, blk, :], identity=ident_f[:, :]),
                     reads=[lfh_s, "ident_f"], writes=[PS(b)])
            P.op("act", lambda e, q=q, b=b: e.activation(out=lfT[:, 512 * q:512 * (q + 1)], in_=psb[b][0:16, :], func=AF.Identity), writes=[PS(b), (lfT_s, q)])
        for q in range(4):
            ini = hcar[:, 0:1] if q == 0 else cH[:, 512 * q - 1:512 * q]
            rd = ["hcar"] if q == 0 else [(cH_s, q - 1)]
            P.op("dve", lambda e, q=q, ini=ini: e.tensor_tensor_scan(out=cH[:, 512 * q:512 * (q + 1)], data0=ones16[:, 0:512], data1=lfT[:, 512 * q:512 * (q + 1)],
                                                                   initial=ini, op0=ALU.mult, op1=ALU.add), reads=[(lfT_s, q), "ones16"] + rd, writes=[(cH_s, q)])
        CH = [(cH_s, q) for q in range(4)]
        LT = [(lfT_s, q) for q in range(4)]
        P.op("dve", lambda e: e.tensor_copy(out=hcar[:, 0:1], in_=cH[:, 2047:2048]), reads=CH, writes=["hcar"])
        if want_split:
            P.op("dve", lambda e: e.tensor_scalar(out=SKh[:, 0, :], in0=cH[:, :], scalar1=-8.0, scalar2=None, op0=ALU.mult), reads=CH, writes=[(SKh_s, 0)])
            P.op("dve", lambda e: e.scalar_tensor_tensor(out=lfT[:, :], in0=cH[:, :], scalar=-8.0, in1=SKh[:, 0, :], op0=ALU.mult, op1=ALU.subtract),
                 reads=CH + [(SKh_s, 0)], writes=LT)
            P.op("dve", lambda e: e.tensor_copy(out=SKh[:, 1, :], in_=lfT[:, :]), reads=LT, writes=[(SKh_s, 1)])
            P.op("dve", lambda e: e.tensor_tensor(out=cH[:, :], in0=lfT[:, :], in1=SKh[:, 1, :], op=ALU.subtract), reads=LT + [(SKh_s, 1)], writes=CH)
            P.op("dve", lambda e: e.tensor_copy(out=SKh[:, 2, :], in_=cH[:, :]), reads=CH, writes=[(SKh_s, 2)])
            P.op("sp", lambda e: e.dma_start(out=ck_scr[s, :, :, r0:r0 + 2048], in_=SKh[:, :, :]), reads=[(SKh_s, j) for j in range(3)],
                 writes=[("ckscr", s, hf)], dsem=s_cks)

    def sample_prepass():
        ar_reset()
        bufs = hist_bufs()
        for s in range(NSTR):
            for hf in range(2):
                hist_half(s, hf, bufs, False)
            P.op("dve", lambda e, s=s: e.tensor_copy(out=hend[:, s:s + 1], in_=hcar[:, 0:1]), reads=["hcar"], writes=["hend"])
        P.end_scope()

    def sample_attention(QTA, RK, KTv, Vn, Vn_s, rec, PT):
        bufs = hist_bufs()
        kc_ = [alloc([128, 2, 1024], BF16) for _ in range(2)]
        vc_ = [alloc([128, 2, 1024], BF16) for _ in range(2)]
        ktc = [alloc([128, 2, 16, 128], BF16) for _ in range(2)]
        for i in range(2):
            kt, kt_s = ktc[i]
            P.op("pool", lambda e, kt=kt: e.memset(kt[64:96, :, :, :], 0.0), writes=[(kt_s, "c0")])
            for g in range(4):
                P.op("sp", lambda e, kt=kt, g=g: e.dma_start(out=kt[67:70, :, :, :].rearrange("p b h i -> p (b h i)")[:, 1024 * g:1024 * (g + 1)], in_=ones3[:, :]),
                     reads=["ones3", (kt_s, "c0")], writes=[(kt_s, "c1", g)], dsem=s_ktc)
        KTNEW = [("KT", h) for h in range(NH)] + RK + KTC
        OB = [4, 5]
        for s in range(NSTR):
            for hf in range(2):
                hist_half(s, hf, bufs, True)
            qs = slice(64 * s, 64 * (s + 1))
            pp = 64 * (s % 2)
            for grp in range(17):
                new = grp == 16
                nblk = 1 if new else 2
                if not new:
                    kcb, kcb_s = kc_[grp % 2]
                    vcb, vcb_s = vc_[grp % 2]
                    kt, kt_s = ktc[grp % 2]
                    r0 = 256 * grp
                    P.op("pool", lambda e, kcb=kcb, r0=r0, s=s: e.dma_start(out=kcb[:, :, :], in_=cache_k[s, r0:r0 + 256, :].rearrange("(b p) d -> p b d", p=128)),
                         writes=[kcb_s], dsem=s_ck[grp % 2])
                    P.op("pool", lambda e, vcb=vcb, r0=r0, s=s: e.dma_start(out=vcb[:, :, :], in_=cache_v[s, r0:r0 + 256, :].rearrange("(b p) d -> p b d", p=128)),
                         writes=[vcb_s], dsem=s_cv[grp % 2])
                    RKH = [(kt_s, "rk", h) for h in range(NH)]
                    for h in range(NH):
                        P.op("sp", lambda e, h=h, kt=kt, r0=r0, s=s: e.dma_start(out=kt[64:67, :, h, :], in_=ck_scr[s, h, :, r0:r0 + 256]),
                             reads=[("ckscr", s, r0 // 2048), (kt_s, "c0")], writes=[RKH[h]], dsem=s_rkh[grp % 2])
                    for c in range(8):
                        b = nb6s()
                        psv = psb[b][:].bitcast(BF16)
                        for j in range(2):
                            P.op("pe", lambda e, c=c, j=j, psv=psv, kcb=kcb: e.transpose(out=psv[:, 128 * j:128 * (j + 1)], in_=kcb[:, j, 128 * c:128 * (c + 1)], identity=ident_b[:, :]),
                                 reads=[kcb_s, "ident_b"], writes=[PS(b)])
                        P.op("act", lambda e, c=c, psv=psv, kt=kt: e.activation(out=kt[0:64, :, 2 * c, :], in_=psv[0:64, 0:256].rearrange("p (b i) -> p b i", b=2), func=AF.Identity),
                             writes=[PS(b), (kt_s, 2 * c)])
                        P.op("dve", lambda e, c=c, psv=psv, kt=kt: e.tensor_copy(out=kt[0:64, :, 2 * c + 1, :], in_=psv[64:128, 0:256].rearrange("p (b i) -> p b i", b=2)),
                             writes=[PS(b), (kt_s, 2 * c + 1)])
                    KTG = [(kt_s, h) for h in range(NH)] + RKH + [(kt_s, "c0")] + [(kt_s, "c1", g) for g in range(4)]
                    if s == 0 and grp == 0:
                        tap('s_kt0', kt[0:80, :, :, :], KTG)
                        tap('s_vc0', vcb[:, :, :], [vcb_s])
                for blk in range(nblk):
                    pts = []
                    for hb in range(2):
                        b = nb6s()
                        pt, pt_s = PT[(2 * blk + hb) % 4]
                        pts.append((pt, pt_s))
                        for hh in range(8):
                            h = 8 * hb + hh
                            if new:
                                P.op("pe", lambda e, h=h, hh=hh, b=b, pp=pp, qs=qs: e.matmul(psb[b][pp:pp + 64, 64 * hh:64 * (hh + 1)], lhsT=KTv[0:80, h, qs], rhs=QT[0:80, h, qs],
                                                                                            start=True, stop=False, skip_group_check=True), reads=KTNEW + QTA, writes=[PS(b)])
                                P.op("pe", lambda e, hh=hh, b=b, pp=pp: e.matmul(psb[b][pp:pp + 64, 64 * hh:64 * (hh + 1)], lhsT=ident_b[:, 0:64], rhs=maskb[:, 0:64],
                                                                                start=False, stop=True, skip_group_check=True), reads=["ident_b", "maskb"], writes=[PS(b)])
                            else:
                                P.op("pe", lambda e, h=h, hh=hh, b=b, blk=blk, kt=kt, qs=qs: e.matmul(psb[b][:, 64 * hh:64 * (hh + 1)], lhsT=kt[0:80, blk, h, :], rhs=QT[0:80, h, qs],
                                                                                                     start=True, stop=True, skip_group_check=True), reads=KTG + QTA, writes=[PS(b)])
                        if new:
                            P.op("pool", lambda e, pt=pt: e.memset(pt[:, :], 0.0), writes=[pt_s])
                            P.op("act", lambda e, b=b, pt=pt, pp=pp: e.activation(out=pt[pp:pp + 64, :], in_=psb[b][pp:pp + 64, :], func=AF.Exp, scale=0.125), writes=[PS(b), pt_s])
                        else:
                            P.op("act", lambda e, b=b, pt=pt: e.activation(out=pt[:, :], in_=psb[b][:, :], func=AF.Exp, scale=0.125), writes=[PS(b), pt_s])
                    if s == 0 and grp == 0 and blk == 0:
                        tap('s_pt0', pts[0][0][:, :], [pts[0][1]])
                    for hb in range(2):
                        pt, pt_s = pts[hb]
                        first = grp == 0 and blk == 0
                        if first:
                            P.op("pe", lambda e, hb=hb, pt=pt: e.matmul(psb[OB[hb]][:, :], lhsT=zerob[:, :], rhs=pt[:, :], start=True, stop=False, skip_group_check=True),
                                 reads=["zerob", pt_s], writes=[PS(OB[hb])])
                        for hh in range(8):
                            h = 8 * hb + hh
                            if new:
                                P.op("pe", lambda e, h=h, hh=hh, hb=hb, pt=pt, pp=pp, s=s: e.matmul(psb[OB[hb]][0:64, 64 * hh:64 * (hh + 1)], lhsT=Vn[:, s // 2, 64 * h:64 * (h + 1)],
                                                                                                   rhs=pt[:, 64 * hh:64 * (hh + 1)], start=False, stop=True, skip_group_check=True),
                                     reads=[(Vn_s, s // 2), pt_s], writes=[PS(OB[hb])])
                            else:
                                P.op("pe", lambda e, h=h, hh=hh, hb=hb, pt=pt, blk=blk, vcb=vcb, first=first: e.matmul(
                                    psb[OB[hb]][0:64, 64 * hh:64 * (hh + 1)], lhsT=vcb[:, blk, 64 * h:64 * (h + 1)], rhs=pt[:, 64 * hh:64 * (hh + 1)],
                                    start=False, stop=False, skip_group_check=True), reads=[vcb_s, pt_s], writes=[PS(OB[hb])])
                        if new:
                            P.op("pe", lambda e, hb=hb, pt=pt, pp=pp: e.matmul(psb[OB[hb]][64:128, :], lhsT=onesb[:, :], rhs=pt[:, :], start=False, stop=True, skip_group_check=True),
                                 reads=["onesb", pt_s], writes=[PS(OB[hb])])
                        else:
                            P.op("pe", lambda e, hb=hb, pt=pt, first=first: e.matmul(psb[OB[hb]][64:128, :], lhsT=onesb[:, :], rhs=pt[:, :], start=False, stop=False, skip_group_check=True),
                                 reads=["onesb", pt_s], writes=[PS(OB[hb])])
            if debug and s == 0:
                dbgn, dbgn_s = alloc([128, 512], F32)
                P.op("act", lambda e, dbgn=dbgn: e.activation(out=dbgn[:, :], in_=psb[OB[0]][:, :], func=AF.Identity), writes=[PS(OB[0]), dbgn_s])
                tap('s_num0', dbgn[:, :], [dbgn_s])
            for hb in range(2):
                rc, rc_s = rec[hb]
                P.op("dve", lambda e, rc=rc, hb=hb: e.reciprocal(out=rc[0:64, :], in_=psb[OB[hb]][64:128, :]), writes=[PS(OB[hb]), rc_s])
                if s == 0 and hb == 0:
                    tap('s_rc0', rc[0:64, :], [rc_s])
                for par in range(2):
                    P.op("dve", lambda e, rc=rc, hb=hb, par=par, qs=qs: e.tensor_tensor(
                        out=oT[64 * par:64 * par + 64, 4 * hb:4 * hb + 4, qs],
                        in0=psb[OB[hb]][0:64, :].rearrange("p (c r q) -> p c r q", c=4, r=2)[:, :, par, :],
                        in1=rc[0:64, :].rearrange("p (c r q) -> p c r q", c=4, r=2)[:, :, par, :], op=ALU.mult),
                        reads=[rc_s], writes=[PS(OB[hb])] + [("oT", 4 * hb + c, par) for c in range(4)])

    rr6 = [0]

    def nb6s():
        b = rr6[0] % 4
        rr6[0] += 1
        return b

    P.end_scope()
    for ti in range(ntiles):
        run_tile("p", ti)
    if do_sample:
        sample_prepass()
        run_tile("s", 0)
    assert wst["pos"] == len(wseq)
    P.emit(final_sems=allsems)
    return nc, P


_CACHE = {}


def _f32(a):
    return np.ascontiguousarray(np.asarray(a, dtype=np.float32))


def kernel(x_prompt, x_sample, cache_k, cache_v, cache_logf, state_conv, state_ffn_conv,
           p_prompt, p_sample, ln0_g, ln0_b, w_in, b_f, conv_dw_w, conv_dw_b, conv_ln_g,
           conv_ln_b, w_conv_out, w_attn_out, w_o, ln1_g, ln1_b, w_ffn_up, ffn_dw_w, ffn_dw_b,
           w_ffn_down, ln2_g, ln2_b, w_ple, w_ple_gate):
    if "nc" not in _CACHE:
        _CACHE["nc"] = build_nc()[0]
    nc = _CACHE["nc"]
    shared = {
        "ln0_g": _f32(ln0_g), "ln0_b": _f32(ln0_b), "w_in": _f32(w_in)[0], "b_f": _f32(b_f)[0],
        "conv_dw_w": _f32(conv_dw_w)[0], "conv_dw_b": _f32(conv_dw_b)[0], "conv_ln_g": _f32(conv_ln_g)[0],
        "conv_ln_b": _f32(conv_ln_b)[0], "w_conv_out": _f32(w_conv_out)[0], "w_attn_out": _f32(w_attn_out)[0],
        "w_o": _f32(w_o)[0], "ln1_g": _f32(ln1_g)[0], "ln1_b": _f32(ln1_b)[0], "w_ffn_up": _f32(w_ffn_up)[0],
        "ffn_dw_w": _f32(ffn_dw_w)[0], "ffn_dw_b": _f32(ffn_dw_b)[0], "w_ffn_down": _f32(w_ffn_down)[0],
        "ln2_g": _f32(ln2_g)[0], "ln2_b": _f32(ln2_b)[0], "w_ple": _f32(w_ple)[0], "w_ple_gate": _f32(w_ple_gate)[0],
    }
    x_prompt = _f32(x_prompt); p_prompt = _f32(p_prompt)[0]
    x_sample = _f32(x_sample); p_sample = _f32(p_sample)[0]
    ck = _f32(cache_k)[0].reshape(32, PAST, D); cv = _f32(cache_v)[0].reshape(32, PAST, D)
    clf = _f32(cache_logf)[0]; sc = _f32(state_conv)[0]; sf = _f32(state_ffn_conv)[0]
    in_maps = []
    for c in range(NCORES):
        m = dict(shared)
        s0 = NSTR * c
        m.update({
            "x_p": x_prompt[c], "p_p": p_prompt[c],
            "x_s": x_sample[s0:s0 + NSTR].reshape(NSTR * DSEQ, D), "p_s": p_sample[s0:s0 + NSTR].reshape(NSTR * DSEQ, PLE),
            "cache_k": ck[s0:s0 + NSTR], "cache_v": cv[s0:s0 + NSTR], "cache_lf": clf[s0:s0 + NSTR],
            "st_conv": sc[s0:s0 + NSTR], "st_ffn": sf[s0:s0 + NSTR],
        })
        in_maps.append(m)
    res = run_bass_kernel_spmd(nc, in_maps, core_ids=list(range(NCORES)))
    R = res.results

    def cat(name, shape):
        return np.stack([np.asarray(r[name], dtype=np.float32) for r in R], 0).reshape(shape)

    y_p = cat("y_p", (8, SEQ, D))
    y_s = cat("y_s", (32, DSEQ, D))
    k_p = cat("k_p", (1, 8, SEQ, NH, 64)); v_p = cat("v_p", (1, 8, SEQ, NH, 64))
    lf_p = cat("lf_p", (1, 8, SEQ, NH))
    cv_p = cat("cv_p", (1, 8, 30, D)); ff_p = cat("ff_p", (1, 8, 2, DFF))
    k_s = cat("k_s", (1, 32, DSEQ, NH, 64)); v_s = cat("v_s", (1, 32, DSEQ, NH, 64))
    lf_s = cat("lf_s", (1, 32, DSEQ, NH))
    cv_s = cat("cv_s", (1, 32, 30, D)); ff_s = cat("ff_s", (1, 32, 2, DFF))
    return (y_p, y_s, k_p, v_p, lf_p, cv_p, ff_p, k_s, v_s, lf_s, cv_s, ff_s)
```

```python
import numpy as np
import concourse.bass as bass
import concourse.mybir as mybir
from concourse.bass_utils import run_bass_kernel_spmd

F32 = mybir.dt.float32
BF16 = mybir.dt.bfloat16
AF = mybir.ActivationFunctionType
ALU = mybir.AluOpType

ENGS = ("pe", "act", "dve", "pool", "sp")

NCORES = 8
D = 1024
NH = 16
DFF = 2816
NFC = 22
PLE = 256
SEQ = 8192
PAST = 4096
DSEQ = 64
NSTR = 4
NTILES = 16
N_IN = 7184
ALPHA = float(2.0 ** 0.25)
EPS = 1e-5
NEG = -30000.0
NWB = 2
NKV = 3


class DmaSem:
    def __init__(self, handle):
        self.h = handle
        self.count = 0


class Op:
    __slots__ = ("eng", "fn", "deps", "need_inc", "seq", "idx", "dsem", "dval", "is_dma")


class Prog:
    def __init__(self, nc):
        self.nc = nc
        self.streams = {e: [] for e in ENGS}
        self.last_w = {}
        self.readers = {}
        self.scoped = set()
        self.inherit = {}
        self.touched = set()

    @staticmethod
    def _root(k):
        while isinstance(k, tuple):
            k = k[0]
        return k

    def end_scope(self, kind="A"):
        last = {}
        dmas = {}
        keys = [k for k in list(self.last_w.keys()) + list(self.readers.keys()) if self._root(k) == kind]
        ops = []
        for k in set(keys):
            w = self.last_w.pop(k, None)
            if w is not None:
                ops.append(w)
            ops.extend(self.readers.pop(k, ()))
        ops.extend(self.inherit.get(kind, []))
        for o in ops:
            if o.is_dma:
                key = id(o.dsem)
                if key not in dmas or dmas[key].dval < o.dval:
                    dmas[key] = o
            else:
                if o.eng not in last or last[o.eng].idx < o.idx:
                    last[o.eng] = o
        self.inherit[kind] = list(last.values()) + list(dmas.values())
        self.touched = {t for t in self.touched if self._root(t) != kind}

    def op(self, eng, fn, reads=(), writes=(), dsem=None):
        o = Op()
        o.eng = eng
        o.fn = fn
        o.need_inc = False
        o.seq = None
        o.is_dma = dsem is not None
        o.dsem = dsem
        if dsem is not None:
            dsem.count += 16
            o.dval = dsem.count
        else:
            o.dval = None
        deps = {}
        for s in reads:
            w = self.last_w.get(s)
            if w is not None:
                deps[id(w)] = w
        for s in writes:
            w = self.last_w.get(s)
            if w is not None and (w.is_dma or w.eng != eng or o.is_dma):
                deps[id(w)] = w
            for r in self.readers.get(s, ()):
                if r.is_dma or r.eng != eng or o.is_dma:
                    deps[id(r)] = r
            if self._root(s) in self.scoped and s not in self.touched:
                self.touched.add(s)
                for r in self.inherit.get(self._root(s), []):
                    if r.is_dma or r.eng != eng or o.is_dma:
                        deps[id(r)] = r
        o.deps = []
        for d in deps.values():
            if d.is_dma:
                v = d.dsem.count - (16 if d.dsem is dsem else 0)
                o.deps.append((d, v))
            elif not (d.eng == "pe" and eng == "pe" and not o.is_dma):
                d.need_inc = True
                o.deps.append((d, None))
        for s in reads:
            self.readers.setdefault(s, []).append(o)
        for s in writes:
            self.last_w[s] = o
            self.readers[s] = []
        o.idx = len(self.streams[eng])
        self.streams[eng].append(o)
        return o

    def emit(self, final_sems=()):
        nc = self.nc
        sems = {e: nc.alloc_semaphore(name=f"s_{e}") for e in ENGS}
        for e in ENGS:
            c = 0
            for o in self.streams[e]:
                if o.need_inc and not o.is_dma:
                    c += 1
                    o.seq = c
        with nc.Block() as block:
            def body(e):
                def run(engh):
                    waited = {}
                    for o in self.streams[e]:
                        for d, dv in o.deps:
                            if d.is_dma:
                                key = ("d", id(d.dsem))
                                val = dv
                                semh = d.dsem.h
                            else:
                                key = ("e", d.eng)
                                val = d.seq
                                semh = sems[d.eng]
                            if waited.get(key, 0) >= val:
                                continue
                            waited[key] = val
                            engh.wait_ge(semh, val)
                        ins = o.fn(engh)
                        if o.is_dma:
                            ins.then_inc(o.dsem.h, 16)
                        elif o.need_inc:
                            ins.then_inc(sems[e], 1)
                    if e == "sp":
                        for ds in final_sems:
                            if ds.count > 0:
                                engh.wait_ge(ds.h, ds.count)
                return run
            block.tensor(body("pe"))
            block.scalar(body("act"))
            block.vector(body("dve"))
            block.gpsimd(body("pool"))
            block.sync(body("sp"))


def build_nc(ntiles=NTILES, do_sample=True, debug=False):
    nc = bass.Bass("TRN2", target_bir_lowering=False)
    P = Prog(nc)
    allsems = []

    def tap(name, ap, reads):
        if not debug:
            return
        shape = list(ap.shape)
        d = nc.dram_tensor(name, shape, ap.dtype, kind="ExternalOutput").ap()
        P.op("sp", lambda e: e.dma_start(out=d, in_=ap), reads=reads, dsem=newsem("t_" + name))

    def newsem(name):
        s = DmaSem(nc.alloc_semaphore(name=name))
        allsems.append(s)
        return s

    def din(name, shape):
        return nc.dram_tensor(name, shape, F32, kind="ExternalInput").ap()

    def dout(name, shape):
        return nc.dram_tensor(name, shape, F32, kind="ExternalOutput").ap()

    x_p = din("x_p", [SEQ, D]); p_p = din("p_p", [SEQ, PLE])
    x_s = din("x_s", [NSTR * DSEQ, D]); p_s = din("p_s", [NSTR * DSEQ, PLE])
    cache_k = din("cache_k", [NSTR, PAST, D]); cache_v = din("cache_v", [NSTR, PAST, D])
    cache_lf = din("cache_lf", [NSTR, PAST, NH])
    st_conv = din("st_conv", [NSTR, 30, D]); st_ffn = din("st_ffn", [NSTR, 2, DFF])
    ln0_g = din("ln0_g", [D]); ln0_b = din("ln0_b", [D])
    w_in = din("w_in", [D, N_IN]); b_f = din("b_f", [NH])
    conv_dw_w = din("conv_dw_w", [31, D]); conv_dw_b = din("conv_dw_b", [D])
    conv_ln_g = din("conv_ln_g", [D]); conv_ln_b = din("conv_ln_b", [D])
    w_conv_out = din("w_conv_out", [D, D]); w_attn_out = din("w_attn_out", [D, D]); w_o = din("w_o", [D, D])
    ln1_g = din("ln1_g", [D]); ln1_b = din("ln1_b", [D])
    w_ffn_up = din("w_ffn_up", [D, 2 * DFF]); ffn_dw_w = din("ffn_dw_w", [3, DFF]); ffn_dw_b = din("ffn_dw_b", [DFF])
    w_ffn_down = din("w_ffn_down", [DFF, D])
    ln2_g = din("ln2_g", [D]); ln2_b = din("ln2_b", [D])
    w_ple = din("w_ple", [PLE, D]); w_ple_gate = din("w_ple_gate", [D, D])

    y_p = dout("y_p", [SEQ, D]); y_s = dout("y_s", [NSTR * DSEQ, D])
    k_p = dout("k_p", [SEQ, D]); v_p = dout("v_p", [SEQ, D]); lf_p = dout("lf_p", [SEQ, NH])
    cv_p = dout("cv_p", [30, D]); ff_p = dout("ff_p", [2, DFF])
    k_s = dout("k_s", [NSTR * DSEQ, D]); v_s = dout("v_s", [NSTR * DSEQ, D]); lf_s = dout("lf_s", [NSTR * DSEQ, NH])
    cv_s = dout("cv_s", [NSTR, 30, D]); ff_s = dout("ff_s", [NSTR, 2, DFF])

    kt_scr = nc.dram_tensor("kt_scr", [8, NTILES, 80, 1024], BF16).ap()
    v_scr = nc.dram_tensor("v_scr", [8, NTILES, 128, 768], BF16).ap()

    def sbt(name, shape, dt):
        return nc.alloc_sbuf_tensor(name, shape, dt)

    ident_f = sbt("ident_f", [128, 128], F32)
    ident_b = sbt("ident_b", [128, 128], BF16)
    maskb = sbt("maskb", [128, 128], BF16)
    onesm = sbt("onesm", [128, 128], BF16)
    onesb = sbt("onesb", [128, 64], BF16)
    zerob = sbt("zerob", [128, 128], BF16)
    ones16 = sbt("ones16", [16, 512], F32)
    ones3 = sbt("ones3", [3, 1024], BF16)
    cvec = sbt("cvec", [128, 8, 34], F32)
    fvec = sbt("fvec", [128, NFC, 4], F32)
    wfl = sbt("wfl", [128, 8, 16], BF16)
    nbf = sbt("nbf", [16, 1], F32)
    carry = sbt("carry", [16, 1], F32)
    hend = sbt("hend", [16, 4], F32)
    gbuf = sbt("gbuf", [128, 2, 1024], F32)
    wbuf = [sbt(f"wbuf{i}", [128, 8192], BF16) for i in range(NWB)]
    xres = sbt("xres", [128, 4, 1024], F32)
    xT = sbt("xT", [128, 8, 512], BF16)
    oT = sbt("oT", [128, 8, 512], BF16)
    gated_c = sbt("gated_c", [128, 8, 512], BF16)
    QT = sbt("QT", [128, 16, 512], BF16)
    KTs = sbt("KTs", [128, 8192], BF16)
    uhist = sbt("uhist", [128, 8, 4, 30], BF16)
    ahist = sbt("ahist", [128, NFC, 4, 2], F32)
    ARENA_W = 18944
    arena = sbt("arena", [128, ARENA_W], F32)
    psb = [nc.alloc_psum_tensor(f"psb{i}", [128, 512], F32) for i in range(8)]

    P.scoped.add("A")
    C_WORDS = 6400
    P.scoped.add("C")
    ar = {"A": C_WORDS, "C": 0, "n": 0}

    def ar_reset(kind="A"):
        ar[kind] = C_WORDS if kind == "A" else 0

    def alloc(shape, dt, parts=128, kind="A"):
        n = 1
        for s_ in shape[1:]:
            n *= s_
        words = n if dt == F32 else (n + 1) // 2
        words = (words + 7) // 8 * 8
        off = ar[kind]
        lim = ARENA_W if kind == "A" else C_WORDS
        assert off + words <= lim, ("arena overflow", kind, off, words)
        ar[kind] = off + words
        v = arena[0:shape[0], off:off + words]
        if dt == BF16:
            v = v.bitcast(BF16)[:, 0:n]
        else:
            v = v[:, 0:n]
        if len(shape) > 2:
            names = " ".join(f"d{i}" for i in range(1, len(shape)))
            kw = {f"d{i}": shape[i] for i in range(1, len(shape))}
            v = v.rearrange(f"p ({names}) -> p {names}", **kw)
        ar["n"] += 1
        return v, (kind, ar["n"])

    bank_rr = [0]

    def nb():
        b = bank_rr[0] % 8
        bank_rr[0] += 1
        return b

    def PS(b):
        return ("ps", b)

    P.op("pool", lambda e: e.memset(ident_f[:], 1.0), writes=["ident_f"])
    P.op("pool", lambda e: e.affine_select(out=ident_f[:], in_=ident_f[:], pattern=[[1, 128]], compare_op=ALU.is_equal,
                                           fill=0.0, base=0, channel_multiplier=-1), reads=["ident_f"], writes=["ident_f"])
    P.op("pool", lambda e: e.tensor_copy(out=ident_b[:], in_=ident_f[:]), reads=["ident_f"], writes=["ident_b"])
    mask32, mask32_s = alloc([128, 128], F32)
    P.op("pool", lambda e: e.memset(mask32[:], 0.0), writes=[mask32_s])
    P.op("pool", lambda e: e.affine_select(out=mask32[:], in_=mask32[:], pattern=[[1, 128]], compare_op=ALU.is_ge,
                                           fill=NEG, base=0, channel_multiplier=-1), reads=[mask32_s], writes=[mask32_s])
    P.op("pool", lambda e: e.tensor_copy(out=maskb[:], in_=mask32[:]), reads=[mask32_s], writes=["maskb"])
    P.op("pool", lambda e: e.memset(onesm[:], 1.0 / 1024.0), writes=["onesm"])
    P.op("pool", lambda e: e.memset(onesb[:], 1.0), writes=["onesb"])
    P.op("pool", lambda e: e.memset(zerob[:], 0.0), writes=["zerob"])
    if debug:
        P.op("pool", lambda e: e.memset(oT[:, :, :], 7.0), writes=[("oT", g_, j_) for g_ in range(8) for j_ in range(2)])
    P.op("pool", lambda e: e.memset(ones16[:], 1.0), writes=["ones16"])
    P.op("pool", lambda e: e.memset(ones3[:], 1.0), writes=["ones3"])
    P.op("pool", lambda e: e.memset(carry[:], 0.0), writes=["carry"])
    P.op("pool", lambda e: e.memset(uhist[:], 0.0), writes=["uhist"])
    P.op("pool", lambda e: e.memset(ahist[:], 0.0), writes=["ahist"])
    P.op("pool", lambda e: e.memset(QT[64:96, :, :], 1.0), writes=["QTc"])
    P.op("pool", lambda e: e.memset(KTs[64:96, :], 0.0), writes=["KTc0"])
    s_c1 = newsem("c1")
    for g in range(8):
        P.op("sp", lambda e, g=g: e.dma_start(out=KTs[67:70, 1024 * g:1024 * (g + 1)], in_=ones3[:, :]),
             reads=["ones3", "KTc0"], writes=[("KTc1", g)], dsem=s_c1)
    KTC = ["KTc0"] + [("KTc1", g) for g in range(8)]

    vrows, vrows_s = alloc([34, 1024], F32, 34)
    frows, frows_s = alloc([4, DFF], F32, 4)
    bft, bft_s = alloc([16, 1], F32, 16)
    s_c2 = newsem("c2")
    vr = [(vrows_s, "r", i) for i in range(5)]
    P.op("sp", lambda e: e.dma_start(out=vrows[0:1, :], in_=conv_dw_b.rearrange("(o n) -> o n", o=1)), writes=[vr[0], vrows_s], dsem=s_c2)
    P.op("sp", lambda e: e.dma_start(out=vrows[1:2, :], in_=conv_ln_g.rearrange("(o n) -> o n", o=1)), writes=[vr[1]], dsem=s_c2)
    P.op("sp", lambda e: e.dma_start(out=vrows[2:3, :], in_=conv_ln_b.rearrange("(o n) -> o n", o=1)), writes=[vr[2]], dsem=s_c2)
    P.op("sp", lambda e: e.dma_start(out=vrows[3:34, :], in_=conv_dw_w), writes=[vr[3]], dsem=s_c2)
    P.op("sp", lambda e: e.dma_start(out=frows[0:3, :], in_=ffn_dw_w), writes=[vr[4], frows_s], dsem=s_c2)
    P.op("sp", lambda e: e.dma_start(out=frows[3:4, :], in_=ffn_dw_b.rearrange("(o n) -> o n", o=1)), writes=[(frows_s, "r5")], dsem=s_c2)
    P.op("sp", lambda e: e.dma_start(out=bft[:, :], in_=b_f.rearrange("(h o) -> h o", o=1)), writes=[(bft_s, "r6"), bft_s], dsem=s_c2)
    VR = vr + [(frows_s, "r5"), (bft_s, "r6"), vrows_s, frows_s, bft_s]
    P.op("dve", lambda e: e.tensor_scalar(out=nbf[:], in0=bft[:, :], scalar1=-1.0, scalar2=None, op0=ALU.mult),
         reads=VR, writes=["nbf"])
    for c in range(8):
        b = nb()
        P.op("pe", lambda e, c=c, b=b: e.transpose(out=psb[b][:, 0:34], in_=vrows[0:34, 128 * c:128 * (c + 1)], identity=ident_f[0:34, 0:34]),
             reads=VR + ["ident_f"], writes=[PS(b)])
        P.op("dve", lambda e, c=c, b=b: e.tensor_copy(out=cvec[:, c, :], in_=psb[b][:, 0:34]), writes=[PS(b), ("cvec", c)])
    for c in range(NFC):
        b = nb()
        P.op("pe", lambda e, c=c, b=b: e.transpose(out=psb[b][:, 0:4], in_=frows[0:4, 128 * c:128 * (c + 1)], identity=ident_f[0:4, 0:4]),
             reads=VR + ["ident_f"], writes=[PS(b)])
        P.op("dve", lambda e, c=c, b=b: e.tensor_copy(out=fvec[:, c, :], in_=psb[b][:, 0:4]), writes=[PS(b), ("fvec", c)])
    CVEC = [("cvec", c) for c in range(8)]
    FVEC = [("fvec", c) for c in range(NFC)]

    wgroups = {}

    def wsrc(W, r0, nk, c0, w):
        return W[r0:r0 + nk * 128, :].rearrange("(kc p) n -> p kc n", p=128)[:, :, c0:c0 + w]

    def wgroup(name, nk, ncols, srcs):
        scr = nc.dram_tensor("ws_" + name, [128, nk, ncols], BF16).ap()
        sem = newsem("wp_" + name)
        slots = []
        for j, (src, off, w) in enumerate(srcs):
            sl = ("wscr", name, j)
            slots.append(sl)
            P.op("pool", lambda e, src=src, off=off, w=w: e.dma_start(out=scr[:, :, off:off + w], in_=src), writes=[sl], dsem=sem)
        wgroups[name] = (scr, nk, ncols, slots)

    wgroup("FL", 8, 16, [(wsrc(w_in, 0, 8, 5120, 16), 0, 16)])
    wgroup("PLE", 2, 1024, [(wsrc(w_ple, 0, 2, 0, 1024), 0, 1024)])
    wgroup("A1", 8, 1024, [(wsrc(w_in, 0, 8, 0, 512), 0, 512), (wsrc(w_in, 0, 8, 1024, 512), 512, 512)])
    wgroup("A2", 8, 1024, [(wsrc(w_in, 0, 8, 512, 512), 0, 512), (wsrc(w_in, 0, 8, 1536, 512), 512, 512)])
    wgroup("Q", 8, 1024, [(wsrc(w_in, 0, 8, 2048, 1024), 0, 1024)])
    wgroup("K", 8, 1024, [(wsrc(w_in, 0, 8, 3072, 1024), 0, 1024)])
    wgroup("V", 8, 1024, [(wsrc(w_in, 0, 8, 4096, 1024), 0, 1024)])
    wgroup("GC", 8, 1024, [(wsrc(w_in, 0, 8, 5136, 1024), 0, 1024)])
    wgroup("CO", 8, 1024, [(wsrc(w_conv_out, 0, 8, 0, 1024), 0, 1024)])
    wgroup("GA", 8, 1024, [(wsrc(w_in, 0, 8, 6160, 1024), 0, 1024)])
    wgroup("AO", 8, 1024, [(wsrc(w_attn_out, 0, 8, 0, 1024), 0, 1024)])
    wgroup("O", 8, 1024, [(wsrc(w_o, 0, 8, 0, 1024), 0, 1024)])
    wgroup("PG", 8, 1024, [(wsrc(w_ple_gate, 0, 8, 0, 1024), 0, 1024)])
    for g in range(6):
        hw = 512 if g < 5 else 256
        wgroup(f"UP{g}", 8, 2 * hw, [(wsrc(w_ffn_up, 0, 8, 512 * g, hw), 0, hw), (wsrc(w_ffn_up, 0, 8, DFF + 512 * g, hw), hw, hw)])
    for n in range(2):
        for kh in range(2):
            wgroup(f"DN{kh}{n}", 11, 512, [(wsrc(w_ffn_down, kh * 11 * 128, 11, 512 * n, 512), 0, 512)])

    s_wres = newsem("wres")
    P.op("sp", lambda e: e.dma_start(out=wfl[:], in_=wgroups["FL"][0]), reads=wgroups["FL"][3], writes=["wfl"], dsem=s_wres)

    tile_seq = ["A1", "A2", "Q", "K", "V", "GC", "CO", "GA", "AO", "O", "PG", "PLE"] + [f"UP{g}" for g in range(6)] + ["DN00", "DN10", "DN01", "DN11"]
    ntot = ntiles + (1 if do_sample else 0)
    wseq = tile_seq * ntot
    wsem = [newsem(f"wl{i}") for i in range(NWB)]
    wst = {"pos": 0, "loaded": 0}

    def w_ensure(n):
        while wst["loaded"] <= n and wst["loaded"] < len(wseq):
            m = wst["loaded"]
            scr, nk, ncols, slots = wgroups[wseq[m]]
            b = m % NWB
            dst = wbuf[b][:, 0:nk * ncols].rearrange("p (k n) -> p k n", k=nk)
            P.op("sp", lambda e, dst=dst, scr=scr: e.dma_start(out=dst, in_=scr), reads=slots, writes=[("wb", b)], dsem=wsem[b])
            wst["loaded"] += 1

    def w_get(name, ahead=NWB - 1):
        n = wst["pos"]
        assert wseq[n] == name, (wseq[n], name)
        w_ensure(n + ahead)
        scr, nk, ncols, slots = wgroups[name]
        b = n % NWB
        wst["pos"] += 1
        return wbuf[b][:, 0:nk * ncols].rearrange("p (k n) -> p k n", k=nk), ("wb", b)

    s_x = [newsem(f"x{i}") for i in range(2)]
    s_g = newsem("g"); s_b = newsem("b")
    s_y = [newsem(f"y{i}") for i in range(4)]
    s_kvo = [newsem(f"kvo{i}") for i in range(2)]
    s_lf = newsem("lf")
    s_ktw = [newsem(f"ktw{i}") for i in range(8)]
    s_vw = [newsem(f"vw{i}") for i in range(8)]
    s_rq = newsem("rq"); s_rk = newsem("rk")
    s_kvK = [newsem(f"kvK{i}") for i in range(NKV)]
    s_kvV = [newsem(f"kvV{i}") for i in range(NKV)]
    s_p = newsem("p")
    s_st = [newsem(f"st{i}") for i in range(2)]
    s_vst = newsem("vst")

    def run_tile(kind, ti):
        prompt = kind == "p"
        Tt = 512 if prompt else 256
        nsub = Tt // 128
        nseg, L = (1, 512) if prompt else (4, 64)
        t0 = ti * 512
        last = prompt and ti == NTILES - 1
        need_state = last or not prompt
        x_src = x_p[t0:t0 + Tt, :] if prompt else x_s
        p_src = p_p[t0:t0 + Tt, :] if prompt else p_s
        y_dst = y_p[t0:t0 + Tt, :] if prompt else y_s
        k_dst = k_p[t0:t0 + Tt, :] if prompt else k_s
        v_dst = v_p[t0:t0 + Tt, :] if prompt else v_s
        lf_dst = lf_p[t0:t0 + Tt, :] if prompt else lf_s
        state_subs = [3] if prompt else [0, 1]

        def load_gb(g_ap, b_ap):
            P.op("sp", lambda e: e.dma_start(out=gbuf[:, 0, :], in_=g_ap.partition_broadcast(128)), writes=["gb0"], dsem=s_g)
            P.op("sp", lambda e: e.dma_start(out=gbuf[:, 1, :], in_=b_ap.partition_broadcast(128)), writes=["gb1"], dsem=s_b)

        def ln_rows(src, src_slots, sub, tmp):
            st, st_s, mv, mv_s, lnv, lnv_s, rstd, rstd_s = tmp
            XR = ("xres", sub)
            P.op("dve", lambda e: e.bn_stats(out=st[:, 0, :], in_=src[:, 0:512]), reads=src_slots, writes=[st_s])
            P.op("dve", lambda e: e.bn_stats(out=st[:, 1, :], in_=src[:, 512:1024]), reads=src_slots, writes=[(st_s, 1)])
            P.op("dve", lambda e: e.bn_aggr(out=mv[:, :], in_=st[:, :, :].rearrange("p a b -> p (a b)")), reads=[st_s, (st_s, 1)], writes=[mv_s])
            P.op("act", lambda e: e.activation(out=lnv[:, :], in_=mv[:, 1:2], func=AF.Ln, bias=EPS, scale=1.0), reads=[mv_s], writes=[lnv_s])
            P.op("act", lambda e: e.activation(out=rstd[:, :], in_=lnv[:, :], func=AF.Exp, scale=-0.5), reads=[lnv_s], writes=[rstd_s])
            P.op("dve", lambda e: e.tensor_scalar(out=xres[:, sub, :], in0=src, scalar1=mv[:, 0:1], scalar2=rstd[:, 0:1],
                                                  op0=ALU.subtract, op1=ALU.mult), reads=src_slots + [mv_s, rstd_s], writes=[XR])
            P.op("pool", lambda e: e.tensor_tensor(out=xres[:, sub, :], in0=xres[:, sub, :], in1=gbuf[:, 0, :], op=ALU.mult),
                 reads=[XR, "gb0"], writes=[XR])
            P.op("pool", lambda e: e.tensor_tensor(out=xres[:, sub, :], in0=xres[:, sub, :], in1=gbuf[:, 1, :], op=ALU.add),
                 reads=[XR, "gb1"], writes=[XR])

        def to_featmajor(sub, xb, xb_s):
            XR = ("xres", sub)
            P.op("act", lambda e: e.activation(out=xb[:, :], in_=xres[:, sub, :], func=AF.Identity), reads=[XR], writes=[xb_s])
            b = nb()
            psv = psb[b][:].bitcast(BF16)
            for c in range(8):
                P.op("pe", lambda e, c=c: e.transpose(out=psv[:, 128 * c:128 * (c + 1)], in_=xb[:, 128 * c:128 * (c + 1)], identity=ident_b[:, :]),
                     reads=[xb_s, "ident_b"], writes=[PS(b)])
            P.op("dve", lambda e: e.tensor_copy(out=xT[:, :, 128 * sub:128 * (sub + 1)], in_=psv[:, :].rearrange("p (c t) -> p c t", c=8)),
                 writes=[PS(b), ("xT", sub)])

        XT = [("xT", s) for s in range(nsub)]

        ar_reset()
        xin = [alloc([128, 1024], F32) for _ in range(2)]
        xb2 = [alloc([128, 1024], BF16)] * 2
        lnt = []
        for _ in range(2):
            st, st_s = alloc([128, 2, 6], F32); mv, mv_s = alloc([128, 2], F32)
            lnv, lnv_s = alloc([128, 1], F32); rstd, rstd_s = alloc([128, 1], F32)
            lnt.append((st, st_s, mv, mv_s, lnv, lnv_s, rstd, rstd_s))
        kvst = [alloc([128, 1024], F32) for _ in range(2)]

        load_gb(ln0_g, ln0_b)
        for sub in range(nsub):
            xi, xi_s = xin[sub % 2]
            P.op("sp", lambda e, sub=sub, xi=xi: e.dma_start(out=xi[:, :], in_=x_src[128 * sub:128 * (sub + 1), :]), writes=[xi_s], dsem=s_x[sub % 2])
            ln_rows(xi, [xi_s], sub, lnt[sub % 2])
            to_featmajor(sub, *xb2[sub % 2])

        if ti == 0:
            tap(('d_' if prompt else 's_') + 'xT', xT[:, :, :], XT)
        lA, lA_s = alloc([16, 512], F32, 16)
        lB, lB_s = alloc([16, 512], F32, 16)
        cT, cT_s = alloc([16, 512], F32, 16)
        r1, r1_s = alloc([16, 512], F32, 16)
        SQ, SQ_s = alloc([16, 3, 512], BF16, 16)
        SK, SK_s = alloc([16, 3, 512], BF16, 16)
        lft, lft_s = alloc([128, 4, 16], F32)
        b = nb()
        for kc in range(8):
            P.op("pe", lambda e, kc=kc, b=b: e.matmul(psb[b][0:16, 0:Tt], lhsT=wfl[:, kc, :], rhs=xT[:, kc, 0:Tt], start=(kc == 0), stop=(kc == 7)),
                 reads=XT + ["wfl"], writes=[PS(b)])
        P.op("act", lambda e, b=b: e.activation(out=lA[:, 0:Tt], in_=psb[b][0:16, 0:Tt], func=AF.Exp, bias=nbf[:, 0:1], scale=-1.0),
             reads=["nbf"], writes=[PS(b), lA_s])
        P.op("act", lambda e: e.activation(out=lB[:, 0:Tt], in_=lA[:, 0:Tt], func=AF.Ln, bias=1.0, scale=1.0), reads=[lA_s], writes=[lB_s])
        P.op("dve", lambda e: e.tensor_scalar(out=lA[:, 0:Tt], in0=lB[:, 0:Tt], scalar1=-1.0, scalar2=None, op0=ALU.mult),
             reads=[lB_s], writes=[lA_s])
        if prompt:
            P.op("dve", lambda e: e.tensor_tensor_scan(out=cT[:, 0:512], data0=ones16[:, 0:512], data1=lA[:, 0:512], initial=carry[:, 0:1],
                                                       op0=ALU.mult, op1=ALU.add), reads=[lA_s, "ones16", "carry"], writes=[cT_s])
            P.op("dve", lambda e: e.tensor_copy(out=carry[:, 0:1], in_=cT[:, 511:512]), reads=[cT_s], writes=["carry"])
        else:
            for s in range(4):
                P.op("dve", lambda e, s=s: e.tensor_tensor_scan(out=cT[:, 64 * s:64 * (s + 1)], data0=ones16[:, 0:64], data1=lA[:, 64 * s:64 * (s + 1)],
                                                                initial=hend[:, s:s + 1], op0=ALU.mult, op1=ALU.add),
                     reads=[lA_s, "ones16", "hend"], writes=[cT_s])
        P.op("dve", lambda e: e.tensor_scalar(out=SQ[:, 0, 0:Tt], in0=cT[:, 0:Tt], scalar1=8.0, scalar2=None, op0=ALU.mult), reads=[cT_s], writes=[SQ_s])
        P.op("dve", lambda e: e.scalar_tensor_tensor(out=r1[:, 0:Tt], in0=cT[:, 0:Tt], scalar=8.0, in1=SQ[:, 0, 0:Tt], op0=ALU.mult, op1=ALU.subtract),
             reads=[cT_s, SQ_s], writes=[r1_s])
        P.op("dve", lambda e: e.tensor_copy(out=SQ[:, 1, 0:Tt], in_=r1[:, 0:Tt]), reads=[r1_s], writes=[(SQ_s, 1)])
        P.op("dve", lambda e: e.tensor_tensor(out=lB[:, 0:Tt], in0=r1[:, 0:Tt], in1=SQ[:, 1, 0:Tt], op=ALU.subtract), reads=[r1_s, (SQ_s, 1)], writes=[lB_s])
        P.op("dve", lambda e: e.tensor_copy(out=SQ[:, 2, 0:Tt], in_=lB[:, 0:Tt]), reads=[lB_s], writes=[(SQ_s, 2)])
        SQA = [SQ_s, (SQ_s, 1), (SQ_s, 2)]
        P.op("dve", lambda e: e.tensor_scalar(out=SK[:, :, 0:Tt], in0=SQ[:, :, 0:Tt], scalar1=-1.0, scalar2=None, op0=ALU.mult), reads=SQA, writes=[SK_s])
        b = nb()
        for sub in range(nsub):
            P.op("pe", lambda e, sub=sub, b=b: e.transpose(out=psb[b][:, 16 * sub:16 * (sub + 1)], in_=lA[:, 128 * sub:128 * (sub + 1)], identity=ident_f[0:16, 0:16]),
                 reads=[lA_s, "ident_f"], writes=[PS(b)])
        P.op("dve", lambda e, b=b: e.tensor_copy(out=lft[:, 0:nsub, :], in_=psb[b][:, 0:16 * nsub].rearrange("p (s h) -> p s h", h=16)), writes=[PS(b), lft_s])
        P.op("pool", lambda e: e.dma_start(out=lf_dst.rearrange("(s p) h -> p s h", p=128), in_=lft[:, 0:nsub, :]), reads=[lft_s], dsem=s_lf)
        RQ = [("rq", h) for h in range(NH)]
        RK = [("rk", h) for h in range(NH)]
        if prompt:
            KTv = KTs[:, :].rearrange("p (g b h i) -> p g b h i", g=8, b=4, h=2)
        else:
            KTv = KTs[:, 0:4096].rearrange("p (h t) -> p h t", h=16)
        for h in range(NH):
            P.op("pool", lambda e, h=h: e.dma_start(out=QT[67:70, h, 0:Tt], in_=SQ[h:h + 1, :, 0:Tt]), reads=SQA + ["QTc"], writes=[RQ[h]], dsem=s_rq)
            if prompt:
                P.op("pool", lambda e, h=h: e.dma_start(out=KTv[64:67, h // 2, :, h % 2, :], in_=SK[h:h + 1, :, 0:512]), reads=[SK_s] + KTC, writes=[RK[h]], dsem=s_rk)
            else:
                P.op("pool", lambda e, h=h: e.dma_start(out=KTv[64:67, h, :], in_=SK[h:h + 1, :, 0:256]), reads=[SK_s] + KTC, writes=[RK[h]], dsem=s_rk)

        ucT, ucT_s = alloc([128, 8, 512], BF16, kind="C")
        sgt = [alloc([128, 512], BF16) for _ in range(2)]
        acc = [alloc([128, 512], F32, kind="C") for _ in range(4)]

        ust, ust_s = kvst[0]
        bmean = 6
        bmsq = 7
        def nb6():
            while True:
                b = nb()
                if b < 6:
                    return b

        def seg3(ap):
            return ap.rearrange("p (s l) -> p s l", s=nseg)

        if not prompt:
            sc, sc_s = alloc([120, 1024], F32, 120)
            P.op("sp", lambda e: e.dma_start(out=sc[:, :], in_=st_conv.rearrange("s r d -> (s r) d")), writes=[sc_s], dsem=s_st[0])
            for c in range(8):
                b = nb6()
                P.op("pe", lambda e, c=c, b=b: e.transpose(out=psb[b][:, 0:120], in_=sc[0:120, 128 * c:128 * (c + 1)], identity=ident_f[0:120, 0:120]),
                     reads=[sc_s, "ident_f"], writes=[PS(b)])
                P.op("dve", lambda e, c=c, b=b: e.tensor_copy(out=uhist[:, c, :, :], in_=psb[b][:, 0:120].rearrange("p (s r) -> p s r", s=4)),
                     writes=[PS(b), ("uhist", c)])

        utl = [alloc([128, nseg, 30 + L], BF16, kind="C") for _ in range(8)]
        for half in range(2):
            W, W_s = w_get("A1" if half == 0 else "A2")
            for cc in range(4):
                c = 4 * half + cc
                ba = nb6(); bg = nb6()
                for kc in range(8):
                    P.op("pe", lambda e, kc=kc, cc=cc, ba=ba, W=W: e.matmul(psb[ba][:, 0:Tt], lhsT=W[:, kc, 128 * cc:128 * (cc + 1)], rhs=xT[:, kc, 0:Tt], start=(kc == 0), stop=(kc == 7)),
                         reads=XT + [W_s], writes=[PS(ba)])
                for kc in range(8):
                    P.op("pe", lambda e, kc=kc, cc=cc, bg=bg, W=W: e.matmul(psb[bg][:, 0:Tt], lhsT=W[:, kc, 512 + 128 * cc:512 + 128 * (cc + 1)], rhs=xT[:, kc, 0:Tt], start=(kc == 0), stop=(kc == 7)),
                         reads=XT + [W_s], writes=[PS(bg)])
                sg, sg_s = sgt[c % 2]
                u, u_s = utl[c]
                P.op("act", lambda e, bg=bg, sg=sg: e.activation(out=sg[:, 0:Tt], in_=psb[bg][:, 0:Tt], func=AF.Sigmoid), writes=[PS(bg), sg_s])
                P.op("pool", lambda e, c=c, u=u: e.tensor_copy(out=u[:, :, 0:30], in_=uhist[:, c, 0:nseg, :]), reads=[("uhist", c)], writes=[(u_s, "h")])
                P.op("dve", lambda e, ba=ba, sg=sg, u=u: e.tensor_tensor(out=u[:, :, 30:30 + L], in0=seg3(psb[ba][:, 0:Tt]), in1=seg3(sg[:, 0:Tt]), op=ALU.mult),
                     reads=[sg_s], writes=[PS(ba), u_s])
                if prompt:
                    P.op("pool", lambda e, c=c, u=u: e.tensor_copy(out=uhist[:, c, 0, :], in_=u[:, 0, L:L + 30]), reads=[u_s, (u_s, "h")], writes=[("uhist", c)])
            if need_state:
                for sub in state_subs:
                    ba = nb6(); bg = nb6()
                    for kc in range(8):
                        P.op("pe", lambda e, kc=kc, sub=sub, ba=ba, W=W: e.matmul(psb[ba][:, :], lhsT=xT[:, kc, 128 * sub:128 * (sub + 1)], rhs=W[:, kc, 0:512], start=(kc == 0), stop=(kc == 7)),
                             reads=XT + [W_s], writes=[PS(ba)])
                    for kc in range(8):
                        P.op("pe", lambda e, kc=kc, sub=sub, bg=bg, W=W: e.matmul(psb[bg][:, :], lhsT=xT[:, kc, 128 * sub:128 * (sub + 1)], rhs=W[:, kc, 512:1024], start=(kc == 0), stop=(kc == 7)),
                             reads=XT + [W_s], writes=[PS(bg)])
                    ut, ut_s = kvst[sub % 2]
                    sgf, sgf_s = acc[3]
                    P.op("act", lambda e, bg=bg, sgf=sgf: e.activation(out=sgf[:, :], in_=psb[bg][:, :], func=AF.Sigmoid), writes=[PS(bg), sgf_s])
                    P.op("dve", lambda e, ba=ba, sgf=sgf, ut=ut, half=half: e.tensor_tensor(out=ut[:, 512 * half:512 * (half + 1)], in0=psb[ba][:, :], in1=sgf[:, :], op=ALU.mult),
                         reads=[sgf_s], writes=[PS(ba), (ut_s, half)])
                    if prompt:
                        P.op("pool", lambda e, ut=ut, half=half: e.dma_start(out=cv_p[:, 512 * half:512 * (half + 1)], in_=ut[98:128, 512 * half:512 * (half + 1)]),
                             reads=[(ut_s, half)], dsem=s_kvo[sub % 2])
                    else:
                        for j in range(2):
                            P.op("pool", lambda e, ut=ut, half=half, j=j, sub=sub: e.dma_start(out=cv_s[2 * sub + j, :, 512 * half:512 * (half + 1)],
                                                                                           in_=ut[64 * j + 34:64 * j + 64, 512 * half:512 * (half + 1)]),
                                 reads=[(ut_s, half)], dsem=s_kvo[sub % 2])
        def chain(c):
            u, u_s = utl[c]
            US = [u_s, (u_s, "h")]
            a0, a0_s = acc[(2 * c) % 4]
            a1, a1_s = acc[(2 * c + 1) % 4]
            P.op("dve", lambda e, c=c, u=u, a0=a0: e.tensor_scalar(out=seg3(a0[:, 0:Tt]), in0=u[:, :, 0:L], scalar1=cvec[:, c, 3:4], scalar2=cvec[:, c, 0:1], op0=ALU.mult, op1=ALU.add),
                 reads=US + CVEC, writes=[a0_s])
            P.op("dve", lambda e, c=c, u=u, a1=a1: e.tensor_scalar(out=seg3(a1[:, 0:Tt]), in0=u[:, :, 1:1 + L], scalar1=cvec[:, c, 4:5], scalar2=None, op0=ALU.mult),
                 reads=US + CVEC, writes=[a1_s])
            for k in range(2, 31):
                a, a_s = (a0, a0_s) if k % 2 == 0 else (a1, a1_s)
                P.op("dve", lambda e, c=c, k=k, u=u, a=a: e.scalar_tensor_tensor(out=seg3(a[:, 0:Tt]), in0=u[:, :, k:k + L], scalar=cvec[:, c, 3 + k:4 + k],
                                                                              in1=seg3(a[:, 0:Tt]), op0=ALU.mult, op1=ALU.add),
                     reads=US + CVEC + [a_s], writes=[a_s])
            P.op("dve", lambda e, c=c, a0=a0, a1=a1: e.tensor_tensor(out=ucT[:, c, 0:Tt], in0=a0[:, 0:Tt], in1=a1[:, 0:Tt], op=ALU.add), reads=[a0_s, a1_s], writes=[(ucT_s, c)])

        chains_pending = list(range(8))
        if not prompt:
            while chains_pending:
                chain(chains_pending.pop(0))
        W, W_s = w_get("Q")
        for c in range(8):
            b = nb()
            for kc in range(8):
                P.op("pe", lambda e, kc=kc, c=c, b=b, W=W: e.matmul(psb[b][:, 0:Tt], lhsT=W[:, kc, 128 * c:128 * (c + 1)], rhs=xT[:, kc, 0:Tt], start=(kc == 0), stop=(kc == 7)),
                     reads=XT + [W_s], writes=[PS(b)])
            P.op("act", lambda e, c=c, b=b: e.activation(out=QT[0:64, 2 * c, 0:Tt], in_=psb[b][0:64, 0:Tt], func=AF.Identity), writes=[PS(b), ("QT", 2 * c)])
            P.op("act", lambda e, c=c, b=b: e.activation(out=QT[0:64, 2 * c + 1, 0:Tt], in_=psb[b][64:128, 0:Tt], func=AF.Identity), writes=[PS(b), ("QT", 2 * c + 1)])
        W, W_s = w_get("K")
        for c in range(8):
            b = nb()
            for kc in range(8):
                P.op("pe", lambda e, kc=kc, c=c, b=b, W=W: e.matmul(psb[b][:, 0:Tt], lhsT=W[:, kc, 128 * c:128 * (c + 1)], rhs=xT[:, kc, 0:Tt], start=(kc == 0), stop=(kc == 7)),
                     reads=XT + [W_s], writes=[PS(b)])
            if prompt:
                P.op("act", lambda e, c=c, b=b: e.activation(out=KTv[0:64, c, :, 0, :], in_=psb[b][0:64, :].rearrange("p (b i) -> p b i", b=4), func=AF.Identity),
                     writes=[PS(b), ("KT", 2 * c)])
                P.op("act", lambda e, c=c, b=b: e.activation(out=KTv[0:64, c, :, 1, :], in_=psb[b][64:128, :].rearrange("p (b i) -> p b i", b=4), func=AF.Identity),
                     writes=[PS(b), ("KT", 2 * c + 1)])
            else:
                P.op("act", lambda e, c=c, b=b: e.activation(out=KTv[0:64, 2 * c, :], in_=psb[b][0:64, 0:Tt], func=AF.Identity), writes=[PS(b), ("KT", 2 * c)])
                P.op("act", lambda e, c=c, b=b: e.activation(out=KTv[0:64, 2 * c + 1, :], in_=psb[b][64:128, 0:Tt], func=AF.Identity), writes=[PS(b), ("KT", 2 * c + 1)])

        def tokmajor_out(W, W_s, dst, which):
            for sub in range(nsub):
                stg, stg_s = kvst[sub % 2]
                for half in range(2):
                    b = nb()
                    for kc in range(8):
                        P.op("pe", lambda e, kc=kc, sub=sub, half=half, b=b: e.matmul(psb[b][:, :], lhsT=xT[:, kc, 128 * sub:128 * (sub + 1)], rhs=W[:, kc, 512 * half:512 * (half + 1)],
                                                                                     start=(kc == 0), stop=(kc == 7)), reads=XT + [W_s], writes=[PS(b)])
                    if half == 0:
                        P.op("act", lambda e, b=b, stg=stg: e.activation(out=stg[:, 0:512], in_=psb[b][:, :], func=AF.Identity), writes=[PS(b), (stg_s, 0)])
                    else:
                        P.op("act", lambda e, b=b, stg=stg: e.activation(out=stg[:, 512:1024], in_=psb[b][:, :], func=AF.Identity), writes=[PS(b), (stg_s, 1)])
                P.op("sp", lambda e, sub=sub, stg=stg: e.dma_start(out=dst[128 * sub:128 * (sub + 1), :], in_=stg[:, :]), reads=[(stg_s, 0), (stg_s, 1)], dsem=s_kvo[sub % 2])
                if which == "v":
                    if prompt:
                        for par in range(2):
                            P.op("pool", lambda e, sub=sub, stg=stg, par=par: e.tensor_copy(
                                out=Vs[:, :, sub, 128 * par:128 * par + 64],
                                in_=stg[:, :].rearrange("p (g r) -> p g r", g=8)[:, :, 64 * par:64 * par + 64]),
                                reads=[(stg_s, 0), (stg_s, 1)], writes=[(Vs_s, sub, par)])
                    else:
                        P.op("pool", lambda e, sub=sub, stg=stg: e.tensor_copy(out=Vn[:, sub, :], in_=stg[:, :]), reads=[(stg_s, 0), (stg_s, 1)], writes=[(Vn_s, sub)])

        tokmajor_out(W, W_s, k_dst, "k")
        if prompt:
            Vs, Vs_s = alloc([128, 8, 4, 192], BF16)
            P.op("pool", lambda e: e.memset(Vs[:, :, :, 64:128], 1.0), writes=[(Vs_s, "ones")])
        else:
            Vn = KTs[:, 4096:6144].rearrange("p (s d) -> p s d", s=2)
            Vn_s = "VnP"
        W, W_s = w_get("V")
        tokmajor_out(W, W_s, v_dst, "v")
        if prompt:
            for g in range(8):
                P.op("sp", lambda e, g=g: e.dma_start(out=kt_scr[g, ti, :, :], in_=KTs[0:80, 1024 * g:1024 * (g + 1)]),
                     reads=[("KT", 2 * g), ("KT", 2 * g + 1)] + RK + KTC, writes=[("ktscr", g, ti)], dsem=s_ktw[g])
                P.op("sp", lambda e, g=g: e.dma_start(out=v_scr[g, ti, :, :], in_=Vs[:, g, :, :].rearrange("p b r -> p (b r)")),
                     reads=[(Vs_s, s_, p_) for s_ in range(4) for p_ in range(2)] + [(Vs_s, "ones")], writes=[("vscr", g, ti)], dsem=s_vw[g])

        W, W_s = w_get("GC")
        for c in range(8):
            b = nb()
            for kc in range(8):
                P.op("pe", lambda e, kc=kc, c=c, b=b, W=W: e.matmul(psb[b][:, 0:Tt], lhsT=W[:, kc, 128 * c:128 * (c + 1)], rhs=xT[:, kc, 0:Tt], start=(kc == 0), stop=(kc == 7)),
                     reads=XT + [W_s], writes=[PS(b)])
            P.op("act", lambda e, c=c, b=b: e.activation(out=gated_c[:, c, 0:Tt], in_=psb[b][:, 0:Tt], func=AF.Sigmoid), writes=[PS(b), ("gc", c)])
        def conv_finish():
            sqt = [alloc([128, 512], BF16) for _ in range(2)]
            for c in range(8):
                sq, sq_s = sqt[c % 2]
                P.op("act", lambda e, c=c, sq=sq: e.activation(out=sq[:, 0:Tt], in_=ucT[:, c, 0:Tt], func=AF.Square), reads=[(ucT_s, c)], writes=[sq_s])
                P.op("pe", lambda e, c=c: e.matmul(psb[bmean][:, 0:Tt], lhsT=onesm[:, :], rhs=ucT[:, c, 0:Tt], start=(c == 0), stop=(c == 7)),
                     reads=[(ucT_s, c), "onesm"], writes=[PS(bmean)])
                P.op("pe", lambda e, c=c, sq=sq: e.matmul(psb[bmsq][:, 0:Tt], lhsT=onesm[:, :], rhs=sq[:, 0:Tt], start=(c == 0), stop=(c == 7)),
                     reads=[sq_s, "onesm"], writes=[PS(bmsq)])
            mean_sb, mean_s = alloc([128, 512], F32)
            rstd_sb, rstd_sbs = alloc([128, 512], F32)
            m2, m2_s = acc[0]
            P.op("act", lambda e: e.activation(out=mean_sb[:, 0:Tt], in_=psb[bmean][:, 0:Tt], func=AF.Identity), writes=[PS(bmean), mean_s])
            P.op("pool", lambda e: e.tensor_tensor(out=m2[:, 0:Tt], in0=mean_sb[:, 0:Tt], in1=mean_sb[:, 0:Tt], op=ALU.mult), reads=[mean_s], writes=[m2_s])
            P.op("dve", lambda e: e.tensor_tensor(out=m2[:, 0:Tt], in0=psb[bmsq][:, 0:Tt], in1=m2[:, 0:Tt], op=ALU.subtract), reads=[m2_s], writes=[PS(bmsq), m2_s])
            P.op("act", lambda e: e.activation(out=m2[:, 0:Tt], in_=m2[:, 0:Tt], func=AF.Ln, bias=EPS, scale=1.0), reads=[m2_s], writes=[m2_s])
            P.op("act", lambda e: e.activation(out=rstd_sb[:, 0:Tt], in_=m2[:, 0:Tt], func=AF.Exp, scale=-0.5), reads=[m2_s], writes=[rstd_sbs])
            for c in range(8):
                t, t_s = acc[1 + c % 3]
                P.op("dve", lambda e, c=c, t=t: e.tensor_tensor(out=t[:, 0:Tt], in0=ucT[:, c, 0:Tt], in1=mean_sb[:, 0:Tt], op=ALU.subtract),
                     reads=[(ucT_s, c), mean_s], writes=[t_s])
                P.op("pool", lambda e, t=t: e.tensor_tensor(out=t[:, 0:Tt], in0=t[:, 0:Tt], in1=rstd_sb[:, 0:Tt], op=ALU.mult), reads=[t_s, rstd_sbs], writes=[t_s])
                P.op("act", lambda e, c=c, t=t: e.activation(out=ucT[:, c, 0:Tt], in_=t[:, 0:Tt], func=AF.Silu, scale=cvec[:, c, 1:2], bias=cvec[:, c, 2:3]),
                     reads=[t_s] + CVEC, writes=[(ucT_s, c)])

            if ti == 0:
                tap(('d_' if prompt else 's_') + 'ucT', ucT[:, :, :], [(ucT_s, c) for c in range(8)])
            W, W_s = w_get("CO")
            UCT = [(ucT_s, c) for c in range(8)]
            for c in range(8):
                b = nb()
                for kc in range(8):
                    P.op("pe", lambda e, kc=kc, c=c, b=b, W=W: e.matmul(psb[b][:, 0:Tt], lhsT=W[:, kc, 128 * c:128 * (c + 1)], rhs=ucT[:, kc, 0:Tt], start=(kc == 0), stop=(kc == 7)),
                         reads=UCT + [W_s], writes=[PS(b)])
                P.op("dve", lambda e, c=c, b=b: e.tensor_tensor(out=gated_c[:, c, 0:Tt], in0=psb[b][:, 0:Tt], in1=gated_c[:, c, 0:Tt], op=ALU.mult),
                     reads=[("gc", c)], writes=[PS(b), ("gc", c)])

        if not prompt:
            conv_finish()
        if ti == 0:
            tap(('d_' if prompt else 's_') + 'gc', gated_c[:, :, :], [('gc', c) for c in range(8)])
            tap(('d_' if prompt else 's_') + 'QT', QT[0:80, :, :], [('QT', h) for h in range(NH)] + RQ + ['QTc'])
            tap(('d_' if prompt else 's_') + 'KT', KTs[0:80, :], [('KT', h) for h in range(NH)] + RK + KTC)
        P.end_scope()
        if not prompt:
            P.end_scope("C")
            ar_reset("C")

        ar_reset()
        QTA = [("QT", h) for h in range(NH)] + RQ + ["QTc"]
        rec = [alloc([128, 512], F32) for _ in range(2)]
        PT = [alloc([128, 512], BF16) for _ in range(4)]
        sb_rr = [0]
        pt_rr = [0]
        if prompt:
            kvb = []
            for i in range(NKV):
                ktb, ktb_s = alloc([128, 4, 2, 128], BF16)
                vb, vb_s = alloc([128, 4, 192], BF16)
                kvb.append((ktb, ktb_s, vb, vb_s))
            jobs = [(g, sb) for g in range(8) for sb in range(ti + 1)]
            kvst_ = {"loaded": 0}

            def kv_ensure(n):
                while kvst_["loaded"] <= n and kvst_["loaded"] < len(jobs):
                    m = kvst_["loaded"]
                    g, sb = jobs[m]
                    ktb, ktb_s, vb, vb_s = kvb[m % NKV]
                    P.op("sp", lambda e, g=g, sb=sb, ktb=ktb: e.dma_start(out=ktb[0:80, :, :, :].rearrange("p b h i -> p (b h i)"), in_=kt_scr[g, sb, :, :]),
                         reads=[("ktscr", g, sb)], writes=[ktb_s], dsem=s_kvK[m % NKV])
                    P.op("sp", lambda e, g=g, sb=sb, vb=vb: e.dma_start(out=vb[:, :, :].rearrange("p b r -> p (b r)"), in_=v_scr[g, sb, :, :]),
                         reads=[("vscr", g, sb)], writes=[vb_s], dsem=s_kvV[m % NKV])
                    kvst_["loaded"] += 1

            LA = 2
            steps = []
            for n, (g, sb) in enumerate(jobs):
                ktb, ktb_s, vb, vb_s = kvb[n % NKV]
                ob = [4 + 2 * (g % 2), 5 + 2 * (g % 2)]
                diag = sb == ti
                k_in_job = 0
                for blk in range(4):
                    q0 = 128 * blk if diag else 0
                    for hh in range(2):
                        h = 2 * g + hh
                        sbk = sb_rr[0] % 4
                        sb_rr[0] += 1
                        pt, pt_s = PT[pt_rr[0] % 4]
                        pt_rr[0] += 1
                        first = sb == 0 and blk == 0
                        lastk = diag and blk == 3

                        def front(n=n, k_in_job=k_in_job, blk=blk, hh=hh, h=h, sbk=sbk, q0=q0, ktb=ktb, ktb_s=ktb_s, diag=diag, pt=pt, pt_s=pt_s):
                            if k_in_job == LA:
                                kv_ensure(n + NKV - 1)
                            P.op("pe", lambda e: e.matmul(psb[sbk][:, q0:512], lhsT=ktb[0:80, blk, hh, :], rhs=QT[0:80, h, q0:512], start=True, stop=(not diag)),
                                 reads=[ktb_s] + QTA, writes=[PS(sbk)])
                            if diag:
                                P.op("pe", lambda e: e.matmul(psb[sbk][:, q0:q0 + 128], lhsT=ident_b[:, :], rhs=maskb[:, :], start=False, stop=True),
                                     reads=["ident_b", "maskb"], writes=[PS(sbk)])
                            P.op("act", lambda e: e.activation(out=pt[:, q0:512], in_=psb[sbk][:, q0:512], func=AF.Exp, scale=0.125),
                                 writes=[PS(sbk), pt_s])

                        def back(g=g, blk=blk, hh=hh, q0=q0, pt=pt, pt_s=pt_s, vb=vb, vb_s=vb_s, first=first, lastk=lastk, ob=ob, diag=diag):
                            P.op("pe", lambda e: e.matmul(psb[ob[hh]][:, q0:512], lhsT=vb[:, blk, 64 * hh:64 * hh + 128], rhs=pt[:, q0:512], start=first, stop=lastk),
                                 reads=[vb_s, pt_s], writes=[PS(ob[hh])])
                            if diag and blk == 3 and hh == 1:
                                rc, rc_s = rec[g % 2]
                                P.op("dve", lambda e: e.reciprocal(out=rc[0:64, :], in_=psb[ob[0]][64:128, :]), writes=[PS(ob[0]), (rc_s, 0)])
                                P.op("dve", lambda e: e.tensor_tensor(out=oT[0:64, g, :], in0=psb[ob[0]][0:64, :], in1=rc[0:64, :], op=ALU.mult),
                                     reads=[(rc_s, 0)], writes=[PS(ob[0]), ("oT", g, 0)])
                                P.op("dve", lambda e: e.reciprocal(out=rc[64:128, :], in_=psb[ob[1]][0:64, :]), writes=[PS(ob[1]), (rc_s, 1)])
                                P.op("dve", lambda e: e.tensor_tensor(out=oT[64:128, g, :], in0=psb[ob[1]][64:128, :], in1=rc[64:128, :], op=ALU.mult),
                                     reads=[(rc_s, 1)], writes=[PS(ob[1]), ("oT", g, 1)])
                        steps.append((front, back))
                        k_in_job += 1
            kv_ensure(NKV - 2)
            stride = max(1, min(len(steps) // 8, 40))
            for i in range(len(steps) + LA):
                if i < len(steps):
                    steps[i][0]()
                if i >= LA:
                    steps[i - LA][1]()
                if chains_pending and i % stride == stride - 1:
                    chain(chains_pending.pop(0))
            while chains_pending:
                chain(chains_pending.pop(0))
        else:
            sample_attention(QTA, RK, KTv, Vn, Vn_s, rec, PT)
        P.end_scope()
        if not prompt:
            P.end_scope("C")
            ar_reset("C")

        ar_reset()
        OT = [("oT", g, j) for g in range(8) for j in range(2)]
        gT, gT_s = ucT, ucT_s
        hT, hT_s = alloc([128, NFC, 512], BF16)
        lnt = []
        for _ in range(2):
            st, st_s = alloc([128, 2, 6], F32); mv, mv_s = alloc([128, 2], F32)
            lnv, lnv_s = alloc([128, 1], F32); rstd, rstd_s = alloc([128, 1], F32)
            lnt.append((st, st_s, mv, mv_s, lnv, lnv_s, rstd, rstd_s))
        xb2 = [alloc([128, 1024], BF16) for _ in range(2)]
        tmpf = [alloc([128, 512], F32) for _ in range(4)]
        aext = [alloc([128, nseg, 2 + L], F32) for _ in range(2)]
        pb, pb_s = alloc([128, 4, 256], BF16)
        pT, pT_s = alloc([128, 2, 512], BF16)

        if prompt:
            conv_finish()
        if ti == 0:
            tap(('d_' if prompt else 's_') + 'oT', oT[:, :, :], OT)
        W, W_s = w_get("GA")
        for c in range(8):
            b = nb()
            for kc in range(8):
                P.op("pe", lambda e, kc=kc, c=c, b=b, W=W: e.matmul(psb[b][:, 0:Tt], lhsT=W[:, kc, 128 * c:128 * (c + 1)], rhs=xT[:, kc, 0:Tt], start=(kc == 0), stop=(kc == 7)),
                     reads=XT + [W_s], writes=[PS(b)])
            P.op("act", lambda e, c=c, b=b: e.activation(out=gT[:, c, 0:Tt], in_=psb[b][:, 0:Tt], func=AF.Sigmoid), writes=[PS(b), (gT_s, c)])
        W, W_s = w_get("AO")
        for c in range(8):
            b = nb()
            for kc in range(8):
                P.op("pe", lambda e, kc=kc, c=c, b=b, W=W: e.matmul(psb[b][:, 0:Tt], lhsT=W[:, kc, 128 * c:128 * (c + 1)], rhs=oT[:, kc, 0:Tt], start=(kc == 0), stop=(kc == 7)),
                     reads=OT + [W_s], writes=[PS(b)])
            P.op("dve", lambda e, c=c, b=b: e.tensor_tensor(out=gT[:, c, 0:Tt], in0=psb[b][:, 0:Tt], in1=gT[:, c, 0:Tt], op=ALU.mult),
                 reads=[(gT_s, c)], writes=[PS(b), (gT_s, c)])
            P.op("pool", lambda e, c=c: e.tensor_tensor(out=gT[:, c, 0:Tt], in0=gT[:, c, 0:Tt], in1=gated_c[:, c, 0:Tt], op=ALU.add),
                 reads=[(gT_s, c), ("gc", c)], writes=[(gT_s, c)])
        GT = [(gT_s, c) for c in range(8)]
        if ti == 0:
            tap(('d_' if prompt else 's_') + 'gT', gT[:, :, :], GT)
        W, W_s = w_get("O")
        load_gb(ln1_g, ln1_b)
        for sub in range(nsub):
            for half in range(2):
                b = nb()
                for kc in range(8):
                    P.op("pe", lambda e, kc=kc, sub=sub, half=half, b=b, W=W: e.matmul(psb[b][:, :], lhsT=gT[:, kc, 128 * sub:128 * (sub + 1)], rhs=W[:, kc, 512 * half:512 * (half + 1)],
                                                                                      start=(kc == 0), stop=(kc == 7)), reads=GT + [W_s], writes=[PS(b)])
                P.op("dve", lambda e, sub=sub, half=half, b=b: e.scalar_tensor_tensor(out=xres[:, sub, 512 * half:512 * (half + 1)], in0=xres[:, sub, 512 * half:512 * (half + 1)],
                                                                                   scalar=ALPHA, in1=psb[b][:, :], op0=ALU.mult, op1=ALU.add),
                     reads=[("xres", sub)], writes=[PS(b), ("xres", sub)])
            ln_rows(xres[:, sub, :], [("xres", sub)], sub, lnt[sub % 2])
            to_featmajor(sub, *xb2[sub % 2])

        if ti == 0:
            tap(('d_' if prompt else 's_') + 'x1', xres[:, :, :], [('xres', q_) for q_ in range(4)])
        P.op("pool", lambda e: e.dma_start(out=pb[:, 0:nsub, :], in_=p_src.rearrange("(s p) d -> p s d", p=128)), writes=[pb_s], dsem=s_p)
        for sub in range(nsub):
            b = nb()
            psv = psb[b][:].bitcast(BF16)
            for k2 in range(2):
                P.op("pe", lambda e, sub=sub, k2=k2, psv=psv: e.transpose(out=psv[:, 128 * k2:128 * (k2 + 1)], in_=pb[:, sub, 128 * k2:128 * (k2 + 1)], identity=ident_b[:, :]),
                     reads=[pb_s, "ident_b"], writes=[PS(b)])
            P.op("act", lambda e, sub=sub, psv=psv, b=b: e.activation(out=pT[:, :, 128 * sub:128 * (sub + 1)], in_=psv[:, 0:256].rearrange("p (c t) -> p c t", c=2), func=AF.Identity),
                 writes=[PS(b), (pT_s, sub)])
        PTS = [(pT_s, s) for s in range(nsub)]
        W, W_s = w_get("PG")
        wple, wple_s = w_get("PLE", ahead=0)
        for sub in range(nsub):
            for half in range(2):
                bg_ = nb(); bp = nb()
                for kc in range(8):
                    P.op("pe", lambda e, kc=kc, sub=sub, half=half, bg_=bg_, W=W: e.matmul(psb[bg_][:, :], lhsT=xT[:, kc, 128 * sub:128 * (sub + 1)], rhs=W[:, kc, 512 * half:512 * (half + 1)],
                                                                                          start=(kc == 0), stop=(kc == 7)), reads=XT + [W_s], writes=[PS(bg_)])
                for k2 in range(2):
                    P.op("pe", lambda e, k2=k2, sub=sub, half=half, bp=bp, wple=wple: e.matmul(psb[bp][:, :], lhsT=pT[:, k2, 128 * sub:128 * (sub + 1)], rhs=wple[:, k2, 512 * half:512 * (half + 1)],
                                                                                   start=(k2 == 0), stop=(k2 == 1)), reads=PTS + [wple_s], writes=[PS(bp)])
                tg, tg_s = tmpf[(2 * sub + half) % 2]
                P.op("act", lambda e, bg_=bg_, tg=tg: e.activation(out=tg[:, :], in_=psb[bg_][:, :], func=AF.Sigmoid), writes=[PS(bg_), tg_s])
                P.op("dve", lambda e, bp=bp, tg=tg: e.tensor_tensor(out=tg[:, :], in0=psb[bp][:, :], in1=tg[:, :], op=ALU.mult), reads=[tg_s], writes=[PS(bp), tg_s])
                P.op("dve", lambda e, sub=sub, half=half, tg=tg: e.scalar_tensor_tensor(out=xres[:, sub, 512 * half:512 * (half + 1)], in0=xres[:, sub, 512 * half:512 * (half + 1)],
                                                                                     scalar=ALPHA, in1=tg[:, :], op0=ALU.mult, op1=ALU.add),
                     reads=[("xres", sub), tg_s], writes=[("xres", sub)])

        if not prompt:
            ar["C"] = 2048
            sf, sf_s = alloc([8, DFF], F32, 8, kind="C")
            P.op("sp", lambda e: e.dma_start(out=sf[:, :], in_=st_ffn.rearrange("s r d -> (s r) d")), writes=[sf_s], dsem=s_st[1])
            for c in range(NFC):
                b = nb()
                P.op("pe", lambda e, c=c, b=b: e.transpose(out=psb[b][:, 0:8], in_=sf[0:8, 128 * c:128 * (c + 1)], identity=ident_f[0:8, 0:8]),
                     reads=[sf_s, "ident_f"], writes=[PS(b)])
                P.op("dve", lambda e, c=c, b=b: e.tensor_copy(out=ahist[:, c, :, :], in_=psb[b][:, 0:8].rearrange("p (s r) -> p s r", s=4)), writes=[PS(b), ("ahist", c)])
        for g in range(6):
            W, W_s = w_get(f"UP{g}")
            hw = 512 if g < 5 else 256
            for cc in range(hw // 128):
                c = 4 * g + cc
                ba = nb(); bb = nb()
                for kc in range(8):
                    P.op("pe", lambda e, kc=kc, cc=cc, ba=ba, W=W: e.matmul(psb[ba][:, 0:Tt], lhsT=W[:, kc, 128 * cc:128 * (cc + 1)], rhs=xT[:, kc, 0:Tt], start=(kc == 0), stop=(kc == 7)),
                         reads=XT + [W_s], writes=[PS(ba)])
                for kc in range(8):
                    P.op("pe", lambda e, kc=kc, cc=cc, bb=bb, W=W, hw=hw: e.matmul(psb[bb][:, 0:Tt], lhsT=W[:, kc, hw + 128 * cc:hw + 128 * (cc + 1)], rhs=xT[:, kc, 0:Tt], start=(kc == 0), stop=(kc == 7)),
                         reads=XT + [W_s], writes=[PS(bb)])
                ae, ae_s = aext[c % 2]
                ac, ac_s = tmpf[2 + c % 2]
                P.op("act", lambda e, ba=ba, ae=ae: e.activation(out=ae[:, :, 2:2 + L], in_=seg3(psb[ba][:, 0:Tt]), func=AF.Identity), writes=[PS(ba), ae_s])
                P.op("pool", lambda e, c=c, ae=ae: e.tensor_copy(out=ae[:, :, 0:2], in_=ahist[:, c, 0:nseg, :]), reads=[("ahist", c)], writes=[(ae_s, "h")])
                if prompt:
                    P.op("pool", lambda e, c=c, ae=ae: e.tensor_copy(out=ahist[:, c, 0, :], in_=ae[:, 0, L:L + 2]), reads=[ae_s, (ae_s, "h")], writes=[("ahist", c)])
                AE = [ae_s, (ae_s, "h")]
                P.op("act", lambda e, c=c, ae=ae, ac=ac: e.activation(out=seg3(ac[:, 0:Tt]), in_=ae[:, :, 0:L], func=AF.Identity, scale=fvec[:, c, 0:1], bias=fvec[:, c, 3:4]),
                     reads=AE + FVEC, writes=[ac_s])
                for k in (1, 2):
                    P.op("dve", lambda e, c=c, k=k, ae=ae, ac=ac: e.scalar_tensor_tensor(out=seg3(ac[:, 0:Tt]), in0=ae[:, :, k:k + L], scalar=fvec[:, c, k:k + 1],
                                                                                      in1=seg3(ac[:, 0:Tt]), op0=ALU.mult, op1=ALU.add),
                         reads=AE + FVEC + [ac_s], writes=[ac_s])
                P.op("act", lambda e, ac=ac: e.activation(out=ac[:, 0:Tt], in_=ac[:, 0:Tt], func=AF.Silu), reads=[ac_s], writes=[ac_s])
                P.op("dve", lambda e, c=c, bb=bb, ac=ac: e.tensor_tensor(out=hT[:, c, 0:Tt], in0=psb[bb][:, 0:Tt], in1=ac[:, 0:Tt], op=ALU.mult),
                     reads=[ac_s], writes=[PS(bb), (hT_s, c)])
            if need_state:
                for sub in state_subs:
                    b = nb()
                    for kc in range(8):
                        P.op("pe", lambda e, kc=kc, sub=sub, b=b, W=W, hw=hw: e.matmul(psb[b][:, 0:hw], lhsT=xT[:, kc, 128 * sub:128 * (sub + 1)], rhs=W[:, kc, 0:hw], start=(kc == 0), stop=(kc == 7)),
                             reads=XT + [W_s], writes=[PS(b)])
                    sg_, sg_s = tmpf[sub % 2]
                    P.op("act", lambda e, b=b, sg_=sg_, hw=hw: e.activation(out=sg_[:, 0:hw], in_=psb[b][:, 0:hw], func=AF.Identity), writes=[PS(b), sg_s])
                    if prompt:
                        P.op("pool", lambda e, sg_=sg_, g=g, hw=hw: e.dma_start(out=ff_p[:, 512 * g:512 * g + hw], in_=sg_[126:128, 0:hw]), reads=[sg_s], dsem=s_st[sub % 2])
                    else:
                        for j in range(2):
                            P.op("pool", lambda e, sg_=sg_, g=g, hw=hw, j=j, sub=sub: e.dma_start(out=ff_s[2 * sub + j, :, 512 * g:512 * g + hw], in_=sg_[64 * j + 62:64 * j + 64, 0:hw]),
                                 reads=[sg_s], dsem=s_st[sub % 2])
        HT = [(hT_s, c) for c in range(NFC)]
        if ti == 0:
            tap(('d_' if prompt else 's_') + 'hT', hT[:, :, :], HT)
        load_gb(ln2_g, ln2_b)
        for n in range(2):
            banks = [nb() for _ in range(nsub)]
            for kh in range(2):
                W, W_s = w_get(f"DN{kh}{n}")
                for sub in range(nsub):
                    for j in range(11):
                        P.op("pe", lambda e, j=j, kh=kh, sub=sub, W=W, b=banks[sub]: e.matmul(psb[b][:, :], lhsT=hT[:, 11 * kh + j, 128 * sub:128 * (sub + 1)], rhs=W[:, j, :],
                                                                                             start=(kh == 0 and j == 0), stop=(kh == 1 and j == 10)),
                             reads=HT + [W_s], writes=[PS(banks[sub])])
            for sub in range(nsub):
                P.op("dve", lambda e, sub=sub, n=n, b=banks[sub]: e.tensor_tensor(out=xres[:, sub, 512 * n:512 * (n + 1)], in0=psb[b][:, :], in1=xres[:, sub, 512 * n:512 * (n + 1)], op=ALU.add),
                     reads=[("xres", sub)], writes=[PS(banks[sub]), ("xres", sub)])
        if ti == 0:
            tap(('d_' if prompt else 's_') + 'r2', xres[:, :, :], [('xres', q_) for q_ in range(4)])
        for sub in range(nsub):
            ln_rows(xres[:, sub, :], [("xres", sub)], sub, lnt[sub % 2])
            P.op("sp", lambda e, sub=sub: e.dma_start(out=y_dst[128 * sub:128 * (sub + 1), :], in_=xres[:, sub, :]), reads=[("xres", sub)], dsem=s_y[sub])
        P.end_scope()
        P.end_scope("C")
        ar_reset("C")

    s_ck = [newsem(f"ck{i}") for i in range(2)]
    s_cv = [newsem(f"cv{i}") for i in range(2)]
    s_clf = newsem("clf")
    s_rkh = [newsem(f"rkh{i}") for i in range(2)]
    s_ktc = newsem("ktc")
    s_cks = newsem("cks")
    ck_scr = nc.dram_tensor("ck_scr", [NSTR, NH, 3, PAST], BF16).ap()
    hcar = sbt("hcar", [16, 1], F32)

    def hist_bufs():
        lfh, lfh_s = alloc([128, 16, 16], F32)
        lfT, lfT_s = alloc([16, 2048], F32, 16)
        cH, cH_s = alloc([16, 2048], F32, 16)
        SKh, SKh_s = alloc([16, 3, 2048], BF16, 16)
        return lfh, lfh_s, lfT, lfT_s, cH, cH_s, SKh, SKh_s

    def hist_half(s, hf, bufs, want_split):
        lfh, lfh_s, lfT, lfT_s, cH, cH_s, SKh, SKh_s = bufs
        r0 = 2048 * hf
        if hf == 0:
            P.op("dve", lambda e: e.memset(hcar[:, :], 0.0), writes=["hcar"])
        P.op("sp", lambda e: e.dma_start(out=lfh[:, :, :], in_=cache_lf[s, r0:r0 + 2048, :].rearrange("(b p) h -> p b h", p=128)), writes=[lfh_s], dsem=s_clf)
        for q in range(4):
            b = nb6s()
            for j in range(4):
                blk = 4 * q + j
                P.op("pe", lambda e, blk=blk, j=j, b=b: e.transpose(out=psb[b][0:16, 128 * j:128 * (j + 1)], in_=lfh[:, blk, :], identity=ident_f[:, :]),
                     reads=[lfh_s, "ident_f"], writes=[PS(b)])
            P.op("act", lambda e, q=q, b=b: e.activation(out=lfT[:, 512 * q:512 * (q + 1)], in_=psb[b][0:16, :], func=AF.Identity), writes=[PS(b), (lfT_s, q)])
        for q in range(4):
            ini = hcar[:, 0:1] if q == 0 else cH[:, 512 * q - 1:512 * q]
            rd = ["hcar"] if q == 0 else [(cH_s, q - 1)]
            P.op("dve", lambda e, q=q, ini=ini: e.tensor_tensor_scan(out=cH[:, 512 * q:512 * (q + 1)], data0=ones16[:, 0:512], data1=lfT[:, 512 * q:512 * (q + 1)],
                                                                   initial=ini, op0=ALU.mult, op1=ALU.add), reads=[(lfT_s, q), "ones16"] + rd, writes=[(cH_s, q)])
        CH = [(cH_s, q) for q in range(4)]
        LT = [(lfT_s, q) for q in range(4)]
        P.op("dve", lambda e: e.tensor_copy(out=hcar[:, 0:1], in_=cH[:, 2047:2048]), reads=CH, writes=["hcar"])
        if want_split:
            P.op("dve", lambda e: e.tensor_scalar(out=SKh[:, 0, :], in0=cH[:, :], scalar1=-8.0, scalar2=None, op0=ALU.mult), reads=CH, writes=[(SKh_s, 0)])
            P.op("dve", lambda e: e.scalar_tensor_tensor(out=lfT[:, :], in0=cH[:, :], scalar=-8.0, in1=SKh[:, 0, :], op0=ALU.mult, op1=ALU.subtract),
                 reads=CH + [(SKh_s, 0)], writes=LT)
            P.op("dve", lambda e: e.tensor_copy(out=SKh[:, 1, :], in_=lfT[:, :]), reads=LT, writes=[(SKh_s, 1)])
            P.op("dve", lambda e: e.tensor_tensor(out=cH[:, :], in0=lfT[:, :], in1=SKh[:, 1, :], op=ALU.subtract), reads=LT + [(SKh_s, 1)], writes=CH)
            P.op("dve", lambda e: e.tensor_copy(out=SKh[:, 2, :], in_=cH[:, :]), reads=CH, writes=[(SKh_s, 2)])
            P.op("sp", lambda e: e.dma_start(out=ck_scr[s, :, :, r0:r0 + 2048], in_=SKh[:, :, :]), reads=[(SKh_s, j) for j in range(3)],
                 writes=[("ckscr", s, hf)], dsem=s_cks)

    def sample_prepass():
        ar_reset()
        bufs = hist_bufs()
        for s in range(NSTR):
            for hf in range(2):
                hist_half(s, hf, bufs, False)
            P.op("dve", lambda e, s=s: e.tensor_copy(out=hend[:, s:s + 1], in_=hcar[:, 0:1]), reads=["hcar"], writes=["hend"])
        P.end_scope()

    def sample_attention(QTA, RK, KTv, Vn, Vn_s, rec, PT):
        bufs = hist_bufs()
        kc_ = [alloc([128, 2, 1024], BF16, kind="C") for _ in range(2)]
        vc_ = [alloc([128, 2, 1024], BF16) for _ in range(2)]
        ktc = [alloc([128, 2, 16, 128], BF16, kind="C") for _ in range(2)]
        for i in range(2):
            kt, kt_s = ktc[i]
            P.op("pool", lambda e, kt=kt: e.memset(kt[64:96, :, :, :], 0.0), writes=[(kt_s, "c0")])
            for g in range(4):
                P.op("sp", lambda e, kt=kt, g=g: e.dma_start(out=kt[67:70, :, :, :].rearrange("p b h i -> p (b h i)")[:, 1024 * g:1024 * (g + 1)], in_=ones3[:, :]),
                     reads=["ones3", (kt_s, "c0")], writes=[(kt_s, "c1", g)], dsem=s_ktc)
        KTNEW = [("KT", h) for h in range(NH)] + RK + KTC
        OB = [4, 5]
        for s in range(NSTR):
            for hf in range(2):
                hist_half(s, hf, bufs, True)
            qs = slice(64 * s, 64 * (s + 1))
            pp = 64 * (s % 2)
            for grp in range(17):
                new = grp == 16
                nblk = 1 if new else 2
                if not new:
                    kcb, kcb_s = kc_[grp % 2]
                    vcb, vcb_s = vc_[grp % 2]
                    kt, kt_s = ktc[grp % 2]
                    r0 = 256 * grp
                    P.op("pool", lambda e, kcb=kcb, r0=r0, s=s: e.dma_start(out=kcb[:, :, :], in_=cache_k[s, r0:r0 + 256, :].rearrange("(b p) d -> p b d", p=128)),
                         writes=[kcb_s], dsem=s_ck[grp % 2])
                    P.op("pool", lambda e, vcb=vcb, r0=r0, s=s: e.dma_start(out=vcb[:, :, :], in_=cache_v[s, r0:r0 + 256, :].rearrange("(b p) d -> p b d", p=128)),
                         writes=[vcb_s], dsem=s_cv[grp % 2])
                    RKH = [(kt_s, "rk", h) for h in range(NH)]
                    for h in range(NH):
                        P.op("sp", lambda e, h=h, kt=kt, r0=r0, s=s: e.dma_start(out=kt[64:67, :, h, :], in_=ck_scr[s, h, :, r0:r0 + 256]),
                             reads=[("ckscr", s, r0 // 2048), (kt_s, "c0")], writes=[RKH[h]], dsem=s_rkh[grp % 2])
                    for c in range(8):
                        b = nb6s()
                        psv = psb[b][:].bitcast(BF16)
                        for j in range(2):
                            P.op("pe", lambda e, c=c, j=j, psv=psv, kcb=kcb: e.transpose(out=psv[:, 128 * j:128 * (j + 1)], in_=kcb[:, j, 128 * c:128 * (c + 1)], identity=ident_b[:, :]),
                                 reads=[kcb_s, "ident_b"], writes=[PS(b)])
                        P.op("act", lambda e, c=c, psv=psv, kt=kt: e.activation(out=kt[0:64, :, 2 * c, :], in_=psv[0:64, 0:256].rearrange("p (b i) -> p b i", b=2), func=AF.Identity),
                             writes=[PS(b), (kt_s, 2 * c)])
                        P.op("dve", lambda e, c=c, psv=psv, kt=kt: e.tensor_copy(out=kt[0:64, :, 2 * c + 1, :], in_=psv[64:128, 0:256].rearrange("p (b i) -> p b i", b=2)),
                             writes=[PS(b), (kt_s, 2 * c + 1)])
                    KTG = [(kt_s, h) for h in range(NH)] + RKH + [(kt_s, "c0")] + [(kt_s, "c1", g) for g in range(4)]
                    if s == 0 and grp == 0:
                        tap('s_kt0', kt[0:80, :, :, :], KTG)
                        tap('s_vc0', vcb[:, :, :], [vcb_s])
                for blk in range(nblk):
                    pts = []
                    for hb in range(2):
                        b = nb6s()
                        pt, pt_s = PT[(2 * blk + hb) % 4]
                        pts.append((pt, pt_s))
                        for hh in range(8):
                            h = 8 * hb + hh
                            if new:
                                P.op("pe", lambda e, h=h, hh=hh, b=b, pp=pp, qs=qs: e.matmul(psb[b][pp:pp + 64, 64 * hh:64 * (hh + 1)], lhsT=KTv[0:80, h, qs], rhs=QT[0:80, h, qs],
                                                                                            start=True, stop=False, skip_group_check=True), reads=KTNEW + QTA, writes=[PS(b)])
                                P.op("pe", lambda e, hh=hh, b=b, pp=pp: e.matmul(psb[b][pp:pp + 64, 64 * hh:64 * (hh + 1)], lhsT=ident_b[:, 0:64], rhs=maskb[:, 0:64],
                                                                                start=False, stop=True, skip_group_check=True), reads=["ident_b", "maskb"], writes=[PS(b)])
                            else:
                                P.op("pe", lambda e, h=h, hh=hh, b=b, blk=blk, kt=kt, qs=qs: e.matmul(psb[b][:, 64 * hh:64 * (hh + 1)], lhsT=kt[0:80, blk, h, :], rhs=QT[0:80, h, qs],
                                                                                                     start=True, stop=True, skip_group_check=True), reads=KTG + QTA, writes=[PS(b)])
                        if new:
                            P.op("pool", lambda e, pt=pt: e.memset(pt[:, :], 0.0), writes=[pt_s])
                            P.op("act", lambda e, b=b, pt=pt, pp=pp: e.activation(out=pt[pp:pp + 64, :], in_=psb[b][pp:pp + 64, :], func=AF.Exp, scale=0.125), writes=[PS(b), pt_s])
                        else:
                            P.op("act", lambda e, b=b, pt=pt: e.activation(out=pt[:, :], in_=psb[b][:, :], func=AF.Exp, scale=0.125), writes=[PS(b), pt_s])
                    if s == 0 and grp == 0 and blk == 0:
                        tap('s_pt0', pts[0][0][:, :], [pts[0][1]])
                    for hb in range(2):
                        pt, pt_s = pts[hb]
                        first = grp == 0 and blk == 0
                        if first:
                            P.op("pe", lambda e, hb=hb, pt=pt: e.matmul(psb[OB[hb]][:, :], lhsT=zerob[:, :], rhs=pt[:, :], start=True, stop=False, skip_group_check=True),
                                 reads=["zerob", pt_s], writes=[PS(OB[hb])])
                        for hh in range(8):
                            h = 8 * hb + hh
                            if new:
                                P.op("pe", lambda e, h=h, hh=hh, hb=hb, pt=pt, pp=pp, s=s: e.matmul(psb[OB[hb]][0:64, 64 * hh:64 * (hh + 1)], lhsT=Vn[:, s // 2, 64 * h:64 * (h + 1)],
                                                                                                   rhs=pt[:, 64 * hh:64 * (hh + 1)], start=False, stop=True, skip_group_check=True),
                                     reads=[(Vn_s, s // 2), pt_s], writes=[PS(OB[hb])])
                            else:
                                P.op("pe", lambda e, h=h, hh=hh, hb=hb, pt=pt, blk=blk, vcb=vcb, first=first: e.matmul(
                                    psb[OB[hb]][0:64, 64 * hh:64 * (hh + 1)], lhsT=vcb[:, blk, 64 * h:64 * (h + 1)], rhs=pt[:, 64 * hh:64 * (hh + 1)],
                                    start=False, stop=False, skip_group_check=True), reads=[vcb_s, pt_s], writes=[PS(OB[hb])])
                        if new:
                            P.op("pe", lambda e, hb=hb, pt=pt, pp=pp: e.matmul(psb[OB[hb]][64:128, :], lhsT=onesb[:, :], rhs=pt[:, :], start=False, stop=True, skip_group_check=True),
                                 reads=["onesb", pt_s], writes=[PS(OB[hb])])
                        else:
                            P.op("pe", lambda e, hb=hb, pt=pt, first=first: e.matmul(psb[OB[hb]][64:128, :], lhsT=onesb[:, :], rhs=pt[:, :], start=False, stop=False, skip_group_check=True),
                                 reads=["onesb", pt_s], writes=[PS(OB[hb])])
            if debug and s == 0:
                dbgn, dbgn_s = alloc([128, 512], F32)
                P.op("act", lambda e, dbgn=dbgn: e.activation(out=dbgn[:, :], in_=psb[OB[0]][:, :], func=AF.Identity), writes=[PS(OB[0]), dbgn_s])
                tap('s_num0', dbgn[:, :], [dbgn_s])
            for hb in range(2):
                rc, rc_s = rec[hb]
                P.op("dve", lambda e, rc=rc, hb=hb: e.reciprocal(out=rc[0:64, :], in_=psb[OB[hb]][64:128, :]), writes=[PS(OB[hb]), rc_s])
                if s == 0 and hb == 0:
                    tap('s_rc0', rc[0:64, :], [rc_s])
                for par in range(2):
                    P.op("dve", lambda e, rc=rc, hb=hb, par=par, qs=qs: e.tensor_tensor(
                        out=oT[64 * par:64 * par + 64, 4 * hb:4 * hb + 4, qs],
                        in0=psb[OB[hb]][0:64, :].rearrange("p (c r q) -> p c r q", c=4, r=2)[:, :, par, :],
                        in1=rc[0:64, :].rearrange("p (c r q) -> p c r q", c=4, r=2)[:, :, par, :], op=ALU.mult),
                        reads=[rc_s], writes=[PS(OB[hb])] + [("oT", 4 * hb + c, par) for c in range(4)])

    rr6 = [0]

    def nb6s():
        b = rr6[0] % 4
        rr6[0] += 1
        return b

    P.end_scope()
    for ti in range(ntiles):
        run_tile("p", ti)
    if do_sample:
        sample_prepass()
        run_tile("s", 0)
    assert wst["pos"] == len(wseq)
    P.emit(final_sems=allsems)
    return nc, P


_CACHE = {}


def _f32(a):
    return np.ascontiguousarray(np.asarray(a, dtype=np.float32))


def kernel(x_prompt, x_sample, cache_k, cache_v, cache_logf, state_conv, state_ffn_conv,
           p_prompt, p_sample, ln0_g, ln0_b, w_in, b_f, conv_dw_w, conv_dw_b, conv_ln_g,
           conv_ln_b, w_conv_out, w_attn_out, w_o, ln1_g, ln1_b, w_ffn_up, ffn_dw_w, ffn_dw_b,
           w_ffn_down, ln2_g, ln2_b, w_ple, w_ple_gate):
    if "nc" not in _CACHE:
        _CACHE["nc"] = build_nc()[0]
    nc = _CACHE["nc"]
    shared = {
        "ln0_g": _f32(ln0_g), "ln0_b": _f32(ln0_b), "w_in": _f32(w_in)[0], "b_f": _f32(b_f)[0],
        "conv_dw_w": _f32(conv_dw_w)[0], "conv_dw_b": _f32(conv_dw_b)[0], "conv_ln_g": _f32(conv_ln_g)[0],
        "conv_ln_b": _f32(conv_ln_b)[0], "w_conv_out": _f32(w_conv_out)[0], "w_attn_out": _f32(w_attn_out)[0],
        "w_o": _f32(w_o)[0], "ln1_g": _f32(ln1_g)[0], "ln1_b": _f32(ln1_b)[0], "w_ffn_up": _f32(w_ffn_up)[0],
        "ffn_dw_w": _f32(ffn_dw_w)[0], "ffn_dw_b": _f32(ffn_dw_b)[0], "w_ffn_down": _f32(w_ffn_down)[0],
        "ln2_g": _f32(ln2_g)[0], "ln2_b": _f32(ln2_b)[0], "w_ple": _f32(w_ple)[0], "w_ple_gate": _f32(w_ple_gate)[0],
    }
    x_prompt = _f32(x_prompt); p_prompt = _f32(p_prompt)[0]
    x_sample = _f32(x_sample); p_sample = _f32(p_sample)[0]
    ck = _f32(cache_k)[0].reshape(32, PAST, D); cv = _f32(cache_v)[0].reshape(32, PAST, D)
    clf = _f32(cache_logf)[0]; sc = _f32(state_conv)[0]; sf = _f32(state_ffn_conv)[0]
    in_maps = []
    for c in range(NCORES):
        m = dict(shared)
        s0 = NSTR * c
        m.update({
            "x_p": x_prompt[c], "p_p": p_prompt[c],
            "x_s": x_sample[s0:s0 + NSTR].reshape(NSTR * DSEQ, D), "p_s": p_sample[s0:s0 + NSTR].reshape(NSTR * DSEQ, PLE),
            "cache_k": ck[s0:s0 + NSTR], "cache_v": cv[s0:s0 + NSTR], "cache_lf": clf[s0:s0 + NSTR],
            "st_conv": sc[s0:s0 + NSTR], "st_ffn": sf[s0:s0 + NSTR],
        })
        in_maps.append(m)
    res = run_bass_kernel_spmd(nc, in_maps, core_ids=list(range(NCORES)))
    R = res.results

    def cat(name, shape):
        return np.stack([np.asarray(r[name], dtype=np.float32) for r in R], 0).reshape(shape)

    y_p = cat("y_p", (8, SEQ, D))
    y_s = cat("y_s", (32, DSEQ, D))
    k_p = cat("k_p", (1, 8, SEQ, NH, 64)); v_p = cat("v_p", (1, 8, SEQ, NH, 64))
    lf_p = cat("lf_p", (1, 8, SEQ, NH))
    cv_p = cat("cv_p", (1, 8, 30, D)); ff_p = cat("ff_p", (1, 8, 2, DFF))
    k_s = cat("k_s", (1, 32, DSEQ, NH, 64)); v_s = cat("v_s", (1, 32, DSEQ, NH, 64))
    lf_s = cat("lf_s", (1, 32, DSEQ, NH))
    cv_s = cat("cv_s", (1, 32, 30, D)); ff_s = cat("ff_s", (1, 32, 2, DFF))
    return (y_p, y_s, k_p, v_p, lf_p, cv_p, ff_p, k_s, v_s, lf_s, cv_s, ff_s)
```

```python
import numpy as np
import concourse.bass as bass
import concourse.mybir as mybir
from concourse.bass_utils import run_bass_kernel_spmd

F32 = mybir.dt.float32
BF16 = mybir.dt.bfloat16
AF = mybir.ActivationFunctionType
ALU = mybir.AluOpType

ENGS = ("pe", "act", "dve", "pool", "sp")

NCORES = 8
D = 1024
NH = 16
DFF = 2816
NFC = 22
PLE = 256
SEQ = 8192
PAST = 4096
DSEQ = 64
NSTR = 4
NTILES = 16
N_IN = 7184
ALPHA = float(2.0 ** 0.25)
EPS = 1e-5
NEG = -30000.0
NWB = 2
NKV = 3


class DmaSem:
    def __init__(self, handle):
        self.h = handle
        self.count = 0


class Op:
    __slots__ = ("eng", "fn", "deps", "need_inc", "seq", "idx", "dsem", "dval", "is_dma")


class Prog:
    def __init__(self, nc):
        self.nc = nc
        self.streams = {e: [] for e in ENGS}
        self.last_w = {}
        self.readers = {}
        self.scoped = set()
        self.inherit = {}
        self.touched = set()

    @staticmethod
    def _root(k):
        while isinstance(k, tuple):
            k = k[0]
        return k

    def end_scope(self, kind="A"):
        last = {}
        dmas = {}
        keys = [k for k in list(self.last_w.keys()) + list(self.readers.keys()) if self._root(k) == kind]
        ops = []
        for k in set(keys):
            w = self.last_w.pop(k, None)
            if w is not None:
                ops.append(w)
            ops.extend(self.readers.pop(k, ()))
        ops.extend(self.inherit.get(kind, []))
        for o in ops:
            if o.is_dma:
                key = id(o.dsem)
                if key not in dmas or dmas[key].dval < o.dval:
                    dmas[key] = o
            else:
                if o.eng not in last or last[o.eng].idx < o.idx:
                    last[o.eng] = o
        self.inherit[kind] = list(last.values()) + list(dmas.values())
        self.touched = {t for t in self.touched if self._root(t) != kind}

    def op(self, eng, fn, reads=(), writes=(), dsem=None):
        o = Op()
        o.eng = eng
        o.fn = fn
        o.need_inc = False
        o.seq = None
        o.is_dma = dsem is not None
        o.dsem = dsem
        if dsem is not None:
            dsem.count += 16
            o.dval = dsem.count
        else:
            o.dval = None
        deps = {}
        for s in reads:
            w = self.last_w.get(s)
            if w is not None:
                deps[id(w)] = w
        for s in writes:
            w = self.last_w.get(s)
            if w is not None and (w.is_dma or w.eng != eng or o.is_dma):
                deps[id(w)] = w
            for r in self.readers.get(s, ()):
                if r.is_dma or r.eng != eng or o.is_dma:
                    deps[id(r)] = r
            if self._root(s) in self.scoped and s not in self.touched:
                self.touched.add(s)
                for r in self.inherit.get(self._root(s), []):
                    if r.is_dma or r.eng != eng or o.is_dma:
                        deps[id(r)] = r
        o.deps = []
        for d in deps.values():
            if d.is_dma:
                v = d.dsem.count - (16 if d.dsem is dsem else 0)
                o.deps.append((d, v))
            elif not (d.eng == "pe" and eng == "pe" and not o.is_dma):
                d.need_inc = True
                o.deps.append((d, None))
        for s in reads:
            self.readers.setdefault(s, []).append(o)
        for s in writes:
            self.last_w[s] = o
            self.readers[s] = []
        o.idx = len(self.streams[eng])
        self.streams[eng].append(o)
        return o

    def emit(self, final_sems=()):
        nc = self.nc
        sems = {e: nc.alloc_semaphore(name=f"s_{e}") for e in ENGS}
        for e in ENGS:
            c = 0
            for o in self.streams[e]:
                if o.need_inc and not o.is_dma:
                    c += 1
                    o.seq = c
        with nc.Block() as block:
            def body(e):
                def run(engh):
                    waited = {}
                    for o in self.streams[e]:
                        for d, dv in o.deps:
                            if d.is_dma:
                                key = ("d", id(d.dsem))
                                val = dv
                                semh = d.dsem.h
                            else:
                                key = ("e", d.eng)
                                val = d.seq
                                semh = sems[d.eng]
                            if waited.get(key, 0) >= val:
                                continue
                            waited[key] = val
                            engh.wait_ge(semh, val)
                        ins = o.fn(engh)
                        if o.is_dma:
                            ins.then_inc(o.dsem.h, 16)
                        elif o.need_inc:
                            ins.then_inc(sems[e], 1)
                    if e == "sp":
                        for ds in final_sems:
                            if ds.count > 0:
                                engh.wait_ge(ds.h, ds.count)
                return run
            block.tensor(body("pe"))
            block.scalar(body("act"))
            block.vector(body("dve"))
            block.gpsimd(body("pool"))
            block.sync(body("sp"))


def build_nc(ntiles=NTILES, do_sample=True, debug=False):
    nc = bass.Bass("TRN2", target_bir_lowering=False)
    P = Prog(nc)
    allsems = []

    def tap(name, ap, reads):
        if not debug:
            return
        shape = list(ap.shape)
        d = nc.dram_tensor(name, shape, ap.dtype, kind="ExternalOutput").ap()
        P.op("sp", lambda e: e.dma_start(out=d, in_=ap), reads=reads, dsem=newsem("t_" + name))

    def newsem(name):
        s = DmaSem(nc.alloc_semaphore(name=name))
        allsems.append(s)
        return s

    def din(name, shape):
        return nc.dram_tensor(name, shape, F32, kind="ExternalInput").ap()

    def dout(name, shape):
        return nc.dram_tensor(name, shape, F32, kind="ExternalOutput").ap()

    x_p = din("x_p", [SEQ, D]); p_p = din("p_p", [SEQ, PLE])
    x_s = din("x_s", [NSTR * DSEQ, D]); p_s = din("p_s", [NSTR * DSEQ, PLE])
    cache_k = din("cache_k", [NSTR, PAST, D]); cache_v = din("cache_v", [NSTR, PAST, D])
    cache_lf = din("cache_lf", [NSTR, PAST, NH])
    st_conv = din("st_conv", [NSTR, 30, D]); st_ffn = din("st_ffn", [NSTR, 2, DFF])
    ln0_g = din("ln0_g", [D]); ln0_b = din("ln0_b", [D])
    w_in = din("w_in", [D, N_IN]); b_f = din("b_f", [NH])
    conv_dw_w = din("conv_dw_w", [31, D]); conv_dw_b = din("conv_dw_b", [D])
    conv_ln_g = din("conv_ln_g", [D]); conv_ln_b = din("conv_ln_b", [D])
    w_conv_out = din("w_conv_out", [D, D]); w_attn_out = din("w_attn_out", [D, D]); w_o = din("w_o", [D, D])
    ln1_g = din("ln1_g", [D]); ln1_b = din("ln1_b", [D])
    w_ffn_up = din("w_ffn_up", [D, 2 * DFF]); ffn_dw_w = din("ffn_dw_w", [3, DFF]); ffn_dw_b = din("ffn_dw_b", [DFF])
    w_ffn_down = din("w_ffn_down", [DFF, D])
    ln2_g = din("ln2_g", [D]); ln2_b = din("ln2_b", [D])
    w_ple = din("w_ple", [PLE, D]); w_ple_gate = din("w_ple_gate", [D, D])

    y_p = dout("y_p", [SEQ, D]); y_s = dout("y_s", [NSTR * DSEQ, D])
    k_p = dout("k_p", [SEQ, D]); v_p = dout("v_p", [SEQ, D]); lf_p = dout("lf_p", [SEQ, NH])
    cv_p = dout("cv_p", [30, D]); ff_p = dout("ff_p", [2, DFF])
    k_s = dout("k_s", [NSTR * DSEQ, D]); v_s = dout("v_s", [NSTR * DSEQ, D]); lf_s = dout("lf_s", [NSTR * DSEQ, NH])
    cv_s = dout("cv_s", [NSTR, 30, D]); ff_s = dout("ff_s", [NSTR, 2, DFF])

    kt_scr = nc.dram_tensor("kt_scr", [8, NTILES, 80, 1024], BF16).ap()
    v_scr = nc.dram_tensor("v_scr", [8, NTILES, 128, 768], BF16).ap()

    def sbt(name, shape, dt):
        return nc.alloc_sbuf_tensor(name, shape, dt)

    ident_f = sbt("ident_f", [128, 128], F32)
    ident_b = sbt("ident_b", [128, 128], BF16)
    maskb = sbt("maskb", [128, 128], BF16)
    onesm = sbt("onesm", [128, 128], BF16)
    onesb = sbt("onesb", [128, 64], BF16)
    zerob = sbt("zerob", [128, 128], BF16)
    ones16 = sbt("ones16", [16, 512], F32)
    ones3 = sbt("ones3", [3, 1024], BF16)
    cvec = sbt("cvec", [128, 8, 34], F32)
    fvec = sbt("fvec", [128, NFC, 4], F32)
    wfl = sbt("wfl", [128, 8, 16], BF16)
    nbf = sbt("nbf", [16, 1], F32)
    carry = sbt("carry", [16, 1], F32)
    hend = sbt("hend", [16, 4], F32)
    gbuf = sbt("gbuf", [128, 2, 1024], F32)
    wbuf = [sbt(f"wbuf{i}", [128, 8192], BF16) for i in range(NWB)]
    xres = sbt("xres", [128, 4, 1024], F32)
    xT = sbt("xT", [128, 8, 512], BF16)
    oT = sbt("oT", [128, 8, 512], BF16)
    gated_c = sbt("gated_c", [128, 8, 512], BF16)
    QT = sbt("QT", [128, 16, 512], BF16)
    KTs = sbt("KTs", [128, 8192], BF16)
    uhist = sbt("uhist", [128, 8, 4, 30], BF16)
    ahist = sbt("ahist", [128, NFC, 4, 2], F32)
    ARENA_W = 18944
    arena = sbt("arena", [128, ARENA_W], F32)
    psb = [nc.alloc_psum_tensor(f"psb{i}", [128, 512], F32) for i in range(8)]

    P.scoped.add("A")
    C_WORDS = 6400
    P.scoped.add("C")
    ar = {"A": C_WORDS, "C": 0, "n": 0}

    def ar_reset(kind="A"):
        ar[kind] = C_WORDS if kind == "A" else 0

    def alloc(shape, dt, parts=128, kind="A"):
        n = 1
        for s_ in shape[1:]:
            n *= s_
        words = n if dt == F32 else (n + 1) // 2
        words = (words + 7) // 8 * 8
        off = ar[kind]
        lim = ARENA_W if kind == "A" else C_WORDS
        assert off + words <= lim, ("arena overflow", kind, off, words)
        ar[kind] = off + words
        v = arena[0:shape[0], off:off + words]
        if dt == BF16:
            v = v.bitcast(BF16)[:, 0:n]
        else:
            v = v[:, 0:n]
        if len(shape) > 2:
            names = " ".join(f"d{i}" for i in range(1, len(shape)))
            kw = {f"d{i}": shape[i] for i in range(1, len(shape))}
            v = v.rearrange(f"p ({names}) -> p {names}", **kw)
        ar["n"] += 1
        return v, (kind, ar["n"])

    bank_rr = [0]

    def nb():
        b = bank_rr[0] % 8
        bank_rr[0] += 1
        return b

    def PS(b):
        return ("ps", b)

    P.op("pool", lambda e: e.memset(ident_f[:], 1.0), writes=["ident_f"])
    P.op("pool", lambda e: e.affine_select(out=ident_f[:], in_=ident_f[:], pattern=[[1, 128]], compare_op=ALU.is_equal,
                                           fill=0.0, base=0, channel_multiplier=-1), reads=["ident_f"], writes=["ident_f"])
    P.op("pool", lambda e: e.tensor_copy(out=ident_b[:], in_=ident_f[:]), reads=["ident_f"], writes=["ident_b"])
    mask32, mask32_s = alloc([128, 128], F32)
    P.op("pool", lambda e: e.memset(mask32[:], 0.0), writes=[mask32_s])
    P.op("pool", lambda e: e.affine_select(out=mask32[:], in_=mask32[:], pattern=[[1, 128]], compare_op=ALU.is_ge,
                                           fill=NEG, base=0, channel_multiplier=-1), reads=[mask32_s], writes=[mask32_s])
    P.op("pool", lambda e: e.tensor_copy(out=maskb[:], in_=mask32[:]), reads=[mask32_s], writes=["maskb"])
    P.op("pool", lambda e: e.memset(onesm[:], 1.0 / 1024.0), writes=["onesm"])
    P.op("pool", lambda e: e.memset(onesb[:], 1.0), writes=["onesb"])
    P.op("pool", lambda e: e.memset(zerob[:], 0.0), writes=["zerob"])
    if debug:
        P.op("pool", lambda e: e.memset(oT[:, :, :], 7.0), writes=[("oT", g_, j_) for g_ in range(8) for j_ in range(2)])
    P.op("pool", lambda e: e.memset(ones16[:], 1.0), writes=["ones16"])
    P.op("pool", lambda e: e.memset(ones3[:], 1.0), writes=["ones3"])
    P.op("pool", lambda e: e.memset(carry[:], 0.0), writes=["carry"])
    P.op("pool", lambda e: e.memset(uhist[:], 0.0), writes=["uhist"])
    P.op("pool", lambda e: e.memset(ahist[:], 0.0), writes=["ahist"])
    P.op("pool", lambda e: e.memset(QT[64:96, :, :], 1.0), writes=["QTc"])
    P.op("pool", lambda e: e.memset(KTs[64:96, :], 0.0), writes=["KTc0"])
    s_c1 = newsem("c1")
    for g in range(8):
        P.op("sp", lambda e, g=g: e.dma_start(out=KTs[67:70, 1024 * g:1024 * (g + 1)], in_=ones3[:, :]),
             reads=["ones3", "KTc0"], writes=[("KTc1", g)], dsem=s_c1)
    KTC = ["KTc0"] + [("KTc1", g) for g in range(8)]

    vrows, vrows_s = alloc([34, 1024], F32, 34)
    frows, frows_s = alloc([4, DFF], F32, 4)
    bft, bft_s = alloc([16, 1], F32, 16)
    s_c2 = newsem("c2")
    vr = [(vrows_s, "r", i) for i in range(5)]
    P.op("sp", lambda e: e.dma_start(out=vrows[0:1, :], in_=conv_dw_b.rearrange("(o n) -> o n", o=1)), writes=[vr[0], vrows_s], dsem=s_c2)
    P.op("sp", lambda e: e.dma_start(out=vrows[1:2, :], in_=conv_ln_g.rearrange("(o n) -> o n", o=1)), writes=[vr[1]], dsem=s_c2)
    P.op("sp", lambda e: e.dma_start(out=vrows[2:3, :], in_=conv_ln_b.rearrange("(o n) -> o n", o=1)), writes=[vr[2]], dsem=s_c2)
    P.op("sp", lambda e: e.dma_start(out=vrows[3:34, :], in_=conv_dw_w), writes=[vr[3]], dsem=s_c2)
    P.op("sp", lambda e: e.dma_start(out=frows[0:3, :], in_=ffn_dw_w), writes=[vr[4], frows_s], dsem=s_c2)
    P.op("sp", lambda e: e.dma_start(out=frows[3:4, :], in_=ffn_dw_b.rearrange("(o n) -> o n", o=1)), writes=[(frows_s, "r5")], dsem=s_c2)
    P.op("sp", lambda e: e.dma_start(out=bft[:, :], in_=b_f.rearrange("(h o) -> h o", o=1)), writes=[(bft_s, "r6"), bft_s], dsem=s_c2)
    VR = vr + [(frows_s, "r5"), (bft_s, "r6"), vrows_s, frows_s, bft_s]
    P.op("dve", lambda e: e.tensor_scalar(out=nbf[:], in0=bft[:, :], scalar1=-1.0, scalar2=None, op0=ALU.mult),
         reads=VR, writes=["nbf"])
    for c in range(8):
        b = nb()
        P.op("pe", lambda e, c=c, b=b: e.transpose(out=psb[b][:, 0:34], in_=vrows[0:34, 128 * c:128 * (c + 1)], identity=ident_f[0:34, 0:34]),
             reads=VR + ["ident_f"], writes=[PS(b)])
        P.op("dve", lambda e, c=c, b=b: e.tensor_copy(out=cvec[:, c, :], in_=psb[b][:, 0:34]), writes=[PS(b), ("cvec", c)])
    for c in range(NFC):
        b = nb()
        P.op("pe", lambda e, c=c, b=b: e.transpose(out=psb[b][:, 0:4], in_=frows[0:4, 128 * c:128 * (c + 1)], identity=ident_f[0:4, 0:4]),
             reads=VR + ["ident_f"], writes=[PS(b)])
        P.op("dve", lambda e, c=c, b=b: e.tensor_copy(out=fvec[:, c, :], in_=psb[b][:, 0:4]), writes=[PS(b), ("fvec", c)])
    CVEC = [("cvec", c) for c in range(8)]
    FVEC = [("fvec", c) for c in range(NFC)]

    wgroups = {}

    def wsrc(W, r0, nk, c0, w):
        return W[r0:r0 + nk * 128, :].rearrange("(kc p) n -> p kc n", p=128)[:, :, c0:c0 + w]

    def wgroup(name, nk, ncols, srcs):
        scr = nc.dram_tensor("ws_" + name, [128, nk, ncols], BF16).ap()
        sem = newsem("wp_" + name)
        slots = []
        for j, (src, off, w) in enumerate(srcs):
            sl = ("wscr", name, j)
            slots.append(sl)
            P.op("pool", lambda e, src=src, off=off, w=w: e.dma_start(out=scr[:, :, off:off + w], in_=src), writes=[sl], dsem=sem)
        wgroups[name] = (scr, nk, ncols, slots)

    wgroup("FL", 8, 16, [(wsrc(w_in, 0, 8, 5120, 16), 0, 16)])
    wgroup("PLE", 2, 1024, [(wsrc(w_ple, 0, 2, 0, 1024), 0, 1024)])
    wgroup("A1", 8, 1024, [(wsrc(w_in, 0, 8, 0, 512), 0, 512), (wsrc(w_in, 0, 8, 1024, 512), 512, 512)])
    wgroup("A2", 8, 1024, [(wsrc(w_in, 0, 8, 512, 512), 0, 512), (wsrc(w_in, 0, 8, 1536, 512), 512, 512)])
    wgroup("Q", 8, 1024, [(wsrc(w_in, 0, 8, 2048, 1024), 0, 1024)])
    wgroup("K", 8, 1024, [(wsrc(w_in, 0, 8, 3072, 1024), 0, 1024)])
    wgroup("V", 8, 1024, [(wsrc(w_in, 0, 8, 4096, 1024), 0, 1024)])
    wgroup("GC", 8, 1024, [(wsrc(w_in, 0, 8, 5136, 1024), 0, 1024)])
    wgroup("CO", 8, 1024, [(wsrc(w_conv_out, 0, 8, 0, 1024), 0, 1024)])
    wgroup("GA", 8, 1024, [(wsrc(w_in, 0, 8, 6160, 1024), 0, 1024)])
    wgroup("AO", 8, 1024, [(wsrc(w_attn_out, 0, 8, 0, 1024), 0, 1024)])
    wgroup("O", 8, 1024, [(wsrc(w_o, 0, 8, 0, 1024), 0, 1024)])
    wgroup("PG", 8, 1024, [(wsrc(w_ple_gate, 0, 8, 0, 1024), 0, 1024)])
    for g in range(6):
        hw = 512 if g < 5 else 256
        wgroup(f"UP{g}", 8, 2 * hw, [(wsrc(w_ffn_up, 0, 8, 512 * g, hw), 0, hw), (wsrc(w_ffn_up, 0, 8, DFF + 512 * g, hw), hw, hw)])
    for n in range(2):
        for kh in range(2):
            wgroup(f"DN{kh}{n}", 11, 512, [(wsrc(w_ffn_down, kh * 11 * 128, 11, 512 * n, 512), 0, 512)])

    s_wres = newsem("wres")
    P.op("sp", lambda e: e.dma_start(out=wfl[:], in_=wgroups["FL"][0]), reads=wgroups["FL"][3], writes=["wfl"], dsem=s_wres)

    tile_seq = ["A1", "A2", "Q", "K", "V", "GC", "CO", "GA", "AO", "O", "PG", "PLE"] + [f"UP{g}" for g in range(6)] + ["DN00", "DN10", "DN01", "DN11"]
    ntot = ntiles + (1 if do_sample else 0)
    wseq = tile_seq * ntot
    wsem = [newsem(f"wl{i}") for i in range(NWB)]
    wst = {"pos": 0, "loaded": 0}

    def w_ensure(n):
        while wst["loaded"] <= n and wst["loaded"] < len(wseq):
            m = wst["loaded"]
            scr, nk, ncols, slots = wgroups[wseq[m]]
            b = m % NWB
            dst = wbuf[b][:, 0:nk * ncols].rearrange("p (k n) -> p k n", k=nk)
            P.op("sp", lambda e, dst=dst, scr=scr: e.dma_start(out=dst, in_=scr), reads=slots, writes=[("wb", b)], dsem=wsem[b])
            wst["loaded"] += 1

    def w_get(name, ahead=NWB - 1):
        n = wst["pos"]
        assert wseq[n] == name, (wseq[n], name)
        w_ensure(n + ahead)
        scr, nk, ncols, slots = wgroups[name]
        b = n % NWB
        wst["pos"] += 1
        return wbuf[b][:, 0:nk * ncols].rearrange("p (k n) -> p k n", k=nk), ("wb", b)

    s_x = [newsem(f"x{i}") for i in range(2)]
    s_g = newsem("g"); s_b = newsem("b")
    s_y = [newsem(f"y{i}") for i in range(4)]
    s_kvo = [newsem(f"kvo{i}") for i in range(2)]
    s_lf = newsem("lf")
    s_ktw = [newsem(f"ktw{i}") for i in range(8)]
    s_vw = [newsem(f"vw{i}") for i in range(8)]
    s_rq = newsem("rq"); s_rk = newsem("rk")
    s_kvK = [newsem(f"kvK{i}") for i in range(NKV)]
    s_kvV = [newsem(f"kvV{i}") for i in range(NKV)]
    s_p = newsem("p")
    s_st = [newsem(f"st{i}") for i in range(2)]
    s_vst = newsem("vst")

    def run_tile(kind, ti):
        prompt = kind == "p"
        Tt = 512 if prompt else 256
        nsub = Tt // 128
        nseg, L = (1, 512) if prompt else (4, 64)
        t0 = ti * 512
        last = prompt and ti == NTILES - 1
        need_state = last or not prompt
        x_src = x_p[t0:t0 + Tt, :] if prompt else x_s
        p_src = p_p[t0:t0 + Tt, :] if prompt else p_s
        y_dst = y_p[t0:t0 + Tt, :] if prompt else y_s
        k_dst = k_p[t0:t0 + Tt, :] if prompt else k_s
        v_dst = v_p[t0:t0 + Tt, :] if prompt else v_s
        lf_dst = lf_p[t0:t0 + Tt, :] if prompt else lf_s
        state_subs = [3] if prompt else [0, 1]

        def load_gb(g_ap, b_ap):
            P.op("sp", lambda e: e.dma_start(out=gbuf[:, 0, :], in_=g_ap.partition_broadcast(128)), writes=["gb0"], dsem=s_g)
            P.op("sp", lambda e: e.dma_start(out=gbuf[:, 1, :], in_=b_ap.partition_broadcast(128)), writes=["gb1"], dsem=s_b)

        def ln_rows(src, src_slots, sub, tmp):
            st, st_s, mv, mv_s, lnv, lnv_s, rstd, rstd_s = tmp
            XR = ("xres", sub)
            P.op("dve", lambda e: e.bn_stats(out=st[:, 0, :], in_=src[:, 0:512]), reads=src_slots, writes=[st_s])
            P.op("dve", lambda e: e.bn_stats(out=st[:, 1, :], in_=src[:, 512:1024]), reads=src_slots, writes=[(st_s, 1)])
            P.op("dve", lambda e: e.bn_aggr(out=mv[:, :], in_=st[:, :, :].rearrange("p a b -> p (a b)")), reads=[st_s, (st_s, 1)], writes=[mv_s])
            P.op("act", lambda e: e.activation(out=lnv[:, :], in_=mv[:, 1:2], func=AF.Ln, bias=EPS, scale=1.0), reads=[mv_s], writes=[lnv_s])
            P.op("act", lambda e: e.activation(out=rstd[:, :], in_=lnv[:, :], func=AF.Exp, scale=-0.5), reads=[lnv_s], writes=[rstd_s])
            P.op("dve", lambda e: e.tensor_scalar(out=xres[:, sub, :], in0=src, scalar1=mv[:, 0:1], scalar2=rstd[:, 0:1],
                                                  op0=ALU.subtract, op1=ALU.mult), reads=src_slots + [mv_s, rstd_s], writes=[XR])
            P.op("pool", lambda e: e.tensor_tensor(out=xres[:, sub, :], in0=xres[:, sub, :], in1=gbuf[:, 0, :], op=ALU.mult),
                 reads=[XR, "gb0"], writes=[XR])
            P.op("pool", lambda e: e.tensor_tensor(out=xres[:, sub, :], in0=xres[:, sub, :], in1=gbuf[:, 1, :], op=ALU.add),
                 reads=[XR, "gb1"], writes=[XR])

        def to_featmajor(sub, xb, xb_s):
            XR = ("xres", sub)
            P.op("act", lambda e: e.activation(out=xb[:, :], in_=xres[:, sub, :], func=AF.Identity), reads=[XR], writes=[xb_s])
            b = nb()
            psv = psb[b][:].bitcast(BF16)
            for c in range(8):
                P.op("pe", lambda e, c=c: e.transpose(out=psv[:, 128 * c:128 * (c + 1)], in_=xb[:, 128 * c:128 * (c + 1)], identity=ident_b[:, :]),
                     reads=[xb_s, "ident_b"], writes=[PS(b)])
            P.op("dve", lambda e: e.tensor_copy(out=xT[:, :, 128 * sub:128 * (sub + 1)], in_=psv[:, :].rearrange("p (c t) -> p c t", c=8)),
                 writes=[PS(b), ("xT", sub)])

        XT = [("xT", s) for s in range(nsub)]

        ar_reset()
        xin = [alloc([128, 1024], F32) for _ in range(2)]
        xb2 = [alloc([128, 1024], BF16)] * 2
        lnt = []
        for _ in range(2):
            st, st_s = alloc([128, 2, 6], F32); mv, mv_s = alloc([128, 2], F32)
            lnv, lnv_s = alloc([128, 1], F32); rstd, rstd_s = alloc([128, 1], F32)
            lnt.append((st, st_s, mv, mv_s, lnv, lnv_s, rstd, rstd_s))
        kvst = [alloc([128, 1024], F32) for _ in range(2)]

        load_gb(ln0_g, ln0_b)
        for sub in range(nsub):
            xi, xi_s = xin[sub % 2]
            P.op("sp", lambda e, sub=sub, xi=xi: e.dma_start(out=xi[:, :], in_=x_src[128 * sub:128 * (sub + 1), :]), writes=[xi_s], dsem=s_x[sub % 2])
            ln_rows(xi, [xi_s], sub, lnt[sub % 2])
            to_featmajor(sub, *xb2[sub % 2])

        if ti == 0:
            tap(('d_' if prompt else 's_') + 'xT', xT[:, :, :], XT)
        lA, lA_s = alloc([16, 512], F32, 16)
        lB, lB_s = alloc([16, 512], F32, 16)
        cT, cT_s = alloc([16, 512], F32, 16)
        r1, r1_s = alloc([16, 512], F32, 16)
        SQ, SQ_s = alloc([16, 3, 512], BF16, 16)
        SK, SK_s = alloc([16, 3, 512], BF16, 16)
        lft, lft_s = alloc([128, 4, 16], F32)
        b = nb()
        for kc in range(8):
            P.op("pe", lambda e, kc=kc, b=b: e.matmul(psb[b][0:16, 0:Tt], lhsT=wfl[:, kc, :], rhs=xT[:, kc, 0:Tt], start=(kc == 0), stop=(kc == 7)),
                 reads=XT + ["wfl"], writes=[PS(b)])
        P.op("act", lambda e, b=b: e.activation(out=lA[:, 0:Tt], in_=psb[b][0:16, 0:Tt], func=AF.Exp, bias=nbf[:, 0:1], scale=-1.0),
             reads=["nbf"], writes=[PS(b), lA_s])
        P.op("act", lambda e: e.activation(out=lB[:, 0:Tt], in_=lA[:, 0:Tt], func=AF.Ln, bias=1.0, scale=1.0), reads=[lA_s], writes=[lB_s])
        P.op("dve", lambda e: e.tensor_scalar(out=lA[:, 0:Tt], in0=lB[:, 0:Tt], scalar1=-1.0, scalar2=None, op0=ALU.mult),
             reads=[lB_s], writes=[lA_s])
        if prompt:
            P.op("dve", lambda e: e.tensor_tensor_scan(out=cT[:, 0:512], data0=ones16[:, 0:512], data1=lA[:, 0:512], initial=carry[:, 0:1],
                                                       op0=ALU.mult, op1=ALU.add), reads=[lA_s, "ones16", "carry"], writes=[cT_s])
            P.op("dve", lambda e: e.tensor_copy(out=carry[:, 0:1], in_=cT[:, 511:512]), reads=[cT_s], writes=["carry"])
        else:
            for s in range(4):
                P.op("dve", lambda e, s=s: e.tensor_tensor_scan(out=cT[:, 64 * s:64 * (s + 1)], data0=ones16[:, 0:64], data1=lA[:, 64 * s:64 * (s + 1)],
                                                                initial=hend[:, s:s + 1], op0=ALU.mult, op1=ALU.add),
                     reads=[lA_s, "ones16", "hend"], writes=[cT_s])
        P.op("dve", lambda e: e.tensor_scalar(out=SQ[:, 0, 0:Tt], in0=cT[:, 0:Tt], scalar1=8.0, scalar2=None, op0=ALU.mult), reads=[cT_s], writes=[SQ_s])
        P.op("dve", lambda e: e.scalar_tensor_tensor(out=r1[:, 0:Tt], in0=cT[:, 0:Tt], scalar=8.0, in1=SQ[:, 0, 0:Tt], op0=ALU.mult, op1=ALU.subtract),
             reads=[cT_s, SQ_s], writes=[r1_s])
        P.op("dve", lambda e: e.tensor_copy(out=SQ[:, 1, 0:Tt], in_=r1[:, 0:Tt]), reads=[r1_s], writes=[(SQ_s, 1)])
        P.op("dve", lambda e: e.tensor_tensor(out=lB[:, 0:Tt], in0=r1[:, 0:Tt], in1=SQ[:, 1, 0:Tt], op=ALU.subtract), reads=[r1_s, (SQ_s, 1)], writes=[lB_s])
        P.op("dve", lambda e: e.tensor_copy(out=SQ[:, 2, 0:Tt], in_=lB[:, 0:Tt]), reads=[lB_s], writes=[(SQ_s, 2)])
        SQA = [SQ_s, (SQ_s, 1), (SQ_s, 2)]
        P.op("dve", lambda e: e.tensor_scalar(out=SK[:, :, 0:Tt], in0=SQ[:, :, 0:Tt], scalar1=-1.0, scalar2=None, op0=ALU.mult), reads=SQA, writes=[SK_s])
        b = nb()
        for sub in range(nsub):
            P.op("pe", lambda e, sub=sub, b=b: e.transpose(out=psb[b][:, 16 * sub:16 * (sub + 1)], in_=lA[:, 128 * sub:128 * (sub + 1)], identity=ident_f[0:16, 0:16]),
                 reads=[lA_s, "ident_f"], writes=[PS(b)])
        P.op("dve", lambda e, b=b: e.tensor_copy(out=lft[:, 0:nsub, :], in_=psb[b][:, 0:16 * nsub].rearrange("p (s h) -> p s h", h=16)), writes=[PS(b), lft_s])
        P.op("pool", lambda e: e.dma_start(out=lf_dst.rearrange("(s p) h -> p s h", p=128), in_=lft[:, 0:nsub, :]), reads=[lft_s], dsem=s_lf)
        RQ = [("rq", h) for h in range(NH)]
        RK = [("rk", h) for h in range(NH)]
        if prompt:
            KTv = KTs[:, :].rearrange("p (g b h i) -> p g b h i", g=8, b=4, h=2)
        else:
            KTv = KTs[:, 0:4096].rearrange("p (h t) -> p h t", h=16)
        for h in range(NH):
            P.op("pool", lambda e, h=h: e.dma_start(out=QT[67:70, h, 0:Tt], in_=SQ[h:h + 1, :, 0:Tt]), reads=SQA + ["QTc"], writes=[RQ[h]], dsem=s_rq)
            if prompt:
                P.op("pool", lambda e, h=h: e.dma_start(out=KTv[64:67, h // 2, :, h % 2, :], in_=SK[h:h + 1, :, 0:512]), reads=[SK_s] + KTC, writes=[RK[h]], dsem=s_rk)
            else:
                P.op("pool", lambda e, h=h: e.dma_start(out=KTv[64:67, h, :], in_=SK[h:h + 1, :, 0:256]), reads=[SK_s] + KTC, writes=[RK[h]], dsem=s_rk)

        ucT, ucT_s = alloc([128, 8, 512], BF16, kind="C")
        sgt = [alloc([128, 512], BF16) for _ in range(2)]
        acc = [alloc([128, 512], F32, kind="C") for _ in range(4)]

        ust, ust_s = kvst[0]
        bmean = 6
        bmsq = 7
        def nb6():
            while True:
                b = nb()
                if b < 6:
                    return b

        def seg3(ap):
            return ap.rearrange("p (s l) -> p s l", s=nseg)

        if not prompt:
            sc, sc_s = alloc([120, 1024], F32, 120)
            P.op("sp", lambda e: e.dma_start(out=sc[:, :], in_=st_conv.rearrange("s r d -> (s r) d")), writes=[sc_s], dsem=s_st[0])
            for c in range(8):
                b = nb6()
                P.op("pe", lambda e, c=c, b=b: e.transpose(out=psb[b][:, 0:120], in_=sc[0:120, 128 * c:128 * (c + 1)], identity=ident_f[0:120, 0:120]),
                     reads=[sc_s, "ident_f"], writes=[PS(b)])
                P.op("dve", lambda e, c=c, b=b: e.tensor_copy(out=uhist[:, c, :, :], in_=psb[b][:, 0:120].rearrange("p (s r) -> p s r", s=4)),
                     writes=[PS(b), ("uhist", c)])

        utl = [alloc([128, nseg, 30 + L], BF16, kind="C") for _ in range(8)]
        for half in range(2):
            W, W_s = w_get("A1" if half == 0 else "A2")
            for cc in range(4):
                c = 4 * half + cc
                ba = nb6(); bg = nb6()
                for kc in range(8):
                    P.op("pe", lambda e, kc=kc, cc=cc, ba=ba, W=W: e.matmul(psb[ba][:, 0:Tt], lhsT=W[:, kc, 128 * cc:128 * (cc + 1)], rhs=xT[:, kc, 0:Tt], start=(kc == 0), stop=(kc == 7)),
                         reads=XT + [W_s], writes=[PS(ba)])
                for kc in range(8):
                    P.op("pe", lambda e, kc=kc, cc=cc, bg=bg, W=W: e.matmul(psb[bg][:, 0:Tt], lhsT=W[:, kc, 512 + 128 * cc:512 + 128 * (cc + 1)], rhs=xT[:, kc, 0:Tt], start=(kc == 0), stop=(kc == 7)),
                         reads=XT + [W_s], writes=[PS(bg)])
                sg, sg_s = sgt[c % 2]
                u, u_s = utl[c]
                P.op("act", lambda e, bg=bg, sg=sg: e.activation(out=sg[:, 0:Tt], in_=psb[bg][:, 0:Tt], func=AF.Sigmoid), writes=[PS(bg), sg_s])
                P.op("pool", lambda e, c=c, u=u: e.tensor_copy(out=u[:, :, 0:30], in_=uhist[:, c, 0:nseg, :]), reads=[("uhist", c)], writes=[(u_s, "h")])
                P.op("dve", lambda e, ba=ba, sg=sg, u=u: e.tensor_tensor(out=u[:, :, 30:30 + L], in0=seg3(psb[ba][:, 0:Tt]), in1=seg3(sg[:, 0:Tt]), op=ALU.mult),
                     reads=[sg_s], writes=[PS(ba), u_s])
                if prompt:
                    P.op("pool", lambda e, c=c, u=u: e.tensor_copy(out=uhist[:, c, 0, :], in_=u[:, 0, L:L + 30]), reads=[u_s, (u_s, "h")], writes=[("uhist", c)])
            if need_state:
                for sub in state_subs:
                    ba = nb6(); bg = nb6()
                    for kc in range(8):
                        P.op("pe", lambda e, kc=kc, sub=sub, ba=ba, W=W: e.matmul(psb[ba][:, :], lhsT=xT[:, kc, 128 * sub:128 * (sub + 1)], rhs=W[:, kc, 0:512], start=(kc == 0), stop=(kc == 7)),
                             reads=XT + [W_s], writes=[PS(ba)])
                    for kc in range(8):
                        P.op("pe", lambda e, kc=kc, sub=sub, bg=bg, W=W: e.matmul(psb[bg][:, :], lhsT=xT[:, kc, 128 * sub:128 * (sub + 1)], rhs=W[:, kc, 512:1024], start=(kc == 0), stop=(kc == 7)),
                             reads=XT + [W_s], writes=[PS(bg)])
                    ut, ut_s = kvst[sub % 2]
                    sgf, sgf_s = acc[3]
                    P.op("act", lambda e, bg=bg, sgf=sgf: e.activation(out=sgf[:, :], in_=psb[bg][:, :], func=AF.Sigmoid), writes=[PS(bg), sgf_s])
                    P.op("dve", lambda e, ba=ba, sgf=sgf, ut=ut, half=half: e.tensor_tensor(out=ut[:, 512 * half:512 * (half + 1)], in0=psb[ba][:, :], in1=sgf[:, :], op=ALU.mult),
                         reads=[sgf_s], writes=[PS(ba), (ut_s, half)])
                    if prompt:
                        P.op("pool", lambda e, ut=ut, half=half: e.dma_start(out=cv_p[:, 512 * half:512 * (half + 1)], in_=ut[98:128, 512 * half:512 * (half + 1)]),
                             reads=[(ut_s, half)], dsem=s_kvo[sub % 2])
                    else:
                        for j in range(2):
                            P.op("pool", lambda e, ut=ut, half=half, j=j, sub=sub: e.dma_start(out=cv_s[2 * sub + j, :, 512 * half:512 * (half + 1)],
                                                                                           in_=ut[64 * j + 34:64 * j + 64, 512 * half:512 * (half + 1)]),
                                 reads=[(ut_s, half)], dsem=s_kvo[sub % 2])
        def chain(c):
            u, u_s = utl[c]
            US = [u_s, (u_s, "h")]
            a0, a0_s = acc[(2 * c) % 4]
            a1, a1_s = acc[(2 * c + 1) % 4]
            P.op("dve", lambda e, c=c, u=u, a0=a0: e.tensor_scalar(out=seg3(a0[:, 0:Tt]), in0=u[:, :, 0:L], scalar1=cvec[:, c, 3:4], scalar2=cvec[:, c, 0:1], op0=ALU.mult, op1=ALU.add),
                 reads=US + CVEC, writes=[a0_s])
            P.op("dve", lambda e, c=c, u=u, a1=a1: e.tensor_scalar(out=seg3(a1[:, 0:Tt]), in0=u[:, :, 1:1 + L], scalar1=cvec[:, c, 4:5], scalar2=None, op0=ALU.mult),
                 reads=US + CVEC, writes=[a1_s])
            for k in range(2, 31):
                a, a_s = (a0, a0_s) if k % 2 == 0 else (a1, a1_s)
                P.op("dve", lambda e, c=c, k=k, u=u, a=a: e.scalar_tensor_tensor(out=seg3(a[:, 0:Tt]), in0=u[:, :, k:k + L], scalar=cvec[:, c, 3 + k:4 + k],
                                                                              in1=seg3(a[:, 0:Tt]), op0=ALU.mult, op1=ALU.add),
                     reads=US + CVEC + [a_s], writes=[a_s])
            P.op("dve", lambda e, c=c, a0=a0, a1=a1: e.tensor_tensor(out=ucT[:, c, 0:Tt], in0=a0[:, 0:Tt], in1=a1[:, 0:Tt], op=ALU.add), reads=[a0_s, a1_s], writes=[(ucT_s, c)])

        chains_pending = list(range(8))
        if not prompt:
            while chains_pending:
                chain(chains_pending.pop(0))
        W, W_s = w_get("Q")
        for c in range(8):
            b = nb()
            for kc in range(8):
                P.op("pe", lambda e, kc=kc, c=c, b=b, W=W: e.matmul(psb[b][:, 0:Tt], lhsT=W[:, kc, 128 * c:128 * (c + 1)], rhs=xT[:, kc, 0:Tt], start=(kc == 0), stop=(kc == 7)),
                     reads=XT + [W_s], writes=[PS(b)])
            P.op("act", lambda e, c=c, b=b: e.activation(out=QT[0:64, 2 * c, 0:Tt], in_=psb[b][0:64, 0:Tt], func=AF.Identity), writes=[PS(b), ("QT", 2 * c)])
            P.op("act", lambda e, c=c, b=b: e.activation(out=QT[0:64, 2 * c + 1, 0:Tt], in_=psb[b][64:128, 0:Tt], func=AF.Identity), writes=[PS(b), ("QT", 2 * c + 1)])
        W, W_s = w_get("K")
        for c in range(8):
            b = nb()
            for kc in range(8):
                P.op("pe", lambda e, kc=kc, c=c, b=b, W=W: e.matmul(psb[b][:, 0:Tt], lhsT=W[:, kc, 128 * c:128 * (c + 1)], rhs=xT[:, kc, 0:Tt], start=(kc == 0), stop=(kc == 7)),
                     reads=XT + [W_s], writes=[PS(b)])
            if prompt:
                P.op("act", lambda e, c=c, b=b: e.activation(out=KTv[0:64, c, :, 0, :], in_=psb[b][0:64, :].rearrange("p (b i) -> p b i", b=4), func=AF.Identity),
                     writes=[PS(b), ("KT", 2 * c)])
                P.op("act", lambda e, c=c, b=b: e.activation(out=KTv[0:64, c, :, 1, :], in_=psb[b][64:128, :].rearrange("p (b i) -> p b i", b=4), func=AF.Identity),
                     writes=[PS(b), ("KT", 2 * c + 1)])
            else:
                P.op("act", lambda e, c=c, b=b: e.activation(out=KTv[0:64, 2 * c, :], in_=psb[b][0:64, 0:Tt], func=AF.Identity), writes=[PS(b), ("KT", 2 * c)])
                P.op("act", lambda e, c=c, b=b: e.activation(out=KTv[0:64, 2 * c + 1, :], in_=psb[b][64:128, 0:Tt], func=AF.Identity), writes=[PS(b), ("KT", 2 * c + 1)])

        def tokmajor_out(W, W_s, dst, which):
            for sub in range(nsub):
                stg, stg_s = kvst[sub % 2]
                for half in range(2):
                    b = nb()
                    for kc in range(8):
                        P.op("pe", lambda e, kc=kc, sub=sub, half=half, b=b: e.matmul(psb[b][:, :], lhsT=xT[:, kc, 128 * sub:128 * (sub + 1)], rhs=W[:, kc, 512 * half:512 * (half + 1)],
                                                                                     start=(kc == 0), stop=(kc == 7)), reads=XT + [W_s], writes=[PS(b)])
                    if half == 0:
                        P.op("act", lambda e, b=b, stg=stg: e.activation(out=stg[:, 0:512], in_=psb[b][:, :], func=AF.Identity), writes=[PS(b), (stg_s, 0)])
                    else:
                        P.op("act", lambda e, b=b, stg=stg: e.activation(out=stg[:, 512:1024], in_=psb[b][:, :], func=AF.Identity), writes=[PS(b), (stg_s, 1)])
                P.op("sp", lambda e, sub=sub, stg=stg: e.dma_start(out=dst[128 * sub:128 * (sub + 1), :], in_=stg[:, :]), reads=[(stg_s, 0), (stg_s, 1)], dsem=s_kvo[sub % 2])
                if which == "v":
                    if prompt:
                        for par in range(2):
                            P.op("pool", lambda e, sub=sub, stg=stg, par=par: e.tensor_copy(
                                out=Vs[:, :, sub, 128 * par:128 * par + 64],
                                in_=stg[:, :].rearrange("p (g r) -> p g r", g=8)[:, :, 64 * par:64 * par + 64]),
                                reads=[(stg_s, 0), (stg_s, 1)], writes=[(Vs_s, sub, par)])
                    else:
                        P.op("pool", lambda e, sub=sub, stg=stg: e.tensor_copy(out=Vn[:, sub, :], in_=stg[:, :]), reads=[(stg_s, 0), (stg_s, 1)], writes=[(Vn_s, sub)])

        tokmajor_out(W, W_s, k_dst, "k")
        if prompt:
            Vs, Vs_s = alloc([128, 8, 4, 192], BF16)
            P.op("pool", lambda e: e.memset(Vs[:, :, :, 64:128], 1.0), writes=[(Vs_s, "ones")])
        else:
            Vn = KTs[:, 4096:6144].rearrange("p (s d) -> p s d", s=2)
            Vn_s = "VnP"
        W, W_s = w_get("V")
        tokmajor_out(W, W_s, v_dst, "v")
        if prompt:
            for g in range(8):
                P.op("sp", lambda e, g=g: e.dma_start(out=kt_scr[g, ti, :, :], in_=KTs[0:80, 1024 * g:1024 * (g + 1)]),
                     reads=[("KT", 2 * g), ("KT", 2 * g + 1)] + RK + KTC, writes=[("ktscr", g, ti)], dsem=s_ktw[g])
                P.op("sp", lambda e, g=g: e.dma_start(out=v_scr[g, ti, :, :], in_=Vs[:, g, :, :].rearrange("p b r -> p (b r)")),
                     reads=[(Vs_s, s_, p_) for s_ in range(4) for p_ in range(2)] + [(Vs_s, "ones")], writes=[("vscr", g, ti)], dsem=s_vw[g])

        W, W_s = w_get("GC")
        for c in range(8):
            b = nb()
            for kc in range(8):
                P.op("pe", lambda e, kc=kc, c=c, b=b, W=W: e.matmul(psb[b][:, 0:Tt], lhsT=W[:, kc, 128 * c:128 * (c + 1)], rhs=xT[:, kc, 0:Tt], start=(kc == 0), stop=(kc == 7)),
                     reads=XT + [W_s], writes=[PS(b)])
            P.op("act", lambda e, c=c, b=b: e.activation(out=gated_c[:, c, 0:Tt], in_=psb[b][:, 0:Tt], func=AF.Sigmoid), writes=[PS(b), ("gc", c)])
        def conv_finish():
            sqt = [alloc([128, 512], BF16) for _ in range(2)]
            for c in range(8):
                sq, sq_s = sqt[c % 2]
                P.op("act", lambda e, c=c, sq=sq: e.activation(out=sq[:, 0:Tt], in_=ucT[:, c, 0:Tt], func=AF.Square), reads=[(ucT_s, c)], writes=[sq_s])
                P.op("pe", lambda e, c=c: e.matmul(psb[bmean][:, 0:Tt], lhsT=onesm[:, :], rhs=ucT[:, c, 0:Tt], start=(c == 0), stop=(c == 7)),
                     reads=[(ucT_s, c), "onesm"], writes=[PS(bmean)])
                P.op("pe", lambda e, c=c, sq=sq: e.matmul(psb[bmsq][:, 0:Tt], lhsT=onesm[:, :], rhs=sq[:, 0:Tt], start=(c == 0), stop=(c == 7)),
                     reads=[sq_s, "onesm"], writes=[PS(bmsq)])
            mean_sb, mean_s = alloc([128, 512], F32)
            rstd_sb, rstd_sbs = alloc([128, 512], F32)
            m2, m2_s = acc[0]
            P.op("act", lambda e: e.activation(out=mean_sb[:, 0:Tt], in_=psb[bmean][:, 0:Tt], func=AF.Identity), writes=[PS(bmean), mean_s])
            P.op("pool", lambda e: e.tensor_tensor(out=m2[:, 0:Tt], in0=mean_sb[:, 0:Tt], in1=mean_sb[:, 0:Tt], op=ALU.mult), reads=[mean_s], writes=[m2_s])
            P.op("dve", lambda e: e.tensor_tensor(out=m2[:, 0:Tt], in0=psb[bmsq][:, 0:Tt], in1=m2[:, 0:Tt], op=ALU.subtract), reads=[m2_s], writes=[PS(bmsq), m2_s])
            P.op("act", lambda e: e.activation(out=m2[:, 0:Tt], in_=m2[:, 0:Tt], func=AF.Ln, bias=EPS, scale=1.0), reads=[m2_s], writes=[m2_s])
            P.op("act", lambda e: e.activation(out=rstd_sb[:, 0:Tt], in_=m2[:, 0:Tt], func=AF.Exp, scale=-0.5), reads=[m2_s], writes=[rstd_sbs])
            for c in range(8):
                t, t_s = acc[1 + c % 3]
                P.op("dve", lambda e, c=c, t=t: e.tensor_tensor(out=t[:, 0:Tt], in0=ucT[:, c, 0:Tt], in1=mean_sb[:, 0:Tt], op=ALU.subtract),
                     reads=[(ucT_s, c), mean_s], writes=[t_s])
                P.op("pool", lambda e, t=t: e.tensor_tensor(out=t[:, 0:Tt], in0=t[:, 0:Tt], in1=rstd_sb[:, 0:Tt], op=ALU.mult), reads=[t_s, rstd_sbs], writes=[t_s])
                P.op("act", lambda e, c=c, t=t: e.activation(out=ucT[:, c, 0:Tt], in_=t[:, 0:Tt], func=AF.Silu, scale=cvec[:, c, 1:2], bias=cvec[:, c, 2:3]),
                     reads=[t_s] + CVEC, writes=[(ucT_s, c)])

            if ti == 0:
                tap(('d_' if prompt else 's_') + 'ucT', ucT[:, :, :], [(ucT_s, c) for c in range(8)])
            W, W_s = w_get("CO")
            UCT = [(ucT_s, c) for c in range(8)]
            for c in range(8):
                b = nb()
                for kc in range(8):
                    P.op("pe", lambda e, kc=kc, c=c, b=b, W=W: e.matmul(psb[b][:, 0:Tt], lhsT=W[:, kc, 128 * c:128 * (c + 1)], rhs=ucT[:, kc, 0:Tt], start=(kc == 0), stop=(kc == 7)),
                         reads=UCT + [W_s], writes=[PS(b)])
                P.op("dve", lambda e, c=c, b=b: e.tensor_tensor(out=gated_c[:, c, 0:Tt], in0=psb[b][:, 0:Tt], in1=gated_c[:, c, 0:Tt], op=ALU.mult),
                     reads=[("gc", c)], writes=[PS(b), ("gc", c)])

        if not prompt:
            conv_finish()
        if ti == 0:
            tap(('d_' if prompt else 's_') + 'gc', gated_c[:, :, :], [('gc', c) for c in range(8)])
            tap(('d_' if prompt else 's_') + 'QT', QT[0:80, :, :], [('QT', h) for h in range(NH)] + RQ + ['QTc'])
            tap(('d_' if prompt else 's_') + 'KT', KTs[0:80, :], [('KT', h) for h in range(NH)] + RK + KTC)
        P.end_scope()
        if not prompt:
            P.end_scope("C")
            ar_reset("C")

        ar_reset()
        QTA = [("QT", h) for h in range(NH)] + RQ + ["QTc"]
        rec = [alloc([128, 512], F32) for _ in range(2)]
        PT = [alloc([128, 512], BF16) for _ in range(4)]
        sb_rr = [0]
        pt_rr = [0]
        if prompt:
            kvb = []
            for i in range(NKV):
                ktb, ktb_s = alloc([128, 4, 2, 128], BF16)
                vb, vb_s = alloc([128, 4, 192], BF16)
                kvb.append((ktb, ktb_s, vb, vb_s))
            jobs = [(g, sb) for g in range(8) for sb in range(ti + 1)]
            kvst_ = {"loaded": 0}

            def kv_ensure(n):
                while kvst_["loaded"] <= n and kvst_["loaded"] < len(jobs):
                    m = kvst_["loaded"]
                    g, sb = jobs[m]
                    ktb, ktb_s, vb, vb_s = kvb[m % NKV]
                    P.op("sp", lambda e, g=g, sb=sb, ktb=ktb: e.dma_start(out=ktb[0:80, :, :, :].rearrange("p b h i -> p (b h i)"), in_=kt_scr[g, sb, :, :]),
                         reads=[("ktscr", g, sb)], writes=[ktb_s], dsem=s_kvK[m % NKV])
                    P.op("sp", lambda e, g=g, sb=sb, vb=vb: e.dma_start(out=vb[:, :, :].rearrange("p b r -> p (b r)"), in_=v_scr[g, sb, :, :]),
                         reads=[("vscr", g, sb)], writes=[vb_s], dsem=s_kvV[m % NKV])
                    kvst_["loaded"] += 1

            LA = 3
            steps = []
            for n, (g, sb) in enumerate(jobs):
                ktb, ktb_s, vb, vb_s = kvb[n % NKV]
                ob = [4 + 2 * (g % 2), 5 + 2 * (g % 2)]
                diag = sb == ti
                k_in_job = 0
                for blk in range(4):
                    q0 = 128 * blk if diag else 0
                    for hh in range(2):
                        h = 2 * g + hh
                        sbk = sb_rr[0] % 4
                        sb_rr[0] += 1
                        pt, pt_s = PT[pt_rr[0] % 4]
                        pt_rr[0] += 1
                        first = sb == 0 and blk == 0
                        lastk = diag and blk == 3

                        def front(n=n, k_in_job=k_in_job, blk=blk, hh=hh, h=h, sbk=sbk, q0=q0, ktb=ktb, ktb_s=ktb_s, diag=diag, pt=pt, pt_s=pt_s):
                            if k_in_job == LA:
                                kv_ensure(n + NKV - 1)
                            P.op("pe", lambda e: e.matmul(psb[sbk][:, q0:512], lhsT=ktb[0:80, blk, hh, :], rhs=QT[0:80, h, q0:512], start=True, stop=(not diag)),
                                 reads=[ktb_s] + QTA, writes=[PS(sbk)])
                            if diag:
                                P.op("pe", lambda e: e.matmul(psb[sbk][:, q0:q0 + 128], lhsT=ident_b[:, :], rhs=maskb[:, :], start=False, stop=True),
                                     reads=["ident_b", "maskb"], writes=[PS(sbk)])
                            P.op("act", lambda e: e.activation(out=pt[:, q0:512], in_=psb[sbk][:, q0:512], func=AF.Exp, scale=0.125),
                                 writes=[PS(sbk), pt_s])

                        def back(g=g, blk=blk, hh=hh, q0=q0, pt=pt, pt_s=pt_s, vb=vb, vb_s=vb_s, first=first, lastk=lastk, ob=ob, diag=diag):
                            P.op("pe", lambda e: e.matmul(psb[ob[hh]][:, q0:512], lhsT=vb[:, blk, 64 * hh:64 * hh + 128], rhs=pt[:, q0:512], start=first, stop=lastk),
                                 reads=[vb_s, pt_s], writes=[PS(ob[hh])])
                            if diag and blk == 3 and hh == 1:
                                rc, rc_s = rec[g % 2]
                                P.op("dve", lambda e: e.reciprocal(out=rc[0:64, :], in_=psb[ob[0]][64:128, :]), writes=[PS(ob[0]), (rc_s, 0)])
                                P.op("dve", lambda e: e.tensor_tensor(out=oT[0:64, g, :], in0=psb[ob[0]][0:64, :], in1=rc[0:64, :], op=ALU.mult),
                                     reads=[(rc_s, 0)], writes=[PS(ob[0]), ("oT", g, 0)])
                                P.op("dve", lambda e: e.reciprocal(out=rc[64:128, :], in_=psb[ob[1]][0:64, :]), writes=[PS(ob[1]), (rc_s, 1)])
                                P.op("dve", lambda e: e.tensor_tensor(out=oT[64:128, g, :], in0=psb[ob[1]][64:128, :], in1=rc[64:128, :], op=ALU.mult),
                                     reads=[(rc_s, 1)], writes=[PS(ob[1]), ("oT", g, 1)])
                        steps.append((front, back))
                        k_in_job += 1
            kv_ensure(NKV - 2)
            stride = max(1, min(len(steps) // 8, 40))
            for i in range(len(steps) + LA):
                if i < len(steps):
                    steps[i][0]()
                if i >= LA:
                    steps[i - LA][1]()
                if chains_pending and i % stride == stride - 1:
                    chain(chains_pending.pop(0))
            while chains_pending:
                chain(chains_pending.pop(0))
        else:
            sample_attention(QTA, RK, KTv, Vn, Vn_s, rec, PT)
        P.end_scope()
        if not prompt:
            P.end_scope("C")
            ar_reset("C")

        ar_reset()
        OT = [("oT", g, j) for g in range(8) for j in range(2)]
        gT, gT_s = ucT, ucT_s
        hT, hT_s = alloc([128, NFC, 512], BF16)
        lnt = []
        for _ in range(2):
            st, st_s = alloc([128, 2, 6], F32); mv, mv_s = alloc([128, 2], F32)
            lnv, lnv_s = alloc([128, 1], F32); rstd, rstd_s = alloc([128, 1], F32)
            lnt.append((st, st_s, mv, mv_s, lnv, lnv_s, rstd, rstd_s))
        xb2 = [alloc([128, 1024], BF16) for _ in range(2)]
        tmpf = [alloc([128, 512], F32) for _ in range(4)]
        aext = [alloc([128, nseg, 2 + L], F32) for _ in range(2)]
        pb, pb_s = alloc([128, 4, 256], BF16)
        pT, pT_s = alloc([128, 2, 512], BF16)

        if prompt:
            conv_finish()
        if ti == 0:
            tap(('d_' if prompt else 's_') + 'oT', oT[:, :, :], OT)
        W, W_s = w_get("GA")
        for c in range(8):
            b = nb()
            for kc in range(8):
                P.op("pe", lambda e, kc=kc, c=c, b=b, W=W: e.matmul(psb[b][:, 0:Tt], lhsT=W[:, kc, 128 * c:128 * (c + 1)], rhs=xT[:, kc, 0:Tt], start=(kc == 0), stop=(kc == 7)),
                     reads=XT + [W_s], writes=[PS(b)])
            P.op("act", lambda e, c=c, b=b: e.activation(out=gT[:, c, 0:Tt], in_=psb[b][:, 0:Tt], func=AF.Sigmoid), writes=[PS(b), (gT_s, c)])
        W, W_s = w_get("AO")
        for c in range(8):
            b = nb()
            for kc in range(8):
                P.op("pe", lambda e, kc=kc, c=c, b=b, W=W: e.matmul(psb[b][:, 0:Tt], lhsT=W[:, kc, 128 * c:128 * (c + 1)], rhs=oT[:, kc, 0:Tt], start=(kc == 0), stop=(kc == 7)),
                     reads=OT + [W_s], writes=[PS(b)])
            P.op("dve", lambda e, c=c, b=b: e.tensor_tensor(out=gT[:, c, 0:Tt], in0=psb[b][:, 0:Tt], in1=gT[:, c, 0:Tt], op=ALU.mult),
                 reads=[(gT_s, c)], writes=[PS(b), (gT_s, c)])
            P.op("pool", lambda e, c=c: e.tensor_tensor(out=gT[:, c, 0:Tt], in0=gT[:, c, 0:Tt], in1=gated_c[:, c, 0:Tt], op=ALU.add),
                 reads=[(gT_s, c), ("gc", c)], writes=[(gT_s, c)])
        GT = [(gT_s, c) for c in range(8)]
        if ti == 0:
            tap(('d_' if prompt else 's_') + 'gT', gT[:, :, :], GT)
        W, W_s = w_get("O")
        load_gb(ln1_g, ln1_b)
        for sub in range(nsub):
            for half in range(2):
                b = nb()
                for kc in range(8):
                    P.op("pe", lambda e, kc=kc, sub=sub, half=half, b=b, W=W: e.matmul(psb[b][:, :], lhsT=gT[:, kc, 128 * sub:128 * (sub + 1)], rhs=W[:, kc, 512 * half:512 * (half + 1)],
                                                                                      start=(kc == 0), stop=(kc == 7)), reads=GT + [W_s], writes=[PS(b)])
                P.op("dve", lambda e, sub=sub, half=half, b=b: e.scalar_tensor_tensor(out=xres[:, sub, 512 * half:512 * (half + 1)], in0=xres[:, sub, 512 * half:512 * (half + 1)],
                                                                                   scalar=ALPHA, in1=psb[b][:, :], op0=ALU.mult, op1=ALU.add),
                     reads=[("xres", sub)], writes=[PS(b), ("xres", sub)])
            ln_rows(xres[:, sub, :], [("xres", sub)], sub, lnt[sub % 2])
            to_featmajor(sub, *xb2[sub % 2])

        if ti == 0:
            tap(('d_' if prompt else 's_') + 'x1', xres[:, :, :], [('xres', q_) for q_ in range(4)])
        P.op("pool", lambda e: e.dma_start(out=pb[:, 0:nsub, :], in_=p_src.rearrange("(s p) d -> p s d", p=128)), writes=[pb_s], dsem=s_p)
        for sub in range(nsub):
            b = nb()
            psv = psb[b][:].bitcast(BF16)
            for k2 in range(2):
                P.op("pe", lambda e, sub=sub, k2=k2, psv=psv: e.transpose(out=psv[:, 128 * k2:128 * (k2 + 1)], in_=pb[:, sub, 128 * k2:128 * (k2 + 1)], identity=ident_b[:, :]),
                     reads=[pb_s, "ident_b"], writes=[PS(b)])
            P.op("act", lambda e, sub=sub, psv=psv, b=b: e.activation(out=pT[:, :, 128 * sub:128 * (sub + 1)], in_=psv[:, 0:256].rearrange("p (c t) -> p c t", c=2), func=AF.Identity),
                 writes=[PS(b), (pT_s, sub)])
        PTS = [(pT_s, s) for s in range(nsub)]
        W, W_s = w_get("PG")
        wple, wple_s = w_get("PLE", ahead=0)
        for sub in range(nsub):
            for half in range(2):
                bg_ = nb(); bp = nb()
                for kc in range(8):
                    P.op("pe", lambda e, kc=kc, sub=sub, half=half, bg_=bg_, W=W: e.matmul(psb[bg_][:, :], lhsT=xT[:, kc, 128 * sub:128 * (sub + 1)], rhs=W[:, kc, 512 * half:512 * (half + 1)],
                                                                                          start=(kc == 0), stop=(kc == 7)), reads=XT + [W_s], writes=[PS(bg_)])
                for k2 in range(2):
                    P.op("pe", lambda e, k2=k2, sub=sub, half=half, bp=bp, wple=wple: e.matmul(psb[bp][:, :], lhsT=pT[:, k2, 128 * sub:128 * (sub + 1)], rhs=wple[:, k2, 512 * half:512 * (half + 1)],
                                                                                   start=(k2 == 0), stop=(k2 == 1)), reads=PTS + [wple_s], writes=[PS(bp)])
                tg, tg_s = tmpf[(2 * sub + half) % 2]
                P.op("act", lambda e, bg_=bg_, tg=tg: e.activation(out=tg[:, :], in_=psb[bg_][:, :], func=AF.Sigmoid), writes=[PS(bg_), tg_s])
                P.op("dve", lambda e, bp=bp, tg=tg: e.tensor_tensor(out=tg[:, :], in0=psb[bp][:, :], in1=tg[:, :], op=ALU.mult), reads=[tg_s], writes=[PS(bp), tg_s])
                P.op("dve", lambda e, sub=sub, half=half, tg=tg: e.scalar_tensor_tensor(out=xres[:, sub, 512 * half:512 * (half + 1)], in0=xres[:, sub, 512 * half:512 * (half + 1)],
                                                                                     scalar=ALPHA, in1=tg[:, :], op0=ALU.mult, op1=ALU.add),
                     reads=[("xres", sub), tg_s], writes=[("xres", sub)])

        if not prompt:
            ar["C"] = 2048
            sf, sf_s = alloc([8, DFF], F32, 8, kind="C")
            P.op("sp", lambda e: e.dma_start(out=sf[:, :], in_=st_ffn.rearrange("s r d -> (s r) d")), writes=[sf_s], dsem=s_st[1])
            for c in range(NFC):
                b = nb()
                P.op("pe", lambda e, c=c, b=b: e.transpose(out=psb[b][:, 0:8], in_=sf[0:8, 128 * c:128 * (c + 1)], identity=ident_f[0:8, 0:8]),
                     reads=[sf_s, "ident_f"], writes=[PS(b)])
                P.op("dve", lambda e, c=c, b=b: e.tensor_copy(out=ahist[:, c, :, :], in_=psb[b][:, 0:8].rearrange("p (s r) -> p s r", s=4)), writes=[PS(b), ("ahist", c)])
        for g in range(6):
            W, W_s = w_get(f"UP{g}")
            hw = 512 if g < 5 else 256
            for cc in range(hw // 128):
                c = 4 * g + cc
                ba = nb(); bb = nb()
                for kc in range(8):
                    P.op("pe", lambda e, kc=kc, cc=cc, ba=ba, W=W: e.matmul(psb[ba][:, 0:Tt], lhsT=W[:, kc, 128 * cc:128 * (cc + 1)], rhs=xT[:, kc, 0:Tt], start=(kc == 0), stop=(kc == 7)),
                         reads=XT + [W_s], writes=[PS(ba)])
                for kc in range(8):
                    P.op("pe", lambda e, kc=kc, cc=cc, bb=bb, W=W, hw=hw: e.matmul(psb[bb][:, 0:Tt], lhsT=W[:, kc, hw + 128 * cc:hw + 128 * (cc + 1)], rhs=xT[:, kc, 0:Tt], start=(kc == 0), stop=(kc == 7)),
                         reads=XT + [W_s], writes=[PS(bb)])
                ae, ae_s = aext[c % 2]
                ac, ac_s = tmpf[2 + c % 2]
                P.op("act", lambda e, ba=ba, ae=ae: e.activation(out=ae[:, :, 2:2 + L], in_=seg3(psb[ba][:, 0:Tt]), func=AF.Identity), writes=[PS(ba), ae_s])
                P.op("pool", lambda e, c=c, ae=ae: e.tensor_copy(out=ae[:, :, 0:2], in_=ahist[:, c, 0:nseg, :]), reads=[("ahist", c)], writes=[(ae_s, "h")])
                if prompt:
                    P.op("pool", lambda e, c=c, ae=ae: e.tensor_copy(out=ahist[:, c, 0, :], in_=ae[:, 0, L:L + 2]), reads=[ae_s, (ae_s, "h")], writes=[("ahist", c)])
                AE = [ae_s, (ae_s, "h")]
                P.op("act", lambda e, c=c, ae=ae, ac=ac: e.activation(out=seg3(ac[:, 0:Tt]), in_=ae[:, :, 0:L], func=AF.Identity, scale=fvec[:, c, 0:1], bias=fvec[:, c, 3:4]),
                     reads=AE + FVEC, writes=[ac_s])
                for k in (1, 2):
                    P.op("dve", lambda e, c=c, k=k, ae=ae, ac=ac: e.scalar_tensor_tensor(out=seg3(ac[:, 0:Tt]), in0=ae[:, :, k:k + L], scalar=fvec[:, c, k:k + 1],
                                                                                      in1=seg3(ac[:, 0:Tt]), op0=ALU.mult, op1=ALU.add),
                         reads=AE + FVEC + [ac_s], writes=[ac_s])
                P.op("act", lambda e, ac=ac: e.activation(out=ac[:, 0:Tt], in_=ac[:, 0:Tt], func=AF.Silu), reads=[ac_s], writes=[ac_s])
                P.op("dve", lambda e, c=c, bb=bb, ac=ac: e.tensor_tensor(out=hT[:, c, 0:Tt], in0=psb[bb][:, 0:Tt], in1=ac[:, 0:Tt], op=ALU.mult),
                     reads=[ac_s], writes=[PS(bb), (hT_s, c)])
            if need_state:
                for sub in state_subs:
                    b = nb()
                    for kc in range(8):
                        P.op("pe", lambda e, kc=kc, sub=sub, b=b, W=W, hw=hw: e.matmul(psb[b][:, 0:hw], lhsT=xT[:, kc, 128 * sub:128 * (sub + 1)], rhs=W[:, kc, 0:hw], start=(kc == 0), stop=(kc == 7)),
                             reads=XT + [W_s], writes=[PS(b)])
                    sg_, sg_s = tmpf[sub % 2]
                    P.op("act", lambda e, b=b, sg_=sg_, hw=hw: e.activation(out=sg_[:, 0:hw], in_=psb[b][:, 0:hw], func=AF.Identity), writes=[PS(b), sg_s])
                    if prompt:
                        P.op("pool", lambda e, sg_=sg_, g=g, hw=hw: e.dma_start(out=ff_p[:, 512 * g:512 * g + hw], in_=sg_[126:128, 0:hw]), reads=[sg_s], dsem=s_st[sub % 2])
                    else:
                        for j in range(2):
                            P.op("pool", lambda e, sg_=sg_, g=g, hw=hw, j=j, sub=sub: e.dma_start(out=ff_s[2 * sub + j, :, 512 * g:512 * g + hw], in_=sg_[64 * j + 62:64 * j + 64, 0:hw]),
                                 reads=[sg_s], dsem=s_st[sub % 2])
        HT = [(hT_s, c) for c in range(NFC)]
        if ti == 0:
            tap(('d_' if prompt else 's_') + 'hT', hT[:, :, :], HT)
        load_gb(ln2_g, ln2_b)
        for n in range(2):
            banks = [nb() for _ in range(nsub)]
            for kh in range(2):
                W, W_s = w_get(f"DN{kh}{n}")
                for sub in range(nsub):
                    for j in range(11):
                        P.op("pe", lambda e, j=j, kh=kh, sub=sub, W=W, b=banks[sub]: e.matmul(psb[b][:, :], lhsT=hT[:, 11 * kh + j, 128 * sub:128 * (sub + 1)], rhs=W[:, j, :],
                                                                                             start=(kh == 0 and j == 0), stop=(kh == 1 and j == 10)),
                             reads=HT + [W_s], writes=[PS(banks[sub])])
            for sub in range(nsub):
                P.op("dve", lambda e, sub=sub, n=n, b=banks[sub]: e.tensor_tensor(out=xres[:, sub, 512 * n:512 * (n + 1)], in0=psb[b][:, :], in1=xres[:, sub, 512 * n:512 * (n + 1)], op=ALU.add),
                     reads=[("xres", sub)], writes=[PS(banks[sub]), ("xres", sub)])
        if ti == 0:
            tap(('d_' if prompt else 's_') + 'r2', xres[:, :, :], [('xres', q_) for q_ in range(4)])
        for sub in range(nsub):
            ln_rows(xres[:, sub, :], [("xres", sub)], sub, lnt[sub % 2])
            P.op("sp", lambda e, sub=sub: e.dma_start(out=y_dst[128 * sub:128 * (sub + 1), :], in_=xres[:, sub, :]), reads=[("xres", sub)], dsem=s_y[sub])
        P.end_scope()
        P.end_scope("C")
        ar_reset("C")

    s_ck = [newsem(f"ck{i}") for i in range(2)]
    s_cv = [newsem(f"cv{i}") for i in range(2)]
    s_clf = newsem("clf")
    s_rkh = [newsem(f"rkh{i}") for i in range(2)]
    s_ktc = newsem("ktc")
    s_cks = newsem("cks")
    ck_scr = nc.dram_tensor("ck_scr", [NSTR, NH, 3, PAST], BF16).ap()
    hcar = sbt("hcar", [16, 1], F32)

    def hist_bufs():
        lfh, lfh_s = alloc([128, 16, 16], F32)
        lfT, lfT_s = alloc([16, 2048], F32, 16)
        cH, cH_s = alloc([16, 2048], F32, 16)
        SKh, SKh_s = alloc([16, 3, 2048], BF16, 16)
        return lfh, lfh_s, lfT, lfT_s, cH, cH_s, SKh, SKh_s

    def hist_half(s, hf, bufs, want_split):
        lfh, lfh_s, lfT, lfT_s, cH, cH_s, SKh, SKh_s = bufs
        r0 = 2048 * hf
        if hf == 0:
            P.op("dve", lambda e: e.memset(hcar[:, :], 0.0), writes=["hcar"])
        P.op("sp", lambda e: e.dma_start(out=lfh[:, :, :], in_=cache_lf[s, r0:r0 + 2048, :].rearrange("(b p) h -> p b h", p=128)), writes=[lfh_s], dsem=s_clf)
        for q in range(4):
            b = nb6s()
            for j in range(4):
                blk = 4 * q + j
                P.op("pe", lambda e, blk=blk, j=j, b=b: e.transpose(out=psb[b][0:16, 128 * j:128 * (j + 1)], in_=lfh[:, blk, :], identity=ident_f[:, :]),
                     reads=[lfh_s, "ident_f"], writes=[PS(b)])
            P.op("act", lambda e, q=q, b=b: e.activation(out=lfT[:, 512 * q:512 * (q + 1)], in_=psb[b][0:16, :], func=AF.Identity), writes=[PS(b), (lfT_s, q)])
        for q in range(4):
            ini = hcar[:, 0:1] if q == 0 else cH[:, 512 * q - 1:512 * q]
            rd = ["hcar"] if q == 0 else [(cH_s, q - 1)]
            P.op("dve", lambda e, q=q, ini=ini: e.tensor_tensor_scan(out=cH[:, 512 * q:512 * (q + 1)], data0=ones16[:, 0:512], data1=lfT[:, 512 * q:512 * (q + 1)],
                                                                   initial=ini, op0=ALU.mult, op1=ALU.add), reads=[(lfT_s, q), "ones16"] + rd, writes=[(cH_s, q)])
        CH = [(cH_s, q) for q in range(4)]
        LT = [(lfT_s, q) for q in range(4)]
        P.op("dve", lambda e: e.tensor_copy(out=hcar[:, 0:1], in_=cH[:, 2047:2048]), reads=CH, writes=["hcar"])
        if want_split:
            P.op("dve", lambda e: e.tensor_scalar(out=SKh[:, 0, :], in0=cH[:, :], scalar1=-8.0, scalar2=None, op0=ALU.mult), reads=CH, writes=[(SKh_s, 0)])
            P.op("dve", lambda e: e.scalar_tensor_tensor(out=lfT[:, :], in0=cH[:, :], scalar=-8.0, in1=SKh[:, 0, :], op0=ALU.mult, op1=ALU.subtract),
                 reads=CH + [(SKh_s, 0)], writes=LT)
            P.op("dve", lambda e: e.tensor_copy(out=SKh[:, 1, :], in_=lfT[:, :]), reads=LT, writes=[(SKh_s, 1)])
            P.op("dve", lambda e: e.tensor_tensor(out=cH[:, :], in0=lfT[:, :], in1=SKh[:, 1, :], op=ALU.subtract), reads=LT + [(SKh_s, 1)], writes=CH)
            P.op("dve", lambda e: e.tensor_copy(out=SKh[:, 2, :], in_=cH[:, :]), reads=CH, writes=[(SKh_s, 2)])
            P.op("sp", lambda e: e.dma_start(out=ck_scr[s, :, :, r0:r0 + 2048], in_=SKh[:, :, :]), reads=[(SKh_s, j) for j in range(3)],
                 writes=[("ckscr", s, hf)], dsem=s_cks)

    def sample_prepass():
        ar_reset()
        bufs = hist_bufs()
        for s in range(NSTR):
            for hf in range(2):
                hist_half(s, hf, bufs, False)
            P.op("dve", lambda e, s=s: e.tensor_copy(out=hend[:, s:s + 1], in_=hcar[:, 0:1]), reads=["hcar"], writes=["hend"])
        P.end_scope()

    def sample_attention(QTA, RK, KTv, Vn, Vn_s, rec, PT):
        bufs = hist_bufs()
        kc_ = [alloc([128, 2, 1024], BF16, kind="C") for _ in range(2)]
        vc_ = [alloc([128, 2, 1024], BF16) for _ in range(2)]
        ktc = [alloc([128, 2, 16, 128], BF16, kind="C") for _ in range(2)]
        for i in range(2):
            kt, kt_s = ktc[i]
            P.op("pool", lambda e, kt=kt: e.memset(kt[64:96, :, :, :], 0.0), writes=[(kt_s, "c0")])
            for g in range(4):
                P.op("sp", lambda e, kt=kt, g=g: e.dma_start(out=kt[67:70, :, :, :].rearrange("p b h i -> p (b h i)")[:, 1024 * g:1024 * (g + 1)], in_=ones3[:, :]),
                     reads=["ones3", (kt_s, "c0")], writes=[(kt_s, "c1", g)], dsem=s_ktc)
        KTNEW = [("KT", h) for h in range(NH)] + RK + KTC
        OB = [4, 5]
        rrT = [0]

        def nbT():
            b = 6 + rrT[0] % 2
            rrT[0] += 1
            return b

        for s in range(NSTR):
            for hf in range(2):
                hist_half(s, hf, bufs, True)
            qs = slice(64 * s, 64 * (s + 1))
            pp = 64 * (s % 2)

            def T(grp, s=s):
                kcb, kcb_s = kc_[grp % 2]
                vcb, vcb_s = vc_[grp % 2]
                kt, kt_s = ktc[grp % 2]
                r0 = 256 * grp
                P.op("pool", lambda e: e.dma_start(out=kcb[:, :, :], in_=cache_k[s, r0:r0 + 256, :].rearrange("(b p) d -> p b d", p=128)),
                     writes=[kcb_s], dsem=s_ck[grp % 2])
                P.op("pool", lambda e: e.dma_start(out=vcb[:, :, :], in_=cache_v[s, r0:r0 + 256, :].rearrange("(b p) d -> p b d", p=128)),
                     writes=[vcb_s], dsem=s_cv[grp % 2])
                RKH = [(kt_s, "rk", h) for h in range(NH)]
                for h in range(NH):
                    P.op("sp", lambda e, h=h: e.dma_start(out=kt[64:67, :, h, :], in_=ck_scr[s, h, :, r0:r0 + 256]),
                         reads=[("ckscr", s, r0 // 2048), (kt_s, "c0")], writes=[RKH[h]], dsem=s_rkh[grp % 2])
                for q4 in range(2):
                    b = nbT()
                    psv = psb[b][:].bitcast(BF16)
                    for c4 in range(4):
                        c = 4 * q4 + c4
                        for j in range(2):
                            P.op("pe", lambda e, c=c, c4=c4, j=j, psv=psv: e.transpose(out=psv[:, 256 * c4 + 128 * j:256 * c4 + 128 * (j + 1)], in_=kcb[:, j, 128 * c:128 * (c + 1)], identity=ident_b[:, :]),
                                 reads=[kcb_s, "ident_b"], writes=[PS(b)])
                    for j in range(2):
                        P.op("act", lambda e, q4=q4, j=j, psv=psv: e.activation(
                            out=kt[0:64, j, 8 * q4:8 * q4 + 8, :].rearrange("p (c r) i -> p c r i", r=2)[:, :, 0, :],
                            in_=psv[0:64, :].rearrange("p (c j i) -> p c j i", c=4, j=2)[:, :, j, :], func=AF.Identity),
                            writes=[PS(b)] + [(kt_s, 2 * (4 * q4 + c4)) for c4 in range(4)])
                        P.op("dve", lambda e, q4=q4, j=j, psv=psv: e.tensor_copy(
                            out=kt[0:64, j, 8 * q4:8 * q4 + 8, :].rearrange("p (c r) i -> p c r i", r=2)[:, :, 1, :],
                            in_=psv[64:128, :].rearrange("p (c j i) -> p c j i", c=4, j=2)[:, :, j, :]),
                            writes=[PS(b)] + [(kt_s, 2 * (4 * q4 + c4) + 1) for c4 in range(4)])
                if s == 0 and grp == 0:
                    KTG0 = [(kt_s, h) for h in range(NH)] + RKH + [(kt_s, "c0")] + [(kt_s, "c1", g) for g in range(4)]
                    tap('s_kt0', kt[0:80, :, :, :], KTG0)
                    tap('s_vc0', vcb[:, :, :], [vcb_s])

            def F(grp, blk, bi, s=s, qs=qs, pp=pp):
                new = grp == 16
                if not new:
                    kt, kt_s = ktc[grp % 2]
                    KTG = [(kt_s, h) for h in range(NH)] + [(kt_s, "rk", h) for h in range(NH)] + [(kt_s, "c0")] + [(kt_s, "c1", g) for g in range(4)]
                pts = []
                for hb in range(2):
                    b = nb6s()
                    pt, pt_s = PT[(2 * (bi % 2) + hb) % 4]
                    pts.append((pt, pt_s))
                    for hh in range(8):
                        h = 8 * hb + hh
                        if new:
                            P.op("pe", lambda e, h=h, hh=hh, b=b: e.matmul(psb[b][pp:pp + 64, 64 * hh:64 * (hh + 1)], lhsT=KTv[0:80, h, qs], rhs=QT[0:80, h, qs],
                                                                          start=True, stop=False, skip_group_check=True), reads=KTNEW + QTA, writes=[PS(b)])
                            P.op("pe", lambda e, hh=hh, b=b: e.matmul(psb[b][pp:pp + 64, 64 * hh:64 * (hh + 1)], lhsT=ident_b[:, 0:64], rhs=maskb[:, 0:64],
                                                                     start=False, stop=True, skip_group_check=True), reads=["ident_b", "maskb"], writes=[PS(b)])
                        else:
                            P.op("pe", lambda e, h=h, hh=hh, b=b: e.matmul(psb[b][:, 64 * hh:64 * (hh + 1)], lhsT=kt[0:80, blk, h, :], rhs=QT[0:80, h, qs],
                                                                          start=True, stop=True, skip_group_check=True), reads=KTG + QTA, writes=[PS(b)])
                    if new:
                        P.op("pool", lambda e, pt=pt: e.memset(pt[:, :], 0.0), writes=[pt_s])
                        P.op("act", lambda e, b=b, pt=pt: e.activation(out=pt[pp:pp + 64, :], in_=psb[b][pp:pp + 64, :], func=AF.Exp, scale=0.125), writes=[PS(b), pt_s])
                    else:
                        P.op("act", lambda e, b=b, pt=pt: e.activation(out=pt[:, :], in_=psb[b][:, :], func=AF.Exp, scale=0.125), writes=[PS(b), pt_s])
                if s == 0 and grp == 0 and blk == 0:
                    tap('s_pt0', pts[0][0][:, :], [pts[0][1]])
                return pts

            def B(grp, blk, pts, s=s):
                new = grp == 16
                if not new:
                    vcb, vcb_s = vc_[grp % 2]
                for hb in range(2):
                    pt, pt_s = pts[hb]
                    if grp == 0 and blk == 0:
                        P.op("pe", lambda e, hb=hb, pt=pt: e.matmul(psb[OB[hb]][:, :], lhsT=zerob[:, :], rhs=pt[:, :], start=True, stop=False, skip_group_check=True),
                             reads=["zerob", pt_s], writes=[PS(OB[hb])])
                    for hh in range(8):
                        h = 8 * hb + hh
                        if new:
                            P.op("pe", lambda e, h=h, hh=hh, hb=hb, pt=pt: e.matmul(psb[OB[hb]][0:64, 64 * hh:64 * (hh + 1)], lhsT=Vn[:, s // 2, 64 * h:64 * (h + 1)],
                                                                                   rhs=pt[:, 64 * hh:64 * (hh + 1)], start=False, stop=True, skip_group_check=True),
                                 reads=[(Vn_s, s // 2), pt_s], writes=[PS(OB[hb])])
                        else:
                            P.op("pe", lambda e, h=h, hh=hh, hb=hb, pt=pt: e.matmul(psb[OB[hb]][0:64, 64 * hh:64 * (hh + 1)], lhsT=vcb[:, blk, 64 * h:64 * (h + 1)],
                                                                                   rhs=pt[:, 64 * hh:64 * (hh + 1)], start=False, stop=False, skip_group_check=True),
                                 reads=[vcb_s, pt_s], writes=[PS(OB[hb])])
                    P.op("pe", lambda e, hb=hb, pt=pt: e.matmul(psb[OB[hb]][64:128, :], lhsT=onesb[:, :], rhs=pt[:, :], start=False, stop=new, skip_group_check=True),
                         reads=["onesb", pt_s], writes=[PS(OB[hb])])

            blocks = [(grp, blk) for grp in range(16) for blk in range(2)] + [(16, 0)]
            T(0)
            prev = None
            for bi, (grp, blk) in enumerate(blocks):
                pts = F(grp, blk, bi)
                if prev is not None:
                    B(*prev)
                if blk == 0 and grp + 1 < 16:
                    T(grp + 1)
                prev = (grp, blk, pts)
            B(*prev)

            if debug and s == 0:
                dbgn, dbgn_s = alloc([128, 512], F32)
                P.op("act", lambda e, dbgn=dbgn: e.activation(out=dbgn[:, :], in_=psb[OB[0]][:, :], func=AF.Identity), writes=[PS(OB[0]), dbgn_s])
                tap('s_num0', dbgn[:, :], [dbgn_s])
            for hb in range(2):
                rc, rc_s = rec[hb]
                P.op("dve", lambda e, rc=rc, hb=hb: e.reciprocal(out=rc[0:64, :], in_=psb[OB[hb]][64:128, :]), writes=[PS(OB[hb]), rc_s])
                if s == 0 and hb == 0:
                    tap('s_rc0', rc[0:64, :], [rc_s])
                for par in range(2):
                    P.op("dve", lambda e, rc=rc, hb=hb, par=par, qs=qs: e.tensor_tensor(
                        out=oT[64 * par:64 * par + 64, 4 * hb:4 * hb + 4, qs],
                        in0=psb[OB[hb]][0:64, :].rearrange("p (c r q) -> p c r q", c=4, r=2)[:, :, par, :],
                        in1=rc[0:64, :].rearrange("p (c r q) -> p c r q", c=4, r=2)[:, :, par, :], op=ALU.mult),
                        reads=[rc_s], writes=[PS(OB[hb])] + [("oT", 4 * hb + c, par) for c in range(4)])

    rr6 = [0]

    def nb6s():
        b = rr6[0] % 4
        rr6[0] += 1
        return b

    P.end_scope()
    for ti in range(ntiles):
        run_tile("p", ti)
    if do_sample:
        sample_prepass()
        run_tile("s", 0)
    assert wst["pos"] == len(wseq)
    P.emit(final_sems=allsems)
    return nc, P


_CACHE = {}


def _f32(a):
    return np.ascontiguousarray(np.asarray(a, dtype=np.float32))


def kernel(x_prompt, x_sample, cache_k, cache_v, cache_logf, state_conv, state_ffn_conv,
           p_prompt, p_sample, ln0_g, ln0_b, w_in, b_f, conv_dw_w, conv_dw_b, conv_ln_g,
           conv_ln_b, w_conv_out, w_attn_out, w_o, ln1_g, ln1_b, w_ffn_up, ffn_dw_w, ffn_dw_b,
           w_ffn_down, ln2_g, ln2_b, w_ple, w_ple_gate):
    if "nc" not in _CACHE:
        _CACHE["nc"] = build_nc()[0]
    nc = _CACHE["nc"]
    shared = {
        "ln0_g": _f32(ln0_g), "ln0_b": _f32(ln0_b), "w_in": _f32(w_in)[0], "b_f": _f32(b_f)[0],
        "conv_dw_w": _f32(conv_dw_w)[0], "conv_dw_b": _f32(conv_dw_b)[0], "conv_ln_g": _f32(conv_ln_g)[0],
        "conv_ln_b": _f32(conv_ln_b)[0], "w_conv_out": _f32(w_conv_out)[0], "w_attn_out": _f32(w_attn_out)[0],
        "w_o": _f32(w_o)[0], "ln1_g": _f32(ln1_g)[0], "ln1_b": _f32(ln1_b)[0], "w_ffn_up": _f32(w_ffn_up)[0],
        "ffn_dw_w": _f32(ffn_dw_w)[0], "ffn_dw_b": _f32(ffn_dw_b)[0], "w_ffn_down": _f32(w_ffn_down)[0],
        "ln2_g": _f32(ln2_g)[0], "ln2_b": _f32(ln2_b)[0], "w_ple": _f32(w_ple)[0], "w_ple_gate": _f32(w_ple_gate)[0],
    }
    x_prompt = _f32(x_prompt); p_prompt = _f32(p_prompt)[0]
    x_sample = _f32(x_sample); p_sample = _f32(p_sample)[0]
    ck = _f32(cache_k)[0].reshape(32, PAST, D); cv = _f32(cache_v)[0].reshape(32, PAST, D)
    clf = _f32(cache_logf)[0]; sc = _f32(state_conv)[0]; sf = _f32(state_ffn_conv)[0]
    in_maps = []
    for c in range(NCORES):
        m = dict(shared)
        s0 = NSTR * c
        m.update({
            "x_p": x_prompt[c], "p_p": p_prompt[c],
            "x_s": x_sample[s0:s0 + NSTR].reshape(NSTR * DSEQ, D), "p_s": p_sample[s0:s0 + NSTR].reshape(NSTR * DSEQ, PLE),
            "cache_k": ck[s0:s0 + NSTR], "cache_v": cv[s0:s0 + NSTR], "cache_lf": clf[s0:s0 + NSTR],
            "st_conv": sc[s0:s0 + NSTR], "st_ffn": sf[s0:s0 + NSTR],
        })
        in_maps.append(m)
    res = run_bass_kernel_spmd(nc, in_maps, core_ids=list(range(NCORES)))
    R = res.results

    def cat(name, shape):
        return np.stack([np.asarray(r[name], dtype=np.float32) for r in R], 0).reshape(shape)

    y_p = cat("y_p", (8, SEQ, D))
    y_s = cat("y_s", (32, DSEQ, D))
    k_p = cat("k_p", (1, 8, SEQ, NH, 64)); v_p = cat("v_p", (1, 8, SEQ, NH, 64))
    lf_p = cat("lf_p", (1, 8, SEQ, NH))
    cv_p = cat("cv_p", (1, 8, 30, D)); ff_p = cat("ff_p", (1, 8, 2, DFF))
    k_s = cat("k_s", (1, 32, DSEQ, NH, 64)); v_s = cat("v_s", (1, 32, DSEQ, NH, 64))
    lf_s = cat("lf_s", (1, 32, DSEQ, NH))
    cv_s = cat("cv_s", (1, 32, 30, D)); ff_s = cat("ff_s", (1, 32, 2, DFF))
    return (y_p, y_s, k_p, v_p, lf_p, cv_p, ff_p, k_s, v_s, lf_s, cv_s, ff_s)
```

```python
import numpy as np
import concourse.bass as bass
import concourse.mybir as mybir
from concourse.bass_utils import run_bass_kernel_spmd

F32 = mybir.dt.float32
BF16 = mybir.dt.bfloat16
AF = mybir.ActivationFunctionType
ALU = mybir.AluOpType

ENGS = ("pe", "act", "dve", "pool", "sp")

NCORES = 8
D = 1024
NH = 16
DFF = 2816
NFC = 22
PLE = 256
SEQ = 8192
PAST = 4096
DSEQ = 64
NSTR = 4
NTILES = 16
N_IN = 7184
ALPHA = float(2.0 ** 0.25)
EPS = 1e-5
NEG = -30000.0
NWB = 2
NKV = 3


class DmaSem:
    def __init__(self, handle):
        self.h = handle
        self.count = 0


class Op:
    __slots__ = ("eng", "fn", "deps", "need_inc", "seq", "idx", "dsem", "dval", "is_dma")


class Prog:
    def __init__(self, nc):
        self.nc = nc
        self.streams = {e: [] for e in ENGS}
        self.last_w = {}
        self.readers = {}
        self.scoped = set()
        self.inherit = {}
        self.touched = set()

    @staticmethod
    def _root(k):
        while isinstance(k, tuple):
            k = k[0]
        return k

    def end_scope(self, kind="A"):
        last = {}
        dmas = {}
        keys = [k for k in list(self.last_w.keys()) + list(self.readers.keys()) if self._root(k) == kind]
        ops = []
        for k in set(keys):
            w = self.last_w.pop(k, None)
            if w is not None:
                ops.append(w)
            ops.extend(self.readers.pop(k, ()))
        ops.extend(self.inherit.get(kind, []))
        for o in ops:
            if o.is_dma:
                key = id(o.dsem)
                if key not in dmas or dmas[key].dval < o.dval:
                    dmas[key] = o
            else:
                if o.eng not in last or last[o.eng].idx < o.idx:
                    last[o.eng] = o
        self.inherit[kind] = list(last.values()) + list(dmas.values())
        self.touched = {t for t in self.touched if self._root(t) != kind}

    def op(self, eng, fn, reads=(), writes=(), dsem=None):
        o = Op()
        o.eng = eng
        o.fn = fn
        o.need_inc = False
        o.seq = None
        o.is_dma = dsem is not None
        o.dsem = dsem
        if dsem is not None:
            dsem.count += 16
            o.dval = dsem.count
        else:
            o.dval = None
        deps = {}
        for s in reads:
            w = self.last_w.get(s)
            if w is not None:
                deps[id(w)] = w
        for s in writes:
            w = self.last_w.get(s)
            if w is not None and (w.is_dma or w.eng != eng or o.is_dma):
                deps[id(w)] = w
            for r in self.readers.get(s, ()):
                if r.is_dma or r.eng != eng or o.is_dma:
                    deps[id(r)] = r
            if self._root(s) in self.scoped and s not in self.touched:
                self.touched.add(s)
                for r in self.inherit.get(self._root(s), []):
                    if r.is_dma or r.eng != eng or o.is_dma:
                        deps[id(r)] = r
        o.deps = []
        for d in deps.values():
            if d.is_dma:
                v = d.dsem.count - (16 if d.dsem is dsem else 0)
                o.deps.append((d, v))
            elif not (d.eng == "pe" and eng == "pe" and not o.is_dma):
                d.need_inc = True
                o.deps.append((d, None))
        for s in reads:
            self.readers.setdefault(s, []).append(o)
        for s in writes:
            self.last_w[s] = o
            self.readers[s] = []
        o.idx = len(self.streams[eng])
        self.streams[eng].append(o)
        return o

    def emit(self, final_sems=()):
        nc = self.nc
        sems = {e: nc.alloc_semaphore(name=f"s_{e}") for e in ENGS}
        for e in ENGS:
            c = 0
            for o in self.streams[e]:
                if o.need_inc and not o.is_dma:
                    c += 1
                    o.seq = c
        with nc.Block() as block:
            def body(e):
                def run(engh):
                    waited = {}
                    for o in self.streams[e]:
                        for d, dv in o.deps:
                            if d.is_dma:
                                key = ("d", id(d.dsem))
                                val = dv
                                semh = d.dsem.h
                            else:
                                key = ("e", d.eng)
                                val = d.seq
                                semh = sems[d.eng]
                            if waited.get(key, 0) >= val:
                                continue
                            waited[key] = val
                            engh.wait_ge(semh, val)
                        ins = o.fn(engh)
                        if o.is_dma:
                            ins.then_inc(o.dsem.h, 16)
                        elif o.need_inc:
                            ins.then_inc(sems[e], 1)
                    if e == "sp":
                        for ds in final_sems:
                            if ds.count > 0:
                                engh.wait_ge(ds.h, ds.count)
                return run
            block.tensor(body("pe"))
            block.scalar(body("act"))
            block.vector(body("dve"))
            block.gpsimd(body("pool"))
            block.sync(body("sp"))


def build_nc(ntiles=NTILES, do_sample=True, debug=False):
    nc = bass.Bass("TRN2", target_bir_lowering=False)
    P = Prog(nc)
    allsems = []

    def tap(name, ap, reads):
        if not debug:
            return
        shape = list(ap.shape)
        d = nc.dram_tensor(name, shape, ap.dtype, kind="ExternalOutput").ap()
        P.op("sp", lambda e: e.dma_start(out=d, in_=ap), reads=reads, dsem=newsem("t_" + name))

    def newsem(name):
        s = DmaSem(nc.alloc_semaphore(name=name))
        allsems.append(s)
        return s

    def din(name, shape):
        return nc.dram_tensor(name, shape, F32, kind="ExternalInput").ap()

    def dout(name, shape):
        return nc.dram_tensor(name, shape, F32, kind="ExternalOutput").ap()

    x_p = din("x_p", [SEQ, D]); p_p = din("p_p", [SEQ, PLE])
    x_s = din("x_s", [NSTR * DSEQ, D]); p_s = din("p_s", [NSTR * DSEQ, PLE])
    cache_k = din("cache_k", [NSTR, PAST, D]); cache_v = din("cache_v", [NSTR, PAST, D])
    cache_lf = din("cache_lf", [NSTR, PAST, NH])
    st_conv = din("st_conv", [NSTR, 30, D]); st_ffn = din("st_ffn", [NSTR, 2, DFF])
    ln0_g = din("ln0_g", [D]); ln0_b = din("ln0_b", [D])
    w_in = din("w_in", [D, N_IN]); b_f = din("b_f", [NH])
    conv_dw_w = din("conv_dw_w", [31, D]); conv_dw_b = din("conv_dw_b", [D])
    conv_ln_g = din("conv_ln_g", [D]); conv_ln_b = din("conv_ln_b", [D])
    w_conv_out = din("w_conv_out", [D, D]); w_attn_out = din("w_attn_out", [D, D]); w_o = din("w_o", [D, D])
    ln1_g = din("ln1_g", [D]); ln1_b = din("ln1_b", [D])
    w_ffn_up = din("w_ffn_up", [D, 2 * DFF]); ffn_dw_w = din("ffn_dw_w", [3, DFF]); ffn_dw_b = din("ffn_dw_b", [DFF])
    w_ffn_down = din("w_ffn_down", [DFF, D])
    ln2_g = din("ln2_g", [D]); ln2_b = din("ln2_b", [D])
    w_ple = din("w_ple", [PLE, D]); w_ple_gate = din("w_ple_gate", [D, D])

    y_p = dout("y_p", [SEQ, D]); y_s = dout("y_s", [NSTR * DSEQ, D])
    k_p = dout("k_p", [SEQ, D]); v_p = dout("v_p", [SEQ, D]); lf_p = dout("lf_p", [SEQ, NH])
    cv_p = dout("cv_p", [30, D]); ff_p = dout("ff_p", [2, DFF])
    k_s = dout("k_s", [NSTR * DSEQ, D]); v_s = dout("v_s", [NSTR * DSEQ, D]); lf_s = dout("lf_s", [NSTR * DSEQ, NH])
    cv_s = dout("cv_s", [NSTR, 30, D]); ff_s = dout("ff_s", [NSTR, 2, DFF])

    kt_scr = nc.dram_tensor("kt_scr", [8, NTILES, 80, 1024], BF16).ap()
    v_scr = nc.dram_tensor("v_scr", [8, NTILES, 128, 768], BF16).ap()

    def sbt(name, shape, dt):
        return nc.alloc_sbuf_tensor(name, shape, dt)

    ident_f = sbt("ident_f", [128, 128], F32)
    ident_b = sbt("ident_b", [128, 128], BF16)
    maskb = sbt("maskb", [128, 128], BF16)
    onesm = sbt("onesm", [128, 128], BF16)
    onesb = sbt("onesb", [128, 64], BF16)
    zerob = sbt("zerob", [128, 128], BF16)
    ones16 = sbt("ones16", [16, 512], F32)
    ones3 = sbt("ones3", [3, 1024], BF16)
    cvec = sbt("cvec", [128, 8, 34], F32)
    fvec = sbt("fvec", [128, NFC, 4], F32)
    wfl = sbt("wfl", [128, 8, 16], BF16)
    nbf = sbt("nbf", [16, 1], F32)
    carry = sbt("carry", [16, 1], F32)
    hend = sbt("hend", [16, 4], F32)
    gbuf = sbt("gbuf", [128, 2, 1024], F32)
    wbuf = [sbt(f"wbuf{i}", [128, 8192], BF16) for i in range(NWB)]
    xres = sbt("xres", [128, 4, 1024], F32)
    xT = sbt("xT", [128, 8, 512], BF16)
    oT = sbt("oT", [128, 8, 512], BF16)
    gated_c = sbt("gated_c", [128, 8, 512], BF16)
    QT = sbt("QT", [128, 16, 512], BF16)
    KTs = sbt("KTs", [128, 8192], BF16)
    uhist = sbt("uhist", [128, 8, 4, 30], BF16)
    ahist = sbt("ahist", [128, NFC, 4, 2], F32)
    ARENA_W = 18944
    arena = sbt("arena", [128, ARENA_W], F32)
    psb = [nc.alloc_psum_tensor(f"psb{i}", [128, 512], F32) for i in range(8)]

    P.scoped.add("A")
    C_WORDS = 6400
    P.scoped.add("C")
    ar = {"A": C_WORDS, "C": 0, "n": 0}

    def ar_reset(kind="A"):
        ar[kind] = C_WORDS if kind == "A" else 0

    def alloc(shape, dt, parts=128, kind="A"):
        n = 1
        for s_ in shape[1:]:
            n *= s_
        words = n if dt == F32 else (n + 1) // 2
        words = (words + 7) // 8 * 8
        off = ar[kind]
        lim = ARENA_W if kind == "A" else C_WORDS
        assert off + words <= lim, ("arena overflow", kind, off, words)
        ar[kind] = off + words
        v = arena[0:shape[0], off:off + words]
        if dt == BF16:
            v = v.bitcast(BF16)[:, 0:n]
        else:
            v = v[:, 0:n]
        if len(shape) > 2:
            names = " ".join(f"d{i}" for i in range(1, len(shape)))
            kw = {f"d{i}": shape[i] for i in range(1, len(shape))}
            v = v.rearrange(f"p ({names}) -> p {names}", **kw)
        ar["n"] += 1
        return v, (kind, ar["n"])

    bank_rr = [0]

    def nb():
        b = bank_rr[0] % 8
        bank_rr[0] += 1
        return b

    def PS(b):
        return ("ps", b)

    P.op("pool", lambda e: e.memset(ident_f[:], 1.0), writes=["ident_f"])
    P.op("pool", lambda e: e.affine_select(out=ident_f[:], in_=ident_f[:], pattern=[[1, 128]], compare_op=ALU.is_equal,
                                           fill=0.0, base=0, channel_multiplier=-1), reads=["ident_f"], writes=["ident_f"])
    P.op("pool", lambda e: e.tensor_copy(out=ident_b[:], in_=ident_f[:]), reads=["ident_f"], writes=["ident_b"])
    mask32, mask32_s = alloc([128, 128], F32)
    P.op("pool", lambda e: e.memset(mask32[:], 0.0), writes=[mask32_s])
    P.op("pool", lambda e: e.affine_select(out=mask32[:], in_=mask32[:], pattern=[[1, 128]], compare_op=ALU.is_ge,
                                           fill=NEG, base=0, channel_multiplier=-1), reads=[mask32_s], writes=[mask32_s])
    P.op("pool", lambda e: e.tensor_copy(out=maskb[:], in_=mask32[:]), reads=[mask32_s], writes=["maskb"])
    P.op("pool", lambda e: e.memset(onesm[:], 1.0 / 1024.0), writes=["onesm"])
    P.op("pool", lambda e: e.memset(onesb[:], 1.0), writes=["onesb"])
    P.op("pool", lambda e: e.memset(zerob[:], 0.0), writes=["zerob"])
    if debug:
        P.op("pool", lambda e: e.memset(oT[:, :, :], 7.0), writes=[("oT", g_, j_) for g_ in range(8) for j_ in range(2)])
    P.op("pool", lambda e: e.memset(ones16[:], 1.0), writes=["ones16"])
    P.op("pool", lambda e: e.memset(ones3[:], 1.0), writes=["ones3"])
    P.op("pool", lambda e: e.memset(carry[:], 0.0), writes=["carry"])
    P.op("pool", lambda e: e.memset(uhist[:], 0.0), writes=["uhist"])
    P.op("pool", lambda e: e.memset(ahist[:], 0.0), writes=["ahist"])
    P.op("pool", lambda e: e.memset(QT[64:96, :, :], 1.0), writes=["QTc"])
    P.op("pool", lambda e: e.memset(KTs[64:96, :], 0.0), writes=["KTc0"])
    s_c1 = newsem("c1")
    for g in range(8):
        P.op("sp", lambda e, g=g: e.dma_start(out=KTs[67:70, 1024 * g:1024 * (g + 1)], in_=ones3[:, :]),
             reads=["ones3", "KTc0"], writes=[("KTc1", g)], dsem=s_c1)
    KTC = ["KTc0"] + [("KTc1", g) for g in range(8)]

    vrows, vrows_s = alloc([34, 1024], F32, 34)
    frows, frows_s = alloc([4, DFF], F32, 4)
    bft, bft_s = alloc([16, 1], F32, 16)
    s_c2 = newsem("c2")
    vr = [(vrows_s, "r", i) for i in range(5)]
    P.op("sp", lambda e: e.dma_start(out=vrows[0:1, :], in_=conv_dw_b.rearrange("(o n) -> o n", o=1)), writes=[vr[0], vrows_s], dsem=s_c2)
    P.op("sp", lambda e: e.dma_start(out=vrows[1:2, :], in_=conv_ln_g.rearrange("(o n) -> o n", o=1)), writes=[vr[1]], dsem=s_c2)
    P.op("sp", lambda e: e.dma_start(out=vrows[2:3, :], in_=conv_ln_b.rearrange("(o n) -> o n", o=1)), writes=[vr[2]], dsem=s_c2)
    P.op("sp", lambda e: e.dma_start(out=vrows[3:34, :], in_=conv_dw_w), writes=[vr[3]], dsem=s_c2)
    P.op("sp", lambda e: e.dma_start(out=frows[0:3, :], in_=ffn_dw_w), writes=[vr[4], frows_s], dsem=s_c2)
    P.op("sp", lambda e: e.dma_start(out=frows[3:4, :], in_=ffn_dw_b.rearrange("(o n) -> o n", o=1)), writes=[(frows_s, "r5")], dsem=s_c2)
    P.op("sp", lambda e: e.dma_start(out=bft[:, :], in_=b_f.rearrange("(h o) -> h o", o=1)), writes=[(bft_s, "r6"), bft_s], dsem=s_c2)
    VR = vr + [(frows_s, "r5"), (bft_s, "r6"), vrows_s, frows_s, bft_s]
    P.op("dve", lambda e: e.tensor_scalar(out=nbf[:], in0=bft[:, :], scalar1=-1.0, scalar2=None, op0=ALU.mult),
         reads=VR, writes=["nbf"])
    for c in range(8):
        b = nb()
        P.op("pe", lambda e, c=c, b=b: e.transpose(out=psb[b][:, 0:34], in_=vrows[0:34, 128 * c:128 * (c + 1)], identity=ident_f[0:34, 0:34]),
             reads=VR + ["ident_f"], writes=[PS(b)])
        P.op("dve", lambda e, c=c, b=b: e.tensor_copy(out=cvec[:, c, :], in_=psb[b][:, 0:34]), writes=[PS(b), ("cvec", c)])
    for c in range(NFC):
        b = nb()
        P.op("pe", lambda e, c=c, b=b: e.transpose(out=psb[b][:, 0:4], in_=frows[0:4, 128 * c:128 * (c + 1)], identity=ident_f[0:4, 0:4]),
             reads=VR + ["ident_f"], writes=[PS(b)])
        P.op("dve", lambda e, c=c, b=b: e.tensor_copy(out=fvec[:, c, :], in_=psb[b][:, 0:4]), writes=[PS(b), ("fvec", c)])
    CVEC = [("cvec", c) for c in range(8)]
    FVEC = [("fvec", c) for c in range(NFC)]

    wgroups = {}

    def wsrc(W, r0, nk, c0, w):
        return W[r0:r0 + nk * 128, :].rearrange("(kc p) n -> p kc n", p=128)[:, :, c0:c0 + w]

    def wgroup(name, nk, ncols, srcs):
        scr = nc.dram_tensor("ws_" + name, [128, nk, ncols], BF16).ap()
        sem = newsem("wp_" + name)
        slots = []
        for j, (src, off, w) in enumerate(srcs):
            sl = ("wscr", name, j)
            slots.append(sl)
            P.op("pool", lambda e, src=src, off=off, w=w: e.dma_start(out=scr[:, :, off:off + w], in_=src), writes=[sl], dsem=sem)
        wgroups[name] = (scr, nk, ncols, slots)

    wgroup("FL", 8, 16, [(wsrc(w_in, 0, 8, 5120, 16), 0, 16)])
    wgroup("PLE", 2, 1024, [(wsrc(w_ple, 0, 2, 0, 1024), 0, 1024)])
    wgroup("A1", 8, 1024, [(wsrc(w_in, 0, 8, 0, 512), 0, 512), (wsrc(w_in, 0, 8, 1024, 512), 512, 512)])
    wgroup("A2", 8, 1024, [(wsrc(w_in, 0, 8, 512, 512), 0, 512), (wsrc(w_in, 0, 8, 1536, 512), 512, 512)])
    wgroup("Q", 8, 1024, [(wsrc(w_in, 0, 8, 2048, 1024), 0, 1024)])
    wgroup("K", 8, 1024, [(wsrc(w_in, 0, 8, 3072, 1024), 0, 1024)])
    wgroup("V", 8, 1024, [(wsrc(w_in, 0, 8, 4096, 1024), 0, 1024)])
    wgroup("GC", 8, 1024, [(wsrc(w_in, 0, 8, 5136, 1024), 0, 1024)])
    wgroup("CO", 8, 1024, [(wsrc(w_conv_out, 0, 8, 0, 1024), 0, 1024)])
    wgroup("GA", 8, 1024, [(wsrc(w_in, 0, 8, 6160, 1024), 0, 1024)])
    wgroup("AO", 8, 1024, [(wsrc(w_attn_out, 0, 8, 0, 1024), 0, 1024)])
    wgroup("O", 8, 1024, [(wsrc(w_o, 0, 8, 0, 1024), 0, 1024)])
    wgroup("PG", 8, 1024, [(wsrc(w_ple_gate, 0, 8, 0, 1024), 0, 1024)])
    for g in range(6):
        hw = 512 if g < 5 else 256
        wgroup(f"UP{g}", 8, 2 * hw, [(wsrc(w_ffn_up, 0, 8, 512 * g, hw), 0, hw), (wsrc(w_ffn_up, 0, 8, DFF + 512 * g, hw), hw, hw)])
    for n in range(2):
        for kh in range(2):
            wgroup(f"DN{kh}{n}", 11, 512, [(wsrc(w_ffn_down, kh * 11 * 128, 11, 512 * n, 512), 0, 512)])

    s_wres = newsem("wres")
    P.op("sp", lambda e: e.dma_start(out=wfl[:], in_=wgroups["FL"][0]), reads=wgroups["FL"][3], writes=["wfl"], dsem=s_wres)

    tile_seq = ["A1", "A2", "Q", "K", "V", "GC", "CO", "GA", "AO", "O", "PG", "PLE"] + [f"UP{g}" for g in range(6)] + ["DN00", "DN10", "DN01", "DN11"]
    ntot = ntiles + (1 if do_sample else 0)
    wseq = tile_seq * ntot
    wsem = [newsem(f"wl{i}") for i in range(NWB)]
    wst = {"pos": 0, "loaded": 0}

    def w_ensure(n):
        while wst["loaded"] <= n and wst["loaded"] < len(wseq):
            m = wst["loaded"]
            scr, nk, ncols, slots = wgroups[wseq[m]]
            b = m % NWB
            dst = wbuf[b][:, 0:nk * ncols].rearrange("p (k n) -> p k n", k=nk)
            P.op("sp", lambda e, dst=dst, scr=scr: e.dma_start(out=dst, in_=scr), reads=slots, writes=[("wb", b)], dsem=wsem[b])
            wst["loaded"] += 1

    def w_get(name, ahead=NWB - 1):
        n = wst["pos"]
        assert wseq[n] == name, (wseq[n], name)
        w_ensure(n + ahead)
        scr, nk, ncols, slots = wgroups[name]
        b = n % NWB
        wst["pos"] += 1
        return wbuf[b][:, 0:nk * ncols].rearrange("p (k n) -> p k n", k=nk), ("wb", b)

    s_x = [newsem(f"x{i}") for i in range(2)]
    s_g = newsem("g"); s_b = newsem("b")
    s_y = [newsem(f"y{i}") for i in range(4)]
    s_kvo = [newsem(f"kvo{i}") for i in range(2)]
    s_lf = newsem("lf")
    s_ktw = [newsem(f"ktw{i}") for i in range(8)]
    s_vw = [newsem(f"vw{i}") for i in range(8)]
    s_rq = newsem("rq"); s_rk = newsem("rk")
    s_kvK = [newsem(f"kvK{i}") for i in range(NKV)]
    s_kvV = [newsem(f"kvV{i}") for i in range(NKV)]
    s_p = newsem("p")
    s_st = [newsem(f"st{i}") for i in range(2)]
    s_vst = newsem("vst")

    def run_tile(kind, ti):
        prompt = kind == "p"
        Tt = 512 if prompt else 256
        nsub = Tt // 128
        nseg, L = (1, 512) if prompt else (4, 64)
        t0 = ti * 512
        last = prompt and ti == NTILES - 1
        need_state = last or not prompt
        x_src = x_p[t0:t0 + Tt, :] if prompt else x_s
        p_src = p_p[t0:t0 + Tt, :] if prompt else p_s
        y_dst = y_p[t0:t0 + Tt, :] if prompt else y_s
        k_dst = k_p[t0:t0 + Tt, :] if prompt else k_s
        v_dst = v_p[t0:t0 + Tt, :] if prompt else v_s
        lf_dst = lf_p[t0:t0 + Tt, :] if prompt else lf_s
        state_subs = [3] if prompt else [0, 1]

        def load_gb(g_ap, b_ap):
            P.op("sp", lambda e: e.dma_start(out=gbuf[:, 0, :], in_=g_ap.partition_broadcast(128)), writes=["gb0"], dsem=s_g)
            P.op("sp", lambda e: e.dma_start(out=gbuf[:, 1, :], in_=b_ap.partition_broadcast(128)), writes=["gb1"], dsem=s_b)

        def ln_rows(src, src_slots, sub, tmp):
            st, st_s, mv, mv_s, lnv, lnv_s, rstd, rstd_s = tmp
            XR = ("xres", sub)
            P.op("dve", lambda e: e.bn_stats(out=st[:, 0, :], in_=src[:, 0:512]), reads=src_slots, writes=[st_s])
            P.op("dve", lambda e: e.bn_stats(out=st[:, 1, :], in_=src[:, 512:1024]), reads=src_slots, writes=[(st_s, 1)])
            P.op("dve", lambda e: e.bn_aggr(out=mv[:, :], in_=st[:, :, :].rearrange("p a b -> p (a b)")), reads=[st_s, (st_s, 1)], writes=[mv_s])
            P.op("act", lambda e: e.activation(out=lnv[:, :], in_=mv[:, 1:2], func=AF.Ln, bias=EPS, scale=1.0), reads=[mv_s], writes=[lnv_s])
            P.op("act", lambda e: e.activation(out=rstd[:, :], in_=lnv[:, :], func=AF.Exp, scale=-0.5), reads=[lnv_s], writes=[rstd_s])
            P.op("dve", lambda e: e.tensor_scalar(out=xres[:, sub, :], in0=src, scalar1=mv[:, 0:1], scalar2=rstd[:, 0:1],
                                                  op0=ALU.subtract, op1=ALU.mult), reads=src_slots + [mv_s, rstd_s], writes=[XR])
            P.op("pool", lambda e: e.tensor_tensor(out=xres[:, sub, :], in0=xres[:, sub, :], in1=gbuf[:, 0, :], op=ALU.mult),
                 reads=[XR, "gb0"], writes=[XR])
            P.op("pool", lambda e: e.tensor_tensor(out=xres[:, sub, :], in0=xres[:, sub, :], in1=gbuf[:, 1, :], op=ALU.add),
                 reads=[XR, "gb1"], writes=[XR])

        def to_featmajor(sub, xb, xb_s):
            XR = ("xres", sub)
            P.op("act", lambda e: e.activation(out=xb[:, :], in_=xres[:, sub, :], func=AF.Identity), reads=[XR], writes=[xb_s])
            b = nb()
            psv = psb[b][:].bitcast(BF16)
            for c in range(8):
                P.op("pe", lambda e, c=c: e.transpose(out=psv[:, 128 * c:128 * (c + 1)], in_=xb[:, 128 * c:128 * (c + 1)], identity=ident_b[:, :]),
                     reads=[xb_s, "ident_b"], writes=[PS(b)])
            P.op("dve", lambda e: e.tensor_copy(out=xT[:, :, 128 * sub:128 * (sub + 1)], in_=psv[:, :].rearrange("p (c t) -> p c t", c=8)),
                 writes=[PS(b), ("xT", sub)])

        XT = [("xT", s) for s in range(nsub)]

        ar_reset()
        xin = [alloc([128, 1024], F32) for _ in range(2)]
        xb2 = [alloc([128, 1024], BF16)] * 2
        lnt = []
        for _ in range(2):
            st, st_s = alloc([128, 2, 6], F32); mv, mv_s = alloc([128, 2], F32)
            lnv, lnv_s = alloc([128, 1], F32); rstd, rstd_s = alloc([128, 1], F32)
            lnt.append((st, st_s, mv, mv_s, lnv, lnv_s, rstd, rstd_s))
        kvst = [alloc([128, 1024], F32) for _ in range(2)]

        load_gb(ln0_g, ln0_b)
        for sub in range(nsub):
            xi, xi_s = xin[sub % 2]
            P.op("sp", lambda e, sub=sub, xi=xi: e.dma_start(out=xi[:, :], in_=x_src[128 * sub:128 * (sub + 1), :]), writes=[xi_s], dsem=s_x[sub % 2])
            ln_rows(xi, [xi_s], sub, lnt[sub % 2])
            to_featmajor(sub, *xb2[sub % 2])

        if ti == 0:
            tap(('d_' if prompt else 's_') + 'xT', xT[:, :, :], XT)
        lA, lA_s = alloc([16, 512], F32, 16)
        lB, lB_s = alloc([16, 512], F32, 16)
        cT, cT_s = alloc([16, 512], F32, 16)
        r1, r1_s = alloc([16, 512], F32, 16)
        SQ, SQ_s = alloc([16, 3, 512], BF16, 16)
        SK, SK_s = alloc([16, 3, 512], BF16, 16)
        lft, lft_s = alloc([128, 4, 16], F32)
        b = nb()
        for kc in range(8):
            P.op("pe", lambda e, kc=kc, b=b: e.matmul(psb[b][0:16, 0:Tt], lhsT=wfl[:, kc, :], rhs=xT[:, kc, 0:Tt], start=(kc == 0), stop=(kc == 7)),
                 reads=XT + ["wfl"], writes=[PS(b)])
        P.op("act", lambda e, b=b: e.activation(out=lA[:, 0:Tt], in_=psb[b][0:16, 0:Tt], func=AF.Exp, bias=nbf[:, 0:1], scale=-1.0),
             reads=["nbf"], writes=[PS(b), lA_s])
        P.op("act", lambda e: e.activation(out=lB[:, 0:Tt], in_=lA[:, 0:Tt], func=AF.Ln, bias=1.0, scale=1.0), reads=[lA_s], writes=[lB_s])
        P.op("dve", lambda e: e.tensor_scalar(out=lA[:, 0:Tt], in0=lB[:, 0:Tt], scalar1=-1.0, scalar2=None, op0=ALU.mult),
             reads=[lB_s], writes=[lA_s])
        if prompt:
            P.op("dve", lambda e: e.tensor_tensor_scan(out=cT[:, 0:512], data0=ones16[:, 0:512], data1=lA[:, 0:512], initial=carry[:, 0:1],
                                                       op0=ALU.mult, op1=ALU.add), reads=[lA_s, "ones16", "carry"], writes=[cT_s])
            P.op("dve", lambda e: e.tensor_copy(out=carry[:, 0:1], in_=cT[:, 511:512]), reads=[cT_s], writes=["carry"])
        else:
            for s in range(4):
                P.op("dve", lambda e, s=s: e.tensor_tensor_scan(out=cT[:, 64 * s:64 * (s + 1)], data0=ones16[:, 0:64], data1=lA[:, 64 * s:64 * (s + 1)],
                                                                initial=hend[:, s:s + 1], op0=ALU.mult, op1=ALU.add),
                     reads=[lA_s, "ones16", "hend"], writes=[cT_s])
        P.op("dve", lambda e: e.tensor_scalar(out=SQ[:, 0, 0:Tt], in0=cT[:, 0:Tt], scalar1=8.0, scalar2=None, op0=ALU.mult), reads=[cT_s], writes=[SQ_s])
        P.op("dve", lambda e: e.scalar_tensor_tensor(out=r1[:, 0:Tt], in0=cT[:, 0:Tt], scalar=8.0, in1=SQ[:, 0, 0:Tt], op0=ALU.mult, op1=ALU.subtract),
             reads=[cT_s, SQ_s], writes=[r1_s])
        P.op("dve", lambda e: e.tensor_copy(out=SQ[:, 1, 0:Tt], in_=r1[:, 0:Tt]), reads=[r1_s], writes=[(SQ_s, 1)])
        P.op("dve", lambda e: e.tensor_tensor(out=lB[:, 0:Tt], in0=r1[:, 0:Tt], in1=SQ[:, 1, 0:Tt], op=ALU.subtract), reads=[r1_s, (SQ_s, 1)], writes=[lB_s])
        P.op("dve", lambda e: e.tensor_copy(out=SQ[:, 2, 0:Tt], in_=lB[:, 0:Tt]), reads=[lB_s], writes=[(SQ_s, 2)])
        SQA = [SQ_s, (SQ_s, 1), (SQ_s, 2)]
        P.op("dve", lambda e: e.tensor_scalar(out=SK[:, :, 0:Tt], in0=SQ[:, :, 0:Tt], scalar1=-1.0, scalar2=None, op0=ALU.mult), reads=SQA, writes=[SK_s])
        b = nb()
        for sub in range(nsub):
            P.op("pe", lambda e, sub=sub, b=b: e.transpose(out=psb[b][:, 16 * sub:16 * (sub + 1)], in_=lA[:, 128 * sub:128 * (sub + 1)], identity=ident_f[0:16, 0:16]),
                 reads=[lA_s, "ident_f"], writes=[PS(b)])
        P.op("dve", lambda e, b=b: e.tensor_copy(out=lft[:, 0:nsub, :], in_=psb[b][:, 0:16 * nsub].rearrange("p (s h) -> p s h", h=16)), writes=[PS(b), lft_s])
        P.op("pool", lambda e: e.dma_start(out=lf_dst.rearrange("(s p) h -> p s h", p=128), in_=lft[:, 0:nsub, :]), reads=[lft_s], dsem=s_lf)
        RQ = [("rq", h) for h in range(NH)]
        RK = [("rk", h) for h in range(NH)]
        if prompt:
            KTv = KTs[:, :].rearrange("p (g b h i) -> p g b h i", g=8, b=4, h=2)
        else:
            KTv = KTs[:, 0:4096].rearrange("p (h t) -> p h t", h=16)
        for h in range(NH):
            P.op("pool", lambda e, h=h: e.dma_start(out=QT[67:70, h, 0:Tt], in_=SQ[h:h + 1, :, 0:Tt]), reads=SQA + ["QTc"], writes=[RQ[h]], dsem=s_rq)
            if prompt:
                P.op("pool", lambda e, h=h: e.dma_start(out=KTv[64:67, h // 2, :, h % 2, :], in_=SK[h:h + 1, :, 0:512]), reads=[SK_s] + KTC, writes=[RK[h]], dsem=s_rk)
            else:
                P.op("pool", lambda e, h=h: e.dma_start(out=KTv[64:67, h, :], in_=SK[h:h + 1, :, 0:256]), reads=[SK_s] + KTC, writes=[RK[h]], dsem=s_rk)

        ucT, ucT_s = alloc([128, 8, 512], BF16, kind="C")
        sgt = [alloc([128, 512], BF16) for _ in range(2)]
        acc = [alloc([128, 512], F32, kind="C") for _ in range(4)]

        ust, ust_s = kvst[0]
        bmean = 6
        bmsq = 7
        def nb6():
            while True:
                b = nb()
                if b < 6:
                    return b

        def seg3(ap):
            return ap.rearrange("p (s l) -> p s l", s=nseg)

        if not prompt:
            sc, sc_s = alloc([120, 1024], F32, 120)
            P.op("sp", lambda e: e.dma_start(out=sc[:, :], in_=st_conv.rearrange("s r d -> (s r) d")), writes=[sc_s], dsem=s_st[0])
            for c in range(8):
                b = nb6()
                P.op("pe", lambda e, c=c, b=b: e.transpose(out=psb[b][:, 0:120], in_=sc[0:120, 128 * c:128 * (c + 1)], identity=ident_f[0:120, 0:120]),
                     reads=[sc_s, "ident_f"], writes=[PS(b)])
                P.op("dve", lambda e, c=c, b=b: e.tensor_copy(out=uhist[:, c, :, :], in_=psb[b][:, 0:120].rearrange("p (s r) -> p s r", s=4)),
                     writes=[PS(b), ("uhist", c)])

        utl = [alloc([128, nseg, 30 + L], BF16, kind="C") for _ in range(8)]
        for half in range(2):
            W, W_s = w_get("A1" if half == 0 else "A2")
            for cc in range(4):
                c = 4 * half + cc
                ba = nb6(); bg = nb6()
                for kc in range(8):
                    P.op("pe", lambda e, kc=kc, cc=cc, ba=ba, W=W: e.matmul(psb[ba][:, 0:Tt], lhsT=W[:, kc, 128 * cc:128 * (cc + 1)], rhs=xT[:, kc, 0:Tt], start=(kc == 0), stop=(kc == 7)),
                         reads=XT + [W_s], writes=[PS(ba)])
                for kc in range(8):
                    P.op("pe", lambda e, kc=kc, cc=cc, bg=bg, W=W: e.matmul(psb[bg][:, 0:Tt], lhsT=W[:, kc, 512 + 128 * cc:512 + 128 * (cc + 1)], rhs=xT[:, kc, 0:Tt], start=(kc == 0), stop=(kc == 7)),
                         reads=XT + [W_s], writes=[PS(bg)])
                sg, sg_s = sgt[c % 2]
                u, u_s = utl[c]
                P.op("act", lambda e, bg=bg, sg=sg: e.activation(out=sg[:, 0:Tt], in_=psb[bg][:, 0:Tt], func=AF.Sigmoid), writes=[PS(bg), sg_s])
                P.op("pool", lambda e, c=c, u=u: e.tensor_copy(out=u[:, :, 0:30], in_=uhist[:, c, 0:nseg, :]), reads=[("uhist", c)], writes=[(u_s, "h")])
                P.op("dve", lambda e, ba=ba, sg=sg, u=u: e.tensor_tensor(out=u[:, :, 30:30 + L], in0=seg3(psb[ba][:, 0:Tt]), in1=seg3(sg[:, 0:Tt]), op=ALU.mult),
                     reads=[sg_s], writes=[PS(ba), u_s])
                if prompt:
                    P.op("pool", lambda e, c=c, u=u: e.tensor_copy(out=uhist[:, c, 0, :], in_=u[:, 0, L:L + 30]), reads=[u_s, (u_s, "h")], writes=[("uhist", c)])
            if need_state:
                for sub in state_subs:
                    ba = nb6(); bg = nb6()
                    for kc in range(8):
                        P.op("pe", lambda e, kc=kc, sub=sub, ba=ba, W=W: e.matmul(psb[ba][:, :], lhsT=xT[:, kc, 128 * sub:128 * (sub + 1)], rhs=W[:, kc, 0:512], start=(kc == 0), stop=(kc == 7)),
                             reads=XT + [W_s], writes=[PS(ba)])
                    for kc in range(8):
                        P.op("pe", lambda e, kc=kc, sub=sub, bg=bg, W=W: e.matmul(psb[bg][:, :], lhsT=xT[:, kc, 128 * sub:128 * (sub + 1)], rhs=W[:, kc, 512:1024], start=(kc == 0), stop=(kc == 7)),
                             reads=XT + [W_s], writes=[PS(bg)])
                    ut, ut_s = kvst[sub % 2]
                    sgf, sgf_s = acc[3]
                    P.op("act", lambda e, bg=bg, sgf=sgf: e.activation(out=sgf[:, :], in_=psb[bg][:, :], func=AF.Sigmoid), writes=[PS(bg), sgf_s])
                    P.op("dve", lambda e, ba=ba, sgf=sgf, ut=ut, half=half: e.tensor_tensor(out=ut[:, 512 * half:512 * (half + 1)], in0=psb[ba][:, :], in1=sgf[:, :], op=ALU.mult),
                         reads=[sgf_s], writes=[PS(ba), (ut_s, half)])
                    if prompt:
                        P.op("pool", lambda e, ut=ut, half=half: e.dma_start(out=cv_p[:, 512 * half:512 * (half + 1)], in_=ut[98:128, 512 * half:512 * (half + 1)]),
                             reads=[(ut_s, half)], dsem=s_kvo[sub % 2])
                    else:
                        for j in range(2):
                            P.op("pool", lambda e, ut=ut, half=half, j=j, sub=sub: e.dma_start(out=cv_s[2 * sub + j, :, 512 * half:512 * (half + 1)],
                                                                                           in_=ut[64 * j + 34:64 * j + 64, 512 * half:512 * (half + 1)]),
                                 reads=[(ut_s, half)], dsem=s_kvo[sub % 2])
        def chain(c):
            u, u_s = utl[c]
            US = [u_s, (u_s, "h")]
            a0, a0_s = acc[(2 * c) % 4]
            a1, a1_s = acc[(2 * c + 1) % 4]
            P.op("dve", lambda e, c=c, u=u, a0=a0: e.tensor_scalar(out=seg3(a0[:, 0:Tt]), in0=u[:, :, 0:L], scalar1=cvec[:, c, 3:4], scalar2=cvec[:, c, 0:1], op0=ALU.mult, op1=ALU.add),
                 reads=US + CVEC, writes=[a0_s])
            P.op("dve", lambda e, c=c, u=u, a1=a1: e.tensor_scalar(out=seg3(a1[:, 0:Tt]), in0=u[:, :, 1:1 + L], scalar1=cvec[:, c, 4:5], scalar2=None, op0=ALU.mult),
                 reads=US + CVEC, writes=[a1_s])
            for k in range(2, 31):
                a, a_s = (a0, a0_s) if k % 2 == 0 else (a1, a1_s)
                P.op("dve", lambda e, c=c, k=k, u=u, a=a: e.scalar_tensor_tensor(out=seg3(a[:, 0:Tt]), in0=u[:, :, k:k + L], scalar=cvec[:, c, 3 + k:4 + k],
                                                                              in1=seg3(a[:, 0:Tt]), op0=ALU.mult, op1=ALU.add),
                     reads=US + CVEC + [a_s], writes=[a_s])
            P.op("dve", lambda e, c=c, a0=a0, a1=a1: e.tensor_tensor(out=ucT[:, c, 0:Tt], in0=a0[:, 0:Tt], in1=a1[:, 0:Tt], op=ALU.add), reads=[a0_s, a1_s], writes=[(ucT_s, c)])

        chains_pending = list(range(8))
        if not prompt:
            while chains_pending:
                chain(chains_pending.pop(0))
        W, W_s = w_get("Q")
        for c in range(8):
            b = nb()
            for kc in range(8):
                P.op("pe", lambda e, kc=kc, c=c, b=b, W=W: e.matmul(psb[b][:, 0:Tt], lhsT=W[:, kc, 128 * c:128 * (c + 1)], rhs=xT[:, kc, 0:Tt], start=(kc == 0), stop=(kc == 7)),
                     reads=XT + [W_s], writes=[PS(b)])
            P.op("act", lambda e, c=c, b=b: e.activation(out=QT[0:64, 2 * c, 0:Tt], in_=psb[b][0:64, 0:Tt], func=AF.Identity), writes=[PS(b), ("QT", 2 * c)])
            P.op("act", lambda e, c=c, b=b: e.activation(out=QT[0:64, 2 * c + 1, 0:Tt], in_=psb[b][64:128, 0:Tt], func=AF.Identity), writes=[PS(b), ("QT", 2 * c + 1)])
        W, W_s = w_get("K")
        for c in range(8):
            b = nb()
            for kc in range(8):
                P.op("pe", lambda e, kc=kc, c=c, b=b, W=W: e.matmul(psb[b][:, 0:Tt], lhsT=W[:, kc, 128 * c:128 * (c + 1)], rhs=xT[:, kc, 0:Tt], start=(kc == 0), stop=(kc == 7)),
                     reads=XT + [W_s], writes=[PS(b)])
            if prompt:
                P.op("act", lambda e, c=c, b=b: e.activation(out=KTv[0:64, c, :, 0, :], in_=psb[b][0:64, :].rearrange("p (b i) -> p b i", b=4), func=AF.Identity),
                     writes=[PS(b), ("KT", 2 * c)])
                P.op("act", lambda e, c=c, b=b: e.activation(out=KTv[0:64, c, :, 1, :], in_=psb[b][64:128, :].rearrange("p (b i) -> p b i", b=4), func=AF.Identity),
                     writes=[PS(b), ("KT", 2 * c + 1)])
            else:
                P.op("act", lambda e, c=c, b=b: e.activation(out=KTv[0:64, 2 * c, :], in_=psb[b][0:64, 0:Tt], func=AF.Identity), writes=[PS(b), ("KT", 2 * c)])
                P.op("act", lambda e, c=c, b=b: e.activation(out=KTv[0:64, 2 * c + 1, :], in_=psb[b][64:128, 0:Tt], func=AF.Identity), writes=[PS(b), ("KT", 2 * c + 1)])

        def tokmajor_out(W, W_s, dst, which):
            for sub in range(nsub):
                stg, stg_s = kvst[sub % 2]
                for half in range(2):
                    b = nb()
                    for kc in range(8):
                        P.op("pe", lambda e, kc=kc, sub=sub, half=half, b=b: e.matmul(psb[b][:, :], lhsT=xT[:, kc, 128 * sub:128 * (sub + 1)], rhs=W[:, kc, 512 * half:512 * (half + 1)],
                                                                                     start=(kc == 0), stop=(kc == 7)), reads=XT + [W_s], writes=[PS(b)])
                    if half == 0:
                        P.op("act", lambda e, b=b, stg=stg: e.activation(out=stg[:, 0:512], in_=psb[b][:, :], func=AF.Identity), writes=[PS(b), (stg_s, 0)])
                    else:
                        P.op("act", lambda e, b=b, stg=stg: e.activation(out=stg[:, 512:1024], in_=psb[b][:, :], func=AF.Identity), writes=[PS(b), (stg_s, 1)])
                P.op("sp", lambda e, sub=sub, stg=stg: e.dma_start(out=dst[128 * sub:128 * (sub + 1), :], in_=stg[:, :]), reads=[(stg_s, 0), (stg_s, 1)], dsem=s_kvo[sub % 2])
                if which == "v":
                    if prompt:
                        for par in range(2):
                            P.op("pool", lambda e, sub=sub, stg=stg, par=par: e.tensor_copy(
                                out=Vs[:, :, sub, 128 * par:128 * par + 64],
                                in_=stg[:, :].rearrange("p (g r) -> p g r", g=8)[:, :, 64 * par:64 * par + 64]),
                                reads=[(stg_s, 0), (stg_s, 1)], writes=[(Vs_s, sub, par)])
                    else:
                        P.op("pool", lambda e, sub=sub, stg=stg: e.tensor_copy(out=Vn[:, sub, :], in_=stg[:, :]), reads=[(stg_s, 0), (stg_s, 1)], writes=[(Vn_s, sub)])

        tokmajor_out(W, W_s, k_dst, "k")
        if prompt:
            Vs, Vs_s = alloc([128, 8, 4, 192], BF16)
            P.op("pool", lambda e: e.memset(Vs[:, :, :, 64:128], 1.0), writes=[(Vs_s, "ones")])
        else:
            Vn = KTs[:, 4096:6144].rearrange("p (s d) -> p s d", s=2)
            Vn_s = "VnP"
        W, W_s = w_get("V")
        tokmajor_out(W, W_s, v_dst, "v")
        if prompt:
            for g in range(8):
                P.op("sp", lambda e, g=g: e.dma_start(out=kt_scr[g, ti, :, :], in_=KTs[0:80, 1024 * g:1024 * (g + 1)]),
                     reads=[("KT", 2 * g), ("KT", 2 * g + 1)] + RK + KTC, writes=[("ktscr", g, ti)], dsem=s_ktw[g])
                P.op("sp", lambda e, g=g: e.dma_start(out=v_scr[g, ti, :, :], in_=Vs[:, g, :, :].rearrange("p b r -> p (b r)")),
                     reads=[(Vs_s, s_, p_) for s_ in range(4) for p_ in range(2)] + [(Vs_s, "ones")], writes=[("vscr", g, ti)], dsem=s_vw[g])

        W, W_s = w_get("GC")
        for c in range(8):
            b = nb()
            for kc in range(8):
                P.op("pe", lambda e, kc=kc, c=c, b=b, W=W: e.matmul(psb[b][:, 0:Tt], lhsT=W[:, kc, 128 * c:128 * (c + 1)], rhs=xT[:, kc, 0:Tt], start=(kc == 0), stop=(kc == 7)),
                     reads=XT + [W_s], writes=[PS(b)])
            P.op("act", lambda e, c=c, b=b: e.activation(out=gated_c[:, c, 0:Tt], in_=psb[b][:, 0:Tt], func=AF.Sigmoid), writes=[PS(b), ("gc", c)])
        def conv_finish():
            sqt = [alloc([128, 512], BF16) for _ in range(2)]
            for c in range(8):
                sq, sq_s = sqt[c % 2]
                P.op("act", lambda e, c=c, sq=sq: e.activation(out=sq[:, 0:Tt], in_=ucT[:, c, 0:Tt], func=AF.Square), reads=[(ucT_s, c)], writes=[sq_s])
                P.op("pe", lambda e, c=c: e.matmul(psb[bmean][:, 0:Tt], lhsT=onesm[:, :], rhs=ucT[:, c, 0:Tt], start=(c == 0), stop=(c == 7)),
                     reads=[(ucT_s, c), "onesm"], writes=[PS(bmean)])
                P.op("pe", lambda e, c=c, sq=sq: e.matmul(psb[bmsq][:, 0:Tt], lhsT=onesm[:, :], rhs=sq[:, 0:Tt], start=(c == 0), stop=(c == 7)),
                     reads=[sq_s, "onesm"], writes=[PS(bmsq)])
            mean_sb, mean_s = alloc([128, 512], F32)
            rstd_sb, rstd_sbs = alloc([128, 512], F32)
            m2, m2_s = acc[0]
            P.op("act", lambda e: e.activation(out=mean_sb[:, 0:Tt], in_=psb[bmean][:, 0:Tt], func=AF.Identity), writes=[PS(bmean), mean_s])
            P.op("pool", lambda e: e.tensor_tensor(out=m2[:, 0:Tt], in0=mean_sb[:, 0:Tt], in1=mean_sb[:, 0:Tt], op=ALU.mult), reads=[mean_s], writes=[m2_s])
            P.op("dve", lambda e: e.tensor_tensor(out=m2[:, 0:Tt], in0=psb[bmsq][:, 0:Tt], in1=m2[:, 0:Tt], op=ALU.subtract), reads=[m2_s], writes=[PS(bmsq), m2_s])
            P.op("act", lambda e: e.activation(out=m2[:, 0:Tt], in_=m2[:, 0:Tt], func=AF.Ln, bias=EPS, scale=1.0), reads=[m2_s], writes=[m2_s])
            P.op("act", lambda e: e.activation(out=rstd_sb[:, 0:Tt], in_=m2[:, 0:Tt], func=AF.Exp, scale=-0.5), reads=[m2_s], writes=[rstd_sbs])
            for c in range(8):
                t, t_s = acc[1 + c % 3]
                P.op("dve", lambda e, c=c, t=t: e.tensor_tensor(out=t[:, 0:Tt], in0=ucT[:, c, 0:Tt], in1=mean_sb[:, 0:Tt], op=ALU.subtract),
                     reads=[(ucT_s, c), mean_s], writes=[t_s])
                P.op("pool", lambda e, t=t: e.tensor_tensor(out=t[:, 0:Tt], in0=t[:, 0:Tt], in1=rstd_sb[:, 0:Tt], op=ALU.mult), reads=[t_s, rstd_sbs], writes=[t_s])
                P.op("act", lambda e, c=c, t=t: e.activation(out=ucT[:, c, 0:Tt], in_=t[:, 0:Tt], func=AF.Silu, scale=cvec[:, c, 1:2], bias=cvec[:, c, 2:3]),
                     reads=[t_s] + CVEC, writes=[(ucT_s, c)])

            if ti == 0:
                tap(('d_' if prompt else 's_') + 'ucT', ucT[:, :, :], [(ucT_s, c) for c in range(8)])
            W, W_s = w_get("CO")
            UCT = [(ucT_s, c) for c in range(8)]
            for c in range(8):
                b = nb()
                for kc in range(8):
                    P.op("pe", lambda e, kc=kc, c=c, b=b, W=W: e.matmul(psb[b][:, 0:Tt], lhsT=W[:, kc, 128 * c:128 * (c + 1)], rhs=ucT[:, kc, 0:Tt], start=(kc == 0), stop=(kc == 7)),
                         reads=UCT + [W_s], writes=[PS(b)])
                P.op("dve", lambda e, c=c, b=b: e.tensor_tensor(out=gated_c[:, c, 0:Tt], in0=psb[b][:, 0:Tt], in1=gated_c[:, c, 0:Tt], op=ALU.mult),
                     reads=[("gc", c)], writes=[PS(b), ("gc", c)])

        if not prompt:
            conv_finish()
        if ti == 0:
            tap(('d_' if prompt else 's_') + 'gc', gated_c[:, :, :], [('gc', c) for c in range(8)])
            tap(('d_' if prompt else 's_') + 'QT', QT[0:80, :, :], [('QT', h) for h in range(NH)] + RQ + ['QTc'])
            tap(('d_' if prompt else 's_') + 'KT', KTs[0:80, :], [('KT', h) for h in range(NH)] + RK + KTC)
        P.end_scope()
        if not prompt:
            P.end_scope("C")
            ar_reset("C")

        ar_reset()
        QTA = [("QT", h) for h in range(NH)] + RQ + ["QTc"]
        rec = [alloc([128, 512], F32) for _ in range(2)]
        PT = [alloc([128, 512], BF16) for _ in range(4)]
        sb_rr = [0]
        pt_rr = [0]
        if prompt:
            kvb = []
            for i in range(NKV):
                ktb, ktb_s = alloc([128, 4, 2, 128], BF16)
                vb, vb_s = alloc([128, 4, 192], BF16)
                kvb.append((ktb, ktb_s, vb, vb_s))
            jobs = [(g, sb) for g in range(8) for sb in range(ti + 1)]
            kvst_ = {"loaded": 0}

            def kv_ensure(n):
                while kvst_["loaded"] <= n and kvst_["loaded"] < len(jobs):
                    m = kvst_["loaded"]
                    g, sb = jobs[m]
                    ktb, ktb_s, vb, vb_s = kvb[m % NKV]
                    P.op("sp", lambda e, g=g, sb=sb, ktb=ktb: e.dma_start(out=ktb[0:80, :, :, :].rearrange("p b h i -> p (b h i)"), in_=kt_scr[g, sb, :, :]),
                         reads=[("ktscr", g, sb)], writes=[ktb_s], dsem=s_kvK[m % NKV])
                    P.op("sp", lambda e, g=g, sb=sb, vb=vb: e.dma_start(out=vb[:, :, :].rearrange("p b r -> p (b r)"), in_=v_scr[g, sb, :, :]),
                         reads=[("vscr", g, sb)], writes=[vb_s], dsem=s_kvV[m % NKV])
                    kvst_["loaded"] += 1

            LA = 3
            steps = []
            for n, (g, sb) in enumerate(jobs):
                ktb, ktb_s, vb, vb_s = kvb[n % NKV]
                ob = [4 + 2 * (g % 2), 5 + 2 * (g % 2)]
                diag = sb == ti
                k_in_job = 0
                for blk in range(4):
                    q0 = 128 * blk if diag else 0
                    for hh in range(2):
                        h = 2 * g + hh
                        sbk = sb_rr[0] % 4
                        sb_rr[0] += 1
                        pt, pt_s = PT[pt_rr[0] % 4]
                        pt_rr[0] += 1
                        first = sb == 0 and blk == 0
                        lastk = diag and blk == 3

                        def front(n=n, k_in_job=k_in_job, blk=blk, hh=hh, h=h, sbk=sbk, q0=q0, ktb=ktb, ktb_s=ktb_s, diag=diag, pt=pt, pt_s=pt_s):
                            if k_in_job == LA:
                                kv_ensure(n + NKV - 1)
                            P.op("pe", lambda e: e.matmul(psb[sbk][:, q0:512], lhsT=ktb[0:80, blk, hh, :], rhs=QT[0:80, h, q0:512], start=True, stop=(not diag)),
                                 reads=[ktb_s] + QTA, writes=[PS(sbk)])
                            if diag:
                                P.op("pe", lambda e: e.matmul(psb[sbk][:, q0:q0 + 128], lhsT=ident_b[:, :], rhs=maskb[:, :], start=False, stop=True),
                                     reads=["ident_b", "maskb"], writes=[PS(sbk)])
                            P.op("act", lambda e: e.activation(out=pt[:, q0:512], in_=psb[sbk][:, q0:512], func=AF.Exp, scale=0.125),
                                 writes=[PS(sbk), pt_s])

                        def back(g=g, blk=blk, hh=hh, q0=q0, pt=pt, pt_s=pt_s, vb=vb, vb_s=vb_s, first=first, lastk=lastk, ob=ob, diag=diag):
                            P.op("pe", lambda e: e.matmul(psb[ob[hh]][:, q0:512], lhsT=vb[:, blk, 64 * hh:64 * hh + 128], rhs=pt[:, q0:512], start=first, stop=lastk),
                                 reads=[vb_s, pt_s], writes=[PS(ob[hh])])
                            if diag and blk == 3 and hh == 1:
                                rc, rc_s = rec[g % 2]
                                P.op("dve", lambda e: e.reciprocal(out=rc[0:64, :], in_=psb[ob[0]][64:128, :]), writes=[PS(ob[0]), (rc_s, 0)])
                                P.op("dve", lambda e: e.tensor_tensor(out=oT[0:64, g, :], in0=psb[ob[0]][0:64, :], in1=rc[0:64, :], op=ALU.mult),
                                     reads=[(rc_s, 0)], writes=[PS(ob[0]), ("oT", g, 0)])
                                P.op("dve", lambda e: e.reciprocal(out=rc[64:128, :], in_=psb[ob[1]][0:64, :]), writes=[PS(ob[1]), (rc_s, 1)])
                                P.op("dve", lambda e: e.tensor_tensor(out=oT[64:128, g, :], in0=psb[ob[1]][64:128, :], in1=rc[64:128, :], op=ALU.mult),
                                     reads=[(rc_s, 1)], writes=[PS(ob[1]), ("oT", g, 1)])
                        steps.append((front, back))
                        k_in_job += 1
            kv_ensure(NKV - 2)
            stride = max(1, min(len(steps) // 8, 40))
            for i in range(len(steps) + LA):
                if i < len(steps):
                    steps[i][0]()
                if i >= LA:
                    steps[i - LA][1]()
                if chains_pending and i % stride == stride - 1:
                    chain(chains_pending.pop(0))
            while chains_pending:
                chain(chains_pending.pop(0))
        else:
            sample_attention(QTA, RK, KTv, Vn, Vn_s, rec, PT)
        P.end_scope()
        if not prompt:
            P.end_scope("C")
            ar_reset("C")

        ar_reset()
        OT = [("oT", g, j) for g in range(8) for j in range(2)]
        gT, gT_s = ucT, ucT_s
        hT, hT_s = alloc([128, NFC, 512], BF16)
        lnt = []
        for _ in range(2):
            st, st_s = alloc([128, 2, 6], F32); mv, mv_s = alloc([128, 2], F32)
            lnv, lnv_s = alloc([128, 1], F32); rstd, rstd_s = alloc([128, 1], F32)
            lnt.append((st, st_s, mv, mv_s, lnv, lnv_s, rstd, rstd_s))
        xb2 = [alloc([128, 1024], BF16) for _ in range(2)]
        tmpf = [alloc([128, 512], F32) for _ in range(4)]
        aext = [alloc([128, nseg, 2 + L], F32) for _ in range(2)]
        pb, pb_s = alloc([128, 4, 256], BF16)
        pT, pT_s = alloc([128, 2, 512], BF16)

        if prompt:
            conv_finish()
        if ti == 0:
            tap(('d_' if prompt else 's_') + 'oT', oT[:, :, :], OT)
        W, W_s = w_get("GA")
        for c in range(8):
            b = nb()
            for kc in range(8):
                P.op("pe", lambda e, kc=kc, c=c, b=b, W=W: e.matmul(psb[b][:, 0:Tt], lhsT=W[:, kc, 128 * c:128 * (c + 1)], rhs=xT[:, kc, 0:Tt], start=(kc == 0), stop=(kc == 7)),
                     reads=XT + [W_s], writes=[PS(b)])
            P.op("act", lambda e, c=c, b=b: e.activation(out=gT[:, c, 0:Tt], in_=psb[b][:, 0:Tt], func=AF.Sigmoid), writes=[PS(b), (gT_s, c)])
        W, W_s = w_get("AO")
        for c in range(8):
            b = nb()
            for kc in range(8):
                P.op("pe", lambda e, kc=kc, c=c, b=b, W=W: e.matmul(psb[b][:, 0:Tt], lhsT=W[:, kc, 128 * c:128 * (c + 1)], rhs=oT[:, kc, 0:Tt], start=(kc == 0), stop=(kc == 7)),
                     reads=OT + [W_s], writes=[PS(b)])
            P.op("dve", lambda e, c=c, b=b: e.tensor_tensor(out=gT[:, c, 0:Tt], in0=psb[b][:, 0:Tt], in1=gT[:, c, 0:Tt], op=ALU.mult),
                 reads=[(gT_s, c)], writes=[PS(b), (gT_s, c)])
            P.op("pool", lambda e, c=c: e.tensor_tensor(out=gT[:, c, 0:Tt], in0=gT[:, c, 0:Tt], in1=gated_c[:, c, 0:Tt], op=ALU.add),
                 reads=[(gT_s, c), ("gc", c)], writes=[(gT_s, c)])
        GT = [(gT_s, c) for c in range(8)]
        if ti == 0:
            tap(('d_' if prompt else 's_') + 'gT', gT[:, :, :], GT)
        W, W_s = w_get("O")
        load_gb(ln1_g, ln1_b)
        for sub in range(nsub):
            for half in range(2):
                b = nb()
                for kc in range(8):
                    P.op("pe", lambda e, kc=kc, sub=sub, half=half, b=b, W=W: e.matmul(psb[b][:, :], lhsT=gT[:, kc, 128 * sub:128 * (sub + 1)], rhs=W[:, kc, 512 * half:512 * (half + 1)],
                                                                                      start=(kc == 0), stop=(kc == 7)), reads=GT + [W_s], writes=[PS(b)])
                P.op("dve", lambda e, sub=sub, half=half, b=b: e.scalar_tensor_tensor(out=xres[:, sub, 512 * half:512 * (half + 1)], in0=xres[:, sub, 512 * half:512 * (half + 1)],
                                                                                   scalar=ALPHA, in1=psb[b][:, :], op0=ALU.mult, op1=ALU.add),
                     reads=[("xres", sub)], writes=[PS(b), ("xres", sub)])
            ln_rows(xres[:, sub, :], [("xres", sub)], sub, lnt[sub % 2])
            to_featmajor(sub, *xb2[sub % 2])

        if ti == 0:
            tap(('d_' if prompt else 's_') + 'x1', xres[:, :, :], [('xres', q_) for q_ in range(4)])
        P.op("pool", lambda e: e.dma_start(out=pb[:, 0:nsub, :], in_=p_src.rearrange("(s p) d -> p s d", p=128)), writes=[pb_s], dsem=s_p)
        for sub in range(nsub):
            b = nb()
            psv = psb[b][:].bitcast(BF16)
            for k2 in range(2):
                P.op("pe", lambda e, sub=sub, k2=k2, psv=psv: e.transpose(out=psv[:, 128 * k2:128 * (k2 + 1)], in_=pb[:, sub, 128 * k2:128 * (k2 + 1)], identity=ident_b[:, :]),
                     reads=[pb_s, "ident_b"], writes=[PS(b)])
            P.op("act", lambda e, sub=sub, psv=psv, b=b: e.activation(out=pT[:, :, 128 * sub:128 * (sub + 1)], in_=psv[:, 0:256].rearrange("p (c t) -> p c t", c=2), func=AF.Identity),
                 writes=[PS(b), (pT_s, sub)])
        PTS = [(pT_s, s) for s in range(nsub)]
        W, W_s = w_get("PG")
        wple, wple_s = w_get("PLE", ahead=0)
        for sub in range(nsub):
            for half in range(2):
                bg_ = nb(); bp = nb()
                for kc in range(8):
                    P.op("pe", lambda e, kc=kc, sub=sub, half=half, bg_=bg_, W=W: e.matmul(psb[bg_][:, :], lhsT=xT[:, kc, 128 * sub:128 * (sub + 1)], rhs=W[:, kc, 512 * half:512 * (half + 1)],
                                                                                          start=(kc == 0), stop=(kc == 7)), reads=XT + [W_s], writes=[PS(bg_)])
                for k2 in range(2):
                    P.op("pe", lambda e, k2=k2, sub=sub, half=half, bp=bp, wple=wple: e.matmul(psb[bp][:, :], lhsT=pT[:, k2, 128 * sub:128 * (sub + 1)], rhs=wple[:, k2, 512 * half:512 * (half + 1)],
                                                                                   start=(k2 == 0), stop=(k2 == 1)), reads=PTS + [wple_s], writes=[PS(bp)])
                tg, tg_s = tmpf[(2 * sub + half) % 2]
                P.op("act", lambda e, bg_=bg_, tg=tg: e.activation(out=tg[:, :], in_=psb[bg_][:, :], func=AF.Sigmoid), writes=[PS(bg_), tg_s])
                P.op("dve", lambda e, bp=bp, tg=tg: e.tensor_tensor(out=tg[:, :], in0=psb[bp][:, :], in1=tg[:, :], op=ALU.mult), reads=[tg_s], writes=[PS(bp), tg_s])
                P.op("dve", lambda e, sub=sub, half=half, tg=tg: e.scalar_tensor_tensor(out=xres[:, sub, 512 * half:512 * (half + 1)], in0=xres[:, sub, 512 * half:512 * (half + 1)],
                                                                                     scalar=ALPHA, in1=tg[:, :], op0=ALU.mult, op1=ALU.add),
                     reads=[("xres", sub), tg_s], writes=[("xres", sub)])

        if not prompt:
            ar["C"] = 2048
            sf, sf_s = alloc([8, DFF], F32, 8, kind="C")
            P.op("sp", lambda e: e.dma_start(out=sf[:, :], in_=st_ffn.rearrange("s r d -> (s r) d")), writes=[sf_s], dsem=s_st[1])
            for c in range(NFC):
                b = nb()
                P.op("pe", lambda e, c=c, b=b: e.transpose(out=psb[b][:, 0:8], in_=sf[0:8, 128 * c:128 * (c + 1)], identity=ident_f[0:8, 0:8]),
                     reads=[sf_s, "ident_f"], writes=[PS(b)])
                P.op("dve", lambda e, c=c, b=b: e.tensor_copy(out=ahist[:, c, :, :], in_=psb[b][:, 0:8].rearrange("p (s r) -> p s r", s=4)), writes=[PS(b), ("ahist", c)])
        for g in range(6):
            W, W_s = w_get(f"UP{g}")
            hw = 512 if g < 5 else 256
            for cc in range(hw // 128):
                c = 4 * g + cc
                ba = nb(); bb = nb()
                for kc in range(8):
                    P.op("pe", lambda e, kc=kc, cc=cc, ba=ba, W=W: e.matmul(psb[ba][:, 0:Tt], lhsT=W[:, kc, 128 * cc:128 * (cc + 1)], rhs=xT[:, kc, 0:Tt], start=(kc == 0), stop=(kc == 7)),
                         reads=XT + [W_s], writes=[PS(ba)])
                for kc in range(8):
                    P.op("pe", lambda e, kc=kc, cc=cc, bb=bb, W=W, hw=hw: e.matmul(psb[bb][:, 0:Tt], lhsT=W[:, kc, hw + 128 * cc:hw + 128 * (cc + 1)], rhs=xT[:, kc, 0:Tt], start=(kc == 0), stop=(kc == 7)),
                         reads=XT + [W_s], writes=[PS(bb)])
                ae, ae_s = aext[c % 2]
                ac, ac_s = tmpf[2 + c % 2]
                P.op("act", lambda e, ba=ba, ae=ae: e.activation(out=ae[:, :, 2:2 + L], in_=seg3(psb[ba][:, 0:Tt]), func=AF.Identity), writes=[PS(ba), ae_s])
                P.op("pool", lambda e, c=c, ae=ae: e.tensor_copy(out=ae[:, :, 0:2], in_=ahist[:, c, 0:nseg, :]), reads=[("ahist", c)], writes=[(ae_s, "h")])
                if prompt:
                    P.op("pool", lambda e, c=c, ae=ae: e.tensor_copy(out=ahist[:, c, 0, :], in_=ae[:, 0, L:L + 2]), reads=[ae_s, (ae_s, "h")], writes=[("ahist", c)])
                AE = [ae_s, (ae_s, "h")]
                P.op("act", lambda e, c=c, ae=ae, ac=ac: e.activation(out=seg3(ac[:, 0:Tt]), in_=ae[:, :, 0:L], func=AF.Identity, scale=fvec[:, c, 0:1], bias=fvec[:, c, 3:4]),
                     reads=AE + FVEC, writes=[ac_s])
                for k in (1, 2):
                    P.op("dve", lambda e, c=c, k=k, ae=ae, ac=ac: e.scalar_tensor_tensor(out=seg3(ac[:, 0:Tt]), in0=ae[:, :, k:k + L], scalar=fvec[:, c, k:k + 1],
                                                                                      in1=seg3(ac[:, 0:Tt]), op0=ALU.mult, op1=ALU.add),
                         reads=AE + FVEC + [ac_s], writes=[ac_s])
                P.op("act", lambda e, ac=ac: e.activation(out=ac[:, 0:Tt], in_=ac[:, 0:Tt], func=AF.Silu), reads=[ac_s], writes=[ac_s])
                P.op("dve", lambda e, c=c, bb=bb, ac=ac: e.tensor_tensor(out=hT[:, c, 0:Tt], in0=psb[bb][:, 0:Tt], in1=ac[:, 0:Tt], op=ALU.mult),
                     reads=[ac_s], writes=[PS(bb), (hT_s, c)])
            if need_state:
                for sub in state_subs:
                    b = nb()
                    for kc in range(8):
                        P.op("pe", lambda e, kc=kc, sub=sub, b=b, W=W, hw=hw: e.matmul(psb[b][:, 0:hw], lhsT=xT[:, kc, 128 * sub:128 * (sub + 1)], rhs=W[:, kc, 0:hw], start=(kc == 0), stop=(kc == 7)),
                             reads=XT + [W_s], writes=[PS(b)])
                    sg_, sg_s = tmpf[sub % 2]
                    P.op("act", lambda e, b=b, sg_=sg_, hw=hw: e.activation(out=sg_[:, 0:hw], in_=psb[b][:, 0:hw], func=AF.Identity), writes=[PS(b), sg_s])
                    if prompt:
                        P.op("pool", lambda e, sg_=sg_, g=g, hw=hw: e.dma_start(out=ff_p[:, 512 * g:512 * g + hw], in_=sg_[126:128, 0:hw]), reads=[sg_s], dsem=s_st[sub % 2])
                    else:
                        for j in range(2):
                            P.op("pool", lambda e, sg_=sg_, g=g, hw=hw, j=j, sub=sub: e.dma_start(out=ff_s[2 * sub + j, :, 512 * g:512 * g + hw], in_=sg_[64 * j + 62:64 * j + 64, 0:hw]),
                                 reads=[sg_s], dsem=s_st[sub % 2])
        HT = [(hT_s, c) for c in range(NFC)]
        if ti == 0:
            tap(('d_' if prompt else 's_') + 'hT', hT[:, :, :], HT)
        load_gb(ln2_g, ln2_b)
        for n in range(2):
            banks = [nb() for _ in range(nsub)]
            for kh in range(2):
                W, W_s = w_get(f"DN{kh}{n}")
                for sub in range(nsub):
                    for j in range(11):
                        P.op("pe", lambda e, j=j, kh=kh, sub=sub, W=W, b=banks[sub]: e.matmul(psb[b][:, :], lhsT=hT[:, 11 * kh + j, 128 * sub:128 * (sub + 1)], rhs=W[:, j, :],
                                                                                             start=(kh == 0 and j == 0), stop=(kh == 1 and j == 10)),
                             reads=HT + [W_s], writes=[PS(banks[sub])])
            for sub in range(nsub):
                P.op("dve", lambda e, sub=sub, n=n, b=banks[sub]: e.tensor_tensor(out=xres[:, sub, 512 * n:512 * (n + 1)], in0=psb[b][:, :], in1=xres[:, sub, 512 * n:512 * (n + 1)], op=ALU.add),
                     reads=[("xres", sub)], writes=[PS(banks[sub]), ("xres", sub)])
        if ti == 0:
            tap(('d_' if prompt else 's_') + 'r2', xres[:, :, :], [('xres', q_) for q_ in range(4)])
        for sub in range(nsub):
            ln_rows(xres[:, sub, :], [("xres", sub)], sub, lnt[sub % 2])
            P.op("sp", lambda e, sub=sub: e.dma_start(out=y_dst[128 * sub:128 * (sub + 1), :], in_=xres[:, sub, :]), reads=[("xres", sub)], dsem=s_y[sub])
        P.end_scope()
        P.end_scope("C")
        ar_reset("C")

    s_ck = [newsem(f"ck{i}") for i in range(2)]
    s_cv = [newsem(f"cv{i}") for i in range(2)]
    s_clf = newsem("clf")
    s_rkh = [newsem(f"rkh{i}") for i in range(2)]
    s_ktc = newsem("ktc")
    s_cks = newsem("cks")
    ck_scr = nc.dram_tensor("ck_scr", [NSTR, NH, 3, PAST], BF16).ap()
    hcar = sbt("hcar", [16, 1], F32)

    def hist_bufs():
        lfh, lfh_s = alloc([128, 16, 16], F32)
        lfT, lfT_s = alloc([16, 2048], F32, 16)
        cH, cH_s = alloc([16, 2048], F32, 16)
        SKh, SKh_s = alloc([16, 3, 2048], BF16, 16)
        return lfh, lfh_s, lfT, lfT_s, cH, cH_s, SKh, SKh_s

    def hist_half(s, hf, bufs, want_split):
        lfh, lfh_s, lfT, lfT_s, cH, cH_s, SKh, SKh_s = bufs
        r0 = 2048 * hf
        if hf == 0:
            P.op("dve", lambda e: e.memset(hcar[:, :], 0.0), writes=["hcar"])
        P.op("sp", lambda e: e.dma_start(out=lfh[:, :, :], in_=cache_lf[s, r0:r0 + 2048, :].rearrange("(b p) h -> p b h", p=128)), writes=[lfh_s], dsem=s_clf)
        for q in range(4):
            b = nb6s()
            for j in range(4):
                blk = 4 * q + j
                P.op("pe", lambda e, blk=blk, j=j, b=b: e.transpose(out=psb[b][0:16, 128 * j:128 * (j + 1)], in_=lfh[:, blk, :], identity=ident_f[:, :]),
                     reads=[lfh_s, "ident_f"], writes=[PS(b)])
            P.op("act", lambda e, q=q, b=b: e.activation(out=lfT[:, 512 * q:512 * (q + 1)], in_=psb[b][0:16, :], func=AF.Identity), writes=[PS(b), (lfT_s, q)])
        for q in range(4):
            ini = hcar[:, 0:1] if q == 0 else cH[:, 512 * q - 1:512 * q]
            rd = ["hcar"] if q == 0 else [(cH_s, q - 1)]
            P.op("dve", lambda e, q=q, ini=ini: e.tensor_tensor_scan(out=cH[:, 512 * q:512 * (q + 1)], data0=ones16[:, 0:512], data1=lfT[:, 512 * q:512 * (q + 1)],
                                                                   initial=ini, op0=ALU.mult, op1=ALU.add), reads=[(lfT_s, q), "ones16"] + rd, writes=[(cH_s, q)])
        CH = [(cH_s, q) for q in range(4)]
        LT = [(lfT_s, q) for q in range(4)]
        P.op("dve", lambda e: e.tensor_copy(out=hcar[:, 0:1], in_=cH[:, 2047:2048]), reads=CH, writes=["hcar"])
        if want_split:
            P.op("dve", lambda e: e.tensor_scalar(out=SKh[:, 0, :], in0=cH[:, :], scalar1=-8.0, scalar2=None, op0=ALU.mult), reads=CH, writes=[(SKh_s, 0)])
            P.op("dve", lambda e: e.scalar_tensor_tensor(out=lfT[:, :], in0=cH[:, :], scalar=-8.0, in1=SKh[:, 0, :], op0=ALU.mult, op1=ALU.subtract),
                 reads=CH + [(SKh_s, 0)], writes=LT)
            P.op("dve", lambda e: e.tensor_copy(out=SKh[:, 1, :], in_=lfT[:, :]), reads=LT, writes=[(SKh_s, 1)])
            P.op("dve", lambda e: e.tensor_tensor(out=cH[:, :], in0=lfT[:, :], in1=SKh[:, 1, :], op=ALU.subtract), reads=LT + [(SKh_s, 1)], writes=CH)
            P.op("dve", lambda e: e.tensor_copy(out=SKh[:, 2, :], in_=cH[:, :]), reads=CH, writes=[(SKh_s, 2)])
            P.op("sp", lambda e: e.dma_start(out=ck_scr[s, :, :, r0:r0 + 2048], in_=SKh[:, :, :]), reads=[(SKh_s, j) for j in range(3)],
                 writes=[("ckscr", s, hf)], dsem=s_cks)

    def sample_prepass():
        ar_reset()
        bufs = hist_bufs()
        for s in range(NSTR):
            for hf in range(2):
                hist_half(s, hf, bufs, True)
            P.op("dve", lambda e, s=s: e.tensor_copy(out=hend[:, s:s + 1], in_=hcar[:, 0:1]), reads=["hcar"], writes=["hend"])
        P.end_scope()

    def sample_attention(QTA, RK, KTv, Vn, Vn_s, rec, PT):
        kc_ = [alloc([128, 2, 1024], BF16, kind="C") for _ in range(2)]
        vc_ = [alloc([128, 2, 1024], BF16) for _ in range(2)]
        ktc = [alloc([128, 2, 16, 128], BF16, kind="C") for _ in range(2)]
        for i in range(2):
            kt, kt_s = ktc[i]
            P.op("pool", lambda e, kt=kt: e.memset(kt[64:96, :, :, :], 0.0), writes=[(kt_s, "c0")])
            for g in range(4):
                P.op("sp", lambda e, kt=kt, g=g: e.dma_start(out=kt[67:70, :, :, :].rearrange("p b h i -> p (b h i)")[:, 1024 * g:1024 * (g + 1)], in_=ones3[:, :]),
                     reads=["ones3", (kt_s, "c0")], writes=[(kt_s, "c1", g)], dsem=s_ktc)
        KTNEW = [("KT", h) for h in range(NH)] + RK + KTC
        OB = [4, 5]
        rrT = [0]

        def nbT():
            b = 6 + rrT[0] % 2
            rrT[0] += 1
            return b

        for s in range(NSTR):
            qs = slice(64 * s, 64 * (s + 1))
            pp = 64 * (s % 2)

            def T(grp, s=s):
                kcb, kcb_s = kc_[grp % 2]
                vcb, vcb_s = vc_[grp % 2]
                kt, kt_s = ktc[grp % 2]
                r0 = 256 * grp
                P.op("pool", lambda e: e.dma_start(out=kcb[:, :, :], in_=cache_k[s, r0:r0 + 256, :].rearrange("(b p) d -> p b d", p=128)),
                     writes=[kcb_s], dsem=s_ck[grp % 2])
                P.op("pool", lambda e: e.dma_start(out=vcb[:, :, :], in_=cache_v[s, r0:r0 + 256, :].rearrange("(b p) d -> p b d", p=128)),
                     writes=[vcb_s], dsem=s_cv[grp % 2])
                RKH = [(kt_s, "rk", h) for h in range(NH)]
                for h in range(NH):
                    P.op("sp", lambda e, h=h: e.dma_start(out=kt[64:67, :, h, :], in_=ck_scr[s, h, :, r0:r0 + 256]),
                         reads=[("ckscr", s, r0 // 2048), (kt_s, "c0")], writes=[RKH[h]], dsem=s_rkh[grp % 2])
                for q4 in range(2):
                    b = nbT()
                    psv = psb[b][:].bitcast(BF16)
                    for c4 in range(4):
                        c = 4 * q4 + c4
                        for j in range(2):
                            P.op("pe", lambda e, c=c, c4=c4, j=j, psv=psv: e.transpose(out=psv[:, 256 * c4 + 128 * j:256 * c4 + 128 * (j + 1)], in_=kcb[:, j, 128 * c:128 * (c + 1)], identity=ident_b[:, :]),
                                 reads=[kcb_s, "ident_b"], writes=[PS(b)])
                    for j in range(2):
                        P.op("act", lambda e, q4=q4, j=j, psv=psv: e.activation(
                            out=kt[0:64, j, 8 * q4:8 * q4 + 8, :].rearrange("p (c r) i -> p c r i", r=2)[:, :, 0, :],
                            in_=psv[0:64, :].rearrange("p (c j i) -> p c j i", c=4, j=2)[:, :, j, :], func=AF.Identity),
                            writes=[PS(b)] + [(kt_s, 2 * (4 * q4 + c4)) for c4 in range(4)])
                        P.op("dve", lambda e, q4=q4, j=j, psv=psv: e.tensor_copy(
                            out=kt[0:64, j, 8 * q4:8 * q4 + 8, :].rearrange("p (c r) i -> p c r i", r=2)[:, :, 1, :],
                            in_=psv[64:128, :].rearrange("p (c j i) -> p c j i", c=4, j=2)[:, :, j, :]),
                            writes=[PS(b)] + [(kt_s, 2 * (4 * q4 + c4) + 1) for c4 in range(4)])
                if s == 0 and grp == 0:
                    KTG0 = [(kt_s, h) for h in range(NH)] + RKH + [(kt_s, "c0")] + [(kt_s, "c1", g) for g in range(4)]
                    tap('s_kt0', kt[0:80, :, :, :], KTG0)
                    tap('s_vc0', vcb[:, :, :], [vcb_s])

            def F(grp, blk, bi, s=s, qs=qs, pp=pp):
                new = grp == 16
                if not new:
                    kt, kt_s = ktc[grp % 2]
                    KTG = [(kt_s, h) for h in range(NH)] + [(kt_s, "rk", h) for h in range(NH)] + [(kt_s, "c0")] + [(kt_s, "c1", g) for g in range(4)]
                pts = []
                for hb in range(2):
                    b = nb6s()
                    pt, pt_s = PT[(2 * (bi % 2) + hb) % 4]
                    pts.append((pt, pt_s))
                    for hh in range(8):
                        h = 8 * hb + hh
                        if new:
                            P.op("pe", lambda e, h=h, hh=hh, b=b: e.matmul(psb[b][pp:pp + 64, 64 * hh:64 * (hh + 1)], lhsT=KTv[0:80, h, qs], rhs=QT[0:80, h, qs],
                                                                          start=True, stop=False, skip_group_check=True), reads=KTNEW + QTA, writes=[PS(b)])
                            P.op("pe", lambda e, hh=hh, b=b: e.matmul(psb[b][pp:pp + 64, 64 * hh:64 * (hh + 1)], lhsT=ident_b[:, 0:64], rhs=maskb[:, 0:64],
                                                                     start=False, stop=True, skip_group_check=True), reads=["ident_b", "maskb"], writes=[PS(b)])
                        else:
                            P.op("pe", lambda e, h=h, hh=hh, b=b: e.matmul(psb[b][:, 64 * hh:64 * (hh + 1)], lhsT=kt[0:80, blk, h, :], rhs=QT[0:80, h, qs],
                                                                          start=True, stop=True, skip_group_check=True), reads=KTG + QTA, writes=[PS(b)])
                    if new:
                        P.op("pool", lambda e, pt=pt: e.memset(pt[:, :], 0.0), writes=[pt_s])
                        P.op("act", lambda e, b=b, pt=pt: e.activation(out=pt[pp:pp + 64, :], in_=psb[b][pp:pp + 64, :], func=AF.Exp, scale=0.125), writes=[PS(b), pt_s])
                    else:
                        P.op("act", lambda e, b=b, pt=pt: e.activation(out=pt[:, :], in_=psb[b][:, :], func=AF.Exp, scale=0.125), writes=[PS(b), pt_s])
                if s == 0 and grp == 0 and blk == 0:
                    tap('s_pt0', pts[0][0][:, :], [pts[0][1]])
                return pts

            def B(grp, blk, pts, s=s):
                new = grp == 16
                if not new:
                    vcb, vcb_s = vc_[grp % 2]
                for hb in range(2):
                    pt, pt_s = pts[hb]
                    if grp == 0 and blk == 0:
                        P.op("pe", lambda e, hb=hb, pt=pt: e.matmul(psb[OB[hb]][:, :], lhsT=zerob[:, :], rhs=pt[:, :], start=True, stop=False, skip_group_check=True),
                             reads=["zerob", pt_s], writes=[PS(OB[hb])])
                    for hh in range(8):
                        h = 8 * hb + hh
                        if new:
                            P.op("pe", lambda e, h=h, hh=hh, hb=hb, pt=pt: e.matmul(psb[OB[hb]][0:64, 64 * hh:64 * (hh + 1)], lhsT=Vn[:, s // 2, 64 * h:64 * (h + 1)],
                                                                                   rhs=pt[:, 64 * hh:64 * (hh + 1)], start=False, stop=True, skip_group_check=True),
                                 reads=[(Vn_s, s // 2), pt_s], writes=[PS(OB[hb])])
                        else:
                            P.op("pe", lambda e, h=h, hh=hh, hb=hb, pt=pt: e.matmul(psb[OB[hb]][0:64, 64 * hh:64 * (hh + 1)], lhsT=vcb[:, blk, 64 * h:64 * (h + 1)],
                                                                                   rhs=pt[:, 64 * hh:64 * (hh + 1)], start=False, stop=False, skip_group_check=True),
                                 reads=[vcb_s, pt_s], writes=[PS(OB[hb])])
                    P.op("pe", lambda e, hb=hb, pt=pt: e.matmul(psb[OB[hb]][64:128, :], lhsT=onesb[:, :], rhs=pt[:, :], start=False, stop=new, skip_group_check=True),
                         reads=["onesb", pt_s], writes=[PS(OB[hb])])

            blocks = [(grp, blk) for grp in range(16) for blk in range(2)] + [(16, 0)]
            T(0)
            prev = None
            for bi, (grp, blk) in enumerate(blocks):
                pts = F(grp, blk, bi)
                if prev is not None:
                    B(*prev)
                if blk == 0 and grp + 1 < 16:
                    T(grp + 1)
                prev = (grp, blk, pts)
            B(*prev)

            if debug and s == 0:
                dbgn, dbgn_s = alloc([128, 512], F32)
                P.op("act", lambda e, dbgn=dbgn: e.activation(out=dbgn[:, :], in_=psb[OB[0]][:, :], func=AF.Identity), writes=[PS(OB[0]), dbgn_s])
                tap('s_num0', dbgn[:, :], [dbgn_s])
            for hb in range(2):
                rc, rc_s = rec[hb]
                P.op("dve", lambda e, rc=rc, hb=hb: e.reciprocal(out=rc[0:64, :], in_=psb[OB[hb]][64:128, :]), writes=[PS(OB[hb]), rc_s])
                if s == 0 and hb == 0:
                    tap('s_rc0', rc[0:64, :], [rc_s])
                for par in range(2):
                    P.op("dve", lambda e, rc=rc, hb=hb, par=par, qs=qs: e.tensor_tensor(
                        out=oT[64 * par:64 * par + 64, 4 * hb:4 * hb + 4, qs],
                        in0=psb[OB[hb]][0:64, :].rearrange("p (c r q) -> p c r q", c=4, r=2)[:, :, par, :],
                        in1=rc[0:64, :].rearrange("p (c r q) -> p c r q", c=4, r=2)[:, :, par, :], op=ALU.mult),
                        reads=[rc_s], writes=[PS(OB[hb])] + [("oT", 4 * hb + c, par) for c in range(4)])

    rr6 = [0]

    def nb6s():
        b = rr6[0] % 4
        rr6[0] += 1
        return b

    P.end_scope()
    if do_sample:
        sample_prepass()
    for ti in range(ntiles):
        run_tile("p", ti)
    if do_sample:
        run_tile("s", 0)
    assert wst["pos"] == len(wseq)
    P.emit(final_sems=allsems)
    return nc, P


_CACHE = {}


def _f32(a):
    return np.ascontiguousarray(np.asarray(a, dtype=np.float32))


def kernel(x_prompt, x_sample, cache_k, cache_v, cache_logf, state_conv, state_ffn_conv,
           p_prompt, p_sample, ln0_g, ln0_b, w_in, b_f, conv_dw_w, conv_dw_b, conv_ln_g,
           conv_ln_b, w_conv_out, w_attn_out, w_o, ln1_g, ln1_b, w_ffn_up, ffn_dw_w, ffn_dw_b,
           w_ffn_down, ln2_g, ln2_b, w_ple, w_ple_gate):
    if "nc" not in _CACHE:
        _CACHE["nc"] = build_nc()[0]
    nc = _CACHE["nc"]
    shared = {
        "ln0_g": _f32(ln0_g), "ln0_b": _f32(ln0_b), "w_in": _f32(w_in)[0], "b_f": _f32(b_f)[0],
        "conv_dw_w": _f32(conv_dw_w)[0], "conv_dw_b": _f32(conv_dw_b)[0], "conv_ln_g": _f32(conv_ln_g)[0],
        "conv_ln_b": _f32(conv_ln_b)[0], "w_conv_out": _f32(w_conv_out)[0], "w_attn_out": _f32(w_attn_out)[0],
        "w_o": _f32(w_o)[0], "ln1_g": _f32(ln1_g)[0], "ln1_b": _f32(ln1_b)[0], "w_ffn_up": _f32(w_ffn_up)[0],
        "ffn_dw_w": _f32(ffn_dw_w)[0], "ffn_dw_b": _f32(ffn_dw_b)[0], "w_ffn_down": _f32(w_ffn_down)[0],
        "ln2_g": _f32(ln2_g)[0], "ln2_b": _f32(ln2_b)[0], "w_ple": _f32(w_ple)[0], "w_ple_gate": _f32(w_ple_gate)[0],
    }
    x_prompt = _f32(x_prompt); p_prompt = _f32(p_prompt)[0]
    x_sample = _f32(x_sample); p_sample = _f32(p_sample)[0]
    ck = _f32(cache_k)[0].reshape(32, PAST, D); cv = _f32(cache_v)[0].reshape(32, PAST, D)
    clf = _f32(cache_logf)[0]; sc = _f32(state_conv)[0]; sf = _f32(state_ffn_conv)[0]
    in_maps = []
    for c in range(NCORES):
        m = dict(shared)
        s0 = NSTR * c
        m.update({
            "x_p": x_prompt[c], "p_p": p_prompt[c],
            "x_s": x_sample[s0:s0 + NSTR].reshape(NSTR * DSEQ, D), "p_s": p_sample[s0:s0 + NSTR].reshape(NSTR * DSEQ, PLE),
            "cache_k": ck[s0:s0 + NSTR], "cache_v": cv[s0:s0 + NSTR], "cache_lf": clf[s0:s0 + NSTR],
            "st_conv": sc[s0:s0 + NSTR], "st_ffn": sf[s0:s0 + NSTR],
        })
        in_maps.append(m)
    res = run_bass_kernel_spmd(nc, in_maps, core_ids=list(range(NCORES)))
    R = res.results

    def cat(name, shape):
        return np.stack([np.asarray(r[name], dtype=np.float32) for r in R], 0).reshape(shape)

    y_p = cat("y_p", (8, SEQ, D))
    y_s = cat("y_s", (32, DSEQ, D))
    k_p = cat("k_p", (1, 8, SEQ, NH, 64)); v_p = cat("v_p", (1, 8, SEQ, NH, 64))
    lf_p = cat("lf_p", (1, 8, SEQ, NH))
    cv_p = cat("cv_p", (1, 8, 30, D)); ff_p = cat("ff_p", (1, 8, 2, DFF))
    k_s = cat("k_s", (1, 32, DSEQ, NH, 64)); v_s = cat("v_s", (1, 32, DSEQ, NH, 64))
    lf_s = cat("lf_s", (1, 32, DSEQ, NH))
    cv_s = cat("cv_s", (1, 32, 30, D)); ff_s = cat("ff_s", (1, 32, 2, DFF))
    return (y_p, y_s, k_p, v_p, lf_p, cv_p, ff_p, k_s, v_s, lf_s, cv_s, ff_s)
```
